# Optimizing a Trainium2 kernel written in Bass

```python
import math
import jax, jax.numpy as jnp
from jax import lax
import numpy as np

D_MODEL = 2048
BATCH = 4
SEQ = 4096
DEPTH = 2

EPS = 1e-5
RWKV_HEAD = 64
RWKV_HEADS = D_MODEL // 2 // RWKV_HEAD
RWKV_DIM = RWKV_HEADS * RWKV_HEAD
DECAY_LORA = 64
ICLR_LORA = 64
GATE_LORA = 128
RWKV_LN_EPS = 64e-5
RWKV_PROJ = 3 * RWKV_DIM + DECAY_LORA + ICLR_LORA + GATE_LORA
RWKV_SPLITS = [RWKV_DIM, 2 * RWKV_DIM, 3 * RWKV_DIM, 3 * RWKV_DIM + DECAY_LORA,
               3 * RWKV_DIM + DECAY_LORA + ICLR_LORA]
GDN_HEAD = 128
GDN_HEADS = D_MODEL // 2 // GDN_HEAD
GDN_DIM = GDN_HEADS * GDN_HEAD
GDN_CONV = 4
GDN_CHUNK = 64
GDN_PROJ = 4 * GDN_DIM + 2 * GDN_HEADS
MIX_DIM = RWKV_DIM + GDN_DIM
PROJ_EVEN = RWKV_PROJ + GDN_PROJ
SSM_DINNER = 2 * D_MODEL
SSM_HEAD = 64
SSM_HEADS = SSM_DINNER // SSM_HEAD
SSM_GROUPS = 8
SSM_STATE = 128
SSM_CONV = 4
SSM_CHUNK = 128
SSM_CONV_DIM = SSM_DINNER + 2 * SSM_GROUPS * SSM_STATE
SSM_PROJ = SSM_DINNER + SSM_CONV_DIM + SSM_HEADS
FFN_HIDDEN = ((8 * D_MODEL + 767) // 768) * 256

kernel_name = "hybrid_rwkv7_gdn_mamba2_adaln"


def rms_norm(x, eps=EPS):
    xf = x.astype(jnp.float32)
    return (xf * lax.rsqrt(jnp.mean(xf * xf, -1, keepdims=True) + eps)).astype(x.dtype)


def l2_normalize(x, eps=1e-6):
    return x * lax.rsqrt(jnp.sum(x * x, -1, keepdims=True) + eps)


def ada_modulation(c, w, b):
    mod = (jax.nn.silu(c) @ w + b)[:, None, :]
    shift, scale, gate = jnp.split(mod, 3, axis=-1)
    return shift, scale, gate


def token_shift(p):
    return jnp.pad(p, ((0, 0), (1, 0), (0, 0)))[:, :-1]


def causal_conv1d(x, w, b=None):
    K, C = w.shape
    y = lax.conv_general_dilated(x, w.astype(x.dtype)[:, None, :], window_strides=(1,),
                                 padding=[(K - 1, 0)], dimension_numbers=('NWC', 'WIO', 'NWC'),
                                 feature_group_count=C)
    return y if b is None else y + b


def rwkv7_mix(p, mu, w0, w2, a0, a2, g2, k_k, k_a, r_k, ln_w, ln_b):
    p = p.astype(jnp.float32)
    B, T, _ = p.shape
    p = p + mu * (token_shift(p) - p)
    r, k, v, pw, pa, pg = jnp.split(p, RWKV_SPLITS, axis=-1)
    w = -jax.nn.softplus(-(w0 + jnp.tanh(pw) @ w2)) - 0.5
    a = jax.nn.sigmoid(a0 + pa @ a2)
    g = jax.nn.sigmoid(pg) @ g2
    heads = lambda t: t.reshape(B, T, RWKV_HEADS, RWKV_HEAD)
    kk = l2_normalize(heads(k * k_k), 1e-24)
    k = k * (1.0 + (a - 1.0) * k_a)
    r, k, v, a = heads(r), heads(k), heads(v), heads(a)
    decay = jnp.exp(-jnp.exp(heads(w)))
    b = kk * a

    def step(S, inp):
        r_t, d_t, k_t, v_t, kk_t, b_t = inp
        sa = jnp.einsum('bhvk,bhk->bhv', S, kk_t)
        S = S * d_t[:, :, None, :] - sa[..., None] * b_t[:, :, None, :] + v_t[..., None] * k_t[:, :, None, :]
        return S, jnp.einsum('bhvk,bhk->bhv', S, r_t)

    S0 = jnp.zeros((B, RWKV_HEADS, RWKV_HEAD, RWKV_HEAD), jnp.float32)
    xs = tuple(jnp.moveaxis(t, 1, 0) for t in (r, decay, k, v, kk, b))
    _, y = lax.scan(step, S0, xs)
    y = jnp.moveaxis(y, 0, 1)
    mean = jnp.mean(y, -1, keepdims=True)
    var = jnp.mean(jnp.square(y - mean), -1, keepdims=True)
    y = ((y - mean) * lax.rsqrt(var + RWKV_LN_EPS)).reshape(B, T, RWKV_DIM) * ln_w + ln_b
    y = y + (jnp.sum(r * k * r_k, -1, keepdims=True) * v).reshape(B, T, RWKV_DIM)
    return y * g


def chunk_gated_delta_rule(q, k, v, g, beta):
    B, T, H, Dk = q.shape
    Dv = v.shape[-1]
    C = GDN_CHUNK
    N = T // C

    def chunks(t):
        t = jnp.moveaxis(t, 2, 1)
        return t.reshape(B, H, N, C, *t.shape[3:])

    q, k, v, g, beta = map(chunks, (q, k, v, g, beta))
    gc = jnp.cumsum(g, axis=-1)
    causal = jnp.tril(jnp.ones((C, C), bool))
    strict = jnp.tril(jnp.ones((C, C), bool), -1)
    decay = jnp.exp(jnp.where(causal, gc[..., :, None] - gc[..., None, :], -jnp.inf))
    kb = k * beta[..., None]
    a_mat = jnp.where(strict, jnp.einsum('bhnck,bhnsk->bhncs', kb, k) * decay, 0.0)
    eye = jnp.eye(C, dtype=a_mat.dtype)
    rhs = jnp.concatenate([v * beta[..., None], kb * jnp.exp(gc)[..., None]], -1)
    sol = lax.linalg.triangular_solve(a_mat + eye, rhs, left_side=True, lower=True, unit_diagonal=True)
    u, w = sol[..., :Dv], sol[..., Dv:]
    qk = jnp.einsum('bhnck,bhnsk->bhncs', q, k) * decay
    qg = q * jnp.exp(gc)[..., None]
    g_last = gc[..., -1]
    kd = k * jnp.exp(g_last[..., None] - gc)[..., None]

    def step(S, inp):
        u_c, w_c, qk_c, qg_c, kd_c, gl_c = inp
        v_new = u_c - jnp.einsum('bhck,bhkv->bhcv', w_c, S)
        o_c = jnp.einsum('bhck,bhkv->bhcv', qg_c, S) + jnp.einsum('bhcs,bhsv->bhcv', qk_c, v_new)
        S = S * jnp.exp(gl_c)[..., None, None] + jnp.einsum('bhck,bhcv->bhkv', kd_c, v_new)
        return S, o_c

    xs = tuple(jnp.moveaxis(t, 2, 0) for t in (u, w, qk, qg, kd, g_last))
    S0 = jnp.zeros((B, H, Dk, Dv), jnp.float32)
    _, o = lax.scan(step, S0, xs)
    o = jnp.moveaxis(o, 0, 2).reshape(B, H, T, Dv)
    return jnp.moveaxis(o, 1, 2)


def gated_deltanet_mix(p, conv_w, a_log, dt_bias, norm_w):
    p = p.astype(jnp.float32)
    B, T, _ = p.shape
    qkv, z, pb, pa = jnp.split(p, [3 * GDN_DIM, 4 * GDN_DIM, 4 * GDN_DIM + GDN_HEADS], axis=-1)
    qkv = jax.nn.silu(causal_conv1d(qkv, conv_w))
    q, k, v = [t.reshape(B, T, GDN_HEADS, GDN_HEAD) for t in jnp.split(qkv, 3, axis=-1)]
    q = l2_normalize(q) * GDN_HEAD ** -0.5
    k = l2_normalize(k)
    beta = jax.nn.sigmoid(pb)
    g = -jnp.exp(a_log) * jax.nn.softplus(pa + dt_bias)
    o = chunk_gated_delta_rule(q, k, v, g, beta)
    o = rms_norm(o) * norm_w * jax.nn.silu(z.reshape(B, T, GDN_HEADS, GDN_HEAD))
    return o.reshape(B, T, GDN_DIM)


def ssd_chunked_scan(x, dt, A, bm, cm):
    B, T, H, P = x.shape
    G, S = bm.shape[2], bm.shape[3]
    Hg = H // G
    C = SSM_CHUNK
    N = T // C
    xdt = (x * dt[..., None]).reshape(B, N, C, G, Hg, P)
    a = (dt * A).reshape(B, N, C, G, Hg)
    bm = bm.reshape(B, N, C, G, S)
    cm = cm.reshape(B, N, C, G, S)
    causal = jnp.tril(jnp.ones((C, C), bool))[:, :, None, None]

    def step(state, inp):
        x_c, a_c, b_c, c_c = inp
        acum = jnp.cumsum(a_c, axis=1)
        L = jnp.exp(jnp.where(causal, acum[:, :, None] - acum[:, None, :], -jnp.inf))
        cb = jnp.einsum('blgn,bsgn->blsg', c_c, b_c)
        y = jnp.einsum('blsg,blsgh,bsghp->blghp', cb, L, x_c)
        y = y + jnp.einsum('blgn,bghpn->blghp', c_c, state) * jnp.exp(acum)[..., None]
        a_last = acum[:, -1]
        w_s = jnp.exp(a_last[:, None] - acum)
        state = state * jnp.exp(a_last)[..., None, None] + jnp.einsum('bsgn,bsgh,bsghp->bghpn', b_c, w_s, x_c)
        return state, y

    xs = tuple(jnp.moveaxis(t, 1, 0) for t in (xdt, a, bm, cm))
    state0 = jnp.zeros((B, G, Hg, P, S), jnp.float32)
    _, y = lax.scan(step, state0, xs)
    return jnp.moveaxis(y, 0, 1).reshape(B, T, H, P)


def mamba2_mix(h, w_in, conv_w, conv_b, dt_bias, a_log, d_skip, norm_w, w_out):
    B, T, _ = h.shape
    zxbcdt = (h @ w_in).astype(jnp.float32)
    z, xbc, dt = jnp.split(zxbcdt, [SSM_DINNER, SSM_DINNER + SSM_CONV_DIM], axis=-1)
    xbc = jax.nn.silu(causal_conv1d(xbc, conv_w, conv_b))
    xs, bm, cm = jnp.split(xbc, [SSM_DINNER, SSM_DINNER + SSM_GROUPS * SSM_STATE], axis=-1)
    xs = xs.reshape(B, T, SSM_HEADS, SSM_HEAD)
    bm = bm.reshape(B, T, SSM_GROUPS, SSM_STATE)
    cm = cm.reshape(B, T, SSM_GROUPS, SSM_STATE)
    dt = jax.nn.softplus(dt + dt_bias)
    y = ssd_chunked_scan(xs, dt, -jnp.exp(a_log), bm, cm)
    y = y + xs * d_skip[:, None]
    y = (y.reshape(B, T, SSM_DINNER) * jax.nn.silu(z)).reshape(B, T, SSM_GROUPS, SSM_DINNER // SSM_GROUPS)
    y = rms_norm(y).reshape(B, T, SSM_DINNER) * norm_w
    return y.astype(h.dtype) @ w_out


def swiglu(h, w1, w3, w2):
    return (jax.nn.silu(h @ w1) * (h @ w3)) @ w2


def setup_inputs(seed: int = 0) -> dict:
    key = jax.random.key(seed)
    ks = iter(jax.random.split(key, 64))
    nrm = lambda shape, s: jax.random.normal(next(ks), shape, jnp.float32) * s
    uni = lambda shape, lo, hi: jax.random.uniform(next(ks), shape, jnp.float32, lo, hi)
    NE = (DEPTH + 1) // 2
    NO = DEPTH // 2

    def dt_bias(shape):
        dt = jnp.exp(uni(shape, math.log(1e-3), math.log(1e-1)))
        return dt + jnp.log(-jnp.expm1(-dt))

    return {
        "x": nrm((BATCH, SEQ, D_MODEL), 1.0),
        "c": nrm((BATCH, D_MODEL), 1.0),
        "ada_mix_w": nrm((DEPTH, D_MODEL, 3 * D_MODEL), 0.5 * D_MODEL ** -0.5),
        "ada_mix_b": nrm((DEPTH, 3 * D_MODEL), 0.02),
        "ada_ffn_w": nrm((DEPTH, D_MODEL, 3 * D_MODEL), 0.5 * D_MODEL ** -0.5),
        "ada_ffn_b": nrm((DEPTH, 3 * D_MODEL), 0.02),
        "hg_w_in": nrm((NE, D_MODEL, PROJ_EVEN), D_MODEL ** -0.5),
        "hg_w_out": nrm((NE, MIX_DIM, D_MODEL), MIX_DIM ** -0.5),
        "rwkv_mu": uni((NE, RWKV_PROJ), 0.0, 1.0),
        "rwkv_w0": uni((NE, RWKV_DIM), -6.0, -0.5),
        "rwkv_w2": nrm((NE, DECAY_LORA, RWKV_DIM), 0.5 * DECAY_LORA ** -0.5),
        "rwkv_a0": nrm((NE, RWKV_DIM), 0.1),
        "rwkv_a2": nrm((NE, ICLR_LORA, RWKV_DIM), ICLR_LORA ** -0.5),
        "rwkv_g2": nrm((NE, GATE_LORA, RWKV_DIM), GATE_LORA ** -0.5),
        "rwkv_k_k": 0.85 + nrm((NE, RWKV_DIM), 0.02),
        "rwkv_k_a": 1.0 + nrm((NE, RWKV_DIM), 0.02),
        "rwkv_r_k": nrm((NE, RWKV_HEADS, RWKV_HEAD), 0.1),
        "rwkv_ln_w": 1.0 + nrm((NE, RWKV_DIM), 0.02),
        "rwkv_ln_b": nrm((NE, RWKV_DIM), 0.02),
        "gdn_conv_w": nrm((NE, GDN_CONV, 3 * GDN_DIM), GDN_CONV ** -0.5),
        "gdn_a_log": jnp.log(uni((NE, GDN_HEADS), 1.0, 16.0)),
        "gdn_dt_bias": dt_bias((NE, GDN_HEADS)),
        "gdn_norm_w": 1.0 + nrm((NE, GDN_HEAD), 0.02),
        "ssm_w_in": nrm((NO, D_MODEL, SSM_PROJ), D_MODEL ** -0.5),
        "ssm_conv_w": nrm((NO, SSM_CONV, SSM_CONV_DIM), SSM_CONV ** -0.5),
        "ssm_conv_b": nrm((NO, SSM_CONV_DIM), 0.02),
        "ssm_dt_bias": dt_bias((NO, SSM_HEADS)),
        "ssm_a_log": jnp.log(uni((NO, SSM_HEADS), 1.0, 16.0)),
        "ssm_d": 1.0 + nrm((NO, SSM_HEADS), 0.02),
        "ssm_norm_w": 1.0 + nrm((NO, SSM_DINNER), 0.02),
        "ssm_w_out": nrm((NO, SSM_DINNER, D_MODEL), SSM_DINNER ** -0.5),
        "ffn_w1": nrm((DEPTH, D_MODEL, FFN_HIDDEN), D_MODEL ** -0.5),
        "ffn_w3": nrm((DEPTH, D_MODEL, FFN_HIDDEN), D_MODEL ** -0.5),
        "ffn_w2": nrm((DEPTH, FFN_HIDDEN, D_MODEL), FFN_HIDDEN ** -0.5),
        "final_norm_w": 1.0 + nrm((D_MODEL,), 0.02),
    }


def reference(x, c, ada_mix_w, ada_mix_b, ada_ffn_w, ada_ffn_b, hg_w_in, hg_w_out,
              rwkv_mu, rwkv_w0, rwkv_w2, rwkv_a0, rwkv_a2, rwkv_g2, rwkv_k_k, rwkv_k_a,
              rwkv_r_k, rwkv_ln_w, rwkv_ln_b, gdn_conv_w, gdn_a_log, gdn_dt_bias, gdn_norm_w,
              ssm_w_in, ssm_conv_w, ssm_conv_b, ssm_dt_bias, ssm_a_log, ssm_d, ssm_norm_w,
              ssm_w_out, ffn_w1, ffn_w3, ffn_w2, final_norm_w):
    for i in range(DEPTH):
        j = i // 2
        shift, scale, gate = ada_modulation(c, ada_mix_w[i], ada_mix_b[i])
        h = rms_norm(x) * (1.0 + scale) + shift
        if i % 2 == 0:
            p = h @ hg_w_in[j]
            y_a = rwkv7_mix(p[..., :RWKV_PROJ], rwkv_mu[j], rwkv_w0[j], rwkv_w2[j], rwkv_a0[j],
                            rwkv_a2[j], rwkv_g2[j], rwkv_k_k[j], rwkv_k_a[j], rwkv_r_k[j],
                            rwkv_ln_w[j], rwkv_ln_b[j])
            y_b = gated_deltanet_mix(p[..., RWKV_PROJ:], gdn_conv_w[j], gdn_a_log[j],
                                     gdn_dt_bias[j], gdn_norm_w[j])
            y = jnp.concatenate([y_a, y_b], axis=-1).astype(x.dtype) @ hg_w_out[j]
        else:
            y = mamba2_mix(h, ssm_w_in[j], ssm_conv_w[j], ssm_conv_b[j], ssm_dt_bias[j],
                           ssm_a_log[j], ssm_d[j], ssm_norm_w[j], ssm_w_out[j])
        x = x + gate * y
        shift, scale, gate = ada_modulation(c, ada_ffn_w[i], ada_ffn_b[i])
        h = rms_norm(x) * (1.0 + scale) + shift
        x = x + gate * swiglu(h, ffn_w1[i], ffn_w3[i], ffn_w2[i])
    return rms_norm(x) * final_norm_w
```

```python
import numpy as np
import ml_dtypes
import concourse.bass as bass
import concourse.mybir as mybir
from concourse.bass_utils import run_bass_kernel_spmd

F32 = mybir.dt.float32
BF16 = mybir.dt.bfloat16
AF = mybir.ActivationFunctionType
ALU = mybir.AluOpType
AX = mybir.AxisListType
NPBF = ml_dtypes.bfloat16

D = 2048
TT = 512
TOK = 2048
GS = 8192
EPS = 1e-5


class Prog:
    def __init__(self, n_dma_sems=10, same_engine_sync=True):
        self.nc = bass.Bass("TRN2", target_bir_lowering=False)
        nc = self.nc
        self.eng = {"pe": nc.tensor, "act": nc.scalar, "dve": nc.vector,
                    "pool": nc.gpsimd, "sp": nc.sync}
        self.sem = {e: nc.alloc_semaphore(name="s_" + e) for e in self.eng}
        self.cnt = {e: 0 for e in self.eng}
        self.waited = {e: {} for e in self.eng}
        self.dma_ring = {q: [[nc.alloc_semaphore(name=f"d_{q}{i}"), 0] for i in range(n_dma_sems)]
                         for q in ("sp", "pool", "act")}
        self.dma_pos = {q: 0 for q in self.dma_ring}
        self.last_w = {}
        self.readers = {}
        self.same_engine_sync = same_engine_sync
        self.n_inst = 0
        self._ctx = []

    def sbuf(self, name, shape, dt):
        cm = self.nc.sbuf_tensor("sb_" + getattr(self, "pfx", "") + name, list(shape), dt)
        t = cm.__enter__()
        self._ctx.append(cm)
        return t

    def psum(self, name, shape, dt=F32):
        cm = self.nc.psum_tensor("pp_" + getattr(self, "pfx", "") + name, list(shape), dt)
        t = cm.__enter__()
        self._ctx.append(cm)
        return t

    def dram(self, name, shape, dt, kind):
        return self.nc.dram_tensor(name, list(shape), dt, kind=kind).ap()

    def _wait(self, e, tok):
        if tok is None:
            return
        if tok[0] == "c":
            _, e2, idx = tok
            if e2 == e and not self.same_engine_sync:
                return
            key = ("c", e2)
            if self.waited[e].get(key, 0) >= idx:
                return
            self.eng[e].wait_ge(self.sem[e2], idx)
            self.waited[e][key] = idx
        elif tok[0] == "x":
            _, sem, val, sid = tok
            key = ("x", sid)
            if self.waited[e].get(key, 0) >= val:
                return
            self.eng[e].wait_ge(sem, val)
            self.waited[e][key] = val
        else:
            _, q, slot, val = tok
            key = ("d", q, slot)
            if self.waited[e].get(key, 0) >= val:
                return
            self.eng[e].wait_ge(self.dma_ring[q][slot][0], val)
            self.waited[e][key] = val

    def _deps(self, e, reads, writes):
        for k in reads:
            self._wait(e, self.last_w.get(k))
        for k in writes:
            self._wait(e, self.last_w.get(k))
            for t in self.readers.get(k, ()):
                self._wait(e, t)

    def _commit(self, tok, reads, writes):
        for k in reads:
            self.readers.setdefault(k, []).append(tok)
        for k in writes:
            self.last_w[k] = tok
            self.readers[k] = []

    def op(self, e, fn, reads=(), writes=(), nosync_same=False):
        if nosync_same:
            sv = self.same_engine_sync
            self.same_engine_sync = False
            self._deps(e, reads, writes)
            self.same_engine_sync = sv
        else:
            self._deps(e, reads, writes)
        inst = fn(self.eng[e])
        self.cnt[e] += 1
        inst.then_inc(self.sem[e], 1)
        tok = ("c", e, self.cnt[e])
        self._commit(tok, reads, writes)
        self.n_inst += 1
        return tok

    def dma(self, q, out, in_, reads=(), writes=(), **kw):
        ring = self.dma_ring[q]
        slot = self.dma_pos[q] % len(ring)
        self.dma_pos[q] += 1
        sem, val = ring[slot]
        if val > 0:
            self._wait(q, ("d", q, slot, val))
        self._deps(q, reads, writes)
        inst = self.eng[q].dma_start(out=out, in_=in_, **kw)
        val += 16
        ring[slot][1] = val
        inst.then_inc(sem, 16)
        tok = ("d", q, slot, val)
        self._commit(tok, reads, writes)
        self.n_inst += 1
        return tok

    def scope_begin(self):
        self._nscope = getattr(self, "_nscope", 0) + 1
        self.pfx = f"s{self._nscope}_"
        return len(self._ctx)

    def scope_end(self, mark):
        self.barrier()
        while len(self._ctx) > mark:
            cm = self._ctx.pop()
            cm.__exit__(None, None, None)

    def barrier(self):
        for e in self.eng:
            for e2 in ("pe", "act", "dve", "pool"):
                if e2 != e and self.cnt[e2] > 0:
                    self._wait(e, ("c", e2, self.cnt[e2]))
            for q, ring in self.dma_ring.items():
                for slot, (sem, val) in enumerate(ring):
                    if val > 0:
                        self._wait(e, ("d", q, slot, val))

    def collective(self, kind, op, rg, in_t, out_t, reads, writes):
        if not hasattr(self, "xsems"):
            self.xsems = []
        self._deps("pool", reads, writes)
        inst = self.nc.gpsimd.collective_compute(kind, op, replica_groups=rg, ins=[in_t.ap().opt()], outs=[out_t.ap().opt()])
        sem = self.nc.alloc_semaphore(name=f"cc{len(self.xsems)}")
        inst.then_inc(sem, 1)
        self.xsems.append((sem, 1))
        tok = ("x", sem, 1, len(self.xsems) - 1)
        self._commit(tok, reads, writes)
        return tok

    def finish(self):
        for sid, (sem, val) in enumerate(getattr(self, "xsems", [])):
            self._wait("sp", ("x", sem, val, sid))
        for q, ring in self.dma_ring.items():
            for slot, (sem, val) in enumerate(ring):
                if val > 0:
                    self._wait("sp", ("d", q, slot, val))
        for e in ("pe", "act", "dve", "pool"):
            if self.cnt[e] > 0:
                self._wait("sp", ("c", e, self.cnt[e]))


class WStream:
    def __init__(self, p, ws_ap, seq, nslot=4, q="sp"):
        self.p, self.ws, self.seq, self.nslot, self.q = p, ws_ap, list(seq), nslot, q
        self.slots = [p.sbuf(f"wslot{i}", [128, GS], BF16) for i in range(nslot)]
        self.issued = 0
        self.used = 0

    def _issue(self):
        while self.issued < len(self.seq) and self.issued < self.used + self.nslot:
            s = self.issued % self.nslot
            self.p.dma(self.q, self.slots[s][:], self.ws[self.seq[self.issued]], writes=[("w", s)])
            self.issued += 1

    def next(self):
        self._issue()
        s = self.used % self.nslot
        self.used += 1
        return self.slots[s], ("w", s)

    def prefetch(self):
        self._issue()


class Ctx:
    def __init__(self, p):
        self.p = p
        self.ps = [p.psum(f"ps{i}", [128, 512], F32) for i in range(8)]
        self.ps_i = 0
        self.ev_i = 0
        self.ones = p.sbuf("ones", [128, 128], BF16)
        p.op("dve", lambda e: e.memset(self.ones[:], 1.0), writes=["ones"])

    def next_ps(self):
        i = self.ps_i % 8
        self.ps_i += 1
        return self.ps[i], ("ps", i)

    def ev_eng(self):
        self.ev_i += 1
        return "act" if self.ev_i % 2 else "dve"


def emit_mod(cx, wst, cs_ap, adab_ap, jlist, name):
    p = cx.p
    cs = p.sbuf(name + "_cs", [128, 16], F32)
    scb = p.sbuf(name + "_scb", [128, 16], BF16)
    bias = p.sbuf(name + "_b", [128, 48], F32)
    mod = p.sbuf(name + "_mod", [128, 48], F32)
    p.dma("pool", cs[:], cs_ap, writes=[name + "cs"])
    p.dma("pool", bias[:], adab_ap, writes=[name + "b"])
    p.op("act", lambda e: e.activation(out=scb[:], in_=cs[:], func=AF.Silu), reads=[name + "cs"], writes=[name + "scb"])
    assert len(jlist) % 4 == 0
    for g in range(len(jlist) // 4):
        slot, wk = wst.next()
        ps, pk = cx.next_ps()
        for jj in range(4):
            for kc in range(16):
                off = (jj * 16 + kc) * 128
                p.op("pe", lambda e: e.matmul(ps[:, jj:jj + 1], lhsT=slot[:, off:off + 128], rhs=scb[:, kc:kc + 1],
                                              start=(kc == 0), stop=(kc == 15)),
                     reads=[wk, name + "scb"], writes=[pk], nosync_same=True)
        j0 = jlist[g * 4]
        assert jlist[g * 4:g * 4 + 4] == list(range(j0, j0 + 4))
        p.op("dve", lambda e: e.tensor_tensor(out=mod[:, j0:j0 + 4], in0=ps[:, 0:4], in1=bias[:, j0:j0 + 4], op=ALU.add),
             reads=[pk, name + "b"], writes=[name + "mod"])
    return mod, name + "mod"


def emit_norm_mod(cx, x_sb, xkey, h_sb, hkey, ops_fn, shift_fn, modkeys, tmp_ring, stat):
    p = cx.p
    for kc in range(16):
        p.op("act", lambda e: e.activation(out=h_sb[:, kc, :], in_=x_sb[:, kc, :], func=AF.Square),
             reads=[(xkey, kc)], writes=[(hkey, kc)])
    ps, pk = cx.next_ps()
    for kc in range(16):
        p.op("pe", lambda e: e.matmul(ps[:], lhsT=cx.ones[:], rhs=h_sb[:, kc, :], start=(kc == 0), stop=(kc == 15)),
             reads=["ones", (hkey, kc)], writes=[pk], nosync_same=True)
    rs, rstd = stat
    p.op("act", lambda e: e.activation(out=rs[:], in_=ps[:], func=AF.Sqrt, scale=1.0 / D, bias=cx.epsb[:, 0:1]),
         reads=[pk, "epsb"], writes=["rs"])
    p.op("dve", lambda e: e.reciprocal(out=rstd[:], in_=rs[:]), reads=["rs"], writes=["rstd"])
    if ops_fn is None:
        return
    for kc in range(16):
        t = tmp_ring[kc % len(tmp_ring)]
        tk = ("tmpn", kc % len(tmp_ring))
        p.op("dve", lambda e: e.tensor_tensor(out=t[:], in0=x_sb[:, kc, :], in1=rstd[:], op=ALU.mult),
             reads=[(xkey, kc), "rstd"], writes=[tk])
        p.op("act", lambda e: e.activation(out=h_sb[:, kc, :], in_=t[:], func=AF.Identity,
                                           scale=ops_fn(kc), bias=shift_fn(kc)),
             reads=[tk] + modkeys, writes=[(hkey, kc)])


def emit_gemm(cx, wst, Kc, ncc, rhs_fn, evac_fn, cpg=None, ST=None):
    p = cx.p
    if cpg is None:
        cpg = GS // (Kc * 128)
    n = 0
    while n < ncc:
        slot, wk = wst.next()
        for cc in range(min(cpg, ncc - n)):
            for j in range(ST or 1):
                ps, pk = cx.next_ps()
                for kc in range(Kc):
                    off = (cc * Kc + kc) * 128
                    r_ap, r_key = rhs_fn(kc) if ST is None else rhs_fn(j, kc)
                    p.op("pe", lambda e: e.matmul(ps[:], lhsT=slot[:, off:off + 128], rhs=r_ap,
                                                  start=(kc == 0), stop=(kc == Kc - 1)),
                         reads=[wk, r_key], writes=[pk], nosync_same=True)
                if ST is None:
                    evac_fn(n + cc, ps, pk)
                else:
                    evac_fn(j, n + cc, ps, pk)
        n += cpg


def pack_w(W, Kc=None):
    K, N = W.shape
    Kc = K // 128
    ncc = N // 128
    cpg = GS // (Kc * 128)
    ng = (ncc + cpg - 1) // cpg
    Wp = np.zeros((K, ng * cpg * 128), dtype=W.dtype)
    Wp[:, :N] = W
    A = Wp.reshape(Kc, 128, ng, cpg, 128).transpose(2, 1, 3, 0, 4).reshape(ng, 128, cpg * Kc * 128)
    out = np.zeros((ng, 128, GS), dtype=W.dtype)
    out[:, :, :cpg * Kc * 128] = A
    return out


def build_cast(F):
    p = Prog()
    CH = 8192
    x = p.dram("x", [128, F], F32, "ExternalInput")
    y = p.dram("y", [128, F], BF16, "ExternalOutput")
    NB = 3
    xi = [p.sbuf(f"xi{i}", [128, CH], F32) for i in range(NB)]
    yo = [p.sbuf(f"yo{i}", [128, CH], BF16) for i in range(NB)]
    nch = (F + CH - 1) // CH
    for i in range(nch):
        a, b = i * CH, min(F, (i + 1) * CH)
        s = i % NB
        p.dma("sp", xi[s][:, :b - a], x[:, a:b], writes=[("xi", s)])
        if i % 2 == 0:
            p.op("dve", lambda e: e.tensor_copy(out=yo[s][:, :b - a], in_=xi[s][:, :b - a]), reads=[("xi", s)], writes=[("yo", s)])
        else:
            p.op("act", lambda e: e.activation(out=yo[s][:, :b - a], in_=xi[s][:, :b - a], func=AF.Copy), reads=[("xi", s)], writes=[("yo", s)])
        p.dma("pool", y[:, a:b], yo[s][:, :b - a], reads=[("yo", s)])
    p.finish()
    return p


def build_inproj(ncc, NG_w):
    p = Prog()
    xT = p.dram("xT", [D, TOK], F32, "ExternalInput")
    cs = p.dram("cs", [128, 16], F32, "ExternalInput")
    adab = p.dram("adab", [128, 48], F32, "ExternalInput")
    ws = p.dram("ws", [8 + NG_w, 128, GS], BF16, "ExternalInput")
    pT = p.dram("pT", [ncc * 128, TOK], F32, "ExternalOutput")
    cx = Ctx(p)
    cx.epsb = p.sbuf("epsb", [128, 1], F32)
    p.op("dve", lambda e: e.memset(cx.epsb[:], EPS), writes=["epsb"])
    NTL = TOK // TT
    seq = list(range(8)) + [8 + g for _ in range(NTL) for g in range(NG_w)]
    wst = WStream(p, ws, seq, nslot=4)
    mod, mk = emit_mod(cx, wst, cs, adab, list(range(32)), "m")
    ops = p.sbuf("ops", [128, 16], F32)
    p.op("dve", lambda e: e.tensor_scalar(out=ops[:], in0=mod[:, 16:32], scalar1=1.0, scalar2=None, op0=ALU.add),
         reads=[mk], writes=["ops"])
    xs = p.sbuf("xs", [128, 16, TT], F32)
    hs = p.sbuf("hs", [128, 16, TT], BF16)
    tmp = [p.sbuf(f"tmp{i}", [128, TT], F32) for i in range(3)]
    stat = (p.sbuf("rs", [128, TT], F32), p.sbuf("rstd", [128, TT], F32))
    ost = [p.sbuf(f"ost{i}", [128, TT], F32) for i in range(4)]
    xTv = xT.rearrange("(kc q) t -> q kc t", q=128)
    for tl in range(NTL):
        t0 = tl * TT
        for kc4 in range(4):
            p.dma("pool", xs[:, kc4 * 4:(kc4 + 1) * 4, :], xTv[:, kc4 * 4:(kc4 + 1) * 4, t0:t0 + TT],
                  writes=[("x", kc) for kc in range(kc4 * 4, kc4 * 4 + 4)])
        emit_norm_mod(cx, xs, "x", hs, "h", lambda kc: ops[:, kc:kc + 1], lambda kc: mod[:, kc:kc + 1],
                      [mk, "ops"], tmp, stat)

        def evac(n, ps, pk):
            o = ost[n % 4]
            ok = ("ost", n % 4)
            if cx.ev_eng() == "act":
                p.op("act", lambda e: e.activation(out=o[:], in_=ps[:], func=AF.Copy), reads=[pk], writes=[ok])
            else:
                p.op("dve", lambda e: e.tensor_copy(out=o[:], in_=ps[:]), reads=[pk], writes=[ok])
            p.dma("pool", pT[n * 128:(n + 1) * 128, t0:t0 + TT], o[:], reads=[ok])

        emit_gemm(cx, wst, 16, ncc, lambda kc: (hs[:, kc, :], ("h", kc)), evac)
    p.finish()
    return p


def build_outffn(Kmix_c, NG_out, NG_13, NG_2, final):
    p = Prog()
    xT = p.dram("xT", [D, TOK], F32, "ExternalInput")
    yT = p.dram("yT", [Kmix_c * 128, TOK], F32, "ExternalInput")
    cs = p.dram("cs", [128, 16], F32, "ExternalInput")
    adab_m = p.dram("adab_m", [128, 48], F32, "ExternalInput")
    adab_f = p.dram("adab_f", [128, 48], F32, "ExternalInput")
    NGT = 4 + 12 + NG_out + NG_13 + NG_2
    ws = p.dram("ws", [NGT, 128, GS], BF16, "ExternalInput")
    oT = p.dram("oT", [D, TOK], F32, "ExternalOutput")
    if final:
        fnw = p.dram("fnw", [128, 16], F32, "ExternalInput")
    cx = Ctx(p)
    cx.epsb = p.sbuf("epsb", [128, 1], F32)
    p.op("dve", lambda e: e.memset(cx.epsb[:], EPS), writes=["epsb"])
    NTL = TOK // TT
    seq = list(range(16)) + [16 + g for _ in range(NTL) for g in range(NG_out + NG_13 + NG_2)]
    wst = WStream(p, ws, seq, nslot=3)
    modm, mmk = emit_mod(cx, wst, cs, adab_m, list(range(32, 48)), "mm")
    modf, mfk = emit_mod(cx, wst, cs, adab_f, list(range(48)), "mf")
    ops = p.sbuf("ops", [128, 16], F32)
    p.op("dve", lambda e: e.tensor_scalar(out=ops[:], in0=modf[:, 16:32], scalar1=1.0, scalar2=None, op0=ALU.add),
         reads=[mfk], writes=["ops"])
    if final:
        fw = p.sbuf("fw", [128, 16], F32)
        p.dma("pool", fw[:], fnw, writes=["fw"])
    acc = p.sbuf("acc", [128, 16, TT], F32)
    hs = p.sbuf("hs", [128, 16, TT], BF16)
    big = p.sbuf("big", [128, 44, TT], BF16)
    yst = [p.sbuf(f"yst{i}", [128, 2, TT], F32) for i in range(2)]
    tmp = [p.sbuf(f"tmp{i}", [128, TT], F32) for i in range(3)]
    stat = (p.sbuf("rs", [128, TT], F32), p.sbuf("rstd", [128, TT], F32))
    xTv = xT.rearrange("(kc q) t -> q kc t", q=128)
    yTv = yT.rearrange("(kc q) t -> q kc t", q=128)
    oTv = oT.rearrange("(kc q) t -> q kc t", q=128)
    for tl in range(NTL):
        t0 = tl * TT
        for kc4 in range(4):
            p.dma("pool", acc[:, kc4 * 4:(kc4 + 1) * 4, :], xTv[:, kc4 * 4:(kc4 + 1) * 4, t0:t0 + TT],
                  writes=[("acc", kc) for kc in range(kc4 * 4, kc4 * 4 + 4)])
        for k2 in range(Kmix_c // 2):
            s = k2 % 2
            p.dma("pool", yst[s][:], yTv[:, k2 * 2:k2 * 2 + 2, t0:t0 + TT], writes=[("yst", s)])
            if k2 % 2 == 0:
                p.op("dve", lambda e: e.tensor_copy(out=big[:, k2 * 2:k2 * 2 + 2, :], in_=yst[s][:]),
                     reads=[("yst", s)], writes=[("big", k2 * 2), ("big", k2 * 2 + 1)])
            else:
                p.op("act", lambda e: e.activation(out=big[:, k2 * 2:k2 * 2 + 2, :], in_=yst[s][:], func=AF.Copy),
                     reads=[("yst", s)], writes=[("big", k2 * 2), ("big", k2 * 2 + 1)])

        def evac_res(gate_sb, gk):
            def f(n, ps, pk):
                p.op("dve", lambda e: e.scalar_tensor_tensor(out=acc[:, n, :], in0=ps[:], scalar=gate_sb[:, 32 + n:33 + n],
                                                             in1=acc[:, n, :], op0=ALU.mult, op1=ALU.add),
                     reads=[pk, gk, ("acc", n)], writes=[("acc", n)])
            return f

        emit_gemm(cx, wst, Kmix_c, 16, lambda kc: (big[:, kc, :], ("big", kc)), evac_res(modm, mmk))
        emit_norm_mod(cx, acc, "acc", hs, "h", lambda kc: ops[:, kc:kc + 1], lambda kc: modf[:, kc:kc + 1],
                      [mfk, "ops"], tmp, stat)
        sbuf_s = tmp

        def evac13(n, ps, pk):
            q, r = n // 4, n % 4
            if r < 2:
                j = q * 2 + r
                p.op("act", lambda e: e.activation(out=sbuf_s[r][:], in_=ps[:], func=AF.Silu), reads=[pk], writes=[("tmpn", r)])
            else:
                j = q * 2 + (r - 2)
                p.op("dve", lambda e: e.tensor_tensor(out=big[:, j, :], in0=ps[:], in1=sbuf_s[r - 2][:], op=ALU.mult),
                     reads=[pk, ("tmpn", r - 2)], writes=[("big", j)])

        emit_gemm(cx, wst, 16, 88, lambda kc: (hs[:, kc, :], ("h", kc)), evac13)
        emit_gemm(cx, wst, 44, 16, lambda kc: (big[:, kc, :], ("big", kc)), evac_res(modf, mfk))
        if final:
            emit_norm_mod(cx, acc, "acc", hs, "h", None, None, [], tmp, stat)
            for kc in range(16):
                t = tmp[kc % 3]
                tk = ("tmpn", kc % 3)
                p.op("dve", lambda e: e.scalar_tensor_tensor(out=acc[:, kc, :], in0=acc[:, kc, :], scalar=fw[:, kc:kc + 1],
                                                             in1=stat[1][:], op0=ALU.mult, op1=ALU.mult),
                     reads=[("acc", kc), "fw", "rstd"], writes=[("acc", kc)])
        for kc4 in range(4):
            p.dma("pool", oTv[:, kc4 * 4:(kc4 + 1) * 4, t0:t0 + TT], acc[:, kc4 * 4:(kc4 + 1) * 4, :],
                  reads=[("acc", kc) for kc in range(kc4 * 4, kc4 * 4 + 4)])
    p.finish()
    return p


_PROGS = {}


def _prog(key, fn):
    if key not in _PROGS:
        _PROGS[key] = fn()
    return _PROGS[key]


def _run(p, in_maps):
    res = run_bass_kernel_spmd(p.nc, in_maps, core_ids=list(range(8)))
    return res.results


def cast_weights(wd):
    names = list(wd)
    parts = [np.ascontiguousarray(wd[n]).reshape(8, 128, -1) for n in names]
    sizes = [q.shape[2] for q in parts]
    F = sum(sizes)
    p = _prog(("cast", F), lambda: build_cast(F))
    in_maps = [{"x": np.ascontiguousarray(np.concatenate([q[i] for q in parts], axis=1))} for i in range(8)]
    r = _run(p, in_maps)
    ys = [np.asarray(r[i]["y"]) for i in range(8)]
    out = {}
    o = 0
    for n, sz in zip(names, sizes):
        out[n] = np.stack([ys[i][:, o:o + sz] for i in range(8)]).reshape(wd[n].shape)
        o += sz
    return out


def fm(a):
    return np.ascontiguousarray(a.reshape(-1, 128).T)


def pad_cols(W, n):
    if W.shape[1] == n:
        return W
    o = np.zeros((W.shape[0], n), dtype=W.dtype)
    o[:, :W.shape[1]] = W
    return o


def run_inproj(xT_cores, c, adaw_bf, adab, W_bf):
    ncc = W_bf.shape[1] // 128
    wpk = pack_w(W_bf)
    apk = pack_w(adaw_bf[:, :4096])
    ws = np.ascontiguousarray(np.concatenate([apk, wpk], axis=0))
    p = _prog(("inproj", ncc), lambda: build_inproj(ncc, wpk.shape[0]))
    in_maps = [{"xT": xT_cores[i], "cs": fm(c[i // 2]), "adab": fm(adab), "ws": ws} for i in range(8)]
    r = _run(p, in_maps)
    return [np.asarray(r[i]["pT"]) for i in range(8)]


def run_outffn(xT_cores, yT_cores, c, adaw_m, adab_m, adaw_f, adab_f, Wout, W1, W3, W2, fnw=None):
    Kc = Wout.shape[0] // 128
    W13 = np.concatenate([np.concatenate([W1[:, q * 256:(q + 1) * 256], W3[:, q * 256:(q + 1) * 256]], axis=1)
                          for q in range(22)], axis=1)
    pk = [pack_w(adaw_m[:, 4096:]), pack_w(adaw_f), pack_w(Wout), pack_w(W13), pack_w(W2)]
    ws = np.ascontiguousarray(np.concatenate(pk, axis=0))
    final = fnw is not None
    p = _prog(("outffn", Kc, final), lambda: build_outffn(Kc, pk[2].shape[0], pk[3].shape[0], pk[4].shape[0], final))
    in_maps = []
    for i in range(8):
        m = {"xT": xT_cores[i], "yT": yT_cores[i], "cs": fm(c[i // 2]), "adab_m": fm(adab_m), "adab_f": fm(adab_f), "ws": ws}
        if final:
            m["fnw"] = fm(fnw)
        in_maps.append(m)
    r = _run(p, in_maps)
    return [np.asarray(r[i]["oT"]) for i in range(8)]


SEQ = 4096


def ev_copy(p, eng, out, in_, reads, writes):
    if eng == "act":
        return p.op("act", lambda e: e.activation(out=out, in_=in_, func=AF.Copy), reads=reads, writes=writes)
    return p.op(eng, lambda e: e.tensor_copy(out=out, in_=in_), reads=reads, writes=writes)


def build_ssd(p=None, io=None):
    standalone = p is None
    if standalone:
        p = Prog()
    pre = "" if standalone else "ssd_"
    NH, NG_, CH = 32, 4, 128
    TS = 256
    zT = io["zT"] if io else p.dram("zT", [2048, SEQ], F32, "ExternalInput")
    xT = io["xT"] if io else p.dram("xT", [2048, SEQ + 3], F32, "ExternalInput")
    bcT = io["bcT"] if io else p.dram("bcT", [1024, SEQ + 3], F32, "ExternalInput")
    dtT = io["dtT"] if io else p.dram("dtT", [32, SEQ], F32, "ExternalInput")
    cwx_d = p.dram(pre + "cwx", [128, 16, 4], F32, "ExternalInput")
    cbx_d = p.dram(pre + "cbx", [128, 16], F32, "ExternalInput")
    cwbc_d = p.dram(pre + "cwbc", [128, 8, 4], F32, "ExternalInput")
    cbbc_d = p.dram(pre + "cbbc", [128, 8], F32, "ExternalInput")
    hp_d = p.dram(pre + "hp", [32, 2], F32, "ExternalInput")
    dsk_d = p.dram(pre + "dsk", [128, 16], F32, "ExternalInput")
    nw_d = p.dram(pre + "nw", [128, 16], F32, "ExternalInput")
    ident_d = p.dram(pre + "ident", [128, 128], F32, "ExternalInput")
    selh_d = p.dram(pre + "selh", [32, 32, 128], F32, "ExternalInput")
    maskT_d = p.dram(pre + "maskT", [128, 128], F32, "ExternalInput")
    yT = io["yT"] if io else p.dram("yT", [2048, SEQ], F32, "ExternalOutput")
    cx = Ctx(p)

    def const(name, shape, src):
        t = p.sbuf(name, shape, F32)
        p.dma("pool", t[:], src, writes=[name])
        return t
    cwx = const("cwx", [128, 16, 4], cwx_d)
    cbx = const("cbx", [128, 16], cbx_d)
    cwbc = const("cwbc", [128, 8, 4], cwbc_d)
    cbbc = const("cbbc", [128, 8], cbbc_d)
    hp = const("hp", [32, 2], hp_d)
    dsk = const("dsk", [128, 16], dsk_d)
    nw = const("nw", [128, 16], nw_d)
    ident = const("ident", [128, 128], ident_d)
    selh = const("selh", [32, 32, 128], selh_d)
    maskT = const("maskT", [128, 128], maskT_d)
    epsb = p.sbuf("epsb", [128, 1], F32)
    p.op("dve", lambda e: e.memset(epsb[:], EPS), writes=["epsb"])
    ones32 = p.sbuf("ones32", [32, 128], F32)
    p.op("dve", lambda e: e.memset(ones32[:], 1.0), writes=["ones32"])
    Aneg = p.sbuf("Aneg", [32, 1], F32)
    p.op("act", lambda e: e.activation(out=Aneg[:], in_=hp[:, 1:2], func=AF.Exp), reads=["hp"], writes=["Aneg"])
    p.op("dve", lambda e: e.tensor_scalar(out=Aneg[:], in0=Aneg[:], scalar1=-1.0, scalar2=None, op0=ALU.mult),
         reads=["Aneg"], writes=["Aneg"])

    xin2 = [p.sbuf(f"xin{i}", [128, 16, TS + 3], F32) for i in range(2)]
    zb2 = [p.sbuf(f"zb{i}", [128, 16, TS], F32) for i in range(2)]
    xc = p.sbuf("xc", [128, 16, TS], F32)
    bcin2 = [p.sbuf(f"bcin{i}", [128, 8, TS + 3], F32) for i in range(2)]
    bcc = p.sbuf("bcc", [128, 8, TS], BF16)
    bf = p.sbuf("bf", [128, 4, TS], F32)
    yb = p.sbuf("yb", [128, 16, TS], F32)
    ctmp = [p.sbuf(f"ctmp{i}", [128, TS], F32) for i in range(2)]
    dtin2 = [p.sbuf(f"dtin{i}", [32, TS], F32) for i in range(2)]
    dtp = p.sbuf("dtp", [32, TS], F32)
    av = p.sbuf("av", [32, TS], F32)
    acum = p.sbuf("acum", [32, TS], F32)
    m2 = p.sbuf("m2", [32, TS], F32)
    tmr = p.sbuf("tmr", [128, 3, 32], F32)
    dg = p.sbuf("dg", [32, 32], F32)
    ntm = p.sbuf("ntm", [128, 32], F32)
    negm = p.sbuf("negm", [128, 128], F32)
    p.op("dve", lambda e: e.tensor_scalar(out=negm[:], in0=maskT[:], scalar1=30000.0, scalar2=-30000.0, op0=ALU.mult, op1=ALU.add),
         reads=["maskT"], writes=["negm"])
    ea = p.sbuf("ea", [128, 32], F32)
    xtm = p.sbuf("xtm", [128, 32, 64], BF16)
    xw = p.sbuf("xw", [128, 32, 64], BF16)
    btm = p.sbuf("btm", [128, 4, 128], BF16)
    cbm = p.sbuf("cbm", [128, 4, 128], F32)
    state = p.sbuf("state", [128, 32, 64], F32)
    stbf = p.sbuf("stbf", [128, 32, 64], BF16)
    p.op("dve", lambda e: e.memset(state[:], 0.0), writes=[("st", g) for g in range(4)])
    p.op("pool", lambda e: e.memset(stbf[:], 0.0), writes=[("stb", g) for g in range(4)])
    NR = 16
    t1r = [p.sbuf(f"t1r{i}", [128, 128], F32) for i in range(NR)]
    E2r = [p.sbuf(f"E2r{i}", [128, 128], F32) for i in range(NR)]
    MTh = [p.sbuf(f"MTh{i}", [128, 128], BF16) for i in range(16)]
    Chh = [p.sbuf(f"Chh{i}", [128, 128], BF16) for i in range(16)]
    stat = (p.sbuf("rs", [128, TS], F32), p.sbuf("rstd", [128, TS], F32))
    sq = p.sbuf("sq", [128, 4, TS], BF16)
    ost = [p.sbuf(f"ost{i}", [128, TS], F32) for i in range(2)]

    xTv = xT.rearrange("(kc q) t -> q kc t", q=128)
    bcTv = bcT.rearrange("(kc q) t -> q kc t", q=128)
    zTv = zT.rearrange("(kc q) t -> q kc t", q=128)
    yTv = yT.rearrange("(kc q) t -> q kc t", q=128)
    hcount = 0
    def issue_loads(tl_):
        par_ = tl_ % 2
        t0_ = tl_ * TS
        for k4 in range(4):
            p.dma("sp", xin2[par_][:, k4 * 4:k4 * 4 + 4, :], xTv[:, k4 * 4:k4 * 4 + 4, t0_:t0_ + TS + 3],
                  writes=[(f"xin{par_}", kc) for kc in range(k4 * 4, k4 * 4 + 4)])
        for k4 in range(2):
            p.dma("sp", bcin2[par_][:, k4 * 4:k4 * 4 + 4, :], bcTv[:, k4 * 4:k4 * 4 + 4, t0_:t0_ + TS + 3],
                  writes=[(f"bcin{par_}", kc) for kc in range(k4 * 4, k4 * 4 + 4)])
        p.dma("sp", dtin2[par_][:], dtT[:, t0_:t0_ + TS], writes=[f"dtin{par_}"])
        for k4 in range(4):
            p.dma("sp", zb2[par_][:, k4 * 4:k4 * 4 + 4, :], zTv[:, k4 * 4:k4 * 4 + 4, t0_:t0_ + TS],
                  writes=[(f"zb{par_}", kc) for kc in range(k4 * 4, k4 * 4 + 4)])

    NTS = SEQ // TS
    issue_loads(0)
    for tl in range(NTS):
        t0 = tl * TS
        par = tl % 2
        xin, bcin, dtin, zb = xin2[par], bcin2[par], dtin2[par], zb2[par]
        if tl + 1 < NTS:
            issue_loads(tl + 1)

        def conv(src, skey, kc, w, b, wk, outs):
            t = ctmp[kc % 2]
            tk = ("ctmp", kc % 2)
            p.op("act", lambda e: e.activation(out=t[:], in_=src[:, kc, 0:TS], func=AF.Identity,
                                               scale=w[:, kc, 0:1], bias=b[:, kc:kc + 1]),
                 reads=[(skey, kc)] + wk, writes=[tk])
            for j in (1, 2, 3):
                p.op("dve", lambda e: e.scalar_tensor_tensor(out=t[:], in0=src[:, kc, j:j + TS], scalar=w[:, kc, j:j + 1],
                                                             in1=t[:], op0=ALU.mult, op1=ALU.add),
                     reads=[(skey, kc), tk] + wk, writes=[tk])
            for (o_ap, okey) in outs:
                p.op("act", lambda e: e.activation(out=o_ap, in_=t[:], func=AF.Silu), reads=[tk], writes=[okey])

        for kc in range(16):
            conv(xin, f"xin{par}", kc, cwx, cbx, ["cwx", "cbx"], [(xc[:, kc, :], ("xc", kc))])
        for kc in range(8):
            outs = [(bcc[:, kc, :], ("bcc", kc))]
            if kc < 4:
                outs.append((bf[:, kc, :], ("bf", kc)))
            conv(bcin, f"bcin{par}", kc, cwbc, cbbc, ["cwbc", "cbbc"], outs)

        p.op("act", lambda e: e.activation(out=dtp[:], in_=dtin[:], func=AF.Exp, bias=hp[:, 0:1]),
             reads=[f"dtin{par}", "hp"], writes=["dtp"])
        p.op("act", lambda e: e.activation(out=dtp[:], in_=dtp[:], func=AF.Ln, bias=ones32[:, 0:1]),
             reads=["dtp", "ones32"], writes=["dtp"])
        p.op("dve", lambda e: e.tensor_scalar(out=av[:], in0=dtp[:], scalar1=Aneg[:, 0:1], scalar2=None, op0=ALU.mult),
             reads=["dtp", "Aneg"], writes=["av"])
        for c in range(TS // CH):
            cs_ = slice(c * CH, (c + 1) * CH)
            p.op("dve", lambda e: e.tensor_tensor_scan(out=acum[:, cs_], data0=ones32[:, 0:CH], data1=av[:, cs_],
                                                       initial=0.0, op0=ALU.mult, op1=ALU.add),
                 reads=["av", "ones32"], writes=[("acum", c)])
            p.op("act", lambda e: e.activation(out=m2[:, cs_], in_=acum[:, cs_], func=AF.Exp, scale=-1.0,
                                               bias=acum[:, (c + 1) * CH - 1:(c + 1) * CH]),
                 reads=[("acum", c)], writes=[("m2", c)])
            p.op("dve", lambda e: e.tensor_tensor(out=m2[:, cs_], in0=m2[:, cs_], in1=dtp[:, cs_], op=ALU.mult),
                 reads=[("m2", c), "dtp"], writes=[("m2", c)])

        for c in range(TS // CH):
            cs_ = slice(c * CH, (c + 1) * CH)
            ps, pk = cx.next_ps()
            for i, (src, sk) in enumerate(((dtp, "dtp"), (acum, ("acum", c)), (m2, ("m2", c)))):
                p.op("pe", lambda e: e.matmul(ps[:, i * 32:(i + 1) * 32], lhsT=src[:, cs_], rhs=ident[0:32, 0:32],
                                              start=True, stop=True),
                     reads=[sk, "ident"], writes=[pk], nosync_same=True)
            p.op("dve", lambda e: e.tensor_copy(out=tmr[:].rearrange("q a h -> q (a h)"), in_=ps[:, 0:96]),
                 reads=[pk], writes=["tmr"])
            p.op("dve", lambda e: e.tensor_scalar(out=dg[:], in0=ident[0:32, 0:32],
                                                  scalar1=acum[:, (c + 1) * CH - 1:(c + 1) * CH], scalar2=None, op0=ALU.mult),
                 reads=["ident", ("acum", c)], writes=["dg"])
            ps, pk = cx.next_ps()
            p.op("pe", lambda e: e.matmul(ps[:, 0:32], lhsT=ones32[:, :], rhs=dg[:], start=True, stop=True),
                 reads=["ones32", "dg"], writes=[pk])
            p.op("act", lambda e: e.activation(out=ea[:], in_=ps[:, 0:32], func=AF.Exp), reads=[pk], writes=["ea"])
            for k4 in range(4):
                ps, pk = cx.next_ps()
                for kk in range(4):
                    kc = k4 * 4 + kk
                    p.op("pe", lambda e: e.transpose(out=ps[:, kk * 128:(kk + 1) * 128], in_=xc[:, kc, cs_], identity=ident[:]),
                         reads=[("xc", kc), "ident"], writes=[pk], nosync_same=True)
                ev_copy(p, cx.ev_eng(), xtm[:, k4 * 8:(k4 + 1) * 8, :].rearrange("q h d -> q (h d)"), ps[:],
                        [pk], [("xtm", k4)])
            ps, pk = cx.next_ps()
            for g in range(4):
                p.op("pe", lambda e: e.transpose(out=ps[:, g * 128:(g + 1) * 128], in_=bf[:, g, cs_], identity=ident[:]),
                     reads=[("bf", g), "ident"], writes=[pk], nosync_same=True)
            ev_copy(p, cx.ev_eng(), btm[:].rearrange("q g n -> q (g n)"), ps[:], [pk], ["btm"])
            ps, pk = cx.next_ps()
            for g in range(4):
                p.op("pe", lambda e: e.matmul(ps[:, g * 128:(g + 1) * 128], lhsT=bcc[:, g, cs_], rhs=bcc[:, 4 + g, cs_],
                                              start=True, stop=True),
                     reads=[("bcc", g), ("bcc", 4 + g)], writes=[pk], nosync_same=True)
            for g in range(4):
                p.op("dve", lambda e: e.tensor_tensor(out=cbm[:, g, :], in0=ps[:, g * 128:(g + 1) * 128], in1=maskT[:], op=ALU.mult),
                     reads=[pk, "maskT"], writes=[("cbm", g)])
            p.op("dve", lambda e: e.tensor_tensor(out=xw[:], in0=xtm[:], in1=tmr[:, 2, :].unsqueeze(2).to_broadcast([128, 32, 64]), op=ALU.mult),
                 reads=[("xtm", k4) for k4 in range(4)] + ["tmr"], writes=[("xw", g) for g in range(4)])
            p.op("pool", lambda e: e.tensor_scalar(out=ntm[:], in0=tmr[:, 1, :], scalar1=-1.0, scalar2=0.0, op0=ALU.mult, op1=ALU.add),
                 reads=["tmr"], writes=["ntm"])
            def stageA(hb):
              hs_ = list(range(hb * 8, hb * 8 + 8))
              pbs = {}
              for h in hs_:
                psb, pkb = cx.next_ps()
                pbs[h] = (psb, pkb)
                p.op("pe", lambda e: e.matmul(psb[:, 0:128], lhsT=selh[:, h, :], rhs=acum[:, cs_], start=True, stop=False),
                     reads=["selh", ("acum", c)], writes=[pkb], nosync_same=True)
                p.op("pe", lambda e: e.matmul(psb[:, 0:128], lhsT=ident[:], rhs=negm[:], start=False, stop=True),
                     reads=["ident", "negm"], writes=[pkb], nosync_same=True)
                p.op("pe", lambda e: e.matmul(psb[:, 128:256], lhsT=selh[:, h, :], rhs=acum[:, cs_], start=True, stop=True),
                     reads=["selh", ("acum", c)], writes=[pkb], nosync_same=True)
              for h in hs_:
                psb, pkb = pbs[h]
                r = h % 16
                p.op("act", lambda e: e.activation(out=t1r[r][:], in_=psb[:, 0:128], func=AF.Exp, bias=ntm[:, h:h + 1]),
                     reads=[pkb, "ntm"], writes=[("t1", r)])
                p.op("act", lambda e: e.activation(out=E2r[r][:], in_=psb[:, 128:256], func=AF.Exp), reads=[pkb], writes=[("E2", r)])
              for h in hs_:
                r = h % 16
                g = h // 8
                p.op("dve", lambda e: e.scalar_tensor_tensor(out=MTh[r][:], in0=t1r[r][:], scalar=tmr[:, 0, h:h + 1],
                                                             in1=cbm[:, g, :], op0=ALU.mult, op1=ALU.mult),
                     reads=[("t1", r), "tmr", ("cbm", g)], writes=[("MT", r)])
                p.op("pool", lambda e: e.tensor_tensor(out=Chh[r][:], in0=bcc[:, 4 + g, cs_], in1=E2r[r][:], op=ALU.mult),
                     reads=[("bcc", 4 + g), ("E2", r)], writes=[("Ch", r)])

            def stageB(hb):
              for kc in range(hb * 4, hb * 4 + 4):
                psY, pkY = cx.next_ps()
                for hh in range(2):
                    h = kc * 2 + hh
                    g = h // 8
                    p.op("pe", lambda e: e.matmul(psY[hh * 64:(hh + 1) * 64, 0:128], lhsT=xtm[:, h, :], rhs=MTh[h % 16][:],
                                                  start=True, stop=False),
                         reads=[("xtm", h // 8), ("MT", h % 16)], writes=[pkY], nosync_same=True)
                    p.op("pe", lambda e: e.matmul(psY[hh * 64:(hh + 1) * 64, 0:128], lhsT=stbf[:, h, :], rhs=Chh[h % 16][:],
                                                  start=False, stop=True),
                         reads=[("stb", g), ("Ch", h % 16)], writes=[pkY], nosync_same=True)
                ev_copy(p, cx.ev_eng(), yb[:, kc, cs_], psY[:, 0:128], [pkY], [("yb", kc)])

            stageA(0)
            for hb in range(4):
                if hb < 3:
                    stageA(hb + 1)
                stageB(hb)
            for g in range(4):
                psS, pkS = cx.next_ps()
                p.op("pe", lambda e: e.matmul(psS[:], lhsT=btm[:, g, :], rhs=xw[:, g * 8:(g + 1) * 8, :].rearrange("q h d -> q (h d)"),
                                              start=True, stop=True),
                     reads=["btm", ("xw", g)], writes=[pkS])
                sg_ = state[:, g * 8:(g + 1) * 8, :]
                p.op("pool", lambda e: e.tensor_tensor(out=sg_, in0=sg_, in1=ea[:, g * 8:(g + 1) * 8].unsqueeze(2).to_broadcast([128, 8, 64]), op=ALU.mult),
                     reads=[("st", g), "ea"], writes=[("st", g)])
                p.op("dve", lambda e: e.tensor_tensor(out=sg_, in0=sg_, in1=psS[:].rearrange("q (h d) -> q h d", d=64), op=ALU.add),
                     reads=[("st", g), pkS], writes=[("st", g)])
                p.op("act", lambda e: e.activation(out=stbf[:, g * 8:(g + 1) * 8, :], in_=state[:, g * 8:(g + 1) * 8, :], func=AF.Copy),
                     reads=[("st", g)], writes=[("stb", g)])

        for kc in range(16):
            p.op("dve", lambda e: e.scalar_tensor_tensor(out=yb[:, kc, :], in0=xc[:, kc, :], scalar=dsk[:, kc:kc + 1],
                                                         in1=yb[:, kc, :], op0=ALU.mult, op1=ALU.add),
                 reads=[("xc", kc), "dsk", ("yb", kc)], writes=[("yb", kc)])
            t = ctmp[kc % 2]
            tk = ("ctmp", kc % 2)
            p.op("act", lambda e: e.activation(out=t[:], in_=zb[:, kc, :], func=AF.Silu), reads=[(f"zb{par}", kc)], writes=[tk])
            p.op("dve", lambda e: e.tensor_tensor(out=yb[:, kc, :], in0=yb[:, kc, :], in1=t[:], op=ALU.mult),
                 reads=[("yb", kc), tk], writes=[("yb", kc)])
        for g in range(4):
            for kk in range(4):
                kc = g * 4 + kk
                p.op("act", lambda e: e.activation(out=sq[:, kk, :], in_=yb[:, kc, :], func=AF.Square),
                     reads=[("yb", kc)], writes=[("sq", kk)])
            ps, pk = cx.next_ps()
            for kk in range(4):
                p.op("pe", lambda e: e.matmul(ps[:, 0:TS], lhsT=cx.ones[:], rhs=sq[:, kk, :], start=(kk == 0), stop=(kk == 3)),
                     reads=["ones", ("sq", kk)], writes=[pk], nosync_same=True)
            rs, rstd = stat
            p.op("act", lambda e: e.activation(out=rs[:], in_=ps[:, 0:TS], func=AF.Sqrt, scale=1.0 / 512, bias=epsb[:, 0:1]),
                 reads=[pk, "epsb"], writes=["rs"])
            p.op("dve", lambda e: e.reciprocal(out=rstd[:], in_=rs[:]), reads=["rs"], writes=["rstd"])
            for kk in range(4):
                kc = g * 4 + kk
                o = ost[kc % 2]
                ok = ("ost", kc % 2)
                p.op("dve", lambda e: e.scalar_tensor_tensor(out=o[:], in0=yb[:, kc, :], scalar=nw[:, kc:kc + 1], in1=rstd[:],
                                                             op0=ALU.mult, op1=ALU.mult),
                     reads=[("yb", kc), "nw", "rstd"], writes=[ok])
                p.dma("pool", yTv[:, kc, t0:t0 + TS], o[:], reads=[ok])
    if standalone:
        p.finish()
    return p


def run_ssd(p1T_b, prm):
    p = _prog("ssd", build_ssd)
    ident = np.eye(128, dtype=np.float32)
    selh = np.zeros((32, 32, 128), np.float32)
    for h in range(32):
        selh[h, h, :] = 1.0
    maskT = np.triu(np.ones((128, 128), np.float32))
    in_maps = []
    for core in range(8):
        b, hh = core // 2, core % 2
        P = p1T_b[b]
        chx = slice(4096 + hh * 2048, 4096 + (hh + 1) * 2048)
        z = P[hh * 2048:(hh + 1) * 2048]
        x = P[chx]
        Bm = P[8192 + hh * 512: 8192 + (hh + 1) * 512]
        Cm = P[9216 + hh * 512: 9216 + (hh + 1) * 512]
        dt = P[10240 + hh * 32: 10240 + (hh + 1) * 32]
        pad = lambda a: np.ascontiguousarray(np.concatenate([np.zeros((a.shape[0], 3), np.float32), a], axis=1))
        cw = prm["ssm_conv_w"]
        cb = prm["ssm_conv_b"]
        cix = np.arange(hh * 2048, (hh + 1) * 2048)
        cibc = np.concatenate([4096 + hh * 512 + np.arange(512), 5120 + hh * 512 + np.arange(512)])
        hs = slice(hh * 32, (hh + 1) * 32)
        m = {
            "zT": np.ascontiguousarray(z), "xT": pad(x), "bcT": pad(np.concatenate([Bm, Cm], axis=0)),
            "dtT": np.ascontiguousarray(dt),
            "cwx": np.ascontiguousarray(cw[:, cix].T.reshape(16, 128, 4).transpose(1, 0, 2)),
            "cbx": fm(cb[cix]),
            "cwbc": np.ascontiguousarray(cw[:, cibc].T.reshape(8, 128, 4).transpose(1, 0, 2)),
            "cbbc": fm(cb[cibc]),
            "hp": np.ascontiguousarray(np.stack([prm["ssm_dt_bias"][hs], prm["ssm_a_log"][hs]], axis=1)),
            "dsk": fm(np.repeat(prm["ssm_d"][hs], 64)),
            "nw": fm(prm["ssm_norm_w"][hh * 2048:(hh + 1) * 2048]),
            "ident": ident, "selh": selh, "maskT": maskT,
        }
        in_maps.append(m)
    r = _run(p, in_maps)
    return [np.asarray(r[i]["yT"]) for i in range(8)]


class PsQ:
    def __init__(self, p, nbanks, name="pq"):
        self.banks = [p.psum(f"{name}{i}", [128, 512], F32) for i in range(nbanks)]
        self.n = nbanks
        self.i = 0
        self.name = name

    def nextbank(self):
        i = self.i % self.n
        self.i += 1
        return self.banks[i], (self.name, i)


INV_DT = BF16


class InvBufs:
    def __init__(self, p, name, dt=F32):
        mk = lambda s_, sh: p.sbuf(f"{name}_{s_}", sh, dt)
        self.name = name
        self.NN, self.PP, self.QQ, self.XX = (mk(x, [128, 2, 128]) for x in ("NN", "PP", "QQ", "XX"))
        self.O, self.WT = mk("O", [128, 128]), mk("WT", [128, 128])

    def k(self, s_):
        return (self.name, s_)


def _kl(k):
    return list(k) if isinstance(k, list) else [k]


def emit_inverse(p, pq, chains, masks, ident2):
    m16, o32, o64 = masks["m16x2"], masks["o32"], masks["o64"]

    def mm(out_ps, pk, lhsT, lk, rhs, rk):
        p.op("pe", lambda e: e.matmul(out_ps, lhsT=lhsT, rhs=rhs, start=True, stop=True), reads=[lk, rk], writes=[pk],
             nosync_same=True)

    def rounds(fn_mm, fn_ev):
        for s0 in range(0, len(chains), pq.n):
            pend = []
            for ch in chains[s0:s0 + pq.n]:
                bk, k = pq.nextbank()
                fn_mm(ch, bk, k)
                pend.append((ch, bk, k))
            for ci, (ch, bk, k) in enumerate(pend):
                fn_ev(ci, ch, bk, k)

    f2 = lambda t: t[:].rearrange("q a c -> q (a c)")
    for ci, (b, AA, AAk) in enumerate(chains):
        p.op("dve", lambda e: e.scalar_tensor_tensor(out=f2(b.NN), in0=f2(AA), scalar=-1.0, in1=f2(m16), op0=ALU.mult, op1=ALU.mult),
             reads=_kl(AAk) + ["m16x2"], writes=[b.k("NN")])
        p.op("pool", lambda e: e.tensor_tensor(out=f2(b.XX), in0=f2(b.NN), in1=f2(ident2), op=ALU.add),
             reads=[b.k("NN"), "ident2"], writes=[b.k("XX")])
    names = ["NN", "PP", "QQ", "PP"]
    for lvl in range(3):
        src, dst = names[lvl], names[lvl + 1]

        def sq_mm(ch, bk, k):
            b = ch[0]
            S = getattr(b, src)
            mm(bk[:, 0:128], k, S[:, 1, :], b.k(src), S[:, 0, :], b.k(src))
            mm(bk[:, 128:256], k, S[:, 0, :], b.k(src), S[:, 1, :], b.k(src))

        def sq_ev(ci, ch, bk, k):
            b = ch[0]
            ev_copy(p, "act" if ci % 2 == 0 else "dve", f2(getattr(b, dst)), bk[:, 0:256], [k], [b.k(dst)])
        rounds(sq_mm, sq_ev)

        def pr_mm(ch, bk, k):
            b = ch[0]
            Pm = getattr(b, dst)
            mm(bk[:, 0:128], k, Pm[:, 1, :], b.k(dst), b.XX[:, 0, :], b.k("XX"))
            mm(bk[:, 128:256], k, Pm[:, 0, :], b.k(dst), b.XX[:, 1, :], b.k("XX"))

        def pr_ev(ci, ch, bk, k):
            b = ch[0]
            p.op("dve", lambda e: e.tensor_tensor(out=f2(b.XX), in0=f2(b.XX), in1=bk[:, 0:256], op=ALU.add),
                 reads=[b.k("XX"), k], writes=[b.k("XX")])
        rounds(pr_mm, pr_ev)
    for li, om in enumerate((o32, o64)):
        def w_mm(ch, bk, k):
            b, AA, AAk = ch
            p.op("pool", lambda e: e.tensor_tensor(out=b.O[:], in0=AA[:, 0, :], in1=om[:], op=ALU.mult),
                 reads=_kl(AAk) + ["o32" if li == 0 else "o64"], writes=[b.k("O")])
            mm(bk[:, 0:128], k, b.O[:], b.k("O"), b.XX[:, 1, :], b.k("XX"))

        def w_ev(ci, ch, bk, k):
            b = ch[0]
            ev_copy(p, "act", b.WT[:], bk[:, 0:128], [k], [b.k("WT")])
        rounds(w_mm, w_ev)

        def z_mm(ch, bk, k):
            b = ch[0]
            mm(bk[:, 0:128], k, b.WT[:], b.k("WT"), b.XX[:, 0, :], b.k("XX"))
            mm(bk[:, 128:256], k, b.XX[:, 0, :], b.k("XX"), b.WT[:], b.k("WT"))

        def z_ev(ci, ch, bk, k):
            b = ch[0]
            p.op("dve", lambda e: e.tensor_tensor(out=f2(b.XX), in0=f2(b.XX), in1=bk[:, 0:256], op=ALU.subtract),
                 reads=[b.k("XX"), k], writes=[b.k("XX")])
        rounds(z_mm, z_ev)


def inv_masks_np():
    i = np.arange(128)
    same = lambda n: ((i[:, None] // n) == (i[None, :] // n)).astype(np.float32)
    m16 = same(16)
    o32 = same(32) - same(16)
    o64 = same(64) - same(32)
    return np.ascontiguousarray(np.stack([m16, m16], axis=1)), o32, o64


def build_gdn(p=None, io=None):
    standalone = p is None
    if standalone:
        p = Prog()
    pre = "" if standalone else "gdn_"
    C = 64
    NCI = TT // C
    qkvT = io["qkvT"] if io else p.dram("qkvT", [1536, SEQ + 3], F32, "ExternalInput")
    ztm = None if io else p.dram("ztm", [SEQ, 512], F32, "ExternalInput")
    zfm = io["zfm"] if io else None
    bT = io["bT"] if io else p.dram("bT", [4, SEQ], F32, "ExternalInput")
    aT = io["aT"] if io else p.dram("aT", [4, SEQ], F32, "ExternalInput")
    cw_d = p.dram(pre + "cw", [128, 12, 4], F32, "ExternalInput")
    hp_d = p.dram(pre + "hp", [4, 2], F32, "ExternalInput")
    nwrow_d = p.dram(pre + "nwrow", [128, 128], F32, "ExternalInput")
    m2_d = p.dram(pre + "m2", [2, 128, 2, 128], F32, "ExternalInput")
    mk_d = p.dram(pre + "mk", [5, 128, 128], F32, "ExternalInput")
    selrow_d = p.dram(pre + "selrow", [4, 4, 128], F32, "ExternalInput")
    i4p_d = p.dram(pre + "i4p", [4, 4, 2], F32, "ExternalInput")
    yb = io["yb"] if io else p.dram("yb", [SEQ, 512], F32, "ExternalOutput")

    def const(name, shape, src, dt=F32):
        t = p.sbuf(name, shape, dt)
        p.dma("pool", t[:], src, writes=[name])
        return t
    cw = const("cw", [128, 12, 4], cw_d)
    hp = const("hp", [4, 2], hp_d)
    nwrow = const("nwrow", [128, 128], nwrow_d)
    m16x2 = const("m16x2", [128, 2, 128], m2_d[0])
    ident2 = const("ident2", [128, 2, 128], m2_d[1])
    o32 = const("o32", [128, 128], mk_d[0])
    o64 = const("o64", [128, 128], mk_d[1])
    maskSL = const("maskSL", [128, 128], mk_d[2])
    maskUT = const("maskUT", [128, 128], mk_d[3])
    ident = const("ident", [128, 128], mk_d[4])
    selrow = const("selrow", [4, 4, 128], selrow_d)
    i4p = const("i4p", [4, 4, 2], i4p_d)
    identb = p.sbuf("identb", [128, 128], BF16)
    p.op("dve", lambda e: e.tensor_copy(out=identb[:], in_=ident[:]), reads=["ident"], writes=["identb"])
    onesb = p.sbuf("onesb", [128, 128], BF16)
    p.op("dve", lambda e: e.memset(onesb[:], 1.0), writes=["onesb"])
    ones4 = p.sbuf("ones4", [4, C], F32)
    p.op("dve", lambda e: e.memset(ones4[:], 1.0), writes=["ones4"])
    eps6 = p.sbuf("eps6", [128, 1], F32)
    p.op("dve", lambda e: e.memset(eps6[:], 1e-6), writes=["eps6"])
    eps5 = p.sbuf("eps5", [128, 1], F32)
    p.op("dve", lambda e: e.memset(eps5[:], EPS), writes=["eps5"])
    Aneg = p.sbuf("Aneg", [4, 1], F32)
    p.op("act", lambda e: e.activation(out=Aneg[:], in_=hp[:, 0:1], func=AF.Exp), reads=["hp"], writes=["Aneg"])
    p.op("dve", lambda e: e.tensor_scalar(out=Aneg[:], in0=Aneg[:], scalar1=-1.0, scalar2=None, op0=ALU.mult),
         reads=["Aneg"], writes=["Aneg"])

    pq = PsQ(p, 4, "pq")
    pg = PsQ(p, 3, "pg")
    psb = p.psum("psb16", [128, 1024], BF16)

    qin = p.sbuf("qin", [128, 12, TT + 3], F32)
    ctmp = [p.sbuf(f"ctmp{i}", [128, TT], F32) for i in range(2)]
    sqb = [p.sbuf(f"sqb{i}", [128, TT], BF16) for i in range(2)]
    rnt = [p.sbuf(f"rnt{i}", [128, TT], F32) for i in range(2)]
    xnp = [[p.sbuf(f"xnp{w}{pr}", [128, NCI, 2, C], BF16) for pr in range(2)] for w in range(3)]
    braw = p.sbuf("braw", [4, TT], F32)
    araw = p.sbuf("araw", [4, TT], F32)
    beta = p.sbuf("beta", [4, TT], F32)
    gv = p.sbuf("gv", [4, TT], F32)
    gc = p.sbuf("gc", [4, TT], F32)
    egc = p.sbuf("egc", [4, TT], F32)
    ekd = p.sbuf("ekd", [4, TT], F32)
    NPC = 2 * NCI
    tm = [p.sbuf(f"tm{i}", [128, 8], F32) for i in range(NPC)]
    uS = [p.sbuf(f"uS{i}", [128, 128], F32) for i in range(NPC)]
    wT = [p.sbuf(f"wT{i}", [128, 128], BF16) for i in range(NPC)]
    qkT = [p.sbuf(f"qkT{i}", [128, 128], BF16) for i in range(NPC)]
    qgT = [p.sbuf(f"qgT{i}", [128, 128], BF16) for i in range(NPC)]
    Kd = [p.sbuf(f"Kd{i}", [128, 128], BF16) for i in range(NPC)]
    egl = [p.sbuf(f"egl{i}", [128, 2], F32) for i in range(NPC)]
    NTR = 8
    Dm = [p.sbuf(f"Dm{i}", [128, 128], F32) for i in range(NTR)]
    DTm = [p.sbuf(f"DTm{i}", [128, 128], F32) for i in range(NTR)]
    tA = [p.sbuf(f"tA{i}", [128, 128], F32) for i in range(NTR)]
    AA = [p.sbuf(f"AA{i}", [128, 2, 128], F32) for i in range(NTR)]
    Kb = [p.sbuf(f"Kb{i}", [128, 128], BF16) for i in range(NTR)]
    Vb = [p.sbuf(f"Vb{i}", [128, 128], BF16) for i in range(NTR)]
    XTb = [p.sbuf(f"XTb{i}", [128, 128], BF16) for i in range(NTR)]
    NIV = 8
    ivb = [InvBufs(p, f"iv{i}", INV_DT) for i in range(NIV)]
    zin = [p.sbuf(f"zin{pr}", [128, NCI, 128], F32) for pr in range(2)]
    if zfm is not None:
        zf_in = p.sbuf("zf_in", [128, 4, TT], F32)
        zsb = p.sbuf("zsb", [128, 4, TT], BF16)
    ost = [p.sbuf(f"ostg{pr}", [128, NCI, 128], F32) for pr in range(2)]
    Sf = [p.sbuf(f"Sf{pr}", [128, 2, 128], F32) for pr in range(2)]
    Sb = [p.sbuf(f"Sb{pr}", [128, 2, 128], BF16) for pr in range(2)]
    for pr in range(2):
        p.op("dve", lambda e: e.memset(Sf[pr][:], 0.0), writes=[("Sf", pr)])
        p.op("pool", lambda e: e.memset(Sb[pr][:], 0.0), writes=[("Sb", pr)])
    vnew = [p.sbuf(f"vnew{i}", [128, 128], BF16) for i in range(2)]
    ss = [p.sbuf(f"ss{i}", [128, 1], F32) for i in range(2)]
    junk = [p.sbuf(f"junk{i}", [128, 128], F32) for i in range(2)]
    yt = [p.sbuf(f"yt{i}", [128, 128], F32) for i in range(2)]
    masks = {"m16x2": m16x2, "o32": o32, "o64": o64}

    qkvTv = qkvT.rearrange("(kc q) t -> q kc t", q=128)
    tctr = 0
    for tl in range(SEQ // TT):
        t0 = tl * TT
        for k4 in range(3):
            p.dma("sp", qin[:, k4 * 4:k4 * 4 + 4, :], qkvTv[:, k4 * 4:k4 * 4 + 4, t0:t0 + TT + 3],
                  writes=[("qin", kc) for kc in range(k4 * 4, k4 * 4 + 4)])
        p.dma("sp", braw[:], bT[:, t0:t0 + TT], writes=["braw"])
        p.dma("sp", araw[:], aT[:, t0:t0 + TT], writes=["araw"])
        if zfm is None:
            for pr in range(2):
                for half in range(2):
                    col0 = (2 * pr + half) * 128
                    p.dma("sp", zin[pr][half * 64:(half + 1) * 64, :, :],
                          ztm[t0:t0 + TT, col0:col0 + 128].rearrange("(ci t) v -> t ci v", t=C),
                          writes=[("zin", pr)])
        else:
            p.dma("sp", zf_in[:], zfm.rearrange("(h q) t -> q h t", q=128)[:, :, t0:t0 + TT], writes=["zf_in"])
            p.op("act", lambda e: e.activation(out=zsb[:], in_=zf_in[:], func=AF.Silu), reads=["zf_in"], writes=["zsb"])
        for kc in range(12):
            w_, hl = kc // 4, kc % 4
            pr, half = hl // 2, hl % 2
            t = ctmp[kc % 2]
            tk = ("ctmp", kc % 2)
            p.op("act", lambda e: e.activation(out=t[:], in_=qin[:, kc, 0:TT], func=AF.Copy, scale=cw[:, kc, 0:1]),
                 reads=[("qin", kc), "cw"], writes=[tk])
            for j in (1, 2, 3):
                p.op("dve", lambda e: e.scalar_tensor_tensor(out=t[:], in0=qin[:, kc, j:j + TT], scalar=cw[:, kc, j:j + 1],
                                                             in1=t[:], op0=ALU.mult, op1=ALU.add),
                     reads=[("qin", kc), tk, "cw"], writes=[tk])
            dst = xnp[w_][pr][:, :, half, :]
            dk = ("xnp", w_, pr)
            if w_ == 2:
                p.op("act", lambda e: e.activation(out=dst, in_=t[:].rearrange("q (a c) -> q a c", c=C), func=AF.Silu),
                     reads=[tk], writes=[dk])
                continue
            p.op("act", lambda e: e.activation(out=t[:], in_=t[:], func=AF.Silu), reads=[tk], writes=[tk])
            sq, sk = sqb[kc % 2], ("sqb", kc % 2)
            p.op("act", lambda e: e.activation(out=sq[:], in_=t[:], func=AF.Square), reads=[tk], writes=[sk])
            bk, k = pg.nextbank()
            p.op("pe", lambda e: e.matmul(bk[:], lhsT=onesb[:], rhs=sq[:], start=True, stop=True), reads=["onesb", sk], writes=[k])
            rn, rk = rnt[kc % 2], ("rnt", kc % 2)
            p.op("act", lambda e: e.activation(out=rn[:], in_=bk[:], func=AF.Sqrt, bias=eps6[:, 0:1]), reads=[k, "eps6"], writes=[rk])
            p.op("dve", lambda e: e.reciprocal(out=rn[:], in_=rn[:]), reads=[rk], writes=[rk])
            sc = 128 ** -0.5 if w_ == 0 else 1.0
            p.op("dve", lambda e: e.scalar_tensor_tensor(out=dst, in0=t[:].rearrange("q (a c) -> q a c", c=C), scalar=sc,
                                                         in1=rn[:].rearrange("q (a c) -> q a c", c=C), op0=ALU.mult, op1=ALU.mult),
                 reads=[tk, rk], writes=[dk])
        p.op("act", lambda e: e.activation(out=beta[:], in_=braw[:], func=AF.Sigmoid), reads=["braw"], writes=["beta"])
        p.op("act", lambda e: e.activation(out=gv[:], in_=araw[:], func=AF.Exp, bias=hp[:, 1:2]), reads=["araw", "hp"], writes=["gv"])
        p.op("act", lambda e: e.activation(out=gv[:], in_=gv[:], func=AF.Ln, bias=ones4[:, 0:1]), reads=["gv", "ones4"], writes=["gv"])
        p.op("dve", lambda e: e.tensor_scalar(out=gv[:], in0=gv[:], scalar1=Aneg[:, 0:1], scalar2=None, op0=ALU.mult),
             reads=["gv", "Aneg"], writes=["gv"])
        for ci in range(NCI):
            cs_ = slice(ci * C, (ci + 1) * C)
            p.op("dve", lambda e: e.tensor_tensor_scan(out=gc[:, cs_], data0=ones4[:, 0:C], data1=gv[:, cs_], initial=0.0,
                                                       op0=ALU.mult, op1=ALU.add),
                 reads=["gv", "ones4"], writes=["gc"])
            p.op("act", lambda e: e.activation(out=ekd[:, cs_], in_=gc[:, cs_], func=AF.Exp, scale=-1.0,
                                               bias=gc[:, (ci + 1) * C - 1:(ci + 1) * C]),
                 reads=["gc"], writes=["ekd"])
        p.op("act", lambda e: e.activation(out=egc[:], in_=gc[:], func=AF.Exp), reads=["gc"], writes=["egc"])
        for pr in range(2):
            if zfm is None:
                p.op("act", lambda e: e.activation(out=zin[pr][:], in_=zin[pr][:], func=AF.Silu), reads=[("zin", pr)], writes=[("zin", pr)])

        pcs = [(ci, pr) for ci in range(NCI) for pr in range(2)]
        for g0 in range(0, len(pcs), NIV):
            grp = pcs[g0:g0 + NIV]
            chains = []
            for gi, (ci, pr) in enumerate(grp):
                pc = ci * 2 + pr
                r = tctr % NTR
                tctr += 1
                hA, hB = 2 * pr, 2 * pr + 1
                cs_ = slice(ci * C, (ci + 1) * C)
                bk, k = pg.nextbank()
                for half in range(2):
                    for j, (qt, qk_) in enumerate(((beta, "beta"), (gc, "gc"), (egc, "egc"), (ekd, "ekd"))):
                        p.op("pe", lambda e: e.matmul(bk[half * 64:(half + 1) * 64, 2 * j:2 * j + 2], lhsT=qt[:, cs_],
                                                      rhs=i4p[:, 2 * pr + half, :], start=True, stop=True),
                             reads=[qk_, "i4p"], writes=[k], nosync_same=True)
                p.op("dve", lambda e: e.tensor_copy(out=tm[pc][:], in_=bk[:, 0:8]), reads=[k], writes=[("tm", pc)])
                p.op("dve", lambda e: e.tensor_tensor(out=tm[pc][:, 1:2], in0=tm[pc][:, 0:1], in1=tm[pc][:, 4:5], op=ALU.mult),
                     reads=[("tm", pc)], writes=[("tm", pc)])
                bk, k = pg.nextbank()
                for half in range(2):
                    p.op("pe", lambda e: e.matmul(bk[:, half * 64:(half + 1) * 64], lhsT=selrow[:, 2 * pr + half, :], rhs=gc[:, cs_],
                                                  start=True, stop=True),
                         reads=["selrow", "gc"], writes=[k], nosync_same=True)
                p.op("dve", lambda e: e.tensor_scalar(out=Dm[r][:], in0=bk[:, 0:128], scalar1=tm[pc][:, 2:3], scalar2=0.0,
                                                      op0=ALU.subtract, op1=ALU.max),
                     reads=[k, ("tm", pc)], writes=[("Dm", r)])
                p.op("dve", lambda e: e.tensor_scalar(out=DTm[r][:], in0=bk[:, 0:128], scalar1=tm[pc][:, 2:3], scalar2=0.0,
                                                      op0=ALU.subtract, op1=ALU.min),
                     reads=[k, ("tm", pc)], writes=[("DTm", r)])
                p.op("act", lambda e: e.activation(out=Dm[r][:], in_=Dm[r][:], func=AF.Exp, scale=-1.0), reads=[("Dm", r)], writes=[("Dm", r)])
                p.op("act", lambda e: e.activation(out=DTm[r][:], in_=DTm[r][:], func=AF.Exp), reads=[("DTm", r)], writes=[("DTm", r)])
                kn, qn, vn = xnp[1][pr], xnp[0][pr], xnp[2][pr]
                bk, k = pg.nextbank()
                for half in range(2):
                    p.op("pe", lambda e: e.matmul(bk[half * 64:(half + 1) * 64, 0:128], lhsT=kn[:, ci, half, :],
                                                  rhs=kn[:, ci, :, :].rearrange("q h t -> q (h t)"), start=True, stop=True),
                         reads=[("xnp", 1, pr)], writes=[k], nosync_same=True)
                    p.op("pe", lambda e: e.matmul(bk[half * 64:(half + 1) * 64, 128:256], lhsT=kn[:, ci, half, :],
                                                  rhs=qn[:, ci, :, :].rearrange("q h t -> q (h t)"), start=True, stop=True),
                         reads=[("xnp", 1, pr), ("xnp", 0, pr)], writes=[k], nosync_same=True)
                p.op("dve", lambda e: e.tensor_tensor(out=tA[r][:], in0=bk[:, 0:128], in1=Dm[r][:], op=ALU.mult),
                     reads=[k, ("Dm", r)], writes=[("tA", r)])
                p.op("dve", lambda e: e.scalar_tensor_tensor(out=AA[r][:, 0, :], in0=tA[r][:], scalar=tm[pc][:, 0:1], in1=maskSL[:],
                                                             op0=ALU.mult, op1=ALU.mult),
                     reads=[("tA", r), ("tm", pc), "maskSL"], writes=[("AA0", r)])
                p.op("dve", lambda e: e.tensor_tensor(out=tA[r][:], in0=bk[:, 128:256], in1=DTm[r][:], op=ALU.mult),
                     reads=[k, ("DTm", r), ("tA", r)], writes=[("tA", r)])
                p.op("pool", lambda e: e.tensor_tensor(out=qkT[pc][:], in0=tA[r][:], in1=maskUT[:], op=ALU.mult),
                     reads=[("tA", r), "maskUT"], writes=[("qkT", pc)])
                bk, k = pg.nextbank()
                p.op("pe", lambda e: e.transpose(out=bk[:, 0:128], in_=AA[r][:, 0, :], identity=ident[:]),
                     reads=[("AA0", r), "ident"], writes=[k])
                p.op("act", lambda e: e.activation(out=AA[r][:, 1, :], in_=bk[:, 0:128], func=AF.Copy), reads=[k], writes=[("AA1", r)])
                for half in range(2):
                    p.op("pe", lambda e: e.transpose(out=psb[half * 64:(half + 1) * 64, 0:128], in_=kn[:, ci, half, :], identity=identb[:]),
                         reads=[("xnp", 1, pr), "identb"], writes=["psb"], nosync_same=True)
                    p.op("pe", lambda e: e.transpose(out=psb[half * 64:(half + 1) * 64, 128:256], in_=vn[:, ci, half, :], identity=identb[:]),
                         reads=[("xnp", 2, pr), "identb"], writes=["psb"], nosync_same=True)
                p.op("dve", lambda e: e.tensor_scalar(out=Kb[r][:], in0=psb[:, 0:128], scalar1=tm[pc][:, 1:2], scalar2=None, op0=ALU.mult),
                     reads=["psb", ("tm", pc)], writes=[("Kb", r)])
                p.op("dve", lambda e: e.tensor_scalar(out=Kd[pc][:], in0=psb[:, 0:128], scalar1=tm[pc][:, 6:7], scalar2=None, op0=ALU.mult),
                     reads=["psb", ("tm", pc)], writes=[("Kd", pc)])
                p.op("dve", lambda e: e.tensor_scalar(out=Vb[r][:], in0=psb[:, 128:256], scalar1=tm[pc][:, 0:1], scalar2=None, op0=ALU.mult),
                     reads=["psb", ("tm", pc)], writes=[("Vb", r)])
                if zfm is not None:
                    for half in range(2):
                        p.op("pe", lambda e: e.transpose(out=psb[half * 64:(half + 1) * 64, 256:384], in_=zsb[:, 2 * pr + half, cs_],
                                                         identity=identb[:]),
                             reads=["zsb", "identb"], writes=["psb"], nosync_same=True)
                    p.op("dve", lambda e: e.tensor_copy(out=zin[pr][:, ci, :], in_=psb[:, 256:384]), reads=["psb"], writes=[("zin", pr)])
                bk, k = pg.nextbank()
                for half in range(2):
                    p.op("pe", lambda e: e.matmul(bk[:, half * 64:(half + 1) * 64], lhsT=selrow[:, 2 * pr + half, :], rhs=egc[:, cs_],
                                                  start=True, stop=True),
                         reads=["selrow", "egc"], writes=[k], nosync_same=True)
                p.op("dve", lambda e: e.tensor_tensor(out=qgT[pc][:], in0=qn[:, ci, :, :].rearrange("q h t -> q (h t)"), in1=bk[:, 0:128],
                                                      op=ALU.mult),
                     reads=[k, ("xnp", 0, pr)], writes=[("qgT", pc)])
                p.op("dve", lambda e: e.tensor_copy(out=egl[pc][:], in_=bk[:, 63:128:64]), reads=[k], writes=[("egl", pc)])
                chains.append((ivb[gi], AA[r], [("AA0", r), ("AA1", r)], pc, r))
            emit_inverse(p, pq, [(b_, aa_, ak_) for (b_, aa_, ak_, _pc, _r) in chains], masks, ident2)
            for (b_, aa_, ak_, pc, r) in chains:
                p.op("act", lambda e: e.activation(out=XTb[r][:], in_=b_.XX[:, 1, :], func=AF.Copy), reads=[b_.k("XX")], writes=[("XTb", r)])
                bk, k = pg.nextbank()
                p.op("pe", lambda e: e.matmul(bk[:, 0:128], lhsT=XTb[r][:], rhs=Vb[r][:], start=True, stop=True),
                     reads=[("XTb", r), ("Vb", r)], writes=[k], nosync_same=True)
                p.op("pe", lambda e: e.matmul(bk[:, 128:256], lhsT=Kb[r][:], rhs=XTb[r][:], start=True, stop=True),
                     reads=[("XTb", r), ("Kb", r)], writes=[k], nosync_same=True)
                p.op("act", lambda e: e.activation(out=uS[pc][:], in_=bk[:, 0:128], func=AF.Copy), reads=[k], writes=[("uS", pc)])
                p.op("act", lambda e: e.activation(out=wT[pc][:], in_=bk[:, 128:256], func=AF.Copy), reads=[k], writes=[("wT", pc)])

        for ci in range(NCI):
            for pr in range(2):
                pc = ci * 2 + pr
                vi = pc % 2
                bk, k = pg.nextbank()
                for half in range(2):
                    p.op("pe", lambda e: e.matmul(bk[half * 64:(half + 1) * 64, 0:128], lhsT=wT[pc][:, half * 64:(half + 1) * 64],
                                                  rhs=Sb[pr][:, half, :], start=True, stop=True),
                         reads=[("wT", pc), ("Sb", pr)], writes=[k], nosync_same=True)
                p.op("dve", lambda e: e.tensor_tensor(out=vnew[vi][:], in0=uS[pc][:], in1=bk[:, 0:128], op=ALU.subtract),
                     reads=[("uS", pc), k], writes=[("vnew", vi)])
                bko, ko = pg.nextbank()
                p.op("pe", lambda e: e.matmul(bko[:, 0:128], lhsT=qkT[pc][:], rhs=vnew[vi][:], start=True, stop=False),
                     reads=[("qkT", pc), ("vnew", vi)], writes=[ko], nosync_same=True)
                for half in range(2):
                    p.op("pe", lambda e: e.matmul(bko[half * 64:(half + 1) * 64, 0:128], lhsT=qgT[pc][:, half * 64:(half + 1) * 64],
                                                  rhs=Sb[pr][:, half, :], start=False, stop=True),
                         reads=[("qgT", pc), ("Sb", pr)], writes=[ko], nosync_same=True)
                bks2 = [pg.nextbank(), pg.nextbank()]
                for half in range(2):
                    bks, ks = bks2[half]
                    p.op("pe", lambda e: e.matmul(bks[:, 0:128], lhsT=Kd[pc][half * 64:(half + 1) * 64, :],
                                                  rhs=vnew[vi][half * 64:(half + 1) * 64, :], start=True, stop=True),
                         reads=[("Kd", pc), ("vnew", vi)], writes=[ks], nosync_same=True)
                for half in range(2):
                    bks, ks = bks2[half]
                    p.op("dve", lambda e: e.scalar_tensor_tensor(out=Sf[pr][:, half, :], in0=Sf[pr][:, half, :], scalar=egl[pc][:, half:half + 1],
                                                                 in1=bks[:, 0:128], op0=ALU.mult, op1=ALU.add),
                         reads=[("Sf", pr), ("egl", pc), ks], writes=[("Sf", pr)])
                p.op("act", lambda e: e.activation(out=Sb[pr][:], in_=Sf[pr][:], func=AF.Copy), reads=[("Sf", pr)], writes=[("Sb", pr)])
                p.op("act", lambda e: e.activation(out=junk[vi][:], in_=bko[:, 0:128], func=AF.Square, accum_out=ss[vi][:]),
                     reads=[ko], writes=[("ss", vi), ("junk", vi)])
                p.op("act", lambda e: e.activation(out=ss[vi][:], in_=ss[vi][:], func=AF.Sqrt, scale=1.0 / 128, bias=eps5[:, 0:1]),
                     reads=[("ss", vi), "eps5"], writes=[("ss", vi)])
                p.op("dve", lambda e: e.reciprocal(out=ss[vi][:], in_=ss[vi][:]), reads=[("ss", vi)], writes=[("ss", vi)])
                p.op("dve", lambda e: e.scalar_tensor_tensor(out=yt[vi][:], in0=bko[:, 0:128], scalar=ss[vi][:, 0:1], in1=nwrow[:],
                                                             op0=ALU.mult, op1=ALU.mult),
                     reads=[ko, ("ss", vi), "nwrow"], writes=[("yt", vi)])
                p.op("pool", lambda e: e.tensor_tensor(out=ost[pr][:, ci, :], in0=yt[vi][:], in1=zin[pr][:, ci, :], op=ALU.mult),
                     reads=[("yt", vi), ("zin", pr)], writes=[("ost", pr)])
        for pr in range(2):
            for half in range(2):
                col0 = (2 * pr + half) * 128
                p.dma("pool", yb[t0:t0 + TT, col0:col0 + 128].rearrange("(ci t) v -> t ci v", t=C),
                      ost[pr][half * 64:(half + 1) * 64, :, :], reads=[("ost", pr)])
    if standalone:
        p.finish()
    return p


def gdn_consts():
    m16x2, o32, o64 = inv_masks_np()
    I = np.eye(128, dtype=np.float32)
    i = np.arange(128)
    same64 = (i[:, None] // 64) == (i[None, :] // 64)
    maskSL = (same64 & (i[:, None] > i[None, :])).astype(np.float32)
    maskUT = (same64 & (i[None, :] >= i[:, None])).astype(np.float32)
    selrow = np.zeros((4, 4, 128), np.float32)
    i4p = np.zeros((4, 4, 2), np.float32)
    for h in range(4):
        selrow[h, h, :] = 1.0
        i4p[h, h, 0] = 1.0
    return {"m2": np.ascontiguousarray(np.stack([m16x2, np.stack([I, I], axis=1)])),
            "mk": np.ascontiguousarray(np.stack([o32, o64, maskSL, maskUT, I])), "selrow": selrow, "i4p": i4p}


def run_gdn(p0T_b, prm):
    p = _prog("gdn", build_gdn)
    cst = gdn_consts()
    in_maps = []
    G0 = 3328
    for core in range(8):
        b, hh = core // 2, core % 2
        P = p0T_b[b]
        sl = lambda w: P[G0 + w * 1024 + hh * 512: G0 + w * 1024 + (hh + 1) * 512]
        qkv = np.concatenate([sl(0), sl(1), sl(2)], axis=0)
        pad = np.ascontiguousarray(np.concatenate([np.zeros((1536, 3), np.float32), qkv], axis=1))
        cidx = np.concatenate([w * 1024 + hh * 512 + np.arange(512) for w in range(3)])
        cw = prm["gdn_conv_w"][:, cidx]
        m = {"qkvT": pad, "ztm": np.ascontiguousarray(sl(3).T),
             "bT": np.ascontiguousarray(P[G0 + 4096 + hh * 4: G0 + 4096 + hh * 4 + 4]),
             "aT": np.ascontiguousarray(P[G0 + 4104 + hh * 4: G0 + 4104 + hh * 4 + 4]),
             "cw": np.ascontiguousarray(cw.T.reshape(12, 128, 4).transpose(1, 0, 2)),
             "hp": np.ascontiguousarray(np.stack([prm["gdn_a_log"][hh * 4:hh * 4 + 4], prm["gdn_dt_bias"][hh * 4:hh * 4 + 4]], axis=1)),
             "nwrow": np.ascontiguousarray(np.broadcast_to(prm["gdn_norm_w"][None, :], (128, 128))),
             }
        m.update(cst)
        in_maps.append(m)
    r = _run(p, in_maps)
    return [np.asarray(r[i]["yb"]) for i in range(8)]


TTR = 256
PRM_COLS = {"mu_r": 0, "mu_k": 4, "mu_v": 8, "om_r": 12, "om_k": 16, "om_v": 20, "w0": 24, "a0": 28,
            "k_k": 32, "k_a": 36, "omka": 40, "r_k": 44}


def build_rwkv(p=None, io=None):
    standalone = p is None
    if standalone:
        p = Prog()
    pre = "" if standalone else "rwkv_"
    C = 64
    NCI = TTR // C
    NP_ = 4
    rkvT = io["rkvT"] if io else p.dram("rkvT", [1536, SEQ + 1], F32, "ExternalInput")
    lwT = io["lwT"] if io else p.dram("lwT", [64, SEQ + 1], F32, "ExternalInput")
    laT = io["laT"] if io else p.dram("laT", [64, SEQ + 1], F32, "ExternalInput")
    lgT = io["lgT"] if io else p.dram("lgT", [128, SEQ + 1], F32, "ExternalInput")
    prm_d = p.dram(pre + "prm", [128, 64], F32, "ExternalInput")
    lmu_d = p.dram(pre + "lmu", [128, 6], F32, "ExternalInput")
    w2_d = p.dram(pre + "w2", [64, 512], F32, "ExternalInput")
    a2_d = p.dram(pre + "a2", [64, 512], F32, "ExternalInput")
    g2_d = p.dram(pre + "g2", [128, 512], F32, "ExternalInput")
    lnw_d = p.dram(pre + "lnw", [128, 4, 64], F32, "ExternalInput")
    lnb_d = p.dram(pre + "lnb", [128, 4, 64], F32, "ExternalInput")
    m2_d = p.dram(pre + "m2", [4, 128, 2, 128], F32, "ExternalInput")
    mk_d = p.dram(pre + "mk", [4, 128, 128], F32, "ExternalInput")
    ya = io["ya"] if io else p.dram("ya", [SEQ, 512], F32, "ExternalOutput")

    def const(name, shape, src, dt=F32):
        t = p.sbuf(name, shape, dt)
        p.dma("pool", t[:], src, writes=[name])
        return t
    prm = const("prm", [128, 64], prm_d)
    lmu = const("lmu", [128, 6], lmu_d)
    w2f = const("w2f", [64, 512], w2_d)
    a2f = const("a2f", [64, 512], a2_d)
    g2f = const("g2f", [128, 512], g2_d)
    lnw = const("lnw", [128, 4, 64], lnw_d)
    lnb = const("lnb", [128, 4, 64], lnb_d)
    m16x2 = const("m16x2", [128, 2, 128], m2_d[0])
    ident2 = const("ident2", [128, 2, 128], m2_d[1])
    mAA = const("mAA", [128, 2, 128], m2_d[2])
    mUT2 = const("mUT2", [128, 2, 128], m2_d[3])
    o32 = const("o32", [128, 128], mk_d[0])
    o64 = const("o64", [128, 128], mk_d[1])
    maskSUT = const("maskSUT", [128, 128], mk_d[2])
    ident = const("ident", [128, 128], mk_d[3])
    masks = {"m16x2": m16x2, "o32": o32, "o64": o64}
    w2b = p.sbuf("w2b", [64, 512], BF16)
    a2b = p.sbuf("a2b", [64, 512], BF16)
    g2b = p.sbuf("g2b", [128, 512], BF16)
    identb = p.sbuf("identb", [128, 128], BF16)
    p.op("dve", lambda e: e.tensor_copy(out=w2b[:], in_=w2f[:]), reads=["w2f"], writes=["w2b"])
    p.op("dve", lambda e: e.tensor_copy(out=a2b[:], in_=a2f[:]), reads=["a2f"], writes=["a2b"])
    p.op("dve", lambda e: e.tensor_copy(out=g2b[:], in_=g2f[:]), reads=["g2f"], writes=["g2b"])
    p.op("dve", lambda e: e.tensor_copy(out=identb[:], in_=ident[:]), reads=["ident"], writes=["identb"])
    bd1 = p.sbuf("bd1", [128, 128], BF16)
    p.op("dve", lambda e: e.memset(bd1[:], 0.0), writes=["bd1"])
    p.op("dve", lambda e: e.memset(bd1[0:64, 0:64], 1.0), writes=["bd1"])
    p.op("dve", lambda e: e.memset(bd1[64:128, 64:128], 1.0), writes=["bd1"])
    ones2 = p.sbuf("ones2", [128, 2], BF16)
    p.op("dve", lambda e: e.memset(ones2[:], 1.0), writes=["ones2"])
    onesf = p.sbuf("onesf", [128, C], F32)
    p.op("dve", lambda e: e.memset(onesf[:], 1.0), writes=["onesf"])
    eps24 = p.sbuf("eps24", [128, 1], F32)
    p.op("dve", lambda e: e.memset(eps24[:], 1e-24), writes=["eps24"])
    epsln = p.sbuf("epsln", [128, 1], F32)
    p.op("dve", lambda e: e.memset(epsln[:], 64e-5), writes=["epsln"])
    PCOL = lambda name, pr: prm[:, PRM_COLS[name] + pr:PRM_COLS[name] + pr + 1]

    pq = PsQ(p, 4, "pq")
    pg = PsQ(p, 3, "pg")
    psb = p.psum("psb16", [128, 1024], BF16)

    rin = p.sbuf("rin", [128, 12, TTR + 1], F32)
    lwin = p.sbuf("lwin", [64, TTR + 1], F32)
    lain = p.sbuf("lain", [64, TTR + 1], F32)
    lgin = p.sbuf("lgin", [128, TTR + 1], F32)
    twb = p.sbuf("twb", [64, TTR], BF16)
    lab = p.sbuf("lab", [64, TTR], BF16)
    sgb = p.sbuf("sgb", [128, TTR], BF16)
    sqb = p.sbuf("sqb", [128, TTR], BF16)
    FT = {n: p.sbuf("ft_" + n, [128, TTR], F32) for n in
          ("rs", "ks", "vs", "logd", "av", "G", "kk", "rn", "kf", "bv", "eG", "eGm", "enG", "lw", "la", "lg")}
    ML = {}
    for nm in ("Rm", "KKm", "Km", "Bm", "Vm", "Pm"):
        ML[nm] = [p.sbuf(f"{nm}{pr}", [128, NCI, 2, C], BF16) for pr in range(NP_)]
        for pr in range(NP_):
            p.op("pool", lambda e: e.memset(ML[nm][pr][:], 0.0), writes=[(nm, pr)])
    eGC = [p.sbuf(f"eGC{pr}", [128, NCI], F32) for pr in range(NP_)]
    NPC = NP_ * NCI
    AakT = [p.sbuf(f"AakT{i}", [128, 128], BF16) for i in range(NPC)]
    Arx = [p.sbuf(f"Arx{i}", [128, 2, 128], BF16) for i in range(NPC)]
    XTb = [p.sbuf(f"XTb{i}", [128, 128], BF16) for i in range(NPC)]
    KBt = [p.sbuf(f"KBt{i}", [128, 2, 128], BF16) for i in range(NPC)]
    Vc = [p.sbuf(f"Vc{i}", [128, C], BF16) for i in range(NPC)]
    bon = [p.sbuf(f"bon{i}", [128, 2], F32) for i in range(NPC)]
    gtm = [p.sbuf(f"gtm{i}", [128, C], F32) for i in range(NPC)]
    NTR = 8
    AA = [p.sbuf(f"AA{i}", [128, 2, 128], F32) for i in range(NTR)]
    ivb = [InvBufs(p, f"iv{i}", INV_DT) for i in range(8)]
    Zf = [p.sbuf(f"Zf{pr}", [128, C], F32) for pr in range(NP_)]
    Zb = [p.sbuf(f"Zb{pr}", [128, C], BF16) for pr in range(NP_)]
    for pr in range(NP_):
        p.op("dve", lambda e: e.memset(Zf[pr][:], 0.0), writes=[("Zf", pr)])
        p.op("pool", lambda e: e.memset(Zb[pr][:], 0.0), writes=[("Zb", pr)])
    r1 = [p.sbuf(f"r1_{pr}", [128, C], BF16) for pr in range(NP_)]
    Us = [p.sbuf(f"Us{pr}", [128, C], BF16) for pr in range(NP_)]
    Ys = [p.sbuf(f"Ys{pr}", [128, C], F32) for pr in range(NP_)]
    yn = [p.sbuf(f"yn{pr}", [128, C], F32) for pr in range(NP_)]
    bst = [p.sbuf(f"bst{pr}", [128, 6], F32) for pr in range(NP_)]
    mv = [p.sbuf(f"mv{pr}", [128, 2], F32) for pr in range(NP_)]
    ostg = [p.sbuf(f"ostg{pr}", [128, NCI, C], F32) for pr in range(NP_)]

    f2 = lambda t: t[:].rearrange("q a c -> q (a c)")
    rkvTv = rkvT.rearrange("(kc q) t -> q kc t", q=128)
    tctr = 0
    for tl in range(SEQ // TTR):
        t0 = tl * TTR
        for k4 in range(3):
            p.dma("sp", rin[:, k4 * 4:k4 * 4 + 4, :], rkvTv[:, k4 * 4:k4 * 4 + 4, t0:t0 + TTR + 1],
                  writes=[("rin", kc) for kc in range(k4 * 4, k4 * 4 + 4)])
        p.dma("sp", lwin[:], lwT[:, t0:t0 + TTR + 1], writes=["lwin"])
        p.dma("sp", lain[:], laT[:, t0:t0 + TTR + 1], writes=["lain"])
        p.dma("sp", lgin[:], lgT[:, t0:t0 + TTR + 1], writes=["lgin"])

        def lerp(out_ap, okey, src0, src1, skey, mu_ap, om_ap):
            p.op("pool", lambda e: e.tensor_scalar(out=out_ap, in0=src0, scalar1=mu_ap, scalar2=None, op0=ALU.mult),
                 reads=[skey, "prm", "lmu"], writes=[okey])
            p.op("dve", lambda e: e.scalar_tensor_tensor(out=out_ap, in0=src1, scalar=om_ap, in1=out_ap, op0=ALU.mult, op1=ALU.add),
                 reads=[skey, okey, "prm", "lmu"], writes=[okey])

        lerp(FT["lw"][0:64, :], "lw", lwin[:, 0:TTR], lwin[:, 1:TTR + 1], "lwin", lmu[0:64, 0:1], lmu[0:64, 1:2])
        lerp(FT["la"][0:64, :], "la", lain[:, 0:TTR], lain[:, 1:TTR + 1], "lain", lmu[0:64, 2:3], lmu[0:64, 3:4])
        lerp(FT["lg"][:, :], "lg", lgin[:, 0:TTR], lgin[:, 1:TTR + 1], "lgin", lmu[:, 4:5], lmu[:, 5:6])
        p.op("act", lambda e: e.activation(out=twb[:], in_=FT["lw"][0:64, :], func=AF.Tanh), reads=["lw"], writes=["twb"])
        p.op("act", lambda e: e.activation(out=lab[:], in_=FT["la"][0:64, :], func=AF.Copy), reads=["la"], writes=["lab"])
        p.op("act", lambda e: e.activation(out=sgb[:], in_=FT["lg"][:, :], func=AF.Sigmoid), reads=["lg"], writes=["sgb"])

        for pr in range(NP_):
            rs, ks, vs, logd, av, G, kk, rn, kf, bv, eG, eGm, enG = (FT[n] for n in
                ("rs", "ks", "vs", "logd", "av", "G", "kk", "rn", "kf", "bv", "eG", "eGm", "enG"))
            lerp(rs[:], "rs", rin[:, pr, 0:TTR], rin[:, pr, 1:TTR + 1], ("rin", pr), PCOL("mu_r", pr), PCOL("om_r", pr))
            lerp(ks[:], "ks", rin[:, 4 + pr, 0:TTR], rin[:, 4 + pr, 1:TTR + 1], ("rin", 4 + pr), PCOL("mu_k", pr), PCOL("om_k", pr))
            lerp(vs[:], "vs", rin[:, 8 + pr, 0:TTR], rin[:, 8 + pr, 1:TTR + 1], ("rin", 8 + pr), PCOL("mu_v", pr), PCOL("om_v", pr))
            bk, k = pg.nextbank()
            p.op("pe", lambda e: e.matmul(bk[:, 0:TTR], lhsT=w2b[:, pr * 128:(pr + 1) * 128], rhs=twb[:], start=True, stop=True),
                 reads=["w2b", "twb"], writes=[k])
            p.op("act", lambda e: e.activation(out=logd[:], in_=bk[:, 0:TTR], func=AF.Sigmoid, bias=PCOL("w0", pr)),
                 reads=[k, "prm"], writes=["logd"])
            p.op("pool", lambda e: e.tensor_scalar(out=logd[:], in0=logd[:], scalar1=-float(np.exp(-0.5)), scalar2=None, op0=ALU.mult),
                 reads=["logd"], writes=["logd"])
            bk, k = pg.nextbank()
            p.op("pe", lambda e: e.matmul(bk[:, 0:TTR], lhsT=a2b[:, pr * 128:(pr + 1) * 128], rhs=lab[:], start=True, stop=True),
                 reads=["a2b", "lab"], writes=[k])
            p.op("act", lambda e: e.activation(out=av[:], in_=bk[:, 0:TTR], func=AF.Sigmoid, bias=PCOL("a0", pr)),
                 reads=[k, "prm"], writes=["av"])
            for ci in range(NCI):
                cs_ = slice(ci * C, (ci + 1) * C)
                p.op("dve", lambda e: e.tensor_tensor_scan(out=G[:, cs_], data0=onesf[:, 0:C], data1=logd[:, cs_], initial=0.0,
                                                           op0=ALU.mult, op1=ALU.add),
                     reads=["logd", "onesf"], writes=["G"])
            p.op("pool", lambda e: e.tensor_scalar(out=kk[:], in0=ks[:], scalar1=PCOL("k_k", pr), scalar2=None, op0=ALU.mult),
                 reads=["ks", "prm"], writes=["kk"])
            p.op("act", lambda e: e.activation(out=sqb[:], in_=kk[:], func=AF.Square), reads=["kk"], writes=["sqb"])
            bk, k = pg.nextbank()
            p.op("pe", lambda e: e.matmul(bk[:, 0:TTR], lhsT=bd1[:], rhs=sqb[:], start=True, stop=True), reads=["bd1", "sqb"], writes=[k])
            p.op("act", lambda e: e.activation(out=rn[:], in_=bk[:, 0:TTR], func=AF.Sqrt, bias=eps24[:, 0:1]), reads=[k, "eps24"], writes=["rn"])
            p.op("dve", lambda e: e.reciprocal(out=rn[:], in_=rn[:]), reads=["rn"], writes=["rn"])
            p.op("dve", lambda e: e.tensor_tensor(out=kk[:], in0=kk[:], in1=rn[:], op=ALU.mult), reads=["kk", "rn"], writes=["kk"])
            p.op("dve", lambda e: e.tensor_scalar(out=kf[:], in0=av[:], scalar1=PCOL("k_a", pr), scalar2=PCOL("omka", pr),
                                                  op0=ALU.mult, op1=ALU.add),
                 reads=["av", "prm"], writes=["kf"])
            p.op("pool", lambda e: e.tensor_tensor(out=ks[:], in0=ks[:], in1=kf[:], op=ALU.mult), reads=["ks", "kf"], writes=["ks"])
            p.op("pool", lambda e: e.tensor_tensor(out=bv[:], in0=kk[:], in1=av[:], op=ALU.mult), reads=["kk", "av"], writes=["bv"])
            p.op("act", lambda e: e.activation(out=eG[:], in_=G[:], func=AF.Exp), reads=["G"], writes=["eG"])
            p.op("act", lambda e: e.activation(out=enG[:], in_=G[:], func=AF.Exp, scale=-1.0), reads=["G"], writes=["enG"])
            p.op("pool", lambda e: e.tensor_tensor(out=eGm[:], in0=G[:], in1=logd[:], op=ALU.subtract), reads=["G", "logd"], writes=["eGm"])
            p.op("act", lambda e: e.activation(out=eGm[:], in_=eGm[:], func=AF.Exp), reads=["eGm"], writes=["eGm"])
            p.op("dve", lambda e: e.tensor_copy(out=eGC[pr][:], in_=eG[:, C - 1:TTR:C]), reads=["eG"], writes=[("eGC", pr)])

            def masked(nm, fn_half):
                for half in range(2):
                    ps_ = slice(half * 64, (half + 1) * 64)
                    out_ap = ML[nm][pr][ps_, :, half, :]
                    fn_half("dve" if half == 0 else "pool", out_ap, ps_)
            v3 = lambda t, ps_: t[ps_, :].rearrange("q (a c) -> q a c", c=C)
            masked("Rm", lambda eng, o, ps_: p.op(eng, lambda e: e.tensor_tensor(out=o, in0=v3(rs, ps_), in1=v3(eG, ps_), op=ALU.mult),
                                                  reads=["rs", "eG"], writes=[("Rm", pr)]))
            masked("KKm", lambda eng, o, ps_: p.op(eng, lambda e: e.tensor_tensor(out=o, in0=v3(kk, ps_), in1=v3(eGm, ps_), op=ALU.mult),
                                                   reads=["kk", "eGm"], writes=[("KKm", pr)]))
            masked("Km", lambda eng, o, ps_: p.op(eng, lambda e: e.tensor_tensor(out=o, in0=v3(ks, ps_), in1=v3(enG, ps_), op=ALU.mult),
                                                  reads=["ks", "enG"], writes=[("Km", pr)]))
            masked("Bm", lambda eng, o, ps_: p.op(eng, lambda e: e.tensor_tensor(out=o, in0=v3(bv, ps_), in1=v3(enG, ps_), op=ALU.mult),
                                                  reads=["bv", "enG"], writes=[("Bm", pr)]))
            masked("Vm", lambda eng, o, ps_: p.op(eng, lambda e: e.tensor_copy(out=o, in_=v3(vs, ps_)),
                                                  reads=["vs"], writes=[("Vm", pr)]))
            p.op("dve", lambda e: e.scalar_tensor_tensor(out=kf[:], in0=rs[:], scalar=PCOL("r_k", pr), in1=ks[:], op0=ALU.mult, op1=ALU.mult),
                 reads=["rs", "ks", "prm", "kf"], writes=["kf"])
            masked("Pm", lambda eng, o, ps_: p.op(eng, lambda e: e.tensor_copy(out=o, in_=v3(kf, ps_)),
                                                  reads=["kf"], writes=[("Pm", pr)]))

        pcs = [(ci, pr) for ci in range(NCI) for pr in range(NP_)]
        for g0 in range(0, len(pcs), 8):
            grp = pcs[g0:g0 + 8]
            chains = []
            for gi, (ci, pr) in enumerate(grp):
                pc = ci * NP_ + pr
                r = tctr % NTR
                tctr += 1
                cs_ = slice(ci * C, (ci + 1) * C)
                m = lambda nm: ML[nm][pr][:, ci, :, :].rearrange("q h t -> q (h t)")
                bk, k = pg.nextbank()
                p.op("pe", lambda e: e.matmul(bk[:, 0:128], lhsT=m("KKm"), rhs=m("Bm"), start=True, stop=True),
                     reads=[("KKm", pr), ("Bm", pr)], writes=[k], nosync_same=True)
                p.op("pe", lambda e: e.matmul(bk[:, 128:256], lhsT=m("Bm"), rhs=m("KKm"), start=True, stop=True),
                     reads=[("KKm", pr), ("Bm", pr)], writes=[k], nosync_same=True)
                p.op("dve", lambda e: e.tensor_tensor(out=f2(AA[r]), in0=bk[:, 0:256], in1=f2(mAA), op=ALU.mult),
                     reads=[k, "mAA"], writes=[("AA", r)])
                bk, k = pg.nextbank()
                p.op("pe", lambda e: e.matmul(bk[:, 0:128], lhsT=m("Km"), rhs=m("KKm"), start=True, stop=True),
                     reads=[("KKm", pr), ("Km", pr)], writes=[k], nosync_same=True)
                p.op("pe", lambda e: e.matmul(bk[:, 128:256], lhsT=m("Km"), rhs=m("Rm"), start=True, stop=True),
                     reads=[("Rm", pr), ("Km", pr)], writes=[k], nosync_same=True)
                p.op("pe", lambda e: e.matmul(bk[:, 256:384], lhsT=m("Bm"), rhs=m("Rm"), start=True, stop=True),
                     reads=[("Rm", pr), ("Bm", pr)], writes=[k], nosync_same=True)
                p.op("dve", lambda e: e.tensor_tensor(out=AakT[pc][:], in0=bk[:, 0:128], in1=maskSUT[:], op=ALU.mult),
                     reads=[k, "maskSUT"], writes=[("AakT", pc)])
                p.op("dve", lambda e: e.tensor_tensor(out=f2(Arx[pc]), in0=bk[:, 128:384], in1=f2(mUT2), op=ALU.mult),
                     reads=[k, "mUT2"], writes=[("Arx", pc)])
                for j, nm in enumerate(("Vm", "Km", "Bm")):
                    p.op("pe", lambda e: e.transpose(out=psb[:, j * 128:(j + 1) * 128], in_=m(nm), identity=identb[:]),
                         reads=[(nm, pr), "identb"], writes=["psb"], nosync_same=True)
                for half in range(2):
                    ps_ = slice(half * 64, (half + 1) * 64)
                    p.op("act", lambda e: e.activation(out=Vc[pc][ps_, :], in_=psb[ps_, half * 64:(half + 1) * 64], func=AF.Copy),
                         reads=["psb"], writes=[("Vc", pc)])
                p.op("act", lambda e: e.activation(out=f2(KBt[pc]), in_=psb[:, 128:384], func=AF.Copy), reads=["psb"], writes=[("KBt", pc)])
                bk, k = pg.nextbank()
                for half in range(2):
                    hcol = (2 * pr + half) * 64
                    p.op("pe", lambda e: e.matmul(bk[half * 64:(half + 1) * 64, 0:64], lhsT=sgb[:, cs_], rhs=g2b[:, hcol:hcol + 64],
                                                  start=True, stop=True),
                         reads=["sgb", "g2b"], writes=[k], nosync_same=True)
                p.op("pe", lambda e: e.matmul(bk[:, 64:66], lhsT=m("Pm"), rhs=ones2[:], start=True, stop=True),
                     reads=[("Pm", pr), "ones2"], writes=[k], nosync_same=True)
                p.op("act", lambda e: e.activation(out=gtm[pc][:], in_=bk[:, 0:64], func=AF.Copy), reads=[k], writes=[("gtm", pc)])
                p.op("act", lambda e: e.activation(out=bon[pc][:], in_=bk[:, 64:66], func=AF.Copy), reads=[k], writes=[("bon", pc)])
                chains.append((ivb[gi], AA[r], ("AA", r), pc))
            emit_inverse(p, pq, [(b_, aa_, ak_) for (b_, aa_, ak_, _pc) in chains], masks, ident2)
            for (b_, aa_, ak_, pc) in chains:
                p.op("pool", lambda e: e.tensor_copy(out=XTb[pc][:], in_=b_.XX[:, 1, :]), reads=[b_.k("XX")], writes=[("XTb", pc)])

        for ci in range(NCI):
            PCI = lambda pr: ci * NP_ + pr
            m = lambda nm, pr: ML[nm][pr][:, ci, :, :].rearrange("q h t -> q (h t)")
            for s0 in range(0, NP_, pg.n):
                st = {}
                for pr in range(s0, min(NP_, s0 + pg.n)):
                    pc = PCI(pr)
                    bk, k = pg.nextbank()
                    p.op("pe", lambda e: e.matmul(bk[:, 0:C], lhsT=AakT[pc][:], rhs=Vc[pc][:], start=True, stop=False),
                         reads=[("AakT", pc), ("Vc", pc)], writes=[k], nosync_same=True)
                    p.op("pe", lambda e: e.matmul(bk[:, 0:C], lhsT=m("KKm", pr), rhs=Zb[pr][:], start=False, stop=True),
                         reads=[("KKm", pr), ("Zb", pr)], writes=[k], nosync_same=True)
                    st[pr] = (bk, k)
                for pr in st:
                    bk, k = st[pr]
                    p.op("act", lambda e: e.activation(out=r1[pr][:], in_=bk[:, 0:C], func=AF.Copy), reads=[k], writes=[("r1", pr)])
            for s0 in range(0, NP_, pg.n):
                st = {}
                for pr in range(s0, min(NP_, s0 + pg.n)):
                    pc = PCI(pr)
                    bk, k = pg.nextbank()
                    p.op("pe", lambda e: e.matmul(bk[:, 0:C], lhsT=XTb[pc][:], rhs=r1[pr][:], start=True, stop=True),
                         reads=[("XTb", pc), ("r1", pr)], writes=[k], nosync_same=True)
                    st[pr] = (bk, k)
                for pr in st:
                    bk, k = st[pr]
                    p.op("act", lambda e: e.activation(out=Us[pr][:], in_=bk[:, 0:C], func=AF.Copy, scale=-1.0), reads=[k], writes=[("Us", pr)])
            for pr in range(NP_):
                pc = PCI(pr)
                bky, ky = pg.nextbank()
                p.op("pe", lambda e: e.matmul(bky[:, 0:C], lhsT=m("Rm", pr), rhs=Zb[pr][:], start=True, stop=False),
                     reads=[("Rm", pr), ("Zb", pr)], writes=[ky], nosync_same=True)
                p.op("pe", lambda e: e.matmul(bky[:, 0:C], lhsT=Arx[pc][:, 0, :], rhs=Vc[pc][:], start=False, stop=False),
                     reads=[("Arx", pc), ("Vc", pc)], writes=[ky], nosync_same=True)
                p.op("pe", lambda e: e.matmul(bky[:, 0:C], lhsT=Arx[pc][:, 1, :], rhs=Us[pr][:], start=False, stop=True),
                     reads=[("Arx", pc), ("Us", pr)], writes=[ky], nosync_same=True)
                bkz, kz = pg.nextbank()
                p.op("pe", lambda e: e.matmul(bkz[:, 0:C], lhsT=KBt[pc][:, 0, :], rhs=Vc[pc][:], start=True, stop=False),
                     reads=[("KBt", pc), ("Vc", pc)], writes=[kz], nosync_same=True)
                p.op("pe", lambda e: e.matmul(bkz[:, 0:C], lhsT=KBt[pc][:, 1, :], rhs=Us[pr][:], start=False, stop=True),
                     reads=[("KBt", pc), ("Us", pr)], writes=[kz], nosync_same=True)
                p.op("act", lambda e: e.activation(out=Ys[pr][:], in_=bky[:, 0:C], func=AF.Copy), reads=[ky], writes=[("Ys", pr)])
                p.op("pool", lambda e: e.tensor_scalar(out=Zf[pr][:], in0=Zf[pr][:], scalar1=eGC[pr][:, ci:ci + 1], scalar2=None, op0=ALU.mult),
                     reads=[("Zf", pr), ("eGC", pr)], writes=[("Zf", pr)])
                p.op("dve", lambda e: e.scalar_tensor_tensor(out=Zf[pr][:], in0=bkz[:, 0:C], scalar=eGC[pr][:, ci:ci + 1], in1=Zf[pr][:],
                                                             op0=ALU.mult, op1=ALU.add),
                     reads=[kz, ("eGC", pr), ("Zf", pr)], writes=[("Zf", pr)])
                p.op("act", lambda e: e.activation(out=Zb[pr][:], in_=Zf[pr][:], func=AF.Copy), reads=[("Zf", pr)], writes=[("Zb", pr)])
            for pr in range(NP_):
                pc = PCI(pr)
                p.op("dve", lambda e: e.bn_stats(out=bst[pr][:], in_=Ys[pr][:]), reads=[("Ys", pr)], writes=[("bst", pr)])
                p.op("dve", lambda e: e.bn_aggr(out=mv[pr][:], in_=bst[pr][:]), reads=[("bst", pr)], writes=[("mv", pr)])
                p.op("act", lambda e: e.activation(out=mv[pr][:, 1:2], in_=mv[pr][:, 1:2], func=AF.Sqrt, bias=epsln[:, 0:1]),
                     reads=[("mv", pr), "epsln"], writes=[("mv", pr)])
                p.op("dve", lambda e: e.reciprocal(out=mv[pr][:, 1:2], in_=mv[pr][:, 1:2]), reads=[("mv", pr)], writes=[("mv", pr)])
                p.op("dve", lambda e: e.tensor_scalar(out=yn[pr][:], in0=Ys[pr][:], scalar1=mv[pr][:, 0:1], scalar2=mv[pr][:, 1:2],
                                                      op0=ALU.subtract, op1=ALU.mult),
                     reads=[("Ys", pr), ("mv", pr)], writes=[("yn", pr)])
                p.op("pool", lambda e: e.tensor_tensor(out=yn[pr][:], in0=yn[pr][:], in1=lnw[:, pr, :], op=ALU.mult),
                     reads=[("yn", pr), "lnw"], writes=[("yn", pr)])
                p.op("pool", lambda e: e.tensor_tensor(out=yn[pr][:], in0=yn[pr][:], in1=lnb[:, pr, :], op=ALU.add),
                     reads=[("yn", pr), "lnb"], writes=[("yn", pr)])
                p.op("dve", lambda e: e.scalar_tensor_tensor(out=yn[pr][:], in0=Vc[pc][:], scalar=bon[pc][:, 0:1], in1=yn[pr][:],
                                                             op0=ALU.mult, op1=ALU.add),
                     reads=[("Vc", pc), ("bon", pc), ("yn", pr)], writes=[("yn", pr)])
                p.op("pool", lambda e: e.tensor_tensor(out=ostg[pr][:, ci, :], in0=yn[pr][:], in1=gtm[pc][:], op=ALU.mult),
                     reads=[("yn", pr), ("gtm", pc)], writes=[("ostg", pr)])
        for pr in range(NP_):
            for half in range(2):
                col0 = (2 * pr + half) * 64
                p.dma("pool", ya[t0:t0 + TTR, col0:col0 + 64].rearrange("(ci t) v -> t ci v", t=C),
                      ostg[pr][half * 64:(half + 1) * 64, :, :], reads=[("ostg", pr)])
    if standalone:
        p.finish()
    return p


def rwkv_consts():
    m16x2, o32, o64 = inv_masks_np()
    I = np.eye(128, dtype=np.float32)
    i = np.arange(128)
    same64 = (i[:, None] // 64) == (i[None, :] // 64)
    maskSL = (same64 & (i[:, None] > i[None, :])).astype(np.float32)
    maskSUT = (same64 & (i[None, :] > i[:, None])).astype(np.float32)
    maskUT = (same64 & (i[None, :] >= i[:, None])).astype(np.float32)
    st = lambda a, b: np.stack([a, b], axis=1)
    return {"m2": np.ascontiguousarray(np.stack([m16x2, st(I, I), st(maskSL, maskSUT), st(maskUT, maskUT)])),
            "mk": np.ascontiguousarray(np.stack([o32, o64, maskSUT, I]))}


def run_rwkv(p0T_b, prm):
    p = _prog("rwkv", build_rwkv)
    cst = rwkv_consts()
    mu = prm["rwkv_mu"]
    in_maps = []
    for core in range(8):
        b, hh = core // 2, core % 2
        P = p0T_b[b]
        my = slice(hh * 512, (hh + 1) * 512)
        z1 = lambda a: np.ascontiguousarray(np.concatenate([np.zeros((a.shape[0], 1), np.float32), a], axis=1))
        rkv = np.concatenate([P[0:1024][my], P[1024:2048][my], P[2048:3072][my]], axis=0)
        cols = np.zeros((128, 64), np.float32)

        def put(name, vec512):
            cols[:, PRM_COLS[name]:PRM_COLS[name] + 4] = vec512.reshape(4, 128).T
        put("mu_r", mu[0:1024][my]); put("mu_k", mu[1024:2048][my]); put("mu_v", mu[2048:3072][my])
        put("om_r", 1 - mu[0:1024][my]); put("om_k", 1 - mu[1024:2048][my]); put("om_v", 1 - mu[2048:3072][my])
        put("w0", prm["rwkv_w0"][my]); put("a0", prm["rwkv_a0"][my]); put("k_k", prm["rwkv_k_k"][my])
        put("k_a", prm["rwkv_k_a"][my]); put("omka", 1 - prm["rwkv_k_a"][my]); put("r_k", prm["rwkv_r_k"].reshape(-1)[my])
        lmu = np.zeros((128, 6), np.float32)
        lmu[0:64, 0] = mu[3072:3136]; lmu[0:64, 1] = 1 - mu[3072:3136]
        lmu[0:64, 2] = mu[3136:3200]; lmu[0:64, 3] = 1 - mu[3136:3200]
        lmu[:, 4] = mu[3200:3328]; lmu[:, 5] = 1 - mu[3200:3328]

        def rows_tm(v512):
            a = v512.reshape(4, 2, 64)
            return np.ascontiguousarray(np.repeat(a.transpose(1, 0, 2)[:, None, :, :], 64, axis=1).reshape(128, 4, 64))
        m = {"rkvT": z1(rkv), "lwT": z1(P[3072:3136]), "laT": z1(P[3136:3200]), "lgT": z1(P[3200:3328]),
             "prm": cols, "lmu": lmu,
             "w2": np.ascontiguousarray(prm["rwkv_w2"][:, my]), "a2": np.ascontiguousarray(prm["rwkv_a2"][:, my]),
             "g2": np.ascontiguousarray(prm["rwkv_g2"][:, my]),
             "lnw": rows_tm(prm["rwkv_ln_w"][my]), "lnb": rows_tm(prm["rwkv_ln_b"][my])}
        m.update(cst)
        in_maps.append(m)
    r = _run(p, in_maps)
    return [np.asarray(r[i]["ya"]) for i in range(8)]


def kernel_unfused(x, c, ada_mix_w, ada_mix_b, ada_ffn_w, ada_ffn_b, hg_w_in, hg_w_out,
           rwkv_mu, rwkv_w0, rwkv_w2, rwkv_a0, rwkv_a2, rwkv_g2, rwkv_k_k, rwkv_k_a,
           rwkv_r_k, rwkv_ln_w, rwkv_ln_b, gdn_conv_w, gdn_a_log, gdn_dt_bias, gdn_norm_w,
           ssm_w_in, ssm_conv_w, ssm_conv_b, ssm_dt_bias, ssm_a_log, ssm_d, ssm_norm_w,
           ssm_w_out, ffn_w1, ffn_w3, ffn_w2, final_norm_w):
    f32 = lambda a: np.ascontiguousarray(np.asarray(a, dtype=np.float32))
    x, c = f32(x), f32(c)
    cw = cast_weights({"ada_mix_w": f32(ada_mix_w), "ada_ffn_w": f32(ada_ffn_w), "hg_w_in": f32(hg_w_in),
                       "hg_w_out": f32(hg_w_out), "ssm_w_in": f32(ssm_w_in), "ssm_w_out": f32(ssm_w_out),
                       "ffn_w1": f32(ffn_w1), "ffn_w3": f32(ffn_w3), "ffn_w2": f32(ffn_w2)})
    ada_mix_b, ada_ffn_b = f32(ada_mix_b), f32(ada_ffn_b)
    xT = [np.ascontiguousarray(x[i // 2, (i % 2) * TOK:(i % 2 + 1) * TOK].T) for i in range(8)]
    pT = run_inproj(xT, c, cw["ada_mix_w"][0], ada_mix_b[0], pad_cols(cw["hg_w_in"][0], 7552))
    p0T = [np.concatenate([pT[2 * b], pT[2 * b + 1]], axis=1) for b in range(4)]
    del pT
    prm_r = {"rwkv_mu": f32(rwkv_mu)[0], "rwkv_w0": f32(rwkv_w0)[0], "rwkv_w2": f32(rwkv_w2)[0], "rwkv_a0": f32(rwkv_a0)[0],
             "rwkv_a2": f32(rwkv_a2)[0], "rwkv_g2": f32(rwkv_g2)[0], "rwkv_k_k": f32(rwkv_k_k)[0], "rwkv_k_a": f32(rwkv_k_a)[0],
             "rwkv_r_k": f32(rwkv_r_k)[0], "rwkv_ln_w": f32(rwkv_ln_w)[0], "rwkv_ln_b": f32(rwkv_ln_b)[0]}
    ya = run_rwkv(p0T, prm_r)
    prm_g = {"gdn_conv_w": f32(gdn_conv_w)[0], "gdn_a_log": f32(gdn_a_log)[0], "gdn_dt_bias": f32(gdn_dt_bias)[0],
             "gdn_norm_w": f32(gdn_norm_w)[0]}
    yb = run_gdn(p0T, prm_g)
    del p0T
    yT = []
    for i in range(8):
        b, s = i // 2, i % 2
        ts = slice(s * TOK, (s + 1) * TOK)
        ym = np.concatenate([ya[2 * b][ts], ya[2 * b + 1][ts], yb[2 * b][ts], yb[2 * b + 1][ts]], axis=1)
        yT.append(np.ascontiguousarray(ym.T))
    del ya, yb
    x1T = run_outffn(xT, yT, c, cw["ada_mix_w"][0], ada_mix_b[0], cw["ada_ffn_w"][0], ada_ffn_b[0],
                     cw["hg_w_out"][0], cw["ffn_w1"][0], cw["ffn_w3"][0], cw["ffn_w2"][0])
    del xT, yT
    pT = run_inproj(x1T, c, cw["ada_mix_w"][1], ada_mix_b[1], pad_cols(cw["ssm_w_in"][0], 10368))
    p1T = [np.concatenate([pT[2 * b], pT[2 * b + 1]], axis=1) for b in range(4)]
    del pT
    prm_s = {"ssm_conv_w": f32(ssm_conv_w)[0], "ssm_conv_b": f32(ssm_conv_b)[0], "ssm_dt_bias": f32(ssm_dt_bias)[0],
             "ssm_a_log": f32(ssm_a_log)[0], "ssm_d": f32(ssm_d)[0], "ssm_norm_w": f32(ssm_norm_w)[0]}
    ys = run_ssd(p1T, prm_s)
    del p1T
    yT = []
    for i in range(8):
        b, s = i // 2, i % 2
        ts = slice(s * TOK, (s + 1) * TOK)
        yT.append(np.ascontiguousarray(np.concatenate([ys[2 * b][:, ts], ys[2 * b + 1][:, ts]], axis=0)))
    del ys
    oT = run_outffn(x1T, yT, c, cw["ada_mix_w"][1], ada_mix_b[1], cw["ada_ffn_w"][1], ada_ffn_b[1],
                    cw["ssm_w_out"][0], cw["ffn_w1"][1], cw["ffn_w3"][1], cw["ffn_w2"][1], fnw=f32(final_norm_w))
    out = np.empty((4, SEQ, D), np.float32)
    for i in range(8):
        out[i // 2, (i % 2) * TOK:(i % 2 + 1) * TOK] = oT[i].T
    return out


GSF = 4096
RG = [[0, 1], [2, 3], [4, 5], [6, 7]]


def pack_wg(W):
    K, N = W.shape
    Kc, ncc = K // 128, N // 128
    cpg = GSF // (Kc * 128)
    ng = (ncc + cpg - 1) // cpg
    Wp = np.zeros((K, ng * cpg * 128), dtype=np.float32)
    Wp[:, :N] = W
    A = Wp.reshape(Kc, 128, ng, cpg, 128).transpose(2, 1, 3, 0, 4).reshape(ng, 128, cpg * Kc * 128)
    out = np.zeros((ng, 128, GSF), dtype=np.float32)
    out[:, :, :cpg * Kc * 128] = A
    return out


class WStreamC:
    def __init__(self, p, ws_ap, seq, nst=2, nslot=3):
        self.p, self.ws, self.seq, self.nst, self.nslot = p, ws_ap, list(seq), nst, nslot
        self.st = [p.sbuf(f"wst{i}", [128, GSF], F32) for i in range(nst)]
        self.slots = [p.sbuf(f"wsl{i}", [128, GSF], BF16) for i in range(nslot)]
        self.issued = 0
        self.used = 0

    def _issue(self):
        p = self.p
        while self.issued < len(self.seq) and self.issued < self.used + self.nslot:
            g, u = self.seq[self.issued]
            a, s = self.issued % self.nst, self.issued % self.nslot
            p.dma("sp", self.st[a][:, :u], self.ws[g][:, :u], writes=[("wf", a)])
            if self.issued % 2 == 0:
                p.op("act", lambda e: e.activation(out=self.slots[s][:, :u], in_=self.st[a][:, :u], func=AF.Copy),
                     reads=[("wf", a)], writes=[("w", s)])
            else:
                p.op("pool", lambda e: e.tensor_copy(out=self.slots[s][:, :u], in_=self.st[a][:, :u]),
                     reads=[("wf", a)], writes=[("w", s)])
            self.issued += 1

    def next(self):
        self._issue()
        s = self.used % self.nslot
        self.used += 1
        return self.slots[s], ("w", s)


def emit_mod2(cx, wst, scb, bias, mod, jlist, name):
    p = cx.p
    assert len(jlist) % 2 == 0
    for g in range(len(jlist) // 2):
        slot, wk = wst.next()
        ps, pk = cx.next_ps()
        for jj in range(2):
            for kc in range(16):
                off = (jj * 16 + kc) * 128
                p.op("pe", lambda e: e.matmul(ps[:, jj:jj + 1], lhsT=slot[:, off:off + 128], rhs=scb[:, kc:kc + 1],
                                              start=(kc == 0), stop=(kc == 15)),
                     reads=[wk, "scb"], writes=[pk], nosync_same=True)
        j0 = jlist[g * 2]
        p.op("dve", lambda e: e.tensor_tensor(out=mod[:, j0:j0 + 2], in0=ps[:, 0:2], in1=bias[:, j0:j0 + 2], op=ALU.add),
             reads=[pk, name + "b"], writes=[name])


def build_fused():
    p = Prog()
    nc = p.nc
    NTL = SEQ // TT
    xT = p.dram("xT", [D, SEQ], F32, "ExternalInput")
    cs_d = p.dram("cs", [128, 16], F32, "ExternalInput")
    adab_d = p.dram("adab", [4, 128, 48], F32, "ExternalInput")
    fnw_d = p.dram("fnw", [128, 16], F32, "ExternalInput")
    NG = {"ada": 24, "in0": 16, "in1": 21, "out0": 4, "out1": 8, "w13": 22, "w2": 16}
    order = ["ada_m0", "ada_f0", "ada_m1", "ada_f1", "in0", "out0", "w13_0", "w2_0", "in1", "out1", "w13_1", "w2_1"]
    gbase, o = {}, 0
    for nm in order:
        gbase[nm] = o
        o += NG[nm.split("_")[0] if not nm.startswith("ada") else "ada"]
    ws = p.dram("ws", [o, 128, GSF], F32, "ExternalInput")
    oT = p.dram("oT", [D, SEQ], F32, "ExternalOutput")
    scr = lambda name, shape: nc.dram_tensor(name, list(shape), F32)
    p0s_t, p1s_t = scr("p0s", [31 * 128, SEQ + 4]), scr("p1s", [41 * 128, SEQ + 4])
    ya_t, yb_t, y1_t = scr("ya_s", [SEQ, 512]), scr("yb_s", [SEQ, 512]), scr("y1_s", [2048, SEQ])
    part_t = [[scr(f"part{i}_{tl}", [D, TT]) for tl in range(NTL)] for i in range(4)]
    summ_t = [[scr(f"summ{i}_{tl}", [D, TT]) for tl in range(NTL)] for i in range(4)]
    xs_t = [scr(f"xres{i}", [D, SEQ]) for i in range(3)]
    p0s, p1s = p0s_t.ap(), p1s_t.ap()
    fm = lambda ap: ap.rearrange("(kc q) t -> q kc t", q=128)

    cs = p.sbuf("cs", [128, 16], F32)
    scb = p.sbuf("scb", [128, 16], BF16)
    bias = [p.sbuf(f"adab{i}", [128, 48], F32) for i in range(4)]
    mod = [p.sbuf(f"mod{i}", [128, 48], F32) for i in range(4)]
    ops = [p.sbuf(f"ops{i}", [128, 16], F32) for i in range(4)]
    fw = p.sbuf("fw", [128, 16], F32)
    epsb = p.sbuf("epsb", [128, 1], F32)
    zt = p.sbuf("zt", [128, 4], F32)
    p.dma("pool", cs[:], cs_d, writes=["cs"])
    p.dma("pool", fw[:], fnw_d, writes=["fw"])
    for i in range(4):
        p.dma("pool", bias[i][:], adab_d[i], writes=[f"mod{i}b"])
    p.op("act", lambda e: e.activation(out=scb[:], in_=cs[:], func=AF.Silu), reads=["cs"], writes=["scb"])
    p.op("dve", lambda e: e.memset(epsb[:], EPS), writes=["epsb"])
    p.op("dve", lambda e: e.memset(zt[:], 0.0), writes=["zt"])
    for (t_, nch) in ((p0s, 31), (p1s, 41)):
        for kc in range(nch):
            p.dma("pool", t_[kc * 128:(kc + 1) * 128, 0:4], zt[:], reads=["zt"])

    def wseq(nm, used):
        ng = NG[nm.split("_")[0] if not nm.startswith("ada") else "ada"]
        return [(gbase[nm] + g, used) for g in range(ng)]

    def mk_ctx():
        cx = Ctx(p)
        cx.epsb = epsb
        return cx

    def mods(cx, wst, which, jlist):
        emit_mod2(cx, wst, scb, bias[which], mod[which], jlist, f"mod{which}")
        if jlist[0] == 0:
            p.op("dve", lambda e: e.tensor_scalar(out=ops[which][:], in0=mod[which][:, 16:32], scalar1=1.0, scalar2=None, op0=ALU.add),
                 reads=[f"mod{which}"], writes=[f"ops{which}"])

    def ada_seq(which, jlist):
        nm = ["ada_m0", "ada_f0", "ada_m1", "ada_f1"][which]
        return [(gbase[nm] + j // 2, 4096) for j in jlist[::2]]

    cc_tok = {}
    summ_idx = {}

    def load_x(xs, tl, base_ap, sum_ap, gate_which, store_ap, stg):
        t0 = tl * TT
        for k4 in range(4):
            p.dma("pool", xs[:, k4 * 4:k4 * 4 + 4, :], fm(base_ap)[:, k4 * 4:k4 * 4 + 4, t0:t0 + TT],
                  writes=[("x", kc) for kc in range(k4 * 4, k4 * 4 + 4)])
        if sum_ap is None:
            return
        p._wait("pool", cc_tok[(summ_idx[id(sum_ap)], tl)])
        for k2 in range(8):
            st_ = stg[k2 % 2]
            p.dma("pool", st_[:], fm(sum_ap[tl].ap())[:, k2 * 2:k2 * 2 + 2, :], writes=[("stg", k2 % 2)])
            for kk in range(2):
                kc = k2 * 2 + kk
                p.op("dve", lambda e: e.scalar_tensor_tensor(out=xs[:, kc, :], in0=st_[:, kk, :], scalar=mod[gate_which][:, 32 + kc:33 + kc],
                                                             in1=xs[:, kc, :], op0=ALU.mult, op1=ALU.add),
                     reads=[("stg", k2 % 2), ("x", kc), f"mod{gate_which}"], writes=[("x", kc)])
            if store_ap is not None and k2 % 2 == 1:
                k4 = k2 // 2
                p.dma("pool", fm(store_ap)[:, k4 * 4:k4 * 4 + 4, t0:t0 + TT], xs[:, k4 * 4:k4 * 4 + 4, :],
                      reads=[("x", kc) for kc in range(k4 * 4, k4 * 4 + 4)])

    def store_chunks(dst_fn):
        ost = [p.sbuf(f"ost{i}", [128, TT], F32) for i in range(4)]

        ctr = [0]

        def f(cx, tl):
            def evac(n, ps, pk):
                i_ = ctr[0] % 4
                ctr[0] += 1
                o_, ok = ost[i_], ("ost", i_)
                ev_copy(p, cx.ev_eng(), o_[:], ps[:], [pk], [ok])
                p.dma("pool", dst_fn(tl, n), o_[:], reads=[ok], writes=[("part", tl, n)])
            return evac
        return f

    def part_dst(i):
        return lambda tl, n: part_t[i][tl].ap()[n * 128:(n + 1) * 128, :]

    def allreduce_tile(i, tl):
        cc_tok[(i, tl)] = p.collective("AllReduce", ALU.add, RG, part_t[i][tl], summ_t[i][tl], [("part", tl, n) for n in range(16)], [])

    def phase_inproj(layer, ncc, dst_ap, base_ap, sum_ap, gate_which, store_ap, mod_prev_j):
        mk = p.scope_begin()
        cx = mk_ctx()
        wnm = f"in{layer}"
        mw = 2 * layer
        seq = []
        if mod_prev_j is not None:
            seq += ada_seq(gate_which, list(range(32, 48)))
        seq += ada_seq(mw, list(range(32)))
        ST = min(4, NTL)
        for _ in range(NTL // ST):
            seq += wseq(wnm, 4096)
        wst = WStreamC(p, ws, seq)
        if mod_prev_j is not None:
            mods(cx, wst, gate_which, list(range(32, 48)))
        mods(cx, wst, mw, list(range(32)))
        xs = p.sbuf("xs", [128, 16, TT], F32)
        hs = [p.sbuf(f"hs{j}", [128, 16, TT], BF16) for j in range(ST)]
        tmp = [p.sbuf(f"tmp{i}", [128, TT], F32) for i in range(3)]
        stat = (p.sbuf("rs", [128, TT], F32), p.sbuf("rstd", [128, TT], F32))
        stg = [p.sbuf(f"stg{i}", [128, 2, TT], F32) for i in range(2)]
        mkev = store_chunks(lambda tl, n: dst_ap[n * 128:(n + 1) * 128, 4 + tl * TT:4 + (tl + 1) * TT])
        for sup in range(NTL // ST):
            for j in range(ST):
                load_x(xs, sup * ST + j, base_ap, sum_ap, gate_which, store_ap, stg)
                emit_norm_mod(cx, xs, "x", hs[j], ("h", j), lambda kc: ops[mw][:, kc:kc + 1], lambda kc: mod[mw][:, kc:kc + 1],
                              [f"mod{mw}", f"ops{mw}"], tmp, stat)
            emit_gemm(cx, wst, 16, ncc, lambda j, kc: (hs[j][:, kc, :], (("h", j), kc)),
                      lambda j, n, ps, pk: mkev(cx, sup * ST + j)(n, ps, pk), cpg=2, ST=ST)
        p.scope_end(mk)

    def phase_outproj(layer, pi):
        mk = p.scope_begin()
        cx = mk_ctx()
        Kc = 8 if layer == 0 else 16
        ST = min(2, NTL)
        seq = []
        for _ in range(NTL // ST):
            seq += wseq(f"out{layer}", 4096)
        wst = WStreamC(p, ws, seq)
        bigs = [p.sbuf(f"big{j}", [128, Kc, TT], BF16) for j in range(ST)]
        mkev = store_chunks(part_dst(pi))
        if layer == 0:
            ident = p.sbuf("identf", [128, 128], F32)
            p.dma("pool", ident[:], ident_d, writes=["identf"])
            ytl = [p.sbuf(f"ytl{i}", [128, 4, 512], F32) for i in range(2)]
        else:
            yst = [p.sbuf(f"yst{i}", [128, 2, TT], F32) for i in range(2)]
        for sup in range(NTL // ST):
          for j in range(ST):
            tl = sup * ST + j
            t0 = tl * TT
            big = bigs[j]
            if layer == 0:
                for si, src in enumerate((ya_t.ap(), yb_t.ap())):
                    yt_, yk = ytl[si], ("ytl", si)
                    p.dma("pool", yt_[:], src[t0:t0 + TT, :].rearrange("(a q) c -> q a c", q=128), writes=[yk])
                    for c in range(4):
                        ps, pk = cx.next_ps()
                        for a in range(4):
                            p.op("pe", lambda e: e.transpose(out=ps[:, a * 128:(a + 1) * 128], in_=yt_[:, a, c * 128:(c + 1) * 128],
                                                             identity=ident[:]),
                                 reads=[yk, "identf"], writes=[pk], nosync_same=True)
                        ev_copy(p, cx.ev_eng(), big[:, si * 4 + c, :], ps[:], [pk], [(("big", j), si * 4 + c)])
            else:
                for k2 in range(8):
                    s_ = k2 % 2
                    p.dma("pool", yst[s_][:], fm(y1_t.ap())[:, k2 * 2:k2 * 2 + 2, t0:t0 + TT], writes=[("yst", s_)])
                    ev_copy(p, cx.ev_eng(), big[:, k2 * 2:k2 * 2 + 2, :], yst[s_][:], [("yst", s_)],
                            [(("big", j), k2 * 2), (("big", j), k2 * 2 + 1)])
          emit_gemm(cx, wst, Kc, 16, lambda j, kc: (bigs[j][:, kc, :], (("big", j), kc)),
                    lambda j, n, ps, pk: mkev(cx, sup * ST + j)(n, ps, pk), cpg=GSF // (Kc * 128), ST=ST)
          for j in range(ST):
            allreduce_tile(pi, sup * ST + j)
        p.scope_end(mk)

    def phase_ffn(layer, base_ap, sum_ap, store_ap, pi):
        mk = p.scope_begin()
        cx = mk_ctx()
        gm, mf = 2 * layer, 2 * layer + 1
        seq = ada_seq(gm, list(range(32, 48))) + ada_seq(mf, list(range(32)))
        ST = min(2, NTL)
        for _ in range(NTL // ST):
            seq += wseq(f"w13_{layer}", 4096) + wseq(f"w2_{layer}", 2816)
        wst = WStreamC(p, ws, seq)
        mods(cx, wst, gm, list(range(32, 48)))
        mods(cx, wst, mf, list(range(32)))
        xs = p.sbuf("xs", [128, 16, TT], F32)
        hs = [p.sbuf(f"hs{j}", [128, 16, TT], BF16) for j in range(ST)]
        gb = [p.sbuf(f"gb{j}", [128, 22, TT], BF16) for j in range(ST)]
        sil = [[p.sbuf(f"sil{r}_{j}", [128, TT], F32) for j in range(ST)] for r in range(2)]
        tmp = [p.sbuf(f"tmp{i}", [128, TT], F32) for i in range(3)]
        stat = (p.sbuf("rs", [128, TT], F32), p.sbuf("rstd", [128, TT], F32))
        stg = [p.sbuf(f"stg{i}", [128, 2, TT], F32) for i in range(2)]
        mkev = store_chunks(part_dst(pi))
        for sup in range(NTL // ST):
            for j in range(ST):
                load_x(xs, sup * ST + j, base_ap, sum_ap, gm, store_ap, stg)
                emit_norm_mod(cx, xs, "x", hs[j], ("h", j), lambda kc: ops[mf][:, kc:kc + 1], lambda kc: mod[mf][:, kc:kc + 1],
                              [f"mod{mf}", f"ops{mf}"], tmp, stat)

            def evac13(j, n, ps, pk):
                q, r = n // 4, n % 4
                if r < 2:
                    p.op("act", lambda e: e.activation(out=sil[r][j][:], in_=ps[:], func=AF.Silu), reads=[pk], writes=[("sil", r, j)])
                else:
                    jj = q * 2 + (r - 2)
                    p.op("dve", lambda e: e.tensor_tensor(out=gb[j][:, jj, :], in0=ps[:], in1=sil[r - 2][j][:], op=ALU.mult),
                         reads=[pk, ("sil", r - 2, j)], writes=[(("gb", j), jj)])
            emit_gemm(cx, wst, 16, 44, lambda j, kc: (hs[j][:, kc, :], (("h", j), kc)), evac13, cpg=2, ST=ST)
            emit_gemm(cx, wst, 22, 16, lambda j, kc: (gb[j][:, kc, :], (("gb", j), kc)),
                      lambda j, n, ps, pk: mkev(cx, sup * ST + j)(n, ps, pk), cpg=1, ST=ST)
            for j in range(ST):
                allreduce_tile(pi, sup * ST + j)
        p.scope_end(mk)

    def phase_final(base_ap, sum_ap):
        mk = p.scope_begin()
        cx = mk_ctx()
        wst = WStreamC(p, ws, ada_seq(3, list(range(32, 48))))
        mods(cx, wst, 3, list(range(32, 48)))
        xs = p.sbuf("xs", [128, 16, TT], F32)
        hs = p.sbuf("hs", [128, 16, TT], BF16)
        stat = (p.sbuf("rs", [128, TT], F32), p.sbuf("rstd", [128, TT], F32))
        stg = [p.sbuf(f"stg{i}", [128, 2, TT], F32) for i in range(2)]
        for tl in range(NTL):
            t0 = tl * TT
            load_x(xs, tl, base_ap, sum_ap, 3, None, stg)
            emit_norm_mod(cx, xs, "x", hs, "h", None, None, [], None, stat)
            for kc in range(16):
                p.op("dve", lambda e: e.scalar_tensor_tensor(out=xs[:, kc, :], in0=xs[:, kc, :], scalar=fw[:, kc:kc + 1], in1=stat[1][:],
                                                             op0=ALU.mult, op1=ALU.mult),
                     reads=[("x", kc), "fw", "rstd"], writes=[("x", kc)])
            for k4 in range(4):
                p.dma("pool", fm(oT)[:, k4 * 4:k4 * 4 + 4, t0:t0 + TT], xs[:, k4 * 4:k4 * 4 + 4, :],
                      reads=[("x", kc) for kc in range(k4 * 4, k4 * 4 + 4)])
        p.scope_end(mk)


    ident_d = p.dram("identf_d", [128, 128], F32, "ExternalInput")
    for i_ in range(4):
        summ_idx[id(summ_t[i_])] = i_
    p.barrier()
    phase_inproj(0, 31, p0s, xT, None, 0, None, None)
    mk = p.scope_begin()
    build_rwkv(p, {"rkvT": p0s[0:1536, 3:SEQ + 4], "lwT": p0s[29 * 128:29 * 128 + 64, 3:SEQ + 4],
                   "laT": p0s[29 * 128 + 64:30 * 128, 3:SEQ + 4], "lgT": p0s[30 * 128:31 * 128, 3:SEQ + 4], "ya": ya_t.ap()})
    p.scope_end(mk)
    mk = p.scope_begin()
    build_gdn(p, {"qkvT": p0s[12 * 128:24 * 128, 1:SEQ + 4], "zfm": p0s[24 * 128:28 * 128, 4:SEQ + 4],
                  "bT": p0s[28 * 128:28 * 128 + 4, 4:SEQ + 4], "aT": p0s[28 * 128 + 4:28 * 128 + 8, 4:SEQ + 4], "yb": yb_t.ap()})
    p.scope_end(mk)
    phase_outproj(0, 0)
    phase_ffn(0, xT, summ_t[0], xs_t[0].ap(), 1)
    phase_inproj(1, 41, p1s, xs_t[0].ap(), summ_t[1], 1, xs_t[1].ap(), True)
    mk = p.scope_begin()
    build_ssd(p, {"zT": p1s[0:2048, 4:SEQ + 4], "xT": p1s[2048:4096, 1:SEQ + 4], "bcT": p1s[4096:5120, 1:SEQ + 4],
                  "dtT": p1s[5120:5152, 4:SEQ + 4], "yT": y1_t.ap()})
    p.scope_end(mk)
    phase_outproj(1, 2)
    phase_ffn(1, xs_t[1].ap(), summ_t[2], xs_t[2].ap(), 3)
    phase_final(xs_t[2].ap(), summ_t[3])
    p.finish()
    return p


def _ssd_params(prm, hh):
    cw, cb = prm["ssm_conv_w"], prm["ssm_conv_b"]
    cix = np.arange(hh * 2048, (hh + 1) * 2048)
    cibc = np.concatenate([4096 + hh * 512 + np.arange(512), 5120 + hh * 512 + np.arange(512)])
    hs = slice(hh * 32, (hh + 1) * 32)
    selh = np.zeros((32, 32, 128), np.float32)
    for h in range(32):
        selh[h, h, :] = 1.0
    return {"cwx": np.ascontiguousarray(cw[:, cix].T.reshape(16, 128, 4).transpose(1, 0, 2)), "cbx": fm(cb[cix]),
            "cwbc": np.ascontiguousarray(cw[:, cibc].T.reshape(8, 128, 4).transpose(1, 0, 2)), "cbbc": fm(cb[cibc]),
            "hp": np.ascontiguousarray(np.stack([prm["ssm_dt_bias"][hs], prm["ssm_a_log"][hs]], axis=1)),
            "dsk": fm(np.repeat(prm["ssm_d"][hs], 64)), "nw": fm(prm["ssm_norm_w"][hh * 2048:(hh + 1) * 2048]),
            "ident": np.eye(128, dtype=np.float32), "selh": selh, "maskT": np.triu(np.ones((128, 128), np.float32))}


def _gdn_params(prm, hh):
    cidx = np.concatenate([w * 1024 + hh * 512 + np.arange(512) for w in range(3)])
    cw = prm["gdn_conv_w"][:, cidx]
    m = {"cw": np.ascontiguousarray(cw.T.reshape(12, 128, 4).transpose(1, 0, 2)),
         "hp": np.ascontiguousarray(np.stack([prm["gdn_a_log"][hh * 4:hh * 4 + 4], prm["gdn_dt_bias"][hh * 4:hh * 4 + 4]], axis=1)),
         "nwrow": np.ascontiguousarray(np.broadcast_to(prm["gdn_norm_w"][None, :], (128, 128)))}
    m.update(gdn_consts())
    return m


def _rwkv_params(prm, hh):
    mu = prm["rwkv_mu"]
    my = slice(hh * 512, (hh + 1) * 512)
    cols = np.zeros((128, 64), np.float32)

    def put(name, vec512):
        cols[:, PRM_COLS[name]:PRM_COLS[name] + 4] = vec512.reshape(4, 128).T
    put("mu_r", mu[0:1024][my]); put("mu_k", mu[1024:2048][my]); put("mu_v", mu[2048:3072][my])
    put("om_r", 1 - mu[0:1024][my]); put("om_k", 1 - mu[1024:2048][my]); put("om_v", 1 - mu[2048:3072][my])
    put("w0", prm["rwkv_w0"][my]); put("a0", prm["rwkv_a0"][my]); put("k_k", prm["rwkv_k_k"][my])
    put("k_a", prm["rwkv_k_a"][my]); put("omka", 1 - prm["rwkv_k_a"][my]); put("r_k", prm["rwkv_r_k"].reshape(-1)[my])
    lmu = np.zeros((128, 6), np.float32)
    lmu[0:64, 0] = mu[3072:3136]; lmu[0:64, 1] = 1 - mu[3072:3136]
    lmu[0:64, 2] = mu[3136:3200]; lmu[0:64, 3] = 1 - mu[3136:3200]
    lmu[:, 4] = mu[3200:3328]; lmu[:, 5] = 1 - mu[3200:3328]

    def rows_tm(v512):
        a = v512.reshape(4, 2, 64)
        return np.ascontiguousarray(np.repeat(a.transpose(1, 0, 2)[:, None, :, :], 64, axis=1).reshape(128, 4, 64))
    m = {"prm": cols, "lmu": lmu, "w2": np.ascontiguousarray(prm["rwkv_w2"][:, my]), "a2": np.ascontiguousarray(prm["rwkv_a2"][:, my]),
         "g2": np.ascontiguousarray(prm["rwkv_g2"][:, my]), "lnw": rows_tm(prm["rwkv_ln_w"][my]), "lnb": rows_tm(prm["rwkv_ln_b"][my])}
    m.update(rwkv_consts())
    return m


def fused_inmaps(I, cores=range(8)):
    f32 = lambda a: np.ascontiguousarray(np.asarray(a, dtype=np.float32))
    I = {k: f32(v) for k, v in I.items()}
    G0 = 3328
    ada = [pack_wg(I["ada_mix_w"][0]), pack_wg(I["ada_ffn_w"][0]), pack_wg(I["ada_mix_w"][1]), pack_wg(I["ada_ffn_w"][1])]
    prm_r = {k: I[k][0] for k in ("rwkv_mu", "rwkv_w0", "rwkv_w2", "rwkv_a0", "rwkv_a2", "rwkv_g2", "rwkv_k_k", "rwkv_k_a",
                                  "rwkv_r_k", "rwkv_ln_w", "rwkv_ln_b")}
    prm_g = {k: I[k][0] for k in ("gdn_conv_w", "gdn_a_log", "gdn_dt_bias", "gdn_norm_w")}
    prm_s = {k: I[k][0] for k in ("ssm_conv_w", "ssm_conv_b", "ssm_dt_bias", "ssm_a_log", "ssm_d", "ssm_norm_w")}
    per_hh = {}
    for hh in sorted({c_ % 2 for c_ in cores}):
        W = I["hg_w_in"][0]
        my = lambda base: W[:, base + hh * 512: base + (hh + 1) * 512]
        ba = np.zeros((2048, 128), np.float32)
        ba[:, 0:4] = W[:, G0 + 4096 + hh * 4: G0 + 4096 + hh * 4 + 4]
        ba[:, 4:8] = W[:, G0 + 4104 + hh * 4: G0 + 4104 + hh * 4 + 4]
        Win0 = np.concatenate([my(0), my(1024), my(2048), my(G0), my(G0 + 1024), my(G0 + 2048), my(G0 + 3072), ba,
                               W[:, 3072:3200], W[:, 3200:3328]], axis=1)
        W1s = I["ssm_w_in"][0]
        dtp = np.zeros((2048, 128), np.float32)
        dtp[:, 0:32] = W1s[:, 10240 + hh * 32: 10240 + (hh + 1) * 32]
        Win1 = np.concatenate([W1s[:, hh * 2048:(hh + 1) * 2048], W1s[:, 4096 + hh * 2048: 4096 + (hh + 1) * 2048],
                               W1s[:, 8192 + hh * 512: 8192 + (hh + 1) * 512], W1s[:, 9216 + hh * 512: 9216 + (hh + 1) * 512], dtp], axis=1)
        Wo0 = np.concatenate([I["hg_w_out"][0][hh * 512:(hh + 1) * 512], I["hg_w_out"][0][1024 + hh * 512: 1024 + (hh + 1) * 512]], axis=0)
        Wo1 = I["ssm_w_out"][0][hh * 2048:(hh + 1) * 2048]
        lay = []
        for l in range(2):
            W1 = I["ffn_w1"][l][:, hh * 2816:(hh + 1) * 2816]
            W3 = I["ffn_w3"][l][:, hh * 2816:(hh + 1) * 2816]
            W13 = np.concatenate([np.concatenate([W1[:, q * 256:(q + 1) * 256], W3[:, q * 256:(q + 1) * 256]], axis=1) for q in range(11)], axis=1)
            lay.append((pack_wg(W13), pack_wg(I["ffn_w2"][l][hh * 2816:(hh + 1) * 2816])))
        ws = np.concatenate(ada + [pack_wg(Win0), pack_wg(Wo0), lay[0][0], lay[0][1], pack_wg(Win1), pack_wg(Wo1), lay[1][0], lay[1][1]], axis=0)
        m = {"ws": np.ascontiguousarray(ws)}
        for pre, d in (("rwkv_", _rwkv_params(prm_r, hh)), ("gdn_", _gdn_params(prm_g, hh)), ("ssd_", _ssd_params(prm_s, hh))):
            for k, v in d.items():
                m[pre + k] = v
        per_hh[hh] = m
    adab = np.ascontiguousarray(np.stack([fm(I["ada_mix_b"][0]), fm(I["ada_ffn_b"][0]), fm(I["ada_mix_b"][1]), fm(I["ada_ffn_b"][1])]))
    in_maps = []
    for core in cores:
        b, hh = core // 2, core % 2
        m = dict(per_hh[hh])
        m.update({"xT": np.ascontiguousarray(I["x"][b, :SEQ].T), "cs": fm(I["c"][b]), "adab": adab, "fnw": fm(I["final_norm_w"]),
                  "identf_d": np.eye(128, dtype=np.float32)})
        in_maps.append(m)
    return in_maps


def kernel(**inputs):
    p = _prog("fused", build_fused)
    in_maps = fused_inmaps(inputs)
    r = _run(p, in_maps)
    out = np.empty((4, SEQ, D), np.float32)
    for b in range(4):
        out[b] = np.asarray(r[2 * b]["oT"]).T
    return out
```

```python
import numpy as np
import ml_dtypes
import concourse.bass as bass
import concourse.mybir as mybir
from concourse.bass_utils import run_bass_kernel_spmd

F32 = mybir.dt.float32
BF16 = mybir.dt.bfloat16
AF = mybir.ActivationFunctionType
ALU = mybir.AluOpType
AX = mybir.AxisListType
NPBF = ml_dtypes.bfloat16

D = 2048
TT = 512
TOK = 2048
GS = 8192
EPS = 1e-5


class Prog:
    def __init__(self, n_dma_sems=10, same_engine_sync=True):
        self.nc = bass.Bass("TRN2", target_bir_lowering=False)
        nc = self.nc
        self.eng = {"pe": nc.tensor, "act": nc.scalar, "dve": nc.vector,
                    "pool": nc.gpsimd, "sp": nc.sync}
        self.sem = {e: nc.alloc_semaphore(name="s_" + e) for e in self.eng}
        self.cnt = {e: 0 for e in self.eng}
        self.waited = {e: {} for e in self.eng}
        self.dma_ring = {q: [[nc.alloc_semaphore(name=f"d_{q}{i}"), 0] for i in range(n_dma_sems)]
                         for q in ("sp", "pool", "act")}
        self.dma_pos = {q: 0 for q in self.dma_ring}
        self.last_w = {}
        self.readers = {}
        self.same_engine_sync = same_engine_sync
        self.n_inst = 0
        self._ctx = []

    def sbuf(self, name, shape, dt):
        cm = self.nc.sbuf_tensor("sb_" + getattr(self, "pfx", "") + name, list(shape), dt)
        t = cm.__enter__()
        self._ctx.append(cm)
        return t

    def psum(self, name, shape, dt=F32):
        cm = self.nc.psum_tensor("pp_" + getattr(self, "pfx", "") + name, list(shape), dt)
        t = cm.__enter__()
        self._ctx.append(cm)
        return t

    def dram(self, name, shape, dt, kind):
        return self.nc.dram_tensor(name, list(shape), dt, kind=kind).ap()

    def _wait(self, e, tok):
        if tok is None:
            return
        if tok[0] == "c":
            _, e2, idx = tok
            if e2 == e and not self.same_engine_sync:
                return
            key = ("c", e2)
            if self.waited[e].get(key, 0) >= idx:
                return
            self.eng[e].wait_ge(self.sem[e2], idx)
            self.waited[e][key] = idx
        elif tok[0] == "x":
            _, sem, val, sid = tok
            key = ("x", sid)
            if self.waited[e].get(key, 0) >= val:
                return
            self.eng[e].wait_ge(sem, val)
            self.waited[e][key] = val
        else:
            _, q, slot, val = tok
            key = ("d", q, slot)
            if self.waited[e].get(key, 0) >= val:
                return
            self.eng[e].wait_ge(self.dma_ring[q][slot][0], val)
            self.waited[e][key] = val

    def _deps(self, e, reads, writes):
        for k in reads:
            self._wait(e, self.last_w.get(k))
        for k in writes:
            self._wait(e, self.last_w.get(k))
            for t in self.readers.get(k, ()):
                self._wait(e, t)

    def _commit(self, tok, reads, writes):
        for k in reads:
            self.readers.setdefault(k, []).append(tok)
        for k in writes:
            self.last_w[k] = tok
            self.readers[k] = []

    def op(self, e, fn, reads=(), writes=(), nosync_same=False):
        if nosync_same:
            sv = self.same_engine_sync
            self.same_engine_sync = False
            self._deps(e, reads, writes)
            self.same_engine_sync = sv
        else:
            self._deps(e, reads, writes)
        inst = fn(self.eng[e])
        self.cnt[e] += 1
        inst.then_inc(self.sem[e], 1)
        tok = ("c", e, self.cnt[e])
        self._commit(tok, reads, writes)
        self.n_inst += 1
        return tok

    def dma(self, q, out, in_, reads=(), writes=(), **kw):
        ring = self.dma_ring[q]
        slot = self.dma_pos[q] % len(ring)
        self.dma_pos[q] += 1
        sem, val = ring[slot]
        if val > 0:
            self._wait(q, ("d", q, slot, val))
        self._deps(q, reads, writes)
        inst = self.eng[q].dma_start(out=out, in_=in_, **kw)
        val += 16
        ring[slot][1] = val
        inst.then_inc(sem, 16)
        tok = ("d", q, slot, val)
        self._commit(tok, reads, writes)
        self.n_inst += 1
        return tok

    def scope_begin(self):
        self._nscope = getattr(self, "_nscope", 0) + 1
        self.pfx = f"s{self._nscope}_"
        return len(self._ctx)

    def scope_end(self, mark):
        self.barrier()
        while len(self._ctx) > mark:
            cm = self._ctx.pop()
            cm.__exit__(None, None, None)

    def barrier(self):
        for e in self.eng:
            for e2 in ("pe", "act", "dve", "pool"):
                if e2 != e and self.cnt[e2] > 0:
                    self._wait(e, ("c", e2, self.cnt[e2]))
            for q, ring in self.dma_ring.items():
                for slot, (sem, val) in enumerate(ring):
                    if val > 0:
                        self._wait(e, ("d", q, slot, val))

    def collective(self, kind, op, rg, in_t, out_t, reads, writes):
        if not hasattr(self, "xsems"):
            self.xsems = []
        self._deps("pool", reads, writes)
        inst = self.nc.gpsimd.collective_compute(kind, op, replica_groups=rg, ins=[in_t.ap().opt()], outs=[out_t.ap().opt()])
        sem = self.nc.alloc_semaphore(name=f"cc{len(self.xsems)}")
        inst.then_inc(sem, 1)
        self.xsems.append((sem, 1))
        tok = ("x", sem, 1, len(self.xsems) - 1)
        self._commit(tok, reads, writes)
        return tok

    def finish(self):
        for sid, (sem, val) in enumerate(getattr(self, "xsems", [])):
            self._wait("sp", ("x", sem, val, sid))
        for q, ring in self.dma_ring.items():
            for slot, (sem, val) in enumerate(ring):
                if val > 0:
                    self._wait("sp", ("d", q, slot, val))
        for e in ("pe", "act", "dve", "pool"):
            if self.cnt[e] > 0:
                self._wait("sp", ("c", e, self.cnt[e]))


class WStream:
    def __init__(self, p, ws_ap, seq, nslot=4, q="sp"):
        self.p, self.ws, self.seq, self.nslot, self.q = p, ws_ap, list(seq), nslot, q
        self.slots = [p.sbuf(f"wslot{i}", [128, GS], BF16) for i in range(nslot)]
        self.issued = 0
        self.used = 0

    def _issue(self):
        while self.issued < len(self.seq) and self.issued < self.used + self.nslot:
            s = self.issued % self.nslot
            self.p.dma(self.q, self.slots[s][:], self.ws[self.seq[self.issued]], writes=[("w", s)])
            self.issued += 1

    def next(self):
        self._issue()
        s = self.used % self.nslot
        self.used += 1
        return self.slots[s], ("w", s)

    def prefetch(self):
        self._issue()


class Ctx:
    def __init__(self, p):
        self.p = p
        self.ps = [p.psum(f"ps{i}", [128, 512], F32) for i in range(8)]
        self.ps_i = 0
        self.ev_i = 0
        self.ones = p.sbuf("ones", [128, 128], BF16)
        p.op("dve", lambda e: e.memset(self.ones[:], 1.0), writes=["ones"])

    def next_ps(self):
        i = self.ps_i % 8
        self.ps_i += 1
        return self.ps[i], ("ps", i)

    def ev_eng(self):
        self.ev_i += 1
        return "act" if self.ev_i % 2 else "dve"


def emit_mod(cx, wst, cs_ap, adab_ap, jlist, name):
    p = cx.p
    cs = p.sbuf(name + "_cs", [128, 16], F32)
    scb = p.sbuf(name + "_scb", [128, 16], BF16)
    bias = p.sbuf(name + "_b", [128, 48], F32)
    mod = p.sbuf(name + "_mod", [128, 48], F32)
    p.dma("pool", cs[:], cs_ap, writes=[name + "cs"])
    p.dma("pool", bias[:], adab_ap, writes=[name + "b"])
    p.op("act", lambda e: e.activation(out=scb[:], in_=cs[:], func=AF.Silu), reads=[name + "cs"], writes=[name + "scb"])
    assert len(jlist) % 4 == 0
    for g in range(len(jlist) // 4):
        slot, wk = wst.next()
        ps, pk = cx.next_ps()
        for jj in range(4):
            for kc in range(16):
                off = (jj * 16 + kc) * 128
                p.op("pe", lambda e: e.matmul(ps[:, jj:jj + 1], lhsT=slot[:, off:off + 128], rhs=scb[:, kc:kc + 1],
                                              start=(kc == 0), stop=(kc == 15)),
                     reads=[wk, name + "scb"], writes=[pk], nosync_same=True)
        j0 = jlist[g * 4]
        assert jlist[g * 4:g * 4 + 4] == list(range(j0, j0 + 4))
        p.op("dve", lambda e: e.tensor_tensor(out=mod[:, j0:j0 + 4], in0=ps[:, 0:4], in1=bias[:, j0:j0 + 4], op=ALU.add),
             reads=[pk, name + "b"], writes=[name + "mod"])
    return mod, name + "mod"


def emit_norm_mod(cx, x_sb, xkey, h_sb, hkey, ops_fn, shift_fn, modkeys, tmp_ring, stat):
    p = cx.p
    for kc in range(16):
        p.op("act", lambda e: e.activation(out=h_sb[:, kc, :], in_=x_sb[:, kc, :], func=AF.Square),
             reads=[(xkey, kc)], writes=[(hkey, kc)])
    ps, pk = cx.next_ps()
    for kc in range(16):
        p.op("pe", lambda e: e.matmul(ps[:], lhsT=cx.ones[:], rhs=h_sb[:, kc, :], start=(kc == 0), stop=(kc == 15)),
             reads=["ones", (hkey, kc)], writes=[pk], nosync_same=True)
    rs, rstd = stat
    p.op("act", lambda e: e.activation(out=rs[:], in_=ps[:], func=AF.Sqrt, scale=1.0 / D, bias=cx.epsb[:, 0:1]),
         reads=[pk, "epsb"], writes=["rs"])
    p.op("dve", lambda e: e.reciprocal(out=rstd[:], in_=rs[:]), reads=["rs"], writes=["rstd"])
    if ops_fn is None:
        return
    for kc in range(16):
        t = tmp_ring[kc % len(tmp_ring)]
        tk = ("tmpn", kc % len(tmp_ring))
        p.op("dve", lambda e: e.tensor_tensor(out=t[:], in0=x_sb[:, kc, :], in1=rstd[:], op=ALU.mult),
             reads=[(xkey, kc), "rstd"], writes=[tk])
        p.op("act", lambda e: e.activation(out=h_sb[:, kc, :], in_=t[:], func=AF.Identity,
                                           scale=ops_fn(kc), bias=shift_fn(kc)),
             reads=[tk] + modkeys, writes=[(hkey, kc)])


def emit_gemm(cx, wst, Kc, ncc, rhs_fn, evac_fn, cpg=None, ST=None):
    p = cx.p
    if cpg is None:
        cpg = GS // (Kc * 128)
    n = 0
    while n < ncc:
        slot, wk = wst.next()
        for cc in range(min(cpg, ncc - n)):
            for j in range(ST or 1):
                ps, pk = cx.next_ps()
                for kc in range(Kc):
                    off = (cc * Kc + kc) * 128
                    r_ap, r_key = rhs_fn(kc) if ST is None else rhs_fn(j, kc)
                    p.op("pe", lambda e: e.matmul(ps[:], lhsT=slot[:, off:off + 128], rhs=r_ap,
                                                  start=(kc == 0), stop=(kc == Kc - 1)),
                         reads=[wk, r_key], writes=[pk], nosync_same=True)
                if ST is None:
                    evac_fn(n + cc, ps, pk)
                else:
                    evac_fn(j, n + cc, ps, pk)
        n += cpg


def pack_w(W, Kc=None):
    K, N = W.shape
    Kc = K // 128
    ncc = N // 128
    cpg = GS // (Kc * 128)
    ng = (ncc + cpg - 1) // cpg
    Wp = np.zeros((K, ng * cpg * 128), dtype=W.dtype)
    Wp[:, :N] = W
    A = Wp.reshape(Kc, 128, ng, cpg, 128).transpose(2, 1, 3, 0, 4).reshape(ng, 128, cpg * Kc * 128)
    out = np.zeros((ng, 128, GS), dtype=W.dtype)
    out[:, :, :cpg * Kc * 128] = A
    return out


def build_cast(F):
    p = Prog()
    CH = 8192
    x = p.dram("x", [128, F], F32, "ExternalInput")
    y = p.dram("y", [128, F], BF16, "ExternalOutput")
    NB = 3
    xi = [p.sbuf(f"xi{i}", [128, CH], F32) for i in range(NB)]
    yo = [p.sbuf(f"yo{i}", [128, CH], BF16) for i in range(NB)]
    nch = (F + CH - 1) // CH
    for i in range(nch):
        a, b = i * CH, min(F, (i + 1) * CH)
        s = i % NB
        p.dma("sp", xi[s][:, :b - a], x[:, a:b], writes=[("xi", s)])
        if i % 2 == 0:
            p.op("dve", lambda e: e.tensor_copy(out=yo[s][:, :b - a], in_=xi[s][:, :b - a]), reads=[("xi", s)], writes=[("yo", s)])
        else:
            p.op("act", lambda e: e.activation(out=yo[s][:, :b - a], in_=xi[s][:, :b - a], func=AF.Copy), reads=[("xi", s)], writes=[("yo", s)])
        p.dma("pool", y[:, a:b], yo[s][:, :b - a], reads=[("yo", s)])
    p.finish()
    return p


def build_inproj(ncc, NG_w):
    p = Prog()
    xT = p.dram("xT", [D, TOK], F32, "ExternalInput")
    cs = p.dram("cs", [128, 16], F32, "ExternalInput")
    adab = p.dram("adab", [128, 48], F32, "ExternalInput")
    ws = p.dram("ws", [8 + NG_w, 128, GS], BF16, "ExternalInput")
    pT = p.dram("pT", [ncc * 128, TOK], F32, "ExternalOutput")
    cx = Ctx(p)
    cx.epsb = p.sbuf("epsb", [128, 1], F32)
    p.op("dve", lambda e: e.memset(cx.epsb[:], EPS), writes=["epsb"])
    NTL = TOK // TT
    seq = list(range(8)) + [8 + g for _ in range(NTL) for g in range(NG_w)]
    wst = WStream(p, ws, seq, nslot=4)
    mod, mk = emit_mod(cx, wst, cs, adab, list(range(32)), "m")
    ops = p.sbuf("ops", [128, 16], F32)
    p.op("dve", lambda e: e.tensor_scalar(out=ops[:], in0=mod[:, 16:32], scalar1=1.0, scalar2=None, op0=ALU.add),
         reads=[mk], writes=["ops"])
    xs = p.sbuf("xs", [128, 16, TT], F32)
    hs = p.sbuf("hs", [128, 16, TT], BF16)
    tmp = [p.sbuf(f"tmp{i}", [128, TT], F32) for i in range(3)]
    stat = (p.sbuf("rs", [128, TT], F32), p.sbuf("rstd", [128, TT], F32))
    ost = [p.sbuf(f"ost{i}", [128, TT], F32) for i in range(4)]
    xTv = xT.rearrange("(kc q) t -> q kc t", q=128)
    for tl in range(NTL):
        t0 = tl * TT
        for kc4 in range(4):
            p.dma("pool", xs[:, kc4 * 4:(kc4 + 1) * 4, :], xTv[:, kc4 * 4:(kc4 + 1) * 4, t0:t0 + TT],
                  writes=[("x", kc) for kc in range(kc4 * 4, kc4 * 4 + 4)])
        emit_norm_mod(cx, xs, "x", hs, "h", lambda kc: ops[:, kc:kc + 1], lambda kc: mod[:, kc:kc + 1],
                      [mk, "ops"], tmp, stat)

        def evac(n, ps, pk):
            o = ost[n % 4]
            ok = ("ost", n % 4)
            if cx.ev_eng() == "act":
                p.op("act", lambda e: e.activation(out=o[:], in_=ps[:], func=AF.Copy), reads=[pk], writes=[ok])
            else:
                p.op("dve", lambda e: e.tensor_copy(out=o[:], in_=ps[:]), reads=[pk], writes=[ok])
            p.dma("pool", pT[n * 128:(n + 1) * 128, t0:t0 + TT], o[:], reads=[ok])

        emit_gemm(cx, wst, 16, ncc, lambda kc: (hs[:, kc, :], ("h", kc)), evac)
    p.finish()
    return p


def build_outffn(Kmix_c, NG_out, NG_13, NG_2, final):
    p = Prog()
    xT = p.dram("xT", [D, TOK], F32, "ExternalInput")
    yT = p.dram("yT", [Kmix_c * 128, TOK], F32, "ExternalInput")
    cs = p.dram("cs", [128, 16], F32, "ExternalInput")
    adab_m = p.dram("adab_m", [128, 48], F32, "ExternalInput")
    adab_f = p.dram("adab_f", [128, 48], F32, "ExternalInput")
    NGT = 4 + 12 + NG_out + NG_13 + NG_2
    ws = p.dram("ws", [NGT, 128, GS], BF16, "ExternalInput")
    oT = p.dram("oT", [D, TOK], F32, "ExternalOutput")
    if final:
        fnw = p.dram("fnw", [128, 16], F32, "ExternalInput")
    cx = Ctx(p)
    cx.epsb = p.sbuf("epsb", [128, 1], F32)
    p.op("dve", lambda e: e.memset(cx.epsb[:], EPS), writes=["epsb"])
    NTL = TOK // TT
    seq = list(range(16)) + [16 + g for _ in range(NTL) for g in range(NG_out + NG_13 + NG_2)]
    wst = WStream(p, ws, seq, nslot=3)
    modm, mmk = emit_mod(cx, wst, cs, adab_m, list(range(32, 48)), "mm")
    modf, mfk = emit_mod(cx, wst, cs, adab_f, list(range(48)), "mf")
    ops = p.sbuf("ops", [128, 16], F32)
    p.op("dve", lambda e: e.tensor_scalar(out=ops[:], in0=modf[:, 16:32], scalar1=1.0, scalar2=None, op0=ALU.add),
         reads=[mfk], writes=["ops"])
    if final:
        fw = p.sbuf("fw", [128, 16], F32)
        p.dma("pool", fw[:], fnw, writes=["fw"])
    acc = p.sbuf("acc", [128, 16, TT], F32)
    hs = p.sbuf("hs", [128, 16, TT], BF16)
    big = p.sbuf("big", [128, 44, TT], BF16)
    yst = [p.sbuf(f"yst{i}", [128, 2, TT], F32) for i in range(2)]
    tmp = [p.sbuf(f"tmp{i}", [128, TT], F32) for i in range(3)]
    stat = (p.sbuf("rs", [128, TT], F32), p.sbuf("rstd", [128, TT], F32))
    xTv = xT.rearrange("(kc q) t -> q kc t", q=128)
    yTv = yT.rearrange("(kc q) t -> q kc t", q=128)
    oTv = oT.rearrange("(kc q) t -> q kc t", q=128)
    for tl in range(NTL):
        t0 = tl * TT
        for kc4 in range(4):
            p.dma("pool", acc[:, kc4 * 4:(kc4 + 1) * 4, :], xTv[:, kc4 * 4:(kc4 + 1) * 4, t0:t0 + TT],
                  writes=[("acc", kc) for kc in range(kc4 * 4, kc4 * 4 + 4)])
        for k2 in range(Kmix_c // 2):
            s = k2 % 2
            p.dma("pool", yst[s][:], yTv[:, k2 * 2:k2 * 2 + 2, t0:t0 + TT], writes=[("yst", s)])
            if k2 % 2 == 0:
                p.op("dve", lambda e: e.tensor_copy(out=big[:, k2 * 2:k2 * 2 + 2, :], in_=yst[s][:]),
                     reads=[("yst", s)], writes=[("big", k2 * 2), ("big", k2 * 2 + 1)])
            else:
                p.op("act", lambda e: e.activation(out=big[:, k2 * 2:k2 * 2 + 2, :], in_=yst[s][:], func=AF.Copy),
                     reads=[("yst", s)], writes=[("big", k2 * 2), ("big", k2 * 2 + 1)])

        def evac_res(gate_sb, gk):
            def f(n, ps, pk):
                p.op("dve", lambda e: e.scalar_tensor_tensor(out=acc[:, n, :], in0=ps[:], scalar=gate_sb[:, 32 + n:33 + n],
                                                             in1=acc[:, n, :], op0=ALU.mult, op1=ALU.add),
                     reads=[pk, gk, ("acc", n)], writes=[("acc", n)])
            return f

        emit_gemm(cx, wst, Kmix_c, 16, lambda kc: (big[:, kc, :], ("big", kc)), evac_res(modm, mmk))
        emit_norm_mod(cx, acc, "acc", hs, "h", lambda kc: ops[:, kc:kc + 1], lambda kc: modf[:, kc:kc + 1],
                      [mfk, "ops"], tmp, stat)
        sbuf_s = tmp

        def evac13(n, ps, pk):
            q, r = n // 4, n % 4
            if r < 2:
                j = q * 2 + r
                p.op("act", lambda e: e.activation(out=sbuf_s[r][:], in_=ps[:], func=AF.Silu), reads=[pk], writes=[("tmpn", r)])
            else:
                j = q * 2 + (r - 2)
                p.op("dve", lambda e: e.tensor_tensor(out=big[:, j, :], in0=ps[:], in1=sbuf_s[r - 2][:], op=ALU.mult),
                     reads=[pk, ("tmpn", r - 2)], writes=[("big", j)])

        emit_gemm(cx, wst, 16, 88, lambda kc: (hs[:, kc, :], ("h", kc)), evac13)
        emit_gemm(cx, wst, 44, 16, lambda kc: (big[:, kc, :], ("big", kc)), evac_res(modf, mfk))
        if final:
            emit_norm_mod(cx, acc, "acc", hs, "h", None, None, [], tmp, stat)
            for kc in range(16):
                t = tmp[kc % 3]
                tk = ("tmpn", kc % 3)
                p.op("dve", lambda e: e.scalar_tensor_tensor(out=acc[:, kc, :], in0=acc[:, kc, :], scalar=fw[:, kc:kc + 1],
                                                             in1=stat[1][:], op0=ALU.mult, op1=ALU.mult),
                     reads=[("acc", kc), "fw", "rstd"], writes=[("acc", kc)])
        for kc4 in range(4):
            p.dma("pool", oTv[:, kc4 * 4:(kc4 + 1) * 4, t0:t0 + TT], acc[:, kc4 * 4:(kc4 + 1) * 4, :],
                  reads=[("acc", kc) for kc in range(kc4 * 4, kc4 * 4 + 4)])
    p.finish()
    return p


_PROGS = {}


def _prog(key, fn):
    if key not in _PROGS:
        _PROGS[key] = fn()
    return _PROGS[key]


def _run(p, in_maps):
    res = run_bass_kernel_spmd(p.nc, in_maps, core_ids=list(range(8)))
    return res.results


def cast_weights(wd):
    names = list(wd)
    parts = [np.ascontiguousarray(wd[n]).reshape(8, 128, -1) for n in names]
    sizes = [q.shape[2] for q in parts]
    F = sum(sizes)
    p = _prog(("cast", F), lambda: build_cast(F))
    in_maps = [{"x": np.ascontiguousarray(np.concatenate([q[i] for q in parts], axis=1))} for i in range(8)]
    r = _run(p, in_maps)
    ys = [np.asarray(r[i]["y"]) for i in range(8)]
    out = {}
    o = 0
    for n, sz in zip(names, sizes):
        out[n] = np.stack([ys[i][:, o:o + sz] for i in range(8)]).reshape(wd[n].shape)
        o += sz
    return out


def fm(a):
    return np.ascontiguousarray(a.reshape(-1, 128).T)


def pad_cols(W, n):
    if W.shape[1] == n:
        return W
    o = np.zeros((W.shape[0], n), dtype=W.dtype)
    o[:, :W.shape[1]] = W
    return o


def run_inproj(xT_cores, c, adaw_bf, adab, W_bf):
    ncc = W_bf.shape[1] // 128
    wpk = pack_w(W_bf)
    apk = pack_w(adaw_bf[:, :4096])
    ws = np.ascontiguousarray(np.concatenate([apk, wpk], axis=0))
    p = _prog(("inproj", ncc), lambda: build_inproj(ncc, wpk.shape[0]))
    in_maps = [{"xT": xT_cores[i], "cs": fm(c[i // 2]), "adab": fm(adab), "ws": ws} for i in range(8)]
    r = _run(p, in_maps)
    return [np.asarray(r[i]["pT"]) for i in range(8)]


def run_outffn(xT_cores, yT_cores, c, adaw_m, adab_m, adaw_f, adab_f, Wout, W1, W3, W2, fnw=None):
    Kc = Wout.shape[0] // 128
    W13 = np.concatenate([np.concatenate([W1[:, q * 256:(q + 1) * 256], W3[:, q * 256:(q + 1) * 256]], axis=1)
                          for q in range(22)], axis=1)
    pk = [pack_w(adaw_m[:, 4096:]), pack_w(adaw_f), pack_w(Wout), pack_w(W13), pack_w(W2)]
    ws = np.ascontiguousarray(np.concatenate(pk, axis=0))
    final = fnw is not None
    p = _prog(("outffn", Kc, final), lambda: build_outffn(Kc, pk[2].shape[0], pk[3].shape[0], pk[4].shape[0], final))
    in_maps = []
    for i in range(8):
        m = {"xT": xT_cores[i], "yT": yT_cores[i], "cs": fm(c[i // 2]), "adab_m": fm(adab_m), "adab_f": fm(adab_f), "ws": ws}
        if final:
            m["fnw"] = fm(fnw)
        in_maps.append(m)
    r = _run(p, in_maps)
    return [np.asarray(r[i]["oT"]) for i in range(8)]


SEQ = 4096


def ev_copy(p, eng, out, in_, reads, writes):
    if eng == "act":
        return p.op("act", lambda e: e.activation(out=out, in_=in_, func=AF.Copy), reads=reads, writes=writes)
    return p.op(eng, lambda e: e.tensor_copy(out=out, in_=in_), reads=reads, writes=writes)


def build_ssd(p=None, io=None):
    standalone = p is None
    if standalone:
        p = Prog()
    pre = "" if standalone else "ssd_"
    NH, NG_, CH = 32, 4, 128
    TS = 256
    zT = io["zT"] if io else p.dram("zT", [2048, SEQ], F32, "ExternalInput")
    xT = io["xT"] if io else p.dram("xT", [2048, SEQ + 3], F32, "ExternalInput")
    bcT = io["bcT"] if io else p.dram("bcT", [1024, SEQ + 3], F32, "ExternalInput")
    dtT = io["dtT"] if io else p.dram("dtT", [32, SEQ], F32, "ExternalInput")
    cwx_d = p.dram(pre + "cwx", [128, 16, 4], F32, "ExternalInput")
    cbx_d = p.dram(pre + "cbx", [128, 16], F32, "ExternalInput")
    cwbc_d = p.dram(pre + "cwbc", [128, 8, 4], F32, "ExternalInput")
    cbbc_d = p.dram(pre + "cbbc", [128, 8], F32, "ExternalInput")
    hp_d = p.dram(pre + "hp", [32, 2], F32, "ExternalInput")
    dsk_d = p.dram(pre + "dsk", [128, 16], F32, "ExternalInput")
    nw_d = p.dram(pre + "nw", [128, 16], F32, "ExternalInput")
    ident_d = p.dram(pre + "ident", [128, 128], F32, "ExternalInput")
    selh_d = p.dram(pre + "selh", [32, 32, 128], F32, "ExternalInput")
    maskT_d = p.dram(pre + "maskT", [128, 128], F32, "ExternalInput")
    yT = io["yT"] if io else p.dram("yT", [2048, SEQ], F32, "ExternalOutput")
    cx = Ctx(p)

    def const(name, shape, src):
        t = p.sbuf(name, shape, F32)
        p.dma("pool", t[:], src, writes=[name])
        return t
    cwx = const("cwx", [128, 16, 4], cwx_d)
    cbx = const("cbx", [128, 16], cbx_d)
    cwbc = const("cwbc", [128, 8, 4], cwbc_d)
    cbbc = const("cbbc", [128, 8], cbbc_d)
    hp = const("hp", [32, 2], hp_d)
    dsk = const("dsk", [128, 16], dsk_d)
    nw = const("nw", [128, 16], nw_d)
    ident = const("ident", [128, 128], ident_d)
    selh = const("selh", [32, 32, 128], selh_d)
    maskT = const("maskT", [128, 128], maskT_d)
    epsb = p.sbuf("epsb", [128, 1], F32)
    p.op("dve", lambda e: e.memset(epsb[:], EPS), writes=["epsb"])
    ones32 = p.sbuf("ones32", [32, 128], F32)
    p.op("dve", lambda e: e.memset(ones32[:], 1.0), writes=["ones32"])
    Aneg = p.sbuf("Aneg", [32, 1], F32)
    p.op("act", lambda e: e.activation(out=Aneg[:], in_=hp[:, 1:2], func=AF.Exp), reads=["hp"], writes=["Aneg"])
    p.op("dve", lambda e: e.tensor_scalar(out=Aneg[:], in0=Aneg[:], scalar1=-1.0, scalar2=None, op0=ALU.mult),
         reads=["Aneg"], writes=["Aneg"])

    xin2 = [p.sbuf(f"xin{i}", [128, 16, TS + 3], F32) for i in range(2)]
    zb2 = [p.sbuf(f"zb{i}", [128, 16, TS], F32) for i in range(2)]
    xc = p.sbuf("xc", [128, 16, TS], F32)
    bcin2 = [p.sbuf(f"bcin{i}", [128, 8, TS + 3], F32) for i in range(2)]
    bcc = p.sbuf("bcc", [128, 8, TS], BF16)
    bf = p.sbuf("bf", [128, 4, TS], F32)
    yb = p.sbuf("yb", [128, 16, TS], F32)
    ctmp = [p.sbuf(f"ctmp{i}", [128, TS], F32) for i in range(4)]
    dtin2 = [p.sbuf(f"dtin{i}", [32, TS], F32) for i in range(2)]
    dtp = p.sbuf("dtp", [32, TS], F32)
    av = p.sbuf("av", [32, TS], F32)
    acum = p.sbuf("acum", [32, TS], F32)
    m2 = p.sbuf("m2", [32, TS], F32)
    tmr = p.sbuf("tmr", [128, 3, 32], F32)
    dg = p.sbuf("dg", [32, 32], F32)
    ntm = p.sbuf("ntm", [128, 32], F32)
    negm = p.sbuf("negm", [128, 128], F32)
    p.op("dve", lambda e: e.tensor_scalar(out=negm[:], in0=maskT[:], scalar1=30000.0, scalar2=-30000.0, op0=ALU.mult, op1=ALU.add),
         reads=["maskT"], writes=["negm"])
    ea = p.sbuf("ea", [128, 32], F32)
    xtm = p.sbuf("xtm", [128, 32, 64], BF16)
    xw = p.sbuf("xw", [128, 32, 64], BF16)
    btm = p.sbuf("btm", [128, 4, 128], BF16)
    cbm = p.sbuf("cbm", [128, 4, 128], F32)
    state = p.sbuf("state", [128, 32, 64], F32)
    stbf = p.sbuf("stbf", [128, 32, 64], BF16)
    p.op("dve", lambda e: e.memset(state[:], 0.0), writes=[("st", g) for g in range(4)])
    p.op("pool", lambda e: e.memset(stbf[:], 0.0), writes=[("stb", g) for g in range(4)])
    NR = 16
    t1r = [p.sbuf(f"t1r{i}", [128, 128], F32) for i in range(NR)]
    E2r = [p.sbuf(f"E2r{i}", [128, 128], F32) for i in range(NR)]
    MTh = [p.sbuf(f"MTh{i}", [128, 128], BF16) for i in range(16)]
    Chh = [p.sbuf(f"Chh{i}", [128, 128], BF16) for i in range(16)]
    stat = (p.sbuf("rs", [128, TS], F32), p.sbuf("rstd", [128, TS], F32))
    sq = p.sbuf("sq", [128, 4, TS], BF16)
    ost = [p.sbuf(f"ost{i}", [128, TS], F32) for i in range(2)]

    xTv = xT.rearrange("(kc q) t -> q kc t", q=128)
    bcTv = bcT.rearrange("(kc q) t -> q kc t", q=128)
    zTv = zT.rearrange("(kc q) t -> q kc t", q=128)
    yTv = yT.rearrange("(kc q) t -> q kc t", q=128)
    hcount = 0
    def issue_loads(tl_):
        par_ = tl_ % 2
        t0_ = tl_ * TS
        for k4 in range(4):
            p.dma("sp", xin2[par_][:, k4 * 4:k4 * 4 + 4, :], xTv[:, k4 * 4:k4 * 4 + 4, t0_:t0_ + TS + 3],
                  writes=[(f"xin{par_}", kc) for kc in range(k4 * 4, k4 * 4 + 4)])
        for k4 in range(2):
            p.dma("sp", bcin2[par_][:, k4 * 4:k4 * 4 + 4, :], bcTv[:, k4 * 4:k4 * 4 + 4, t0_:t0_ + TS + 3],
                  writes=[(f"bcin{par_}", kc) for kc in range(k4 * 4, k4 * 4 + 4)])
        p.dma("sp", dtin2[par_][:], dtT[:, t0_:t0_ + TS], writes=[f"dtin{par_}"])
        for k4 in range(4):
            p.dma("sp", zb2[par_][:, k4 * 4:k4 * 4 + 4, :], zTv[:, k4 * 4:k4 * 4 + 4, t0_:t0_ + TS],
                  writes=[(f"zb{par_}", kc) for kc in range(k4 * 4, k4 * 4 + 4)])

    NTS = SEQ // TS
    issue_loads(0)
    for tl in range(NTS):
        t0 = tl * TS
        par = tl % 2
        xin, bcin, dtin, zb = xin2[par], bcin2[par], dtin2[par], zb2[par]
        if tl + 1 < NTS:
            issue_loads(tl + 1)

        def conv_group(items):
            for gi, (src, skey, kc, w, b, wk, outs) in enumerate(items):
                t, tk = ctmp[gi], ("ctmp", gi)
                p.op("act", lambda e: e.activation(out=t[:], in_=src[:, kc, 0:TS], func=AF.Identity,
                                                   scale=w[:, kc, 0:1], bias=b[:, kc:kc + 1]),
                     reads=[(skey, kc)] + wk, writes=[tk])
            for j in (1, 2, 3):
                for gi, (src, skey, kc, w, b, wk, outs) in enumerate(items):
                    t, tk = ctmp[gi], ("ctmp", gi)
                    p.op("dve", lambda e: e.scalar_tensor_tensor(out=t[:], in0=src[:, kc, j:j + TS], scalar=w[:, kc, j:j + 1],
                                                                 in1=t[:], op0=ALU.mult, op1=ALU.add),
                         reads=[(skey, kc), tk] + wk, writes=[tk])
            for gi, (src, skey, kc, w, b, wk, outs) in enumerate(items):
                t, tk = ctmp[gi], ("ctmp", gi)
                for (o_ap, okey) in outs:
                    p.op("act", lambda e: e.activation(out=o_ap, in_=t[:], func=AF.Silu), reads=[tk], writes=[okey])

        items = [(xin, f"xin{par}", kc, cwx, cbx, ["cwx", "cbx"], [(xc[:, kc, :], ("xc", kc))]) for kc in range(16)]
        for kc in range(8):
            outs = [(bcc[:, kc, :], ("bcc", kc))]
            if kc < 4:
                outs.append((bf[:, kc, :], ("bf", kc)))
            items.append((bcin, f"bcin{par}", kc, cwbc, cbbc, ["cwbc", "cbbc"], outs))
        for g0 in range(0, len(items), 4):
            conv_group(items[g0:g0 + 4])

        p.op("act", lambda e: e.activation(out=dtp[:], in_=dtin[:], func=AF.Exp, bias=hp[:, 0:1]),
             reads=[f"dtin{par}", "hp"], writes=["dtp"])
        p.op("act", lambda e: e.activation(out=dtp[:], in_=dtp[:], func=AF.Ln, bias=ones32[:, 0:1]),
             reads=["dtp", "ones32"], writes=["dtp"])
        p.op("dve", lambda e: e.tensor_scalar(out=av[:], in0=dtp[:], scalar1=Aneg[:, 0:1], scalar2=None, op0=ALU.mult),
             reads=["dtp", "Aneg"], writes=["av"])
        for c in range(TS // CH):
            cs_ = slice(c * CH, (c + 1) * CH)
            p.op("dve", lambda e: e.tensor_tensor_scan(out=acum[:, cs_], data0=ones32[:, 0:CH], data1=av[:, cs_],
                                                       initial=0.0, op0=ALU.mult, op1=ALU.add),
                 reads=["av", "ones32"], writes=[("acum", c)])
            p.op("act", lambda e: e.activation(out=m2[:, cs_], in_=acum[:, cs_], func=AF.Exp, scale=-1.0,
                                               bias=acum[:, (c + 1) * CH - 1:(c + 1) * CH]),
                 reads=[("acum", c)], writes=[("m2", c)])
            p.op("dve", lambda e: e.tensor_tensor(out=m2[:, cs_], in0=m2[:, cs_], in1=dtp[:, cs_], op=ALU.mult),
                 reads=[("m2", c), "dtp"], writes=[("m2", c)])

        for c in range(TS // CH):
            cs_ = slice(c * CH, (c + 1) * CH)
            ps, pk = cx.next_ps()
            for i, (src, sk) in enumerate(((dtp, "dtp"), (acum, ("acum", c)), (m2, ("m2", c)))):
                p.op("pe", lambda e: e.matmul(ps[:, i * 32:(i + 1) * 32], lhsT=src[:, cs_], rhs=ident[0:32, 0:32],
                                              start=True, stop=True),
                     reads=[sk, "ident"], writes=[pk], nosync_same=True)
            p.op("dve", lambda e: e.tensor_copy(out=tmr[:].rearrange("q a h -> q (a h)"), in_=ps[:, 0:96]),
                 reads=[pk], writes=["tmr"])
            p.op("dve", lambda e: e.tensor_scalar(out=dg[:], in0=ident[0:32, 0:32],
                                                  scalar1=acum[:, (c + 1) * CH - 1:(c + 1) * CH], scalar2=None, op0=ALU.mult),
                 reads=["ident", ("acum", c)], writes=["dg"])
            ps, pk = cx.next_ps()
            p.op("pe", lambda e: e.matmul(ps[:, 0:32], lhsT=ones32[:, :], rhs=dg[:], start=True, stop=True),
                 reads=["ones32", "dg"], writes=[pk])
            p.op("act", lambda e: e.activation(out=ea[:], in_=ps[:, 0:32], func=AF.Exp), reads=[pk], writes=["ea"])
            for k4 in range(4):
                ps, pk = cx.next_ps()
                for kk in range(4):
                    kc = k4 * 4 + kk
                    p.op("pe", lambda e: e.transpose(out=ps[:, kk * 128:(kk + 1) * 128], in_=xc[:, kc, cs_], identity=ident[:]),
                         reads=[("xc", kc), "ident"], writes=[pk], nosync_same=True)
                ev_copy(p, cx.ev_eng(), xtm[:, k4 * 8:(k4 + 1) * 8, :].rearrange("q h d -> q (h d)"), ps[:],
                        [pk], [("xtm", k4)])
            ps, pk = cx.next_ps()
            for g in range(4):
                p.op("pe", lambda e: e.transpose(out=ps[:, g * 128:(g + 1) * 128], in_=bf[:, g, cs_], identity=ident[:]),
                     reads=[("bf", g), "ident"], writes=[pk], nosync_same=True)
            ev_copy(p, cx.ev_eng(), btm[:].rearrange("q g n -> q (g n)"), ps[:], [pk], ["btm"])
            ps, pk = cx.next_ps()
            for g in range(4):
                p.op("pe", lambda e: e.matmul(ps[:, g * 128:(g + 1) * 128], lhsT=bcc[:, g, cs_], rhs=bcc[:, 4 + g, cs_],
                                              start=True, stop=True),
                     reads=[("bcc", g), ("bcc", 4 + g)], writes=[pk], nosync_same=True)
            for g in range(4):
                p.op("dve", lambda e: e.tensor_tensor(out=cbm[:, g, :], in0=ps[:, g * 128:(g + 1) * 128], in1=maskT[:], op=ALU.mult),
                     reads=[pk, "maskT"], writes=[("cbm", g)])
            p.op("dve", lambda e: e.tensor_tensor(out=xw[:], in0=xtm[:], in1=tmr[:, 2, :].unsqueeze(2).to_broadcast([128, 32, 64]), op=ALU.mult),
                 reads=[("xtm", k4) for k4 in range(4)] + ["tmr"], writes=[("xw", g) for g in range(4)])
            p.op("pool", lambda e: e.tensor_scalar(out=ntm[:], in0=tmr[:, 1, :], scalar1=-1.0, scalar2=0.0, op0=ALU.mult, op1=ALU.add),
                 reads=["tmr"], writes=["ntm"])
            def stageA(hb):
              hs_ = list(range(hb * 8, hb * 8 + 8))
              pbs = {}
              for h in hs_:
                psb, pkb = cx.next_ps()
                pbs[h] = (psb, pkb)
                p.op("pe", lambda e: e.matmul(psb[:, 0:128], lhsT=selh[:, h, :], rhs=acum[:, cs_], start=True, stop=False),
                     reads=["selh", ("acum", c)], writes=[pkb], nosync_same=True)
                p.op("pe", lambda e: e.matmul(psb[:, 0:128], lhsT=ident[:], rhs=negm[:], start=False, stop=True),
                     reads=["ident", "negm"], writes=[pkb], nosync_same=True)
                p.op("pe", lambda e: e.matmul(psb[:, 128:256], lhsT=selh[:, h, :], rhs=acum[:, cs_], start=True, stop=True),
                     reads=["selh", ("acum", c)], writes=[pkb], nosync_same=True)
              for h in hs_:
                psb, pkb = pbs[h]
                r = h % 16
                p.op("act", lambda e: e.activation(out=t1r[r][:], in_=psb[:, 0:128], func=AF.Exp, bias=ntm[:, h:h + 1]),
                     reads=[pkb, "ntm"], writes=[("t1", r)])
                p.op("act", lambda e: e.activation(out=E2r[r][:], in_=psb[:, 128:256], func=AF.Exp), reads=[pkb], writes=[("E2", r)])
              for h in hs_:
                r = h % 16
                g = h // 8
                p.op("dve", lambda e: e.scalar_tensor_tensor(out=MTh[r][:], in0=t1r[r][:], scalar=tmr[:, 0, h:h + 1],
                                                             in1=cbm[:, g, :], op0=ALU.mult, op1=ALU.mult),
                     reads=[("t1", r), "tmr", ("cbm", g)], writes=[("MT", r)])
                p.op("pool", lambda e: e.tensor_tensor(out=Chh[r][:], in0=bcc[:, 4 + g, cs_], in1=E2r[r][:], op=ALU.mult),
                     reads=[("bcc", 4 + g), ("E2", r)], writes=[("Ch", r)])

            def stageB(hb):
              for kc in range(hb * 4, hb * 4 + 4):
                psY, pkY = cx.next_ps()
                for hh in range(2):
                    h = kc * 2 + hh
                    g = h // 8
                    p.op("pe", lambda e: e.matmul(psY[hh * 64:(hh + 1) * 64, 0:128], lhsT=xtm[:, h, :], rhs=MTh[h % 16][:],
                                                  start=True, stop=False),
                         reads=[("xtm", h // 8), ("MT", h % 16)], writes=[pkY], nosync_same=True)
                    p.op("pe", lambda e: e.matmul(psY[hh * 64:(hh + 1) * 64, 0:128], lhsT=stbf[:, h, :], rhs=Chh[h % 16][:],
                                                  start=False, stop=True),
                         reads=[("stb", g), ("Ch", h % 16)], writes=[pkY], nosync_same=True)
                ev_copy(p, cx.ev_eng(), yb[:, kc, cs_], psY[:, 0:128], [pkY], [("yb", kc)])

            stageA(0)
            for hb in range(4):
                if hb < 3:
                    stageA(hb + 1)
                stageB(hb)
            for g in range(4):
                psS, pkS = cx.next_ps()
                p.op("pe", lambda e: e.matmul(psS[:], lhsT=btm[:, g, :], rhs=xw[:, g * 8:(g + 1) * 8, :].rearrange("q h d -> q (h d)"),
                                              start=True, stop=True),
                     reads=["btm", ("xw", g)], writes=[pkS])
                sg_ = state[:, g * 8:(g + 1) * 8, :]
                p.op("pool", lambda e: e.tensor_tensor(out=sg_, in0=sg_, in1=ea[:, g * 8:(g + 1) * 8].unsqueeze(2).to_broadcast([128, 8, 64]), op=ALU.mult),
                     reads=[("st", g), "ea"], writes=[("st", g)])
                p.op("dve", lambda e: e.tensor_tensor(out=sg_, in0=sg_, in1=psS[:].rearrange("q (h d) -> q h d", d=64), op=ALU.add),
                     reads=[("st", g), pkS], writes=[("st", g)])
                p.op("act", lambda e: e.activation(out=stbf[:, g * 8:(g + 1) * 8, :], in_=state[:, g * 8:(g + 1) * 8, :], func=AF.Copy),
                     reads=[("st", g)], writes=[("stb", g)])

        for g0 in range(0, 16, 4):
            for kc in range(g0, g0 + 4):
                p.op("dve", lambda e: e.scalar_tensor_tensor(out=yb[:, kc, :], in0=xc[:, kc, :], scalar=dsk[:, kc:kc + 1],
                                                             in1=yb[:, kc, :], op0=ALU.mult, op1=ALU.add),
                     reads=[("xc", kc), "dsk", ("yb", kc)], writes=[("yb", kc)])
            for kc in range(g0, g0 + 4):
                t, tk = ctmp[kc % 4], ("ctmp", kc % 4)
                p.op("act", lambda e: e.activation(out=t[:], in_=zb[:, kc, :], func=AF.Silu), reads=[(f"zb{par}", kc)], writes=[tk])
            for kc in range(g0, g0 + 4):
                t, tk = ctmp[kc % 4], ("ctmp", kc % 4)
                p.op("dve", lambda e: e.tensor_tensor(out=yb[:, kc, :], in0=yb[:, kc, :], in1=t[:], op=ALU.mult),
                     reads=[("yb", kc), tk], writes=[("yb", kc)])
        for g in range(4):
            for kk in range(4):
                kc = g * 4 + kk
                p.op("act", lambda e: e.activation(out=sq[:, kk, :], in_=yb[:, kc, :], func=AF.Square),
                     reads=[("yb", kc)], writes=[("sq", kk)])
            ps, pk = cx.next_ps()
            for kk in range(4):
                p.op("pe", lambda e: e.matmul(ps[:, 0:TS], lhsT=cx.ones[:], rhs=sq[:, kk, :], start=(kk == 0), stop=(kk == 3)),
                     reads=["ones", ("sq", kk)], writes=[pk], nosync_same=True)
            rs, rstd = stat
            p.op("act", lambda e: e.activation(out=rs[:], in_=ps[:, 0:TS], func=AF.Sqrt, scale=1.0 / 512, bias=epsb[:, 0:1]),
                 reads=[pk, "epsb"], writes=["rs"])
            p.op("dve", lambda e: e.reciprocal(out=rstd[:], in_=rs[:]), reads=["rs"], writes=["rstd"])
            for kk in range(4):
                kc = g * 4 + kk
                o = ost[kc % 2]
                ok = ("ost", kc % 2)
                p.op("dve", lambda e: e.scalar_tensor_tensor(out=o[:], in0=yb[:, kc, :], scalar=nw[:, kc:kc + 1], in1=rstd[:],
                                                             op0=ALU.mult, op1=ALU.mult),
                     reads=[("yb", kc), "nw", "rstd"], writes=[ok])
                p.dma("pool", yTv[:, kc, t0:t0 + TS], o[:], reads=[ok])
    if standalone:
        p.finish()
    return p


def run_ssd(p1T_b, prm):
    p = _prog("ssd", build_ssd)
    ident = np.eye(128, dtype=np.float32)
    selh = np.zeros((32, 32, 128), np.float32)
    for h in range(32):
        selh[h, h, :] = 1.0
    maskT = np.triu(np.ones((128, 128), np.float32))
    in_maps = []
    for core in range(8):
        b, hh = core // 2, core % 2
        P = p1T_b[b]
        chx = slice(4096 + hh * 2048, 4096 + (hh + 1) * 2048)
        z = P[hh * 2048:(hh + 1) * 2048]
        x = P[chx]
        Bm = P[8192 + hh * 512: 8192 + (hh + 1) * 512]
        Cm = P[9216 + hh * 512: 9216 + (hh + 1) * 512]
        dt = P[10240 + hh * 32: 10240 + (hh + 1) * 32]
        pad = lambda a: np.ascontiguousarray(np.concatenate([np.zeros((a.shape[0], 3), np.float32), a], axis=1))
        cw = prm["ssm_conv_w"]
        cb = prm["ssm_conv_b"]
        cix = np.arange(hh * 2048, (hh + 1) * 2048)
        cibc = np.concatenate([4096 + hh * 512 + np.arange(512), 5120 + hh * 512 + np.arange(512)])
        hs = slice(hh * 32, (hh + 1) * 32)
        m = {
            "zT": np.ascontiguousarray(z), "xT": pad(x), "bcT": pad(np.concatenate([Bm, Cm], axis=0)),
            "dtT": np.ascontiguousarray(dt),
            "cwx": np.ascontiguousarray(cw[:, cix].T.reshape(16, 128, 4).transpose(1, 0, 2)),
            "cbx": fm(cb[cix]),
            "cwbc": np.ascontiguousarray(cw[:, cibc].T.reshape(8, 128, 4).transpose(1, 0, 2)),
            "cbbc": fm(cb[cibc]),
            "hp": np.ascontiguousarray(np.stack([prm["ssm_dt_bias"][hs], prm["ssm_a_log"][hs]], axis=1)),
            "dsk": fm(np.repeat(prm["ssm_d"][hs], 64)),
            "nw": fm(prm["ssm_norm_w"][hh * 2048:(hh + 1) * 2048]),
            "ident": ident, "selh": selh, "maskT": maskT,
        }
        in_maps.append(m)
    r = _run(p, in_maps)
    return [np.asarray(r[i]["yT"]) for i in range(8)]


class PsQ:
    def __init__(self, p, nbanks, name="pq"):
        self.banks = [p.psum(f"{name}{i}", [128, 512], F32) for i in range(nbanks)]
        self.n = nbanks
        self.i = 0
        self.name = name

    def nextbank(self):
        i = self.i % self.n
        self.i += 1
        return self.banks[i], (self.name, i)


INV_DT = BF16


class InvBufs:
    def __init__(self, p, name, dt=F32):
        mk = lambda s_, sh: p.sbuf(f"{name}_{s_}", sh, dt)
        self.name = name
        self.NN, self.PP, self.QQ, self.XX = (mk(x, [128, 2, 128]) for x in ("NN", "PP", "QQ", "XX"))
        self.O, self.WT = mk("O", [128, 128]), mk("WT", [128, 128])

    def k(self, s_):
        return (self.name, s_)


def _kl(k):
    return list(k) if isinstance(k, list) else [k]


def emit_inverse(p, pq, chains, masks, ident2):
    m16, o32, o64 = masks["m16x2"], masks["o32"], masks["o64"]

    def mm(out_ps, pk, lhsT, lk, rhs, rk):
        p.op("pe", lambda e: e.matmul(out_ps, lhsT=lhsT, rhs=rhs, start=True, stop=True), reads=[lk, rk], writes=[pk],
             nosync_same=True)

    def rounds(fn_mm, fn_ev):
        for s0 in range(0, len(chains), pq.n):
            pend = []
            for ch in chains[s0:s0 + pq.n]:
                bk, k = pq.nextbank()
                fn_mm(ch, bk, k)
                pend.append((ch, bk, k))
            for ci, (ch, bk, k) in enumerate(pend):
                fn_ev(ci, ch, bk, k)

    f2 = lambda t: t[:].rearrange("q a c -> q (a c)")
    for ci, (b, AA, AAk) in enumerate(chains):
        p.op("dve", lambda e: e.scalar_tensor_tensor(out=f2(b.NN), in0=f2(AA), scalar=-1.0, in1=f2(m16), op0=ALU.mult, op1=ALU.mult),
             reads=_kl(AAk) + ["m16x2"], writes=[b.k("NN")])
        p.op("pool", lambda e: e.tensor_tensor(out=f2(b.XX), in0=f2(b.NN), in1=f2(ident2), op=ALU.add),
             reads=[b.k("NN"), "ident2"], writes=[b.k("XX")])
    names = ["NN", "PP", "QQ", "PP"]
    for lvl in range(3):
        src, dst = names[lvl], names[lvl + 1]

        def sq_mm(ch, bk, k):
            b = ch[0]
            S = getattr(b, src)
            mm(bk[:, 0:128], k, S[:, 1, :], b.k(src), S[:, 0, :], b.k(src))
            mm(bk[:, 128:256], k, S[:, 0, :], b.k(src), S[:, 1, :], b.k(src))

        def sq_ev(ci, ch, bk, k):
            b = ch[0]
            ev_copy(p, "act" if ci % 2 == 0 else "dve", f2(getattr(b, dst)), bk[:, 0:256], [k], [b.k(dst)])
        rounds(sq_mm, sq_ev)

        def pr_mm(ch, bk, k):
            b = ch[0]
            Pm = getattr(b, dst)
            mm(bk[:, 0:128], k, Pm[:, 1, :], b.k(dst), b.XX[:, 0, :], b.k("XX"))
            mm(bk[:, 128:256], k, Pm[:, 0, :], b.k(dst), b.XX[:, 1, :], b.k("XX"))

        def pr_ev(ci, ch, bk, k):
            b = ch[0]
            p.op("dve", lambda e: e.tensor_tensor(out=f2(b.XX), in0=f2(b.XX), in1=bk[:, 0:256], op=ALU.add),
                 reads=[b.k("XX"), k], writes=[b.k("XX")])
        rounds(pr_mm, pr_ev)
    for li, om in enumerate((o32, o64)):
        def w_mm(ch, bk, k):
            b, AA, AAk = ch
            p.op("pool", lambda e: e.tensor_tensor(out=b.O[:], in0=AA[:, 0, :], in1=om[:], op=ALU.mult),
                 reads=_kl(AAk) + ["o32" if li == 0 else "o64"], writes=[b.k("O")])
            mm(bk[:, 0:128], k, b.O[:], b.k("O"), b.XX[:, 1, :], b.k("XX"))

        def w_ev(ci, ch, bk, k):
            b = ch[0]
            ev_copy(p, "act", b.WT[:], bk[:, 0:128], [k], [b.k("WT")])
        rounds(w_mm, w_ev)

        def z_mm(ch, bk, k):
            b = ch[0]
            mm(bk[:, 0:128], k, b.WT[:], b.k("WT"), b.XX[:, 0, :], b.k("XX"))
            mm(bk[:, 128:256], k, b.XX[:, 0, :], b.k("XX"), b.WT[:], b.k("WT"))

        def z_ev(ci, ch, bk, k):
            b = ch[0]
            p.op("dve", lambda e: e.tensor_tensor(out=f2(b.XX), in0=f2(b.XX), in1=bk[:, 0:256], op=ALU.subtract),
                 reads=[b.k("XX"), k], writes=[b.k("XX")])
        rounds(z_mm, z_ev)


def inv_masks_np():
    i = np.arange(128)
    same = lambda n: ((i[:, None] // n) == (i[None, :] // n)).astype(np.float32)
    m16 = same(16)
    o32 = same(32) - same(16)
    o64 = same(64) - same(32)
    return np.ascontiguousarray(np.stack([m16, m16], axis=1)), o32, o64


def build_gdn(p=None, io=None):
    standalone = p is None
    if standalone:
        p = Prog()
    pre = "" if standalone else "gdn_"
    C = 64
    NCI = TT // C
    qkvT = io["qkvT"] if io else p.dram("qkvT", [1536, SEQ + 3], F32, "ExternalInput")
    ztm = None if io else p.dram("ztm", [SEQ, 512], F32, "ExternalInput")
    zfm = io["zfm"] if io else None
    bT = io["bT"] if io else p.dram("bT", [4, SEQ], F32, "ExternalInput")
    aT = io["aT"] if io else p.dram("aT", [4, SEQ], F32, "ExternalInput")
    cw_d = p.dram(pre + "cw", [128, 12, 4], F32, "ExternalInput")
    hp_d = p.dram(pre + "hp", [4, 2], F32, "ExternalInput")
    nwrow_d = p.dram(pre + "nwrow", [128, 128], F32, "ExternalInput")
    m2_d = p.dram(pre + "m2", [2, 128, 2, 128], F32, "ExternalInput")
    mk_d = p.dram(pre + "mk", [5, 128, 128], F32, "ExternalInput")
    selrow_d = p.dram(pre + "selrow", [4, 4, 128], F32, "ExternalInput")
    i4p_d = p.dram(pre + "i4p", [4, 4, 2], F32, "ExternalInput")
    yb = io["yb"] if io else p.dram("yb", [SEQ, 512], F32, "ExternalOutput")

    def const(name, shape, src, dt=F32):
        t = p.sbuf(name, shape, dt)
        p.dma("pool", t[:], src, writes=[name])
        return t
    cw = const("cw", [128, 12, 4], cw_d)
    hp = const("hp", [4, 2], hp_d)
    nwrow = const("nwrow", [128, 128], nwrow_d)
    m16x2 = const("m16x2", [128, 2, 128], m2_d[0])
    ident2 = const("ident2", [128, 2, 128], m2_d[1])
    o32 = const("o32", [128, 128], mk_d[0])
    o64 = const("o64", [128, 128], mk_d[1])
    maskSL = const("maskSL", [128, 128], mk_d[2])
    maskUT = const("maskUT", [128, 128], mk_d[3])
    ident = const("ident", [128, 128], mk_d[4])
    selrow = const("selrow", [4, 4, 128], selrow_d)
    i4p = const("i4p", [4, 4, 2], i4p_d)
    identb = p.sbuf("identb", [128, 128], BF16)
    p.op("dve", lambda e: e.tensor_copy(out=identb[:], in_=ident[:]), reads=["ident"], writes=["identb"])
    onesb = p.sbuf("onesb", [128, 128], BF16)
    p.op("dve", lambda e: e.memset(onesb[:], 1.0), writes=["onesb"])
    ones4 = p.sbuf("ones4", [4, C], F32)
    p.op("dve", lambda e: e.memset(ones4[:], 1.0), writes=["ones4"])
    eps6 = p.sbuf("eps6", [128, 1], F32)
    p.op("dve", lambda e: e.memset(eps6[:], 1e-6), writes=["eps6"])
    eps5 = p.sbuf("eps5", [128, 1], F32)
    p.op("dve", lambda e: e.memset(eps5[:], EPS), writes=["eps5"])
    Aneg = p.sbuf("Aneg", [4, 1], F32)
    p.op("act", lambda e: e.activation(out=Aneg[:], in_=hp[:, 0:1], func=AF.Exp), reads=["hp"], writes=["Aneg"])
    p.op("dve", lambda e: e.tensor_scalar(out=Aneg[:], in0=Aneg[:], scalar1=-1.0, scalar2=None, op0=ALU.mult),
         reads=["Aneg"], writes=["Aneg"])

    pq = PsQ(p, 4, "pq")
    pg = PsQ(p, 3, "pg")
    psb = p.psum("psb16", [128, 1024], BF16)

    qin = p.sbuf("qin", [128, 12, TT + 3], F32)
    ctmp = [p.sbuf(f"ctmp{i}", [128, TT], F32) for i in range(2)]
    sqb = [p.sbuf(f"sqb{i}", [128, TT], BF16) for i in range(2)]
    rnt = [p.sbuf(f"rnt{i}", [128, TT], F32) for i in range(2)]
    xnp = [[p.sbuf(f"xnp{w}{pr}", [128, NCI, 2, C], BF16) for pr in range(2)] for w in range(3)]
    braw = p.sbuf("braw", [4, TT], F32)
    araw = p.sbuf("araw", [4, TT], F32)
    beta = p.sbuf("beta", [4, TT], F32)
    gv = p.sbuf("gv", [4, TT], F32)
    gc = p.sbuf("gc", [4, TT], F32)
    egc = p.sbuf("egc", [4, TT], F32)
    ekd = p.sbuf("ekd", [4, TT], F32)
    NPC = 2 * NCI
    tm = [p.sbuf(f"tm{i}", [128, 8], F32) for i in range(NPC)]
    uS = [p.sbuf(f"uS{i}", [128, 128], F32) for i in range(NPC)]
    wT = [p.sbuf(f"wT{i}", [128, 128], BF16) for i in range(NPC)]
    qkT = [p.sbuf(f"qkT{i}", [128, 128], BF16) for i in range(NPC)]
    qgT = [p.sbuf(f"qgT{i}", [128, 128], BF16) for i in range(NPC)]
    Kd = [p.sbuf(f"Kd{i}", [128, 128], BF16) for i in range(NPC)]
    egl = [p.sbuf(f"egl{i}", [128, 2], F32) for i in range(NPC)]
    NTR = 8
    Dm = [p.sbuf(f"Dm{i}", [128, 128], F32) for i in range(NTR)]
    DTm = [p.sbuf(f"DTm{i}", [128, 128], F32) for i in range(NTR)]
    tA = [p.sbuf(f"tA{i}", [128, 128], F32) for i in range(NTR)]
    AA = [p.sbuf(f"AA{i}", [128, 2, 128], F32) for i in range(NTR)]
    Kb = [p.sbuf(f"Kb{i}", [128, 128], BF16) for i in range(NTR)]
    Vb = [p.sbuf(f"Vb{i}", [128, 128], BF16) for i in range(NTR)]
    XTb = [p.sbuf(f"XTb{i}", [128, 128], BF16) for i in range(NTR)]
    NIV = 8
    ivb = [InvBufs(p, f"iv{i}", INV_DT) for i in range(NIV)]
    zin = [p.sbuf(f"zin{pr}", [128, NCI, 128], F32) for pr in range(2)]
    if zfm is not None:
        zf_in = p.sbuf("zf_in", [128, 4, TT], F32)
        zsb = p.sbuf("zsb", [128, 4, TT], BF16)
    ost = [p.sbuf(f"ostg{pr}", [128, NCI, 128], F32) for pr in range(2)]
    Sf = [p.sbuf(f"Sf{pr}", [128, 2, 128], F32) for pr in range(2)]
    Sb = [p.sbuf(f"Sb{pr}", [128, 2, 128], BF16) for pr in range(2)]
    for pr in range(2):
        p.op("dve", lambda e: e.memset(Sf[pr][:], 0.0), writes=[("Sf", pr)])
        p.op("pool", lambda e: e.memset(Sb[pr][:], 0.0), writes=[("Sb", pr)])
    vnew = [p.sbuf(f"vnew{i}", [128, 128], BF16) for i in range(2)]
    ss = [p.sbuf(f"ss{i}", [128, 1], F32) for i in range(2)]
    junk = [p.sbuf(f"junk{i}", [128, 128], F32) for i in range(2)]
    yt = [p.sbuf(f"yt{i}", [128, 128], F32) for i in range(2)]
    masks = {"m16x2": m16x2, "o32": o32, "o64": o64}

    qkvTv = qkvT.rearrange("(kc q) t -> q kc t", q=128)
    tctr = 0
    for tl in range(SEQ // TT):
        t0 = tl * TT
        for k4 in range(3):
            p.dma("sp", qin[:, k4 * 4:k4 * 4 + 4, :], qkvTv[:, k4 * 4:k4 * 4 + 4, t0:t0 + TT + 3],
                  writes=[("qin", kc) for kc in range(k4 * 4, k4 * 4 + 4)])
        p.dma("sp", braw[:], bT[:, t0:t0 + TT], writes=["braw"])
        p.dma("sp", araw[:], aT[:, t0:t0 + TT], writes=["araw"])
        if zfm is None:
            for pr in range(2):
                for half in range(2):
                    col0 = (2 * pr + half) * 128
                    p.dma("sp", zin[pr][half * 64:(half + 1) * 64, :, :],
                          ztm[t0:t0 + TT, col0:col0 + 128].rearrange("(ci t) v -> t ci v", t=C),
                          writes=[("zin", pr)])
        else:
            p.dma("sp", zf_in[:], zfm.rearrange("(h q) t -> q h t", q=128)[:, :, t0:t0 + TT], writes=["zf_in"])
            p.op("act", lambda e: e.activation(out=zsb[:], in_=zf_in[:], func=AF.Silu), reads=["zf_in"], writes=["zsb"])
        for kc in range(12):
            w_, hl = kc // 4, kc % 4
            pr, half = hl // 2, hl % 2
            t = ctmp[kc % 2]
            tk = ("ctmp", kc % 2)
            p.op("act", lambda e: e.activation(out=t[:], in_=qin[:, kc, 0:TT], func=AF.Copy, scale=cw[:, kc, 0:1]),
                 reads=[("qin", kc), "cw"], writes=[tk])
            for j in (1, 2, 3):
                p.op("dve", lambda e: e.scalar_tensor_tensor(out=t[:], in0=qin[:, kc, j:j + TT], scalar=cw[:, kc, j:j + 1],
                                                             in1=t[:], op0=ALU.mult, op1=ALU.add),
                     reads=[("qin", kc), tk, "cw"], writes=[tk])
            dst = xnp[w_][pr][:, :, half, :]
            dk = ("xnp", w_, pr)
            if w_ == 2:
                p.op("act", lambda e: e.activation(out=dst, in_=t[:].rearrange("q (a c) -> q a c", c=C), func=AF.Silu),
                     reads=[tk], writes=[dk])
                continue
            p.op("act", lambda e: e.activation(out=t[:], in_=t[:], func=AF.Silu), reads=[tk], writes=[tk])
            sq, sk = sqb[kc % 2], ("sqb", kc % 2)
            p.op("act", lambda e: e.activation(out=sq[:], in_=t[:], func=AF.Square), reads=[tk], writes=[sk])
            bk, k = pg.nextbank()
            p.op("pe", lambda e: e.matmul(bk[:], lhsT=onesb[:], rhs=sq[:], start=True, stop=True), reads=["onesb", sk], writes=[k])
            rn, rk = rnt[kc % 2], ("rnt", kc % 2)
            p.op("act", lambda e: e.activation(out=rn[:], in_=bk[:], func=AF.Sqrt, bias=eps6[:, 0:1]), reads=[k, "eps6"], writes=[rk])
            p.op("dve", lambda e: e.reciprocal(out=rn[:], in_=rn[:]), reads=[rk], writes=[rk])
            sc = 128 ** -0.5 if w_ == 0 else 1.0
            p.op("dve", lambda e: e.scalar_tensor_tensor(out=dst, in0=t[:].rearrange("q (a c) -> q a c", c=C), scalar=sc,
                                                         in1=rn[:].rearrange("q (a c) -> q a c", c=C), op0=ALU.mult, op1=ALU.mult),
                 reads=[tk, rk], writes=[dk])
        p.op("act", lambda e: e.activation(out=beta[:], in_=braw[:], func=AF.Sigmoid), reads=["braw"], writes=["beta"])
        p.op("act", lambda e: e.activation(out=gv[:], in_=araw[:], func=AF.Exp, bias=hp[:, 1:2]), reads=["araw", "hp"], writes=["gv"])
        p.op("act", lambda e: e.activation(out=gv[:], in_=gv[:], func=AF.Ln, bias=ones4[:, 0:1]), reads=["gv", "ones4"], writes=["gv"])
        p.op("dve", lambda e: e.tensor_scalar(out=gv[:], in0=gv[:], scalar1=Aneg[:, 0:1], scalar2=None, op0=ALU.mult),
             reads=["gv", "Aneg"], writes=["gv"])
        for ci in range(NCI):
            cs_ = slice(ci * C, (ci + 1) * C)
            p.op("dve", lambda e: e.tensor_tensor_scan(out=gc[:, cs_], data0=ones4[:, 0:C], data1=gv[:, cs_], initial=0.0,
                                                       op0=ALU.mult, op1=ALU.add),
                 reads=["gv", "ones4"], writes=["gc"])
            p.op("act", lambda e: e.activation(out=ekd[:, cs_], in_=gc[:, cs_], func=AF.Exp, scale=-1.0,
                                               bias=gc[:, (ci + 1) * C - 1:(ci + 1) * C]),
                 reads=["gc"], writes=["ekd"])
        p.op("act", lambda e: e.activation(out=egc[:], in_=gc[:], func=AF.Exp), reads=["gc"], writes=["egc"])
        for pr in range(2):
            if zfm is None:
                p.op("act", lambda e: e.activation(out=zin[pr][:], in_=zin[pr][:], func=AF.Silu), reads=[("zin", pr)], writes=[("zin", pr)])

        pcs = [(ci, pr) for ci in range(NCI) for pr in range(2)]
        for g0 in range(0, len(pcs), NIV):
            grp = pcs[g0:g0 + NIV]
            chains = []
            for gi, (ci, pr) in enumerate(grp):
                pc = ci * 2 + pr
                r = tctr % NTR
                tctr += 1
                hA, hB = 2 * pr, 2 * pr + 1
                cs_ = slice(ci * C, (ci + 1) * C)
                bk, k = pg.nextbank()
                for half in range(2):
                    for j, (qt, qk_) in enumerate(((beta, "beta"), (gc, "gc"), (egc, "egc"), (ekd, "ekd"))):
                        p.op("pe", lambda e: e.matmul(bk[half * 64:(half + 1) * 64, 2 * j:2 * j + 2], lhsT=qt[:, cs_],
                                                      rhs=i4p[:, 2 * pr + half, :], start=True, stop=True),
                             reads=[qk_, "i4p"], writes=[k], nosync_same=True)
                p.op("dve", lambda e: e.tensor_copy(out=tm[pc][:], in_=bk[:, 0:8]), reads=[k], writes=[("tm", pc)])
                p.op("dve", lambda e: e.tensor_tensor(out=tm[pc][:, 1:2], in0=tm[pc][:, 0:1], in1=tm[pc][:, 4:5], op=ALU.mult),
                     reads=[("tm", pc)], writes=[("tm", pc)])
                bk, k = pg.nextbank()
                for half in range(2):
                    p.op("pe", lambda e: e.matmul(bk[:, half * 64:(half + 1) * 64], lhsT=selrow[:, 2 * pr + half, :], rhs=gc[:, cs_],
                                                  start=True, stop=True),
                         reads=["selrow", "gc"], writes=[k], nosync_same=True)
                p.op("dve", lambda e: e.tensor_scalar(out=Dm[r][:], in0=bk[:, 0:128], scalar1=tm[pc][:, 2:3], scalar2=0.0,
                                                      op0=ALU.subtract, op1=ALU.max),
                     reads=[k, ("tm", pc)], writes=[("Dm", r)])
                p.op("dve", lambda e: e.tensor_scalar(out=DTm[r][:], in0=bk[:, 0:128], scalar1=tm[pc][:, 2:3], scalar2=0.0,
                                                      op0=ALU.subtract, op1=ALU.min),
                     reads=[k, ("tm", pc)], writes=[("DTm", r)])
                p.op("act", lambda e: e.activation(out=Dm[r][:], in_=Dm[r][:], func=AF.Exp, scale=-1.0), reads=[("Dm", r)], writes=[("Dm", r)])
                p.op("act", lambda e: e.activation(out=DTm[r][:], in_=DTm[r][:], func=AF.Exp), reads=[("DTm", r)], writes=[("DTm", r)])
                kn, qn, vn = xnp[1][pr], xnp[0][pr], xnp[2][pr]
                bk, k = pg.nextbank()
                for half in range(2):
                    p.op("pe", lambda e: e.matmul(bk[half * 64:(half + 1) * 64, 0:128], lhsT=kn[:, ci, half, :],
                                                  rhs=kn[:, ci, :, :].rearrange("q h t -> q (h t)"), start=True, stop=True),
                         reads=[("xnp", 1, pr)], writes=[k], nosync_same=True)
                    p.op("pe", lambda e: e.matmul(bk[half * 64:(half + 1) * 64, 128:256], lhsT=kn[:, ci, half, :],
                                                  rhs=qn[:, ci, :, :].rearrange("q h t -> q (h t)"), start=True, stop=True),
                         reads=[("xnp", 1, pr), ("xnp", 0, pr)], writes=[k], nosync_same=True)
                p.op("dve", lambda e: e.tensor_tensor(out=tA[r][:], in0=bk[:, 0:128], in1=Dm[r][:], op=ALU.mult),
                     reads=[k, ("Dm", r)], writes=[("tA", r)])
                p.op("dve", lambda e: e.scalar_tensor_tensor(out=AA[r][:, 0, :], in0=tA[r][:], scalar=tm[pc][:, 0:1], in1=maskSL[:],
                                                             op0=ALU.mult, op1=ALU.mult),
                     reads=[("tA", r), ("tm", pc), "maskSL"], writes=[("AA0", r)])
                p.op("dve", lambda e: e.tensor_tensor(out=tA[r][:], in0=bk[:, 128:256], in1=DTm[r][:], op=ALU.mult),
                     reads=[k, ("DTm", r), ("tA", r)], writes=[("tA", r)])
                p.op("pool", lambda e: e.tensor_tensor(out=qkT[pc][:], in0=tA[r][:], in1=maskUT[:], op=ALU.mult),
                     reads=[("tA", r), "maskUT"], writes=[("qkT", pc)])
                bk, k = pg.nextbank()
                p.op("pe", lambda e: e.transpose(out=bk[:, 0:128], in_=AA[r][:, 0, :], identity=ident[:]),
                     reads=[("AA0", r), "ident"], writes=[k])
                p.op("act", lambda e: e.activation(out=AA[r][:, 1, :], in_=bk[:, 0:128], func=AF.Copy), reads=[k], writes=[("AA1", r)])
                for half in range(2):
                    p.op("pe", lambda e: e.transpose(out=psb[half * 64:(half + 1) * 64, 0:128], in_=kn[:, ci, half, :], identity=identb[:]),
                         reads=[("xnp", 1, pr), "identb"], writes=["psb"], nosync_same=True)
                    p.op("pe", lambda e: e.transpose(out=psb[half * 64:(half + 1) * 64, 128:256], in_=vn[:, ci, half, :], identity=identb[:]),
                         reads=[("xnp", 2, pr), "identb"], writes=["psb"], nosync_same=True)
                p.op("dve", lambda e: e.tensor_scalar(out=Kb[r][:], in0=psb[:, 0:128], scalar1=tm[pc][:, 1:2], scalar2=None, op0=ALU.mult),
                     reads=["psb", ("tm", pc)], writes=[("Kb", r)])
                p.op("dve", lambda e: e.tensor_scalar(out=Kd[pc][:], in0=psb[:, 0:128], scalar1=tm[pc][:, 6:7], scalar2=None, op0=ALU.mult),
                     reads=["psb", ("tm", pc)], writes=[("Kd", pc)])
                p.op("dve", lambda e: e.tensor_scalar(out=Vb[r][:], in0=psb[:, 128:256], scalar1=tm[pc][:, 0:1], scalar2=None, op0=ALU.mult),
                     reads=["psb", ("tm", pc)], writes=[("Vb", r)])
                if zfm is not None:
                    for half in range(2):
                        p.op("pe", lambda e: e.transpose(out=psb[half * 64:(half + 1) * 64, 256:384], in_=zsb[:, 2 * pr + half, cs_],
                                                         identity=identb[:]),
                             reads=["zsb", "identb"], writes=["psb"], nosync_same=True)
                    p.op("dve", lambda e: e.tensor_copy(out=zin[pr][:, ci, :], in_=psb[:, 256:384]), reads=["psb"], writes=[("zin", pr)])
                bk, k = pg.nextbank()
                for half in range(2):
                    p.op("pe", lambda e: e.matmul(bk[:, half * 64:(half + 1) * 64], lhsT=selrow[:, 2 * pr + half, :], rhs=egc[:, cs_],
                                                  start=True, stop=True),
                         reads=["selrow", "egc"], writes=[k], nosync_same=True)
                p.op("dve", lambda e: e.tensor_tensor(out=qgT[pc][:], in0=qn[:, ci, :, :].rearrange("q h t -> q (h t)"), in1=bk[:, 0:128],
                                                      op=ALU.mult),
                     reads=[k, ("xnp", 0, pr)], writes=[("qgT", pc)])
                p.op("dve", lambda e: e.tensor_copy(out=egl[pc][:], in_=bk[:, 63:128:64]), reads=[k], writes=[("egl", pc)])
                chains.append((ivb[gi], AA[r], [("AA0", r), ("AA1", r)], pc, r))
            emit_inverse(p, pq, [(b_, aa_, ak_) for (b_, aa_, ak_, _pc, _r) in chains], masks, ident2)
            for (b_, aa_, ak_, pc, r) in chains:
                p.op("act", lambda e: e.activation(out=XTb[r][:], in_=b_.XX[:, 1, :], func=AF.Copy), reads=[b_.k("XX")], writes=[("XTb", r)])
                bk, k = pg.nextbank()
                p.op("pe", lambda e: e.matmul(bk[:, 0:128], lhsT=XTb[r][:], rhs=Vb[r][:], start=True, stop=True),
                     reads=[("XTb", r), ("Vb", r)], writes=[k], nosync_same=True)
                p.op("pe", lambda e: e.matmul(bk[:, 128:256], lhsT=Kb[r][:], rhs=XTb[r][:], start=True, stop=True),
                     reads=[("XTb", r), ("Kb", r)], writes=[k], nosync_same=True)
                p.op("act", lambda e: e.activation(out=uS[pc][:], in_=bk[:, 0:128], func=AF.Copy), reads=[k], writes=[("uS", pc)])
                p.op("act", lambda e: e.activation(out=wT[pc][:], in_=bk[:, 128:256], func=AF.Copy), reads=[k], writes=[("wT", pc)])

        for ci in range(NCI):
            for pr in range(2):
                pc = ci * 2 + pr
                vi = pc % 2
                bk, k = pg.nextbank()
                for half in range(2):
                    p.op("pe", lambda e: e.matmul(bk[half * 64:(half + 1) * 64, 0:128], lhsT=wT[pc][:, half * 64:(half + 1) * 64],
                                                  rhs=Sb[pr][:, half, :], start=True, stop=True),
                         reads=[("wT", pc), ("Sb", pr)], writes=[k], nosync_same=True)
                p.op("dve", lambda e: e.tensor_tensor(out=vnew[vi][:], in0=uS[pc][:], in1=bk[:, 0:128], op=ALU.subtract),
                     reads=[("uS", pc), k], writes=[("vnew", vi)])
                bko, ko = pg.nextbank()
                p.op("pe", lambda e: e.matmul(bko[:, 0:128], lhsT=qkT[pc][:], rhs=vnew[vi][:], start=True, stop=False),
                     reads=[("qkT", pc), ("vnew", vi)], writes=[ko], nosync_same=True)
                for half in range(2):
                    p.op("pe", lambda e: e.matmul(bko[half * 64:(half + 1) * 64, 0:128], lhsT=qgT[pc][:, half * 64:(half + 1) * 64],
                                                  rhs=Sb[pr][:, half, :], start=False, stop=True),
                         reads=[("qgT", pc), ("Sb", pr)], writes=[ko], nosync_same=True)
                bks2 = [pg.nextbank(), pg.nextbank()]
                for half in range(2):
                    bks, ks = bks2[half]
                    p.op("pe", lambda e: e.matmul(bks[:, 0:128], lhsT=Kd[pc][half * 64:(half + 1) * 64, :],
                                                  rhs=vnew[vi][half * 64:(half + 1) * 64, :], start=True, stop=True),
                         reads=[("Kd", pc), ("vnew", vi)], writes=[ks], nosync_same=True)
                for half in range(2):
                    bks, ks = bks2[half]
                    p.op("dve", lambda e: e.scalar_tensor_tensor(out=Sf[pr][:, half, :], in0=Sf[pr][:, half, :], scalar=egl[pc][:, half:half + 1],
                                                                 in1=bks[:, 0:128], op0=ALU.mult, op1=ALU.add),
                         reads=[("Sf", pr), ("egl", pc), ks], writes=[("Sf", pr)])
                p.op("act", lambda e: e.activation(out=Sb[pr][:], in_=Sf[pr][:], func=AF.Copy), reads=[("Sf", pr)], writes=[("Sb", pr)])
                p.op("act", lambda e: e.activation(out=junk[vi][:], in_=bko[:, 0:128], func=AF.Square, accum_out=ss[vi][:]),
                     reads=[ko], writes=[("ss", vi), ("junk", vi)])
                p.op("act", lambda e: e.activation(out=ss[vi][:], in_=ss[vi][:], func=AF.Sqrt, scale=1.0 / 128, bias=eps5[:, 0:1]),
                     reads=[("ss", vi), "eps5"], writes=[("ss", vi)])
                p.op("dve", lambda e: e.reciprocal(out=ss[vi][:], in_=ss[vi][:]), reads=[("ss", vi)], writes=[("ss", vi)])
                p.op("dve", lambda e: e.scalar_tensor_tensor(out=yt[vi][:], in0=bko[:, 0:128], scalar=ss[vi][:, 0:1], in1=nwrow[:],
                                                             op0=ALU.mult, op1=ALU.mult),
                     reads=[ko, ("ss", vi), "nwrow"], writes=[("yt", vi)])
                p.op("pool", lambda e: e.tensor_tensor(out=ost[pr][:, ci, :], in0=yt[vi][:], in1=zin[pr][:, ci, :], op=ALU.mult),
                     reads=[("yt", vi), ("zin", pr)], writes=[("ost", pr)])
        for pr in range(2):
            for half in range(2):
                col0 = (2 * pr + half) * 128
                p.dma("pool", yb[t0:t0 + TT, col0:col0 + 128].rearrange("(ci t) v -> t ci v", t=C),
                      ost[pr][half * 64:(half + 1) * 64, :, :], reads=[("ost", pr)])
    if standalone:
        p.finish()
    return p


def gdn_consts():
    m16x2, o32, o64 = inv_masks_np()
    I = np.eye(128, dtype=np.float32)
    i = np.arange(128)
    same64 = (i[:, None] // 64) == (i[None, :] // 64)
    maskSL = (same64 & (i[:, None] > i[None, :])).astype(np.float32)
    maskUT = (same64 & (i[None, :] >= i[:, None])).astype(np.float32)
    selrow = np.zeros((4, 4, 128), np.float32)
    i4p = np.zeros((4, 4, 2), np.float32)
    for h in range(4):
        selrow[h, h, :] = 1.0
        i4p[h, h, 0] = 1.0
    return {"m2": np.ascontiguousarray(np.stack([m16x2, np.stack([I, I], axis=1)])),
            "mk": np.ascontiguousarray(np.stack([o32, o64, maskSL, maskUT, I])), "selrow": selrow, "i4p": i4p}


def run_gdn(p0T_b, prm):
    p = _prog("gdn", build_gdn)
    cst = gdn_consts()
    in_maps = []
    G0 = 3328
    for core in range(8):
        b, hh = core // 2, core % 2
        P = p0T_b[b]
        sl = lambda w: P[G0 + w * 1024 + hh * 512: G0 + w * 1024 + (hh + 1) * 512]
        qkv = np.concatenate([sl(0), sl(1), sl(2)], axis=0)
        pad = np.ascontiguousarray(np.concatenate([np.zeros((1536, 3), np.float32), qkv], axis=1))
        cidx = np.concatenate([w * 1024 + hh * 512 + np.arange(512) for w in range(3)])
        cw = prm["gdn_conv_w"][:, cidx]
        m = {"qkvT": pad, "ztm": np.ascontiguousarray(sl(3).T),
             "bT": np.ascontiguousarray(P[G0 + 4096 + hh * 4: G0 + 4096 + hh * 4 + 4]),
             "aT": np.ascontiguousarray(P[G0 + 4104 + hh * 4: G0 + 4104 + hh * 4 + 4]),
             "cw": np.ascontiguousarray(cw.T.reshape(12, 128, 4).transpose(1, 0, 2)),
             "hp": np.ascontiguousarray(np.stack([prm["gdn_a_log"][hh * 4:hh * 4 + 4], prm["gdn_dt_bias"][hh * 4:hh * 4 + 4]], axis=1)),
             "nwrow": np.ascontiguousarray(np.broadcast_to(prm["gdn_norm_w"][None, :], (128, 128))),
             }
        m.update(cst)
        in_maps.append(m)
    r = _run(p, in_maps)
    return [np.asarray(r[i]["yb"]) for i in range(8)]


TTR = 256
PRM_COLS = {"mu_r": 0, "mu_k": 4, "mu_v": 8, "om_r": 12, "om_k": 16, "om_v": 20, "w0": 24, "a0": 28,
            "k_k": 32, "k_a": 36, "omka": 40, "r_k": 44}


def build_rwkv(p=None, io=None):
    standalone = p is None
    if standalone:
        p = Prog()
    pre = "" if standalone else "rwkv_"
    C = 64
    NCI = TTR // C
    NP_ = 4
    rkvT = io["rkvT"] if io else p.dram("rkvT", [1536, SEQ + 1], F32, "ExternalInput")
    lwT = io["lwT"] if io else p.dram("lwT", [64, SEQ + 1], F32, "ExternalInput")
    laT = io["laT"] if io else p.dram("laT", [64, SEQ + 1], F32, "ExternalInput")
    lgT = io["lgT"] if io else p.dram("lgT", [128, SEQ + 1], F32, "ExternalInput")
    prm_d = p.dram(pre + "prm", [128, 64], F32, "ExternalInput")
    lmu_d = p.dram(pre + "lmu", [128, 6], F32, "ExternalInput")
    w2_d = p.dram(pre + "w2", [64, 512], F32, "ExternalInput")
    a2_d = p.dram(pre + "a2", [64, 512], F32, "ExternalInput")
    g2_d = p.dram(pre + "g2", [128, 512], F32, "ExternalInput")
    lnw_d = p.dram(pre + "lnw", [128, 4, 64], F32, "ExternalInput")
    lnb_d = p.dram(pre + "lnb", [128, 4, 64], F32, "ExternalInput")
    m2_d = p.dram(pre + "m2", [4, 128, 2, 128], F32, "ExternalInput")
    mk_d = p.dram(pre + "mk", [4, 128, 128], F32, "ExternalInput")
    ya = io["ya"] if io else p.dram("ya", [SEQ, 512], F32, "ExternalOutput")

    def const(name, shape, src, dt=F32):
        t = p.sbuf(name, shape, dt)
        p.dma("pool", t[:], src, writes=[name])
        return t
    prm = const("prm", [128, 64], prm_d)
    lmu = const("lmu", [128, 6], lmu_d)
    w2f = const("w2f", [64, 512], w2_d)
    a2f = const("a2f", [64, 512], a2_d)
    g2f = const("g2f", [128, 512], g2_d)
    lnw = const("lnw", [128, 4, 64], lnw_d)
    lnb = const("lnb", [128, 4, 64], lnb_d)
    m16x2 = const("m16x2", [128, 2, 128], m2_d[0])
    ident2 = const("ident2", [128, 2, 128], m2_d[1])
    mAA = const("mAA", [128, 2, 128], m2_d[2])
    mUT2 = const("mUT2", [128, 2, 128], m2_d[3])
    o32 = const("o32", [128, 128], mk_d[0])
    o64 = const("o64", [128, 128], mk_d[1])
    maskSUT = const("maskSUT", [128, 128], mk_d[2])
    ident = const("ident", [128, 128], mk_d[3])
    masks = {"m16x2": m16x2, "o32": o32, "o64": o64}
    w2b = p.sbuf("w2b", [64, 512], BF16)
    a2b = p.sbuf("a2b", [64, 512], BF16)
    g2b = p.sbuf("g2b", [128, 512], BF16)
    identb = p.sbuf("identb", [128, 128], BF16)
    p.op("dve", lambda e: e.tensor_copy(out=w2b[:], in_=w2f[:]), reads=["w2f"], writes=["w2b"])
    p.op("dve", lambda e: e.tensor_copy(out=a2b[:], in_=a2f[:]), reads=["a2f"], writes=["a2b"])
    p.op("dve", lambda e: e.tensor_copy(out=g2b[:], in_=g2f[:]), reads=["g2f"], writes=["g2b"])
    p.op("dve", lambda e: e.tensor_copy(out=identb[:], in_=ident[:]), reads=["ident"], writes=["identb"])
    bd1 = p.sbuf("bd1", [128, 128], BF16)
    p.op("dve", lambda e: e.memset(bd1[:], 0.0), writes=["bd1"])
    p.op("dve", lambda e: e.memset(bd1[0:64, 0:64], 1.0), writes=["bd1"])
    p.op("dve", lambda e: e.memset(bd1[64:128, 64:128], 1.0), writes=["bd1"])
    ones2 = p.sbuf("ones2", [128, 2], BF16)
    p.op("dve", lambda e: e.memset(ones2[:], 1.0), writes=["ones2"])
    onesf = p.sbuf("onesf", [128, C], F32)
    p.op("dve", lambda e: e.memset(onesf[:], 1.0), writes=["onesf"])
    eps24 = p.sbuf("eps24", [128, 1], F32)
    p.op("dve", lambda e: e.memset(eps24[:], 1e-24), writes=["eps24"])
    epsln = p.sbuf("epsln", [128, 1], F32)
    p.op("dve", lambda e: e.memset(epsln[:], 64e-5), writes=["epsln"])
    PCOL = lambda name, pr: prm[:, PRM_COLS[name] + pr:PRM_COLS[name] + pr + 1]

    pq = PsQ(p, 4, "pq")
    pg = PsQ(p, 3, "pg")
    psb = p.psum("psb16", [128, 1024], BF16)

    rin = p.sbuf("rin", [128, 12, TTR + 1], F32)
    lwin = p.sbuf("lwin", [64, TTR + 1], F32)
    lain = p.sbuf("lain", [64, TTR + 1], F32)
    lgin = p.sbuf("lgin", [128, TTR + 1], F32)
    twb = p.sbuf("twb", [64, TTR], BF16)
    lab = p.sbuf("lab", [64, TTR], BF16)
    sgb = p.sbuf("sgb", [128, TTR], BF16)
    sqb = p.sbuf("sqb", [128, TTR], BF16)
    FT = {n: p.sbuf("ft_" + n, [128, TTR], F32) for n in
          ("rs", "ks", "vs", "logd", "av", "G", "kk", "rn", "kf", "bv", "eG", "eGm", "enG", "lw", "la", "lg")}
    ML = {}
    for nm in ("Rm", "KKm", "Km", "Bm", "Vm", "Pm"):
        ML[nm] = [p.sbuf(f"{nm}{pr}", [128, NCI, 2, C], BF16) for pr in range(NP_)]
        for pr in range(NP_):
            p.op("pool", lambda e: e.memset(ML[nm][pr][:], 0.0), writes=[(nm, pr)])
    eGC = [p.sbuf(f"eGC{pr}", [128, NCI], F32) for pr in range(NP_)]
    NPC = NP_ * NCI
    AakT = [p.sbuf(f"AakT{i}", [128, 128], BF16) for i in range(NPC)]
    Arx = [p.sbuf(f"Arx{i}", [128, 2, 128], BF16) for i in range(NPC)]
    XTb = [p.sbuf(f"XTb{i}", [128, 128], BF16) for i in range(NPC)]
    KBt = [p.sbuf(f"KBt{i}", [128, 2, 128], BF16) for i in range(NPC)]
    Vc = [p.sbuf(f"Vc{i}", [128, C], BF16) for i in range(NPC)]
    bon = [p.sbuf(f"bon{i}", [128, 2], F32) for i in range(NPC)]
    gtm = [p.sbuf(f"gtm{i}", [128, C], F32) for i in range(NPC)]
    NTR = 8
    AA = [p.sbuf(f"AA{i}", [128, 2, 128], F32) for i in range(NTR)]
    ivb = [InvBufs(p, f"iv{i}", INV_DT) for i in range(8)]
    Zf = [p.sbuf(f"Zf{pr}", [128, C], F32) for pr in range(NP_)]
    Zb = [p.sbuf(f"Zb{pr}", [128, C], BF16) for pr in range(NP_)]
    for pr in range(NP_):
        p.op("dve", lambda e: e.memset(Zf[pr][:], 0.0), writes=[("Zf", pr)])
        p.op("pool", lambda e: e.memset(Zb[pr][:], 0.0), writes=[("Zb", pr)])
    r1 = [p.sbuf(f"r1_{pr}", [128, C], BF16) for pr in range(NP_)]
    Us = [p.sbuf(f"Us{pr}", [128, C], BF16) for pr in range(NP_)]
    Ys = [p.sbuf(f"Ys{pr}", [128, C], F32) for pr in range(NP_)]
    yn = [p.sbuf(f"yn{pr}", [128, C], F32) for pr in range(NP_)]
    bst = [p.sbuf(f"bst{pr}", [128, 6], F32) for pr in range(NP_)]
    mv = [p.sbuf(f"mv{pr}", [128, 2], F32) for pr in range(NP_)]
    ostg = [p.sbuf(f"ostg{pr}", [128, NCI, C], F32) for pr in range(NP_)]

    f2 = lambda t: t[:].rearrange("q a c -> q (a c)")
    rkvTv = rkvT.rearrange("(kc q) t -> q kc t", q=128)
    tctr = 0
    for tl in range(SEQ // TTR):
        t0 = tl * TTR
        for k4 in range(3):
            p.dma("sp", rin[:, k4 * 4:k4 * 4 + 4, :], rkvTv[:, k4 * 4:k4 * 4 + 4, t0:t0 + TTR + 1],
                  writes=[("rin", kc) for kc in range(k4 * 4, k4 * 4 + 4)])
        p.dma("sp", lwin[:], lwT[:, t0:t0 + TTR + 1], writes=["lwin"])
        p.dma("sp", lain[:], laT[:, t0:t0 + TTR + 1], writes=["lain"])
        p.dma("sp", lgin[:], lgT[:, t0:t0 + TTR + 1], writes=["lgin"])

        def lerp(out_ap, okey, src0, src1, skey, mu_ap, om_ap):
            p.op("pool", lambda e: e.tensor_scalar(out=out_ap, in0=src0, scalar1=mu_ap, scalar2=None, op0=ALU.mult),
                 reads=[skey, "prm", "lmu"], writes=[okey])
            p.op("dve", lambda e: e.scalar_tensor_tensor(out=out_ap, in0=src1, scalar=om_ap, in1=out_ap, op0=ALU.mult, op1=ALU.add),
                 reads=[skey, okey, "prm", "lmu"], writes=[okey])

        lerp(FT["lw"][0:64, :], "lw", lwin[:, 0:TTR], lwin[:, 1:TTR + 1], "lwin", lmu[0:64, 0:1], lmu[0:64, 1:2])
        lerp(FT["la"][0:64, :], "la", lain[:, 0:TTR], lain[:, 1:TTR + 1], "lain", lmu[0:64, 2:3], lmu[0:64, 3:4])
        lerp(FT["lg"][:, :], "lg", lgin[:, 0:TTR], lgin[:, 1:TTR + 1], "lgin", lmu[:, 4:5], lmu[:, 5:6])
        p.op("act", lambda e: e.activation(out=twb[:], in_=FT["lw"][0:64, :], func=AF.Tanh), reads=["lw"], writes=["twb"])
        p.op("act", lambda e: e.activation(out=lab[:], in_=FT["la"][0:64, :], func=AF.Copy), reads=["la"], writes=["lab"])
        p.op("act", lambda e: e.activation(out=sgb[:], in_=FT["lg"][:, :], func=AF.Sigmoid), reads=["lg"], writes=["sgb"])

        for pr in range(NP_):
            rs, ks, vs, logd, av, G, kk, rn, kf, bv, eG, eGm, enG = (FT[n] for n in
                ("rs", "ks", "vs", "logd", "av", "G", "kk", "rn", "kf", "bv", "eG", "eGm", "enG"))
            lerp(rs[:], "rs", rin[:, pr, 0:TTR], rin[:, pr, 1:TTR + 1], ("rin", pr), PCOL("mu_r", pr), PCOL("om_r", pr))
            lerp(ks[:], "ks", rin[:, 4 + pr, 0:TTR], rin[:, 4 + pr, 1:TTR + 1], ("rin", 4 + pr), PCOL("mu_k", pr), PCOL("om_k", pr))
            lerp(vs[:], "vs", rin[:, 8 + pr, 0:TTR], rin[:, 8 + pr, 1:TTR + 1], ("rin", 8 + pr), PCOL("mu_v", pr), PCOL("om_v", pr))
            bk, k = pg.nextbank()
            p.op("pe", lambda e: e.matmul(bk[:, 0:TTR], lhsT=w2b[:, pr * 128:(pr + 1) * 128], rhs=twb[:], start=True, stop=True),
                 reads=["w2b", "twb"], writes=[k])
            p.op("act", lambda e: e.activation(out=logd[:], in_=bk[:, 0:TTR], func=AF.Sigmoid, bias=PCOL("w0", pr)),
                 reads=[k, "prm"], writes=["logd"])
            p.op("pool", lambda e: e.tensor_scalar(out=logd[:], in0=logd[:], scalar1=-float(np.exp(-0.5)), scalar2=None, op0=ALU.mult),
                 reads=["logd"], writes=["logd"])
            bk, k = pg.nextbank()
            p.op("pe", lambda e: e.matmul(bk[:, 0:TTR], lhsT=a2b[:, pr * 128:(pr + 1) * 128], rhs=lab[:], start=True, stop=True),
                 reads=["a2b", "lab"], writes=[k])
            p.op("act", lambda e: e.activation(out=av[:], in_=bk[:, 0:TTR], func=AF.Sigmoid, bias=PCOL("a0", pr)),
                 reads=[k, "prm"], writes=["av"])
            for ci in range(NCI):
                cs_ = slice(ci * C, (ci + 1) * C)
                p.op("dve", lambda e: e.tensor_tensor_scan(out=G[:, cs_], data0=onesf[:, 0:C], data1=logd[:, cs_], initial=0.0,
                                                           op0=ALU.mult, op1=ALU.add),
                     reads=["logd", "onesf"], writes=["G"])
            p.op("pool", lambda e: e.tensor_scalar(out=kk[:], in0=ks[:], scalar1=PCOL("k_k", pr), scalar2=None, op0=ALU.mult),
                 reads=["ks", "prm"], writes=["kk"])
            p.op("act", lambda e: e.activation(out=sqb[:], in_=kk[:], func=AF.Square), reads=["kk"], writes=["sqb"])
            bk, k = pg.nextbank()
            p.op("pe", lambda e: e.matmul(bk[:, 0:TTR], lhsT=bd1[:], rhs=sqb[:], start=True, stop=True), reads=["bd1", "sqb"], writes=[k])
            p.op("act", lambda e: e.activation(out=rn[:], in_=bk[:, 0:TTR], func=AF.Sqrt, bias=eps24[:, 0:1]), reads=[k, "eps24"], writes=["rn"])
            p.op("dve", lambda e: e.reciprocal(out=rn[:], in_=rn[:]), reads=["rn"], writes=["rn"])
            p.op("dve", lambda e: e.tensor_tensor(out=kk[:], in0=kk[:], in1=rn[:], op=ALU.mult), reads=["kk", "rn"], writes=["kk"])
            p.op("dve", lambda e: e.tensor_scalar(out=kf[:], in0=av[:], scalar1=PCOL("k_a", pr), scalar2=PCOL("omka", pr),
                                                  op0=ALU.mult, op1=ALU.add),
                 reads=["av", "prm"], writes=["kf"])
            p.op("pool", lambda e: e.tensor_tensor(out=ks[:], in0=ks[:], in1=kf[:], op=ALU.mult), reads=["ks", "kf"], writes=["ks"])
            p.op("pool", lambda e: e.tensor_tensor(out=bv[:], in0=kk[:], in1=av[:], op=ALU.mult), reads=["kk", "av"], writes=["bv"])
            p.op("act", lambda e: e.activation(out=eG[:], in_=G[:], func=AF.Exp), reads=["G"], writes=["eG"])
            p.op("act", lambda e: e.activation(out=enG[:], in_=G[:], func=AF.Exp, scale=-1.0), reads=["G"], writes=["enG"])
            p.op("pool", lambda e: e.tensor_tensor(out=eGm[:], in0=G[:], in1=logd[:], op=ALU.subtract), reads=["G", "logd"], writes=["eGm"])
            p.op("act", lambda e: e.activation(out=eGm[:], in_=eGm[:], func=AF.Exp), reads=["eGm"], writes=["eGm"])
            p.op("dve", lambda e: e.tensor_copy(out=eGC[pr][:], in_=eG[:, C - 1:TTR:C]), reads=["eG"], writes=[("eGC", pr)])

            def masked(nm, fn_half):
                for half in range(2):
                    ps_ = slice(half * 64, (half + 1) * 64)
                    out_ap = ML[nm][pr][ps_, :, half, :]
                    fn_half("dve" if half == 0 else "pool", out_ap, ps_)
            v3 = lambda t, ps_: t[ps_, :].rearrange("q (a c) -> q a c", c=C)
            masked("Rm", lambda eng, o, ps_: p.op(eng, lambda e: e.tensor_tensor(out=o, in0=v3(rs, ps_), in1=v3(eG, ps_), op=ALU.mult),
                                                  reads=["rs", "eG"], writes=[("Rm", pr)]))
            masked("KKm", lambda eng, o, ps_: p.op(eng, lambda e: e.tensor_tensor(out=o, in0=v3(kk, ps_), in1=v3(eGm, ps_), op=ALU.mult),
                                                   reads=["kk", "eGm"], writes=[("KKm", pr)]))
            masked("Km", lambda eng, o, ps_: p.op(eng, lambda e: e.tensor_tensor(out=o, in0=v3(ks, ps_), in1=v3(enG, ps_), op=ALU.mult),
                                                  reads=["ks", "enG"], writes=[("Km", pr)]))
            masked("Bm", lambda eng, o, ps_: p.op(eng, lambda e: e.tensor_tensor(out=o, in0=v3(bv, ps_), in1=v3(enG, ps_), op=ALU.mult),
                                                  reads=["bv", "enG"], writes=[("Bm", pr)]))
            masked("Vm", lambda eng, o, ps_: p.op(eng, lambda e: e.tensor_copy(out=o, in_=v3(vs, ps_)),
                                                  reads=["vs"], writes=[("Vm", pr)]))
            p.op("dve", lambda e: e.scalar_tensor_tensor(out=kf[:], in0=rs[:], scalar=PCOL("r_k", pr), in1=ks[:], op0=ALU.mult, op1=ALU.mult),
                 reads=["rs", "ks", "prm", "kf"], writes=["kf"])
            masked("Pm", lambda eng, o, ps_: p.op(eng, lambda e: e.tensor_copy(out=o, in_=v3(kf, ps_)),
                                                  reads=["kf"], writes=[("Pm", pr)]))

        pcs = [(ci, pr) for ci in range(NCI) for pr in range(NP_)]
        for g0 in range(0, len(pcs), 8):
            grp = pcs[g0:g0 + 8]
            chains = []
            for gi, (ci, pr) in enumerate(grp):
                pc = ci * NP_ + pr
                r = tctr % NTR
                tctr += 1
                cs_ = slice(ci * C, (ci + 1) * C)
                m = lambda nm: ML[nm][pr][:, ci, :, :].rearrange("q h t -> q (h t)")
                bk, k = pg.nextbank()
                p.op("pe", lambda e: e.matmul(bk[:, 0:128], lhsT=m("KKm"), rhs=m("Bm"), start=True, stop=True),
                     reads=[("KKm", pr), ("Bm", pr)], writes=[k], nosync_same=True)
                p.op("pe", lambda e: e.matmul(bk[:, 128:256], lhsT=m("Bm"), rhs=m("KKm"), start=True, stop=True),
                     reads=[("KKm", pr), ("Bm", pr)], writes=[k], nosync_same=True)
                p.op("dve", lambda e: e.tensor_tensor(out=f2(AA[r]), in0=bk[:, 0:256], in1=f2(mAA), op=ALU.mult),
                     reads=[k, "mAA"], writes=[("AA", r)])
                bk, k = pg.nextbank()
                p.op("pe", lambda e: e.matmul(bk[:, 0:128], lhsT=m("Km"), rhs=m("KKm"), start=True, stop=True),
                     reads=[("KKm", pr), ("Km", pr)], writes=[k], nosync_same=True)
                p.op("pe", lambda e: e.matmul(bk[:, 128:256], lhsT=m("Km"), rhs=m("Rm"), start=True, stop=True),
                     reads=[("Rm", pr), ("Km", pr)], writes=[k], nosync_same=True)
                p.op("pe", lambda e: e.matmul(bk[:, 256:384], lhsT=m("Bm"), rhs=m("Rm"), start=True, stop=True),
                     reads=[("Rm", pr), ("Bm", pr)], writes=[k], nosync_same=True)
                p.op("dve", lambda e: e.tensor_tensor(out=AakT[pc][:], in0=bk[:, 0:128], in1=maskSUT[:], op=ALU.mult),
                     reads=[k, "maskSUT"], writes=[("AakT", pc)])
                p.op("dve", lambda e: e.tensor_tensor(out=f2(Arx[pc]), in0=bk[:, 128:384], in1=f2(mUT2), op=ALU.mult),
                     reads=[k, "mUT2"], writes=[("Arx", pc)])
                for j, nm in enumerate(("Vm", "Km", "Bm")):
                    p.op("pe", lambda e: e.transpose(out=psb[:, j * 128:(j + 1) * 128], in_=m(nm), identity=identb[:]),
                         reads=[(nm, pr), "identb"], writes=["psb"], nosync_same=True)
                for half in range(2):
                    ps_ = slice(half * 64, (half + 1) * 64)
                    p.op("act", lambda e: e.activation(out=Vc[pc][ps_, :], in_=psb[ps_, half * 64:(half + 1) * 64], func=AF.Copy),
                         reads=["psb"], writes=[("Vc", pc)])
                p.op("act", lambda e: e.activation(out=f2(KBt[pc]), in_=psb[:, 128:384], func=AF.Copy), reads=["psb"], writes=[("KBt", pc)])
                bk, k = pg.nextbank()
                for half in range(2):
                    hcol = (2 * pr + half) * 64
                    p.op("pe", lambda e: e.matmul(bk[half * 64:(half + 1) * 64, 0:64], lhsT=sgb[:, cs_], rhs=g2b[:, hcol:hcol + 64],
                                                  start=True, stop=True),
                         reads=["sgb", "g2b"], writes=[k], nosync_same=True)
                p.op("pe", lambda e: e.matmul(bk[:, 64:66], lhsT=m("Pm"), rhs=ones2[:], start=True, stop=True),
                     reads=[("Pm", pr), "ones2"], writes=[k], nosync_same=True)
                p.op("act", lambda e: e.activation(out=gtm[pc][:], in_=bk[:, 0:64], func=AF.Copy), reads=[k], writes=[("gtm", pc)])
                p.op("act", lambda e: e.activation(out=bon[pc][:], in_=bk[:, 64:66], func=AF.Copy), reads=[k], writes=[("bon", pc)])
                chains.append((ivb[gi], AA[r], ("AA", r), pc))
            emit_inverse(p, pq, [(b_, aa_, ak_) for (b_, aa_, ak_, _pc) in chains], masks, ident2)
            for (b_, aa_, ak_, pc) in chains:
                p.op("pool", lambda e: e.tensor_copy(out=XTb[pc][:], in_=b_.XX[:, 1, :]), reads=[b_.k("XX")], writes=[("XTb", pc)])

        for ci in range(NCI):
            PCI = lambda pr: ci * NP_ + pr
            m = lambda nm, pr: ML[nm][pr][:, ci, :, :].rearrange("q h t -> q (h t)")
            for s0 in range(0, NP_, pg.n):
                st = {}
                for pr in range(s0, min(NP_, s0 + pg.n)):
                    pc = PCI(pr)
                    bk, k = pg.nextbank()
                    p.op("pe", lambda e: e.matmul(bk[:, 0:C], lhsT=AakT[pc][:], rhs=Vc[pc][:], start=True, stop=False),
                         reads=[("AakT", pc), ("Vc", pc)], writes=[k], nosync_same=True)
                    p.op("pe", lambda e: e.matmul(bk[:, 0:C], lhsT=m("KKm", pr), rhs=Zb[pr][:], start=False, stop=True),
                         reads=[("KKm", pr), ("Zb", pr)], writes=[k], nosync_same=True)
                    st[pr] = (bk, k)
                for pr in st:
                    bk, k = st[pr]
                    p.op("act", lambda e: e.activation(out=r1[pr][:], in_=bk[:, 0:C], func=AF.Copy), reads=[k], writes=[("r1", pr)])
            for s0 in range(0, NP_, pg.n):
                st = {}
                for pr in range(s0, min(NP_, s0 + pg.n)):
                    pc = PCI(pr)
                    bk, k = pg.nextbank()
                    p.op("pe", lambda e: e.matmul(bk[:, 0:C], lhsT=XTb[pc][:], rhs=r1[pr][:], start=True, stop=True),
                         reads=[("XTb", pc), ("r1", pr)], writes=[k], nosync_same=True)
                    st[pr] = (bk, k)
                for pr in st:
                    bk, k = st[pr]
                    p.op("act", lambda e: e.activation(out=Us[pr][:], in_=bk[:, 0:C], func=AF.Copy, scale=-1.0), reads=[k], writes=[("Us", pr)])
            for pr in range(NP_):
                pc = PCI(pr)
                bky, ky = pg.nextbank()
                p.op("pe", lambda e: e.matmul(bky[:, 0:C], lhsT=m("Rm", pr), rhs=Zb[pr][:], start=True, stop=False),
                     reads=[("Rm", pr), ("Zb", pr)], writes=[ky], nosync_same=True)
                p.op("pe", lambda e: e.matmul(bky[:, 0:C], lhsT=Arx[pc][:, 0, :], rhs=Vc[pc][:], start=False, stop=False),
                     reads=[("Arx", pc), ("Vc", pc)], writes=[ky], nosync_same=True)
                p.op("pe", lambda e: e.matmul(bky[:, 0:C], lhsT=Arx[pc][:, 1, :], rhs=Us[pr][:], start=False, stop=True),
                     reads=[("Arx", pc), ("Us", pr)], writes=[ky], nosync_same=True)
                bkz, kz = pg.nextbank()
                p.op("pe", lambda e: e.matmul(bkz[:, 0:C], lhsT=KBt[pc][:, 0, :], rhs=Vc[pc][:], start=True, stop=False),
                     reads=[("KBt", pc), ("Vc", pc)], writes=[kz], nosync_same=True)
                p.op("pe", lambda e: e.matmul(bkz[:, 0:C], lhsT=KBt[pc][:, 1, :], rhs=Us[pr][:], start=False, stop=True),
                     reads=[("KBt", pc), ("Us", pr)], writes=[kz], nosync_same=True)
                p.op("act", lambda e: e.activation(out=Ys[pr][:], in_=bky[:, 0:C], func=AF.Copy), reads=[ky], writes=[("Ys", pr)])
                p.op("pool", lambda e: e.tensor_scalar(out=Zf[pr][:], in0=Zf[pr][:], scalar1=eGC[pr][:, ci:ci + 1], scalar2=None, op0=ALU.mult),
                     reads=[("Zf", pr), ("eGC", pr)], writes=[("Zf", pr)])
                p.op("dve", lambda e: e.scalar_tensor_tensor(out=Zf[pr][:], in0=bkz[:, 0:C], scalar=eGC[pr][:, ci:ci + 1], in1=Zf[pr][:],
                                                             op0=ALU.mult, op1=ALU.add),
                     reads=[kz, ("eGC", pr), ("Zf", pr)], writes=[("Zf", pr)])
                p.op("act", lambda e: e.activation(out=Zb[pr][:], in_=Zf[pr][:], func=AF.Copy), reads=[("Zf", pr)], writes=[("Zb", pr)])
            for pr in range(NP_):
                pc = PCI(pr)
                p.op("dve", lambda e: e.bn_stats(out=bst[pr][:], in_=Ys[pr][:]), reads=[("Ys", pr)], writes=[("bst", pr)])
                p.op("dve", lambda e: e.bn_aggr(out=mv[pr][:], in_=bst[pr][:]), reads=[("bst", pr)], writes=[("mv", pr)])
                p.op("act", lambda e: e.activation(out=mv[pr][:, 1:2], in_=mv[pr][:, 1:2], func=AF.Sqrt, bias=epsln[:, 0:1]),
                     reads=[("mv", pr), "epsln"], writes=[("mv", pr)])
                p.op("dve", lambda e: e.reciprocal(out=mv[pr][:, 1:2], in_=mv[pr][:, 1:2]), reads=[("mv", pr)], writes=[("mv", pr)])
                p.op("dve", lambda e: e.tensor_scalar(out=yn[pr][:], in0=Ys[pr][:], scalar1=mv[pr][:, 0:1], scalar2=mv[pr][:, 1:2],
                                                      op0=ALU.subtract, op1=ALU.mult),
                     reads=[("Ys", pr), ("mv", pr)], writes=[("yn", pr)])
                p.op("pool", lambda e: e.tensor_tensor(out=yn[pr][:], in0=yn[pr][:], in1=lnw[:, pr, :], op=ALU.mult),
                     reads=[("yn", pr), "lnw"], writes=[("yn", pr)])
                p.op("pool", lambda e: e.tensor_tensor(out=yn[pr][:], in0=yn[pr][:], in1=lnb[:, pr, :], op=ALU.add),
                     reads=[("yn", pr), "lnb"], writes=[("yn", pr)])
                p.op("dve", lambda e: e.scalar_tensor_tensor(out=yn[pr][:], in0=Vc[pc][:], scalar=bon[pc][:, 0:1], in1=yn[pr][:],
                                                             op0=ALU.mult, op1=ALU.add),
                     reads=[("Vc", pc), ("bon", pc), ("yn", pr)], writes=[("yn", pr)])
                p.op("pool", lambda e: e.tensor_tensor(out=ostg[pr][:, ci, :], in0=yn[pr][:], in1=gtm[pc][:], op=ALU.mult),
                     reads=[("yn", pr), ("gtm", pc)], writes=[("ostg", pr)])
        for pr in range(NP_):
            for half in range(2):
                col0 = (2 * pr + half) * 64
                p.dma("pool", ya[t0:t0 + TTR, col0:col0 + 64].rearrange("(ci t) v -> t ci v", t=C),
                      ostg[pr][half * 64:(half + 1) * 64, :, :], reads=[("ostg", pr)])
    if standalone:
        p.finish()
    return p


def rwkv_consts():
    m16x2, o32, o64 = inv_masks_np()
    I = np.eye(128, dtype=np.float32)
    i = np.arange(128)
    same64 = (i[:, None] // 64) == (i[None, :] // 64)
    maskSL = (same64 & (i[:, None] > i[None, :])).astype(np.float32)
    maskSUT = (same64 & (i[None, :] > i[:, None])).astype(np.float32)
    maskUT = (same64 & (i[None, :] >= i[:, None])).astype(np.float32)
    st = lambda a, b: np.stack([a, b], axis=1)
    return {"m2": np.ascontiguousarray(np.stack([m16x2, st(I, I), st(maskSL, maskSUT), st(maskUT, maskUT)])),
            "mk": np.ascontiguousarray(np.stack([o32, o64, maskSUT, I]))}


def run_rwkv(p0T_b, prm):
    p = _prog("rwkv", build_rwkv)
    cst = rwkv_consts()
    mu = prm["rwkv_mu"]
    in_maps = []
    for core in range(8):
        b, hh = core // 2, core % 2
        P = p0T_b[b]
        my = slice(hh * 512, (hh + 1) * 512)
        z1 = lambda a: np.ascontiguousarray(np.concatenate([np.zeros((a.shape[0], 1), np.float32), a], axis=1))
        rkv = np.concatenate([P[0:1024][my], P[1024:2048][my], P[2048:3072][my]], axis=0)
        cols = np.zeros((128, 64), np.float32)

        def put(name, vec512):
            cols[:, PRM_COLS[name]:PRM_COLS[name] + 4] = vec512.reshape(4, 128).T
        put("mu_r", mu[0:1024][my]); put("mu_k", mu[1024:2048][my]); put("mu_v", mu[2048:3072][my])
        put("om_r", 1 - mu[0:1024][my]); put("om_k", 1 - mu[1024:2048][my]); put("om_v", 1 - mu[2048:3072][my])
        put("w0", prm["rwkv_w0"][my]); put("a0", prm["rwkv_a0"][my]); put("k_k", prm["rwkv_k_k"][my])
        put("k_a", prm["rwkv_k_a"][my]); put("omka", 1 - prm["rwkv_k_a"][my]); put("r_k", prm["rwkv_r_k"].reshape(-1)[my])
        lmu = np.zeros((128, 6), np.float32)
        lmu[0:64, 0] = mu[3072:3136]; lmu[0:64, 1] = 1 - mu[3072:3136]
        lmu[0:64, 2] = mu[3136:3200]; lmu[0:64, 3] = 1 - mu[3136:3200]
        lmu[:, 4] = mu[3200:3328]; lmu[:, 5] = 1 - mu[3200:3328]

        def rows_tm(v512):
            a = v512.reshape(4, 2, 64)
            return np.ascontiguousarray(np.repeat(a.transpose(1, 0, 2)[:, None, :, :], 64, axis=1).reshape(128, 4, 64))
        m = {"rkvT": z1(rkv), "lwT": z1(P[3072:3136]), "laT": z1(P[3136:3200]), "lgT": z1(P[3200:3328]),
             "prm": cols, "lmu": lmu,
             "w2": np.ascontiguousarray(prm["rwkv_w2"][:, my]), "a2": np.ascontiguousarray(prm["rwkv_a2"][:, my]),
             "g2": np.ascontiguousarray(prm["rwkv_g2"][:, my]),
             "lnw": rows_tm(prm["rwkv_ln_w"][my]), "lnb": rows_tm(prm["rwkv_ln_b"][my])}
        m.update(cst)
        in_maps.append(m)
    r = _run(p, in_maps)
    return [np.asarray(r[i]["ya"]) for i in range(8)]


def kernel_unfused(x, c, ada_mix_w, ada_mix_b, ada_ffn_w, ada_ffn_b, hg_w_in, hg_w_out,
           rwkv_mu, rwkv_w0, rwkv_w2, rwkv_a0, rwkv_a2, rwkv_g2, rwkv_k_k, rwkv_k_a,
           rwkv_r_k, rwkv_ln_w, rwkv_ln_b, gdn_conv_w, gdn_a_log, gdn_dt_bias, gdn_norm_w,
           ssm_w_in, ssm_conv_w, ssm_conv_b, ssm_dt_bias, ssm_a_log, ssm_d, ssm_norm_w,
           ssm_w_out, ffn_w1, ffn_w3, ffn_w2, final_norm_w):
    f32 = lambda a: np.ascontiguousarray(np.asarray(a, dtype=np.float32))
    x, c = f32(x), f32(c)
    cw = cast_weights({"ada_mix_w": f32(ada_mix_w), "ada_ffn_w": f32(ada_ffn_w), "hg_w_in": f32(hg_w_in),
                       "hg_w_out": f32(hg_w_out), "ssm_w_in": f32(ssm_w_in), "ssm_w_out": f32(ssm_w_out),
                       "ffn_w1": f32(ffn_w1), "ffn_w3": f32(ffn_w3), "ffn_w2": f32(ffn_w2)})
    ada_mix_b, ada_ffn_b = f32(ada_mix_b), f32(ada_ffn_b)
    xT = [np.ascontiguousarray(x[i // 2, (i % 2) * TOK:(i % 2 + 1) * TOK].T) for i in range(8)]
    pT = run_inproj(xT, c, cw["ada_mix_w"][0], ada_mix_b[0], pad_cols(cw["hg_w_in"][0], 7552))
    p0T = [np.concatenate([pT[2 * b], pT[2 * b + 1]], axis=1) for b in range(4)]
    del pT
    prm_r = {"rwkv_mu": f32(rwkv_mu)[0], "rwkv_w0": f32(rwkv_w0)[0], "rwkv_w2": f32(rwkv_w2)[0], "rwkv_a0": f32(rwkv_a0)[0],
             "rwkv_a2": f32(rwkv_a2)[0], "rwkv_g2": f32(rwkv_g2)[0], "rwkv_k_k": f32(rwkv_k_k)[0], "rwkv_k_a": f32(rwkv_k_a)[0],
             "rwkv_r_k": f32(rwkv_r_k)[0], "rwkv_ln_w": f32(rwkv_ln_w)[0], "rwkv_ln_b": f32(rwkv_ln_b)[0]}
    ya = run_rwkv(p0T, prm_r)
    prm_g = {"gdn_conv_w": f32(gdn_conv_w)[0], "gdn_a_log": f32(gdn_a_log)[0], "gdn_dt_bias": f32(gdn_dt_bias)[0],
             "gdn_norm_w": f32(gdn_norm_w)[0]}
    yb = run_gdn(p0T, prm_g)
    del p0T
    yT = []
    for i in range(8):
        b, s = i // 2, i % 2
        ts = slice(s * TOK, (s + 1) * TOK)
        ym = np.concatenate([ya[2 * b][ts], ya[2 * b + 1][ts], yb[2 * b][ts], yb[2 * b + 1][ts]], axis=1)
        yT.append(np.ascontiguousarray(ym.T))
    del ya, yb
    x1T = run_outffn(xT, yT, c, cw["ada_mix_w"][0], ada_mix_b[0], cw["ada_ffn_w"][0], ada_ffn_b[0],
                     cw["hg_w_out"][0], cw["ffn_w1"][0], cw["ffn_w3"][0], cw["ffn_w2"][0])
    del xT, yT
    pT = run_inproj(x1T, c, cw["ada_mix_w"][1], ada_mix_b[1], pad_cols(cw["ssm_w_in"][0], 10368))
    p1T = [np.concatenate([pT[2 * b], pT[2 * b + 1]], axis=1) for b in range(4)]
    del pT
    prm_s = {"ssm_conv_w": f32(ssm_conv_w)[0], "ssm_conv_b": f32(ssm_conv_b)[0], "ssm_dt_bias": f32(ssm_dt_bias)[0],
             "ssm_a_log": f32(ssm_a_log)[0], "ssm_d": f32(ssm_d)[0], "ssm_norm_w": f32(ssm_norm_w)[0]}
    ys = run_ssd(p1T, prm_s)
    del p1T
    yT = []
    for i in range(8):
        b, s = i // 2, i % 2
        ts = slice(s * TOK, (s + 1) * TOK)
        yT.append(np.ascontiguousarray(np.concatenate([ys[2 * b][:, ts], ys[2 * b + 1][:, ts]], axis=0)))
    del ys
    oT = run_outffn(x1T, yT, c, cw["ada_mix_w"][1], ada_mix_b[1], cw["ada_ffn_w"][1], ada_ffn_b[1],
                    cw["ssm_w_out"][0], cw["ffn_w1"][1], cw["ffn_w3"][1], cw["ffn_w2"][1], fnw=f32(final_norm_w))
    out = np.empty((4, SEQ, D), np.float32)
    for i in range(8):
        out[i // 2, (i % 2) * TOK:(i % 2 + 1) * TOK] = oT[i].T
    return out


GSF = 4096
RG = [[0, 1], [2, 3], [4, 5], [6, 7]]


def pack_wg(W):
    K, N = W.shape
    Kc, ncc = K // 128, N // 128
    cpg = GSF // (Kc * 128)
    ng = (ncc + cpg - 1) // cpg
    Wp = np.zeros((K, ng * cpg * 128), dtype=np.float32)
    Wp[:, :N] = W
    A = Wp.reshape(Kc, 128, ng, cpg, 128).transpose(2, 1, 3, 0, 4).reshape(ng, 128, cpg * Kc * 128)
    out = np.zeros((ng, 128, GSF), dtype=np.float32)
    out[:, :, :cpg * Kc * 128] = A
    return out


class WStreamC:
    def __init__(self, p, ws_ap, seq, nst=2, nslot=3):
        self.p, self.ws, self.seq, self.nst, self.nslot = p, ws_ap, list(seq), nst, nslot
        self.st = [p.sbuf(f"wst{i}", [128, GSF], F32) for i in range(nst)]
        self.slots = [p.sbuf(f"wsl{i}", [128, GSF], BF16) for i in range(nslot)]
        self.issued = 0
        self.used = 0

    def _issue(self):
        p = self.p
        while self.issued < len(self.seq) and self.issued < self.used + self.nslot:
            g, u = self.seq[self.issued]
            a, s = self.issued % self.nst, self.issued % self.nslot
            p.dma("sp", self.st[a][:, :u], self.ws[g][:, :u], writes=[("wf", a)])
            if self.issued % 2 == 0:
                p.op("act", lambda e: e.activation(out=self.slots[s][:, :u], in_=self.st[a][:, :u], func=AF.Copy),
                     reads=[("wf", a)], writes=[("w", s)])
            else:
                p.op("pool", lambda e: e.tensor_copy(out=self.slots[s][:, :u], in_=self.st[a][:, :u]),
                     reads=[("wf", a)], writes=[("w", s)])
            self.issued += 1

    def next(self):
        self._issue()
        s = self.used % self.nslot
        self.used += 1
        return self.slots[s], ("w", s)


def emit_mod2(cx, wst, scb, bias, mod, jlist, name):
    p = cx.p
    assert len(jlist) % 2 == 0
    for g in range(len(jlist) // 2):
        slot, wk = wst.next()
        ps, pk = cx.next_ps()
        for jj in range(2):
            for kc in range(16):
                off = (jj * 16 + kc) * 128
                p.op("pe", lambda e: e.matmul(ps[:, jj:jj + 1], lhsT=slot[:, off:off + 128], rhs=scb[:, kc:kc + 1],
                                              start=(kc == 0), stop=(kc == 15)),
                     reads=[wk, "scb"], writes=[pk], nosync_same=True)
        j0 = jlist[g * 2]
        p.op("dve", lambda e: e.tensor_tensor(out=mod[:, j0:j0 + 2], in0=ps[:, 0:2], in1=bias[:, j0:j0 + 2], op=ALU.add),
             reads=[pk, name + "b"], writes=[name])


def build_fused():
    p = Prog()
    nc = p.nc
    NTL = SEQ // TT
    xT = p.dram("xT", [D, SEQ], F32, "ExternalInput")
    cs_d = p.dram("cs", [128, 16], F32, "ExternalInput")
    adab_d = p.dram("adab", [4, 128, 48], F32, "ExternalInput")
    fnw_d = p.dram("fnw", [128, 16], F32, "ExternalInput")
    NG = {"ada": 24, "in0": 16, "in1": 21, "out0": 4, "out1": 8, "w13": 22, "w2": 16}
    order = ["ada_m0", "ada_f0", "ada_m1", "ada_f1", "in0", "out0", "w13_0", "w2_0", "in1", "out1", "w13_1", "w2_1"]
    gbase, o = {}, 0
    for nm in order:
        gbase[nm] = o
        o += NG[nm.split("_")[0] if not nm.startswith("ada") else "ada"]
    ws = p.dram("ws", [o, 128, GSF], F32, "ExternalInput")
    oT = p.dram("oT", [D, SEQ], F32, "ExternalOutput")
    scr = lambda name, shape: nc.dram_tensor(name, list(shape), F32)
    p0s_t, p1s_t = scr("p0s", [31 * 128, SEQ + 4]), scr("p1s", [41 * 128, SEQ + 4])
    ya_t, yb_t, y1_t = scr("ya_s", [SEQ, 512]), scr("yb_s", [SEQ, 512]), scr("y1_s", [2048, SEQ])
    part_t = [[scr(f"part{i}_{tl}", [D, TT]) for tl in range(NTL)] for i in range(4)]
    summ_t = [[scr(f"summ{i}_{tl}", [D, TT]) for tl in range(NTL)] for i in range(4)]
    xs_t = [scr(f"xres{i}", [D, SEQ]) for i in range(3)]
    p0s, p1s = p0s_t.ap(), p1s_t.ap()
    fm = lambda ap: ap.rearrange("(kc q) t -> q kc t", q=128)

    cs = p.sbuf("cs", [128, 16], F32)
    scb = p.sbuf("scb", [128, 16], BF16)
    bias = [p.sbuf(f"adab{i}", [128, 48], F32) for i in range(4)]
    mod = [p.sbuf(f"mod{i}", [128, 48], F32) for i in range(4)]
    ops = [p.sbuf(f"ops{i}", [128, 16], F32) for i in range(4)]
    fw = p.sbuf("fw", [128, 16], F32)
    epsb = p.sbuf("epsb", [128, 1], F32)
    zt = p.sbuf("zt", [128, 4], F32)
    p.dma("pool", cs[:], cs_d, writes=["cs"])
    p.dma("pool", fw[:], fnw_d, writes=["fw"])
    for i in range(4):
        p.dma("pool", bias[i][:], adab_d[i], writes=[f"mod{i}b"])
    p.op("act", lambda e: e.activation(out=scb[:], in_=cs[:], func=AF.Silu), reads=["cs"], writes=["scb"])
    p.op("dve", lambda e: e.memset(epsb[:], EPS), writes=["epsb"])
    p.op("dve", lambda e: e.memset(zt[:], 0.0), writes=["zt"])
    for (t_, nch) in ((p0s, 31), (p1s, 41)):
        for kc in range(nch):
            p.dma("pool", t_[kc * 128:(kc + 1) * 128, 0:4], zt[:], reads=["zt"])

    def wseq(nm, used):
        ng = NG[nm.split("_")[0] if not nm.startswith("ada") else "ada"]
        return [(gbase[nm] + g, used) for g in range(ng)]

    def mk_ctx():
        cx = Ctx(p)
        cx.epsb = epsb
        return cx

    def mods(cx, wst, which, jlist):
        emit_mod2(cx, wst, scb, bias[which], mod[which], jlist, f"mod{which}")
        if jlist[0] == 0:
            p.op("dve", lambda e: e.tensor_scalar(out=ops[which][:], in0=mod[which][:, 16:32], scalar1=1.0, scalar2=None, op0=ALU.add),
                 reads=[f"mod{which}"], writes=[f"ops{which}"])

    def ada_seq(which, jlist):
        nm = ["ada_m0", "ada_f0", "ada_m1", "ada_f1"][which]
        return [(gbase[nm] + j // 2, 4096) for j in jlist[::2]]

    cc_tok = {}
    summ_idx = {}

    def load_x(xs, tl, base_ap, sum_ap, gate_which, store_ap, stg):
        t0 = tl * TT
        for k4 in range(4):
            p.dma("pool", xs[:, k4 * 4:k4 * 4 + 4, :], fm(base_ap)[:, k4 * 4:k4 * 4 + 4, t0:t0 + TT],
                  writes=[("x", kc) for kc in range(k4 * 4, k4 * 4 + 4)])
        if sum_ap is None:
            return
        p._wait("pool", cc_tok[(summ_idx[id(sum_ap)], tl)])
        for k2 in range(8):
            st_ = stg[k2 % 2]
            p.dma("pool", st_[:], fm(sum_ap[tl].ap())[:, k2 * 2:k2 * 2 + 2, :], writes=[("stg", k2 % 2)])
            for kk in range(2):
                kc = k2 * 2 + kk
                p.op("dve", lambda e: e.scalar_tensor_tensor(out=xs[:, kc, :], in0=st_[:, kk, :], scalar=mod[gate_which][:, 32 + kc:33 + kc],
                                                             in1=xs[:, kc, :], op0=ALU.mult, op1=ALU.add),
                     reads=[("stg", k2 % 2), ("x", kc), f"mod{gate_which}"], writes=[("x", kc)])
            if store_ap is not None and k2 % 2 == 1:
                k4 = k2 // 2
                p.dma("pool", fm(store_ap)[:, k4 * 4:k4 * 4 + 4, t0:t0 + TT], xs[:, k4 * 4:k4 * 4 + 4, :],
                      reads=[("x", kc) for kc in range(k4 * 4, k4 * 4 + 4)])

    def store_chunks(dst_fn):
        ost = [p.sbuf(f"ost{i}", [128, TT], F32) for i in range(4)]

        ctr = [0]

        def f(cx, tl):
            def evac(n, ps, pk):
                i_ = ctr[0] % 4
                ctr[0] += 1
                o_, ok = ost[i_], ("ost", i_)
                ev_copy(p, cx.ev_eng(), o_[:], ps[:], [pk], [ok])
                p.dma("pool", dst_fn(tl, n), o_[:], reads=[ok], writes=[("part", tl, n)])
            return evac
        return f

    def part_dst(i):
        return lambda tl, n: part_t[i][tl].ap()[n * 128:(n + 1) * 128, :]

    def allreduce_tile(i, tl):
        cc_tok[(i, tl)] = p.collective("AllReduce", ALU.add, RG, part_t[i][tl], summ_t[i][tl], [("part", tl, n) for n in range(16)], [])

    def phase_inproj(layer, ncc, dst_ap, base_ap, sum_ap, gate_which, store_ap, mod_prev_j):
        mk = p.scope_begin()
        cx = mk_ctx()
        wnm = f"in{layer}"
        mw = 2 * layer
        seq = []
        if mod_prev_j is not None:
            seq += ada_seq(gate_which, list(range(32, 48)))
        seq += ada_seq(mw, list(range(32)))
        ST = min(4, NTL)
        for _ in range(NTL // ST):
            seq += wseq(wnm, 4096)
        wst = WStreamC(p, ws, seq)
        if mod_prev_j is not None:
            mods(cx, wst, gate_which, list(range(32, 48)))
        mods(cx, wst, mw, list(range(32)))
        xs = p.sbuf("xs", [128, 16, TT], F32)
        hs = [p.sbuf(f"hs{j}", [128, 16, TT], BF16) for j in range(ST)]
        tmp = [p.sbuf(f"tmp{i}", [128, TT], F32) for i in range(3)]
        stat = (p.sbuf("rs", [128, TT], F32), p.sbuf("rstd", [128, TT], F32))
        stg = [p.sbuf(f"stg{i}", [128, 2, TT], F32) for i in range(2)]
        mkev = store_chunks(lambda tl, n: dst_ap[n * 128:(n + 1) * 128, 4 + tl * TT:4 + (tl + 1) * TT])
        for sup in range(NTL // ST):
            for j in range(ST):
                load_x(xs, sup * ST + j, base_ap, sum_ap, gate_which, store_ap, stg)
                emit_norm_mod(cx, xs, "x", hs[j], ("h", j), lambda kc: ops[mw][:, kc:kc + 1], lambda kc: mod[mw][:, kc:kc + 1],
                              [f"mod{mw}", f"ops{mw}"], tmp, stat)
            emit_gemm(cx, wst, 16, ncc, lambda j, kc: (hs[j][:, kc, :], (("h", j), kc)),
                      lambda j, n, ps, pk: mkev(cx, sup * ST + j)(n, ps, pk), cpg=2, ST=ST)
        p.scope_end(mk)

    def phase_outproj(layer, pi):
        mk = p.scope_begin()
        cx = mk_ctx()
        Kc = 8 if layer == 0 else 16
        ST = min(2, NTL)
        seq = []
        for _ in range(NTL // ST):
            seq += wseq(f"out{layer}", 4096)
        wst = WStreamC(p, ws, seq)
        bigs = [p.sbuf(f"big{j}", [128, Kc, TT], BF16) for j in range(ST)]
        mkev = store_chunks(part_dst(pi))
        if layer == 0:
            ident = p.sbuf("identf", [128, 128], F32)
            p.dma("pool", ident[:], ident_d, writes=["identf"])
            ytl = [p.sbuf(f"ytl{i}", [128, 4, 512], F32) for i in range(2)]
        else:
            yst = [p.sbuf(f"yst{i}", [128, 2, TT], F32) for i in range(2)]
        for sup in range(NTL // ST):
          for j in range(ST):
            tl = sup * ST + j
            t0 = tl * TT
            big = bigs[j]
            if layer == 0:
                for si, src in enumerate((ya_t.ap(), yb_t.ap())):
                    yt_, yk = ytl[si], ("ytl", si)
                    p.dma("pool", yt_[:], src[t0:t0 + TT, :].rearrange("(a q) c -> q a c", q=128), writes=[yk])
                    for c in range(4):
                        ps, pk = cx.next_ps()
                        for a in range(4):
                            p.op("pe", lambda e: e.transpose(out=ps[:, a * 128:(a + 1) * 128], in_=yt_[:, a, c * 128:(c + 1) * 128],
                                                             identity=ident[:]),
                                 reads=[yk, "identf"], writes=[pk], nosync_same=True)
                        ev_copy(p, cx.ev_eng(), big[:, si * 4 + c, :], ps[:], [pk], [(("big", j), si * 4 + c)])
            else:
                for k2 in range(8):
                    s_ = k2 % 2
                    p.dma("pool", yst[s_][:], fm(y1_t.ap())[:, k2 * 2:k2 * 2 + 2, t0:t0 + TT], writes=[("yst", s_)])
                    ev_copy(p, cx.ev_eng(), big[:, k2 * 2:k2 * 2 + 2, :], yst[s_][:], [("yst", s_)],
                            [(("big", j), k2 * 2), (("big", j), k2 * 2 + 1)])
          emit_gemm(cx, wst, Kc, 16, lambda j, kc: (bigs[j][:, kc, :], (("big", j), kc)),
                    lambda j, n, ps, pk: mkev(cx, sup * ST + j)(n, ps, pk), cpg=GSF // (Kc * 128), ST=ST)
          for j in range(ST):
            allreduce_tile(pi, sup * ST + j)
        p.scope_end(mk)

    def phase_ffn(layer, base_ap, sum_ap, store_ap, pi):
        mk = p.scope_begin()
        cx = mk_ctx()
        gm, mf = 2 * layer, 2 * layer + 1
        seq = ada_seq(gm, list(range(32, 48))) + ada_seq(mf, list(range(32)))
        ST = min(2, NTL)
        for _ in range(NTL // ST):
            seq += wseq(f"w13_{layer}", 4096) + wseq(f"w2_{layer}", 2816)
        wst = WStreamC(p, ws, seq)
        mods(cx, wst, gm, list(range(32, 48)))
        mods(cx, wst, mf, list(range(32)))
        xs = p.sbuf("xs", [128, 16, TT], F32)
        hs = [p.sbuf(f"hs{j}", [128, 16, TT], BF16) for j in range(ST)]
        gb = [p.sbuf(f"gb{j}", [128, 22, TT], BF16) for j in range(ST)]
        sil = [[p.sbuf(f"sil{r}_{j}", [128, TT], F32) for j in range(ST)] for r in range(2)]
        tmp = [p.sbuf(f"tmp{i}", [128, TT], F32) for i in range(3)]
        stat = (p.sbuf("rs", [128, TT], F32), p.sbuf("rstd", [128, TT], F32))
        stg = [p.sbuf(f"stg{i}", [128, 2, TT], F32) for i in range(2)]
        mkev = store_chunks(part_dst(pi))
        for sup in range(NTL // ST):
            for j in range(ST):
                load_x(xs, sup * ST + j, base_ap, sum_ap, gm, store_ap, stg)
                emit_norm_mod(cx, xs, "x", hs[j], ("h", j), lambda kc: ops[mf][:, kc:kc + 1], lambda kc: mod[mf][:, kc:kc + 1],
                              [f"mod{mf}", f"ops{mf}"], tmp, stat)

            def evac13(j, n, ps, pk):
                q, r = n // 4, n % 4
                if r < 2:
                    p.op("act", lambda e: e.activation(out=sil[r][j][:], in_=ps[:], func=AF.Silu), reads=[pk], writes=[("sil", r, j)])
                else:
                    jj = q * 2 + (r - 2)
                    p.op("dve", lambda e: e.tensor_tensor(out=gb[j][:, jj, :], in0=ps[:], in1=sil[r - 2][j][:], op=ALU.mult),
                         reads=[pk, ("sil", r - 2, j)], writes=[(("gb", j), jj)])
            emit_gemm(cx, wst, 16, 44, lambda j, kc: (hs[j][:, kc, :], (("h", j), kc)), evac13, cpg=2, ST=ST)
            emit_gemm(cx, wst, 22, 16, lambda j, kc: (gb[j][:, kc, :], (("gb", j), kc)),
                      lambda j, n, ps, pk: mkev(cx, sup * ST + j)(n, ps, pk), cpg=1, ST=ST)
            for j in range(ST):
                allreduce_tile(pi, sup * ST + j)
        p.scope_end(mk)

    def phase_final(base_ap, sum_ap):
        mk = p.scope_begin()
        cx = mk_ctx()
        wst = WStreamC(p, ws, ada_seq(3, list(range(32, 48))))
        mods(cx, wst, 3, list(range(32, 48)))
        xs = p.sbuf("xs", [128, 16, TT], F32)
        hs = p.sbuf("hs", [128, 16, TT], BF16)
        stat = (p.sbuf("rs", [128, TT], F32), p.sbuf("rstd", [128, TT], F32))
        stg = [p.sbuf(f"stg{i}", [128, 2, TT], F32) for i in range(2)]
        for tl in range(NTL):
            t0 = tl * TT
            load_x(xs, tl, base_ap, sum_ap, 3, None, stg)
            emit_norm_mod(cx, xs, "x", hs, "h", None, None, [], None, stat)
            for kc in range(16):
                p.op("dve", lambda e: e.scalar_tensor_tensor(out=xs[:, kc, :], in0=xs[:, kc, :], scalar=fw[:, kc:kc + 1], in1=stat[1][:],
                                                             op0=ALU.mult, op1=ALU.mult),
                     reads=[("x", kc), "fw", "rstd"], writes=[("x", kc)])
            for k4 in range(4):
                p.dma("pool", fm(oT)[:, k4 * 4:k4 * 4 + 4, t0:t0 + TT], xs[:, k4 * 4:k4 * 4 + 4, :],
                      reads=[("x", kc) for kc in range(k4 * 4, k4 * 4 + 4)])
        p.scope_end(mk)


    ident_d = p.dram("identf_d", [128, 128], F32, "ExternalInput")
    for i_ in range(4):
        summ_idx[id(summ_t[i_])] = i_
    p.barrier()
    phase_inproj(0, 31, p0s, xT, None, 0, None, None)
    mk = p.scope_begin()
    build_rwkv(p, {"rkvT": p0s[0:1536, 3:SEQ + 4], "lwT": p0s[29 * 128:29 * 128 + 64, 3:SEQ + 4],
                   "laT": p0s[29 * 128 + 64:30 * 128, 3:SEQ + 4], "lgT": p0s[30 * 128:31 * 128, 3:SEQ + 4], "ya": ya_t.ap()})
    p.scope_end(mk)
    mk = p.scope_begin()
    build_gdn(p, {"qkvT": p0s[12 * 128:24 * 128, 1:SEQ + 4], "zfm": p0s[24 * 128:28 * 128, 4:SEQ + 4],
                  "bT": p0s[28 * 128:28 * 128 + 4, 4:SEQ + 4], "aT": p0s[28 * 128 + 4:28 * 128 + 8, 4:SEQ + 4], "yb": yb_t.ap()})
    p.scope_end(mk)
    phase_outproj(0, 0)
    phase_ffn(0, xT, summ_t[0], xs_t[0].ap(), 1)
    phase_inproj(1, 41, p1s, xs_t[0].ap(), summ_t[1], 1, xs_t[1].ap(), True)
    mk = p.scope_begin()
    build_ssd(p, {"zT": p1s[0:2048, 4:SEQ + 4], "xT": p1s[2048:4096, 1:SEQ + 4], "bcT": p1s[4096:5120, 1:SEQ + 4],
                  "dtT": p1s[5120:5152, 4:SEQ + 4], "yT": y1_t.ap()})
    p.scope_end(mk)
    phase_outproj(1, 2)
    phase_ffn(1, xs_t[1].ap(), summ_t[2], xs_t[2].ap(), 3)
    phase_final(xs_t[2].ap(), summ_t[3])
    p.finish()
    return p


def _ssd_params(prm, hh):
    cw, cb = prm["ssm_conv_w"], prm["ssm_conv_b"]
    cix = np.arange(hh * 2048, (hh + 1) * 2048)
    cibc = np.concatenate([4096 + hh * 512 + np.arange(512), 5120 + hh * 512 + np.arange(512)])
    hs = slice(hh * 32, (hh + 1) * 32)
    selh = np.zeros((32, 32, 128), np.float32)
    for h in range(32):
        selh[h, h, :] = 1.0
    return {"cwx": np.ascontiguousarray(cw[:, cix].T.reshape(16, 128, 4).transpose(1, 0, 2)), "cbx": fm(cb[cix]),
            "cwbc": np.ascontiguousarray(cw[:, cibc].T.reshape(8, 128, 4).transpose(1, 0, 2)), "cbbc": fm(cb[cibc]),
            "hp": np.ascontiguousarray(np.stack([prm["ssm_dt_bias"][hs], prm["ssm_a_log"][hs]], axis=1)),
            "dsk": fm(np.repeat(prm["ssm_d"][hs], 64)), "nw": fm(prm["ssm_norm_w"][hh * 2048:(hh + 1) * 2048]),
            "ident": np.eye(128, dtype=np.float32), "selh": selh, "maskT": np.triu(np.ones((128, 128), np.float32))}


def _gdn_params(prm, hh):
    cidx = np.concatenate([w * 1024 + hh * 512 + np.arange(512) for w in range(3)])
    cw = prm["gdn_conv_w"][:, cidx]
    m = {"cw": np.ascontiguousarray(cw.T.reshape(12, 128, 4).transpose(1, 0, 2)),
         "hp": np.ascontiguousarray(np.stack([prm["gdn_a_log"][hh * 4:hh * 4 + 4], prm["gdn_dt_bias"][hh * 4:hh * 4 + 4]], axis=1)),
         "nwrow": np.ascontiguousarray(np.broadcast_to(prm["gdn_norm_w"][None, :], (128, 128)))}
    m.update(gdn_consts())
    return m


def _rwkv_params(prm, hh):
    mu = prm["rwkv_mu"]
    my = slice(hh * 512, (hh + 1) * 512)
    cols = np.zeros((128, 64), np.float32)

    def put(name, vec512):
        cols[:, PRM_COLS[name]:PRM_COLS[name] + 4] = vec512.reshape(4, 128).T
    put("mu_r", mu[0:1024][my]); put("mu_k", mu[1024:2048][my]); put("mu_v", mu[2048:3072][my])
    put("om_r", 1 - mu[0:1024][my]); put("om_k", 1 - mu[1024:2048][my]); put("om_v", 1 - mu[2048:3072][my])
    put("w0", prm["rwkv_w0"][my]); put("a0", prm["rwkv_a0"][my]); put("k_k", prm["rwkv_k_k"][my])
    put("k_a", prm["rwkv_k_a"][my]); put("omka", 1 - prm["rwkv_k_a"][my]); put("r_k", prm["rwkv_r_k"].reshape(-1)[my])
    lmu = np.zeros((128, 6), np.float32)
    lmu[0:64, 0] = mu[3072:3136]; lmu[0:64, 1] = 1 - mu[3072:3136]
    lmu[0:64, 2] = mu[3136:3200]; lmu[0:64, 3] = 1 - mu[3136:3200]
    lmu[:, 4] = mu[3200:3328]; lmu[:, 5] = 1 - mu[3200:3328]

    def rows_tm(v512):
        a = v512.reshape(4, 2, 64)
        return np.ascontiguousarray(np.repeat(a.transpose(1, 0, 2)[:, None, :, :], 64, axis=1).reshape(128, 4, 64))
    m = {"prm": cols, "lmu": lmu, "w2": np.ascontiguousarray(prm["rwkv_w2"][:, my]), "a2": np.ascontiguousarray(prm["rwkv_a2"][:, my]),
         "g2": np.ascontiguousarray(prm["rwkv_g2"][:, my]), "lnw": rows_tm(prm["rwkv_ln_w"][my]), "lnb": rows_tm(prm["rwkv_ln_b"][my])}
    m.update(rwkv_consts())
    return m


def fused_inmaps(I, cores=range(8)):
    f32 = lambda a: np.ascontiguousarray(np.asarray(a, dtype=np.float32))
    I = {k: f32(v) for k, v in I.items()}
    G0 = 3328
    ada = [pack_wg(I["ada_mix_w"][0]), pack_wg(I["ada_ffn_w"][0]), pack_wg(I["ada_mix_w"][1]), pack_wg(I["ada_ffn_w"][1])]
    prm_r = {k: I[k][0] for k in ("rwkv_mu", "rwkv_w0", "rwkv_w2", "rwkv_a0", "rwkv_a2", "rwkv_g2", "rwkv_k_k", "rwkv_k_a",
                                  "rwkv_r_k", "rwkv_ln_w", "rwkv_ln_b")}
    prm_g = {k: I[k][0] for k in ("gdn_conv_w", "gdn_a_log", "gdn_dt_bias", "gdn_norm_w")}
    prm_s = {k: I[k][0] for k in ("ssm_conv_w", "ssm_conv_b", "ssm_dt_bias", "ssm_a_log", "ssm_d", "ssm_norm_w")}
    per_hh = {}
    for hh in sorted({c_ % 2 for c_ in cores}):
        W = I["hg_w_in"][0]
        my = lambda base: W[:, base + hh * 512: base + (hh + 1) * 512]
        ba = np.zeros((2048, 128), np.float32)
        ba[:, 0:4] = W[:, G0 + 4096 + hh * 4: G0 + 4096 + hh * 4 + 4]
        ba[:, 4:8] = W[:, G0 + 4104 + hh * 4: G0 + 4104 + hh * 4 + 4]
        Win0 = np.concatenate([my(0), my(1024), my(2048), my(G0), my(G0 + 1024), my(G0 + 2048), my(G0 + 3072), ba,
                               W[:, 3072:3200], W[:, 3200:3328]], axis=1)
        W1s = I["ssm_w_in"][0]
        dtp = np.zeros((2048, 128), np.float32)
        dtp[:, 0:32] = W1s[:, 10240 + hh * 32: 10240 + (hh + 1) * 32]
        Win1 = np.concatenate([W1s[:, hh * 2048:(hh + 1) * 2048], W1s[:, 4096 + hh * 2048: 4096 + (hh + 1) * 2048],
                               W1s[:, 8192 + hh * 512: 8192 + (hh + 1) * 512], W1s[:, 9216 + hh * 512: 9216 + (hh + 1) * 512], dtp], axis=1)
        Wo0 = np.concatenate([I["hg_w_out"][0][hh * 512:(hh + 1) * 512], I["hg_w_out"][0][1024 + hh * 512: 1024 + (hh + 1) * 512]], axis=0)
        Wo1 = I["ssm_w_out"][0][hh * 2048:(hh + 1) * 2048]
        lay = []
        for l in range(2):
            W1 = I["ffn_w1"][l][:, hh * 2816:(hh + 1) * 2816]
            W3 = I["ffn_w3"][l][:, hh * 2816:(hh + 1) * 2816]
            W13 = np.concatenate([np.concatenate([W1[:, q * 256:(q + 1) * 256], W3[:, q * 256:(q + 1) * 256]], axis=1) for q in range(11)], axis=1)
            lay.append((pack_wg(W13), pack_wg(I["ffn_w2"][l][hh * 2816:(hh + 1) * 2816])))
        ws = np.concatenate(ada + [pack_wg(Win0), pack_wg(Wo0), lay[0][0], lay[0][1], pack_wg(Win1), pack_wg(Wo1), lay[1][0], lay[1][1]], axis=0)
        m = {"ws": np.ascontiguousarray(ws)}
        for pre, d in (("rwkv_", _rwkv_params(prm_r, hh)), ("gdn_", _gdn_params(prm_g, hh)), ("ssd_", _ssd_params(prm_s, hh))):
            for k, v in d.items():
                m[pre + k] = v
        per_hh[hh] = m
    adab = np.ascontiguousarray(np.stack([fm(I["ada_mix_b"][0]), fm(I["ada_ffn_b"][0]), fm(I["ada_mix_b"][1]), fm(I["ada_ffn_b"][1])]))
    in_maps = []
    for core in cores:
        b, hh = core // 2, core % 2
        m = dict(per_hh[hh])
        m.update({"xT": np.ascontiguousarray(I["x"][b, :SEQ].T), "cs": fm(I["c"][b]), "adab": adab, "fnw": fm(I["final_norm_w"]),
                  "identf_d": np.eye(128, dtype=np.float32)})
        in_maps.append(m)
    return in_maps


def kernel(**inputs):
    p = _prog("fused", build_fused)
    in_maps = fused_inmaps(inputs)
    r = _run(p, in_maps)
    out = np.empty((4, SEQ, D), np.float32)
    for b in range(4):
        out[b] = np.asarray(r[2 * b]["oT"]).T
    return out
```

```python
import numpy as np
import ml_dtypes
import concourse.bass as bass
import concourse.mybir as mybir
from concourse.bass_utils import run_bass_kernel_spmd

F32 = mybir.dt.float32
BF16 = mybir.dt.bfloat16
AF = mybir.ActivationFunctionType
ALU = mybir.AluOpType
AX = mybir.AxisListType
NPBF = ml_dtypes.bfloat16

D = 2048
TT = 512
TOK = 2048
GS = 8192
EPS = 1e-5


class Prog:
    def __init__(self, n_dma_sems=10, same_engine_sync=True):
        self.nc = bass.Bass("TRN2", target_bir_lowering=False)
        nc = self.nc
        self.eng = {"pe": nc.tensor, "act": nc.scalar, "dve": nc.vector,
                    "pool": nc.gpsimd, "sp": nc.sync}
        self.sem = {e: nc.alloc_semaphore(name="s_" + e) for e in self.eng}
        self.cnt = {e: 0 for e in self.eng}
        self.waited = {e: {} for e in self.eng}
        self.dma_ring = {q: [[nc.alloc_semaphore(name=f"d_{q}{i}"), 0] for i in range(n_dma_sems)]
                         for q in ("sp", "pool", "act")}
        self.dma_pos = {q: 0 for q in self.dma_ring}
        self.last_w = {}
        self.readers = {}
        self.same_engine_sync = same_engine_sync
        self.n_inst = 0
        self._ctx = []

    def sbuf(self, name, shape, dt):
        cm = self.nc.sbuf_tensor("sb_" + getattr(self, "pfx", "") + name, list(shape), dt)
        t = cm.__enter__()
        self._ctx.append(cm)
        return t

    def psum(self, name, shape, dt=F32):
        cm = self.nc.psum_tensor("pp_" + getattr(self, "pfx", "") + name, list(shape), dt)
        t = cm.__enter__()
        self._ctx.append(cm)
        return t

    def dram(self, name, shape, dt, kind):
        return self.nc.dram_tensor(name, list(shape), dt, kind=kind).ap()

    def _wait(self, e, tok):
        if tok is None:
            return
        if tok[0] == "c":
            _, e2, idx = tok
            if e2 == e and not self.same_engine_sync:
                return
            key = ("c", e2)
            if self.waited[e].get(key, 0) >= idx:
                return
            self.eng[e].wait_ge(self.sem[e2], idx)
            self.waited[e][key] = idx
        elif tok[0] == "x":
            _, sem, val, sid = tok
            key = ("x", sid)
            if self.waited[e].get(key, 0) >= val:
                return
            self.eng[e].wait_ge(sem, val)
            self.waited[e][key] = val
        else:
            _, q, slot, val = tok
            key = ("d", q, slot)
            if self.waited[e].get(key, 0) >= val:
                return
            self.eng[e].wait_ge(self.dma_ring[q][slot][0], val)
            self.waited[e][key] = val

    def _deps(self, e, reads, writes):
        for k in reads:
            self._wait(e, self.last_w.get(k))
        for k in writes:
            self._wait(e, self.last_w.get(k))
            for t in self.readers.get(k, ()):
                self._wait(e, t)

    def _commit(self, tok, reads, writes):
        for k in reads:
            self.readers.setdefault(k, []).append(tok)
        for k in writes:
            self.last_w[k] = tok
            self.readers[k] = []

    def op(self, e, fn, reads=(), writes=(), nosync_same=False):
        if nosync_same:
            sv = self.same_engine_sync
            self.same_engine_sync = False
            self._deps(e, reads, writes)
            self.same_engine_sync = sv
        else:
            self._deps(e, reads, writes)
        inst = fn(self.eng[e])
        self.cnt[e] += 1
        inst.then_inc(self.sem[e], 1)
        tok = ("c", e, self.cnt[e])
        self._commit(tok, reads, writes)
        self.n_inst += 1
        return tok

    def dma(self, q, out, in_, reads=(), writes=(), **kw):
        ring = self.dma_ring[q]
        slot = self.dma_pos[q] % len(ring)
        self.dma_pos[q] += 1
        sem, val = ring[slot]
        if val > 0:
            self._wait(q, ("d", q, slot, val))
        self._deps(q, reads, writes)
        inst = self.eng[q].dma_start(out=out, in_=in_, **kw)
        val += 16
        ring[slot][1] = val
        inst.then_inc(sem, 16)
        tok = ("d", q, slot, val)
        self._commit(tok, reads, writes)
        self.n_inst += 1
        return tok

    def scope_begin(self):
        self._nscope = getattr(self, "_nscope", 0) + 1
        self.pfx = f"s{self._nscope}_"
        return len(self._ctx)

    def scope_end(self, mark):
        self.barrier()
        while len(self._ctx) > mark:
            cm = self._ctx.pop()
            cm.__exit__(None, None, None)

    def barrier(self):
        for e in self.eng:
            for e2 in ("pe", "act", "dve", "pool"):
                if e2 != e and self.cnt[e2] > 0:
                    self._wait(e, ("c", e2, self.cnt[e2]))
            for q, ring in self.dma_ring.items():
                for slot, (sem, val) in enumerate(ring):
                    if val > 0:
                        self._wait(e, ("d", q, slot, val))

    def collective(self, kind, op, rg, in_t, out_t, reads, writes):
        if not hasattr(self, "xsems"):
            self.xsems = []
        self._deps("pool", reads, writes)
        inst = self.nc.gpsimd.collective_compute(kind, op, replica_groups=rg, ins=[in_t.ap().opt()], outs=[out_t.ap().opt()])
        sem = self.nc.alloc_semaphore(name=f"cc{len(self.xsems)}")
        inst.then_inc(sem, 1)
        self.xsems.append((sem, 1))
        tok = ("x", sem, 1, len(self.xsems) - 1)
        self._commit(tok, reads, writes)
        return tok

    def finish(self):
        for sid, (sem, val) in enumerate(getattr(self, "xsems", [])):
            self._wait("sp", ("x", sem, val, sid))
        for q, ring in self.dma_ring.items():
            for slot, (sem, val) in enumerate(ring):
                if val > 0:
                    self._wait("sp", ("d", q, slot, val))
        for e in ("pe", "act", "dve", "pool"):
            if self.cnt[e] > 0:
                self._wait("sp", ("c", e, self.cnt[e]))


class WStream:
    def __init__(self, p, ws_ap, seq, nslot=4, q="sp"):
        self.p, self.ws, self.seq, self.nslot, self.q = p, ws_ap, list(seq), nslot, q
        self.slots = [p.sbuf(f"wslot{i}", [128, GS], BF16) for i in range(nslot)]
        self.issued = 0
        self.used = 0

    def _issue(self):
        while self.issued < len(self.seq) and self.issued < self.used + self.nslot:
            s = self.issued % self.nslot
            self.p.dma(self.q, self.slots[s][:], self.ws[self.seq[self.issued]], writes=[("w", s)])
            self.issued += 1

    def next(self):
        self._issue()
        s = self.used % self.nslot
        self.used += 1
        return self.slots[s], ("w", s)

    def prefetch(self):
        self._issue()


class Ctx:
    def __init__(self, p):
        self.p = p
        self.ps = [p.psum(f"ps{i}", [128, 512], F32) for i in range(8)]
        self.ps_i = 0
        self.ev_i = 0
        self.ones = p.sbuf("ones", [128, 128], BF16)
        p.op("dve", lambda e: e.memset(self.ones[:], 1.0), writes=["ones"])

    def next_ps(self):
        i = self.ps_i % 8
        self.ps_i += 1
        return self.ps[i], ("ps", i)

    def ev_eng(self):
        self.ev_i += 1
        return "act" if self.ev_i % 2 else "dve"


def emit_mod(cx, wst, cs_ap, adab_ap, jlist, name):
    p = cx.p
    cs = p.sbuf(name + "_cs", [128, 16], F32)
    scb = p.sbuf(name + "_scb", [128, 16], BF16)
    bias = p.sbuf(name + "_b", [128, 48], F32)
    mod = p.sbuf(name + "_mod", [128, 48], F32)
    p.dma("pool", cs[:], cs_ap, writes=[name + "cs"])
    p.dma("pool", bias[:], adab_ap, writes=[name + "b"])
    p.op("act", lambda e: e.activation(out=scb[:], in_=cs[:], func=AF.Silu), reads=[name + "cs"], writes=[name + "scb"])
    assert len(jlist) % 4 == 0
    for g in range(len(jlist) // 4):
        slot, wk = wst.next()
        ps, pk = cx.next_ps()
        for jj in range(4):
            for kc in range(16):
                off = (jj * 16 + kc) * 128
                p.op("pe", lambda e: e.matmul(ps[:, jj:jj + 1], lhsT=slot[:, off:off + 128], rhs=scb[:, kc:kc + 1],
                                              start=(kc == 0), stop=(kc == 15)),
                     reads=[wk, name + "scb"], writes=[pk], nosync_same=True)
        j0 = jlist[g * 4]
        assert jlist[g * 4:g * 4 + 4] == list(range(j0, j0 + 4))
        p.op("dve", lambda e: e.tensor_tensor(out=mod[:, j0:j0 + 4], in0=ps[:, 0:4], in1=bias[:, j0:j0 + 4], op=ALU.add),
             reads=[pk, name + "b"], writes=[name + "mod"])
    return mod, name + "mod"


def emit_norm_mod(cx, x_sb, xkey, h_sb, hkey, ops_fn, shift_fn, modkeys, tmp_ring, stat):
    p = cx.p
    for kc in range(16):
        p.op("act", lambda e: e.activation(out=h_sb[:, kc, :], in_=x_sb[:, kc, :], func=AF.Square),
             reads=[(xkey, kc)], writes=[(hkey, kc)])
    ps, pk = cx.next_ps()
    for kc in range(16):
        p.op("pe", lambda e: e.matmul(ps[:], lhsT=cx.ones[:], rhs=h_sb[:, kc, :], start=(kc == 0), stop=(kc == 15)),
             reads=["ones", (hkey, kc)], writes=[pk], nosync_same=True)
    rs, rstd = stat
    p.op("act", lambda e: e.activation(out=rs[:], in_=ps[:], func=AF.Sqrt, scale=1.0 / D, bias=cx.epsb[:, 0:1]),
         reads=[pk, "epsb"], writes=["rs"])
    p.op("dve", lambda e: e.reciprocal(out=rstd[:], in_=rs[:]), reads=["rs"], writes=["rstd"])
    if ops_fn is None:
        return
    for kc in range(16):
        t = tmp_ring[kc % len(tmp_ring)]
        tk = ("tmpn", kc % len(tmp_ring))
        p.op("dve", lambda e: e.tensor_tensor(out=t[:], in0=x_sb[:, kc, :], in1=rstd[:], op=ALU.mult),
             reads=[(xkey, kc), "rstd"], writes=[tk])
        p.op("act", lambda e: e.activation(out=h_sb[:, kc, :], in_=t[:], func=AF.Identity,
                                           scale=ops_fn(kc), bias=shift_fn(kc)),
             reads=[tk] + modkeys, writes=[(hkey, kc)])


def emit_gemm(cx, wst, Kc, ncc, rhs_fn, evac_fn, cpg=None, ST=None):
    p = cx.p
    if cpg is None:
        cpg = GS // (Kc * 128)
    n = 0
    while n < ncc:
        slot, wk = wst.next()
        for cc in range(min(cpg, ncc - n)):
            for j in range(ST or 1):
                ps, pk = cx.next_ps()
                for kc in range(Kc):
                    off = (cc * Kc + kc) * 128
                    r_ap, r_key = rhs_fn(kc) if ST is None else rhs_fn(j, kc)
                    p.op("pe", lambda e: e.matmul(ps[:], lhsT=slot[:, off:off + 128], rhs=r_ap,
                                                  start=(kc == 0), stop=(kc == Kc - 1)),
                         reads=[wk, r_key], writes=[pk], nosync_same=True)
                if ST is None:
                    evac_fn(n + cc, ps, pk)
                else:
                    evac_fn(j, n + cc, ps, pk)
        n += cpg


def pack_w(W, Kc=None):
    K, N = W.shape
    Kc = K // 128
    ncc = N // 128
    cpg = GS // (Kc * 128)
    ng = (ncc + cpg - 1) // cpg
    Wp = np.zeros((K, ng * cpg * 128), dtype=W.dtype)
    Wp[:, :N] = W
    A = Wp.reshape(Kc, 128, ng, cpg, 128).transpose(2, 1, 3, 0, 4).reshape(ng, 128, cpg * Kc * 128)
    out = np.zeros((ng, 128, GS), dtype=W.dtype)
    out[:, :, :cpg * Kc * 128] = A
    return out


def build_cast(F):
    p = Prog()
    CH = 8192
    x = p.dram("x", [128, F], F32, "ExternalInput")
    y = p.dram("y", [128, F], BF16, "ExternalOutput")
    NB = 3
    xi = [p.sbuf(f"xi{i}", [128, CH], F32) for i in range(NB)]
    yo = [p.sbuf(f"yo{i}", [128, CH], BF16) for i in range(NB)]
    nch = (F + CH - 1) // CH
    for i in range(nch):
        a, b = i * CH, min(F, (i + 1) * CH)
        s = i % NB
        p.dma("sp", xi[s][:, :b - a], x[:, a:b], writes=[("xi", s)])
        if i % 2 == 0:
            p.op("dve", lambda e: e.tensor_copy(out=yo[s][:, :b - a], in_=xi[s][:, :b - a]), reads=[("xi", s)], writes=[("yo", s)])
        else:
            p.op("act", lambda e: e.activation(out=yo[s][:, :b - a], in_=xi[s][:, :b - a], func=AF.Copy), reads=[("xi", s)], writes=[("yo", s)])
        p.dma("pool", y[:, a:b], yo[s][:, :b - a], reads=[("yo", s)])
    p.finish()
    return p


def build_inproj(ncc, NG_w):
    p = Prog()
    xT = p.dram("xT", [D, TOK], F32, "ExternalInput")
    cs = p.dram("cs", [128, 16], F32, "ExternalInput")
    adab = p.dram("adab", [128, 48], F32, "ExternalInput")
    ws = p.dram("ws", [8 + NG_w, 128, GS], BF16, "ExternalInput")
    pT = p.dram("pT", [ncc * 128, TOK], F32, "ExternalOutput")
    cx = Ctx(p)
    cx.epsb = p.sbuf("epsb", [128, 1], F32)
    p.op("dve", lambda e: e.memset(cx.epsb[:], EPS), writes=["epsb"])
    NTL = TOK // TT
    seq = list(range(8)) + [8 + g for _ in range(NTL) for g in range(NG_w)]
    wst = WStream(p, ws, seq, nslot=4)
    mod, mk = emit_mod(cx, wst, cs, adab, list(range(32)), "m")
    ops = p.sbuf("ops", [128, 16], F32)
    p.op("dve", lambda e: e.tensor_scalar(out=ops[:], in0=mod[:, 16:32], scalar1=1.0, scalar2=None, op0=ALU.add),
         reads=[mk], writes=["ops"])
    xs = p.sbuf("xs", [128, 16, TT], F32)
    hs = p.sbuf("hs", [128, 16, TT], BF16)
    tmp = [p.sbuf(f"tmp{i}", [128, TT], F32) for i in range(3)]
    stat = (p.sbuf("rs", [128, TT], F32), p.sbuf("rstd", [128, TT], F32))
    ost = [p.sbuf(f"ost{i}", [128, TT], F32) for i in range(4)]
    xTv = xT.rearrange("(kc q) t -> q kc t", q=128)
    for tl in range(NTL):
        t0 = tl * TT
        for kc4 in range(4):
            p.dma("pool", xs[:, kc4 * 4:(kc4 + 1) * 4, :], xTv[:, kc4 * 4:(kc4 + 1) * 4, t0:t0 + TT],
                  writes=[("x", kc) for kc in range(kc4 * 4, kc4 * 4 + 4)])
        emit_norm_mod(cx, xs, "x", hs, "h", lambda kc: ops[:, kc:kc + 1], lambda kc: mod[:, kc:kc + 1],
                      [mk, "ops"], tmp, stat)

        def evac(n, ps, pk):
            o = ost[n % 4]
            ok = ("ost", n % 4)
            if cx.ev_eng() == "act":
                p.op("act", lambda e: e.activation(out=o[:], in_=ps[:], func=AF.Copy), reads=[pk], writes=[ok])
            else:
                p.op("dve", lambda e: e.tensor_copy(out=o[:], in_=ps[:]), reads=[pk], writes=[ok])
            p.dma("pool", pT[n * 128:(n + 1) * 128, t0:t0 + TT], o[:], reads=[ok])

        emit_gemm(cx, wst, 16, ncc, lambda kc: (hs[:, kc, :], ("h", kc)), evac)
    p.finish()
    return p


def build_outffn(Kmix_c, NG_out, NG_13, NG_2, final):
    p = Prog()
    xT = p.dram("xT", [D, TOK], F32, "ExternalInput")
    yT = p.dram("yT", [Kmix_c * 128, TOK], F32, "ExternalInput")
    cs = p.dram("cs", [128, 16], F32, "ExternalInput")
    adab_m = p.dram("adab_m", [128, 48], F32, "ExternalInput")
    adab_f = p.dram("adab_f", [128, 48], F32, "ExternalInput")
    NGT = 4 + 12 + NG_out + NG_13 + NG_2
    ws = p.dram("ws", [NGT, 128, GS], BF16, "ExternalInput")
    oT = p.dram("oT", [D, TOK], F32, "ExternalOutput")
    if final:
        fnw = p.dram("fnw", [128, 16], F32, "ExternalInput")
    cx = Ctx(p)
    cx.epsb = p.sbuf("epsb", [128, 1], F32)
    p.op("dve", lambda e: e.memset(cx.epsb[:], EPS), writes=["epsb"])
    NTL = TOK // TT
    seq = list(range(16)) + [16 + g for _ in range(NTL) for g in range(NG_out + NG_13 + NG_2)]
    wst = WStream(p, ws, seq, nslot=3)
    modm, mmk = emit_mod(cx, wst, cs, adab_m, list(range(32, 48)), "mm")
    modf, mfk = emit_mod(cx, wst, cs, adab_f, list(range(48)), "mf")
    ops = p.sbuf("ops", [128, 16], F32)
    p.op("dve", lambda e: e.tensor_scalar(out=ops[:], in0=modf[:, 16:32], scalar1=1.0, scalar2=None, op0=ALU.add),
         reads=[mfk], writes=["ops"])
    if final:
        fw = p.sbuf("fw", [128, 16], F32)
        p.dma("pool", fw[:], fnw, writes=["fw"])
    acc = p.sbuf("acc", [128, 16, TT], F32)
    hs = p.sbuf("hs", [128, 16, TT], BF16)
    big = p.sbuf("big", [128, 44, TT], BF16)
    yst = [p.sbuf(f"yst{i}", [128, 2, TT], F32) for i in range(2)]
    tmp = [p.sbuf(f"tmp{i}", [128, TT], F32) for i in range(3)]
    stat = (p.sbuf("rs", [128, TT], F32), p.sbuf("rstd", [128, TT], F32))
    xTv = xT.rearrange("(kc q) t -> q kc t", q=128)
    yTv = yT.rearrange("(kc q) t -> q kc t", q=128)
    oTv = oT.rearrange("(kc q) t -> q kc t", q=128)
    for tl in range(NTL):
        t0 = tl * TT
        for kc4 in range(4):
            p.dma("pool", acc[:, kc4 * 4:(kc4 + 1) * 4, :], xTv[:, kc4 * 4:(kc4 + 1) * 4, t0:t0 + TT],
                  writes=[("acc", kc) for kc in range(kc4 * 4, kc4 * 4 + 4)])
        for k2 in range(Kmix_c // 2):
            s = k2 % 2
            p.dma("pool", yst[s][:], yTv[:, k2 * 2:k2 * 2 + 2, t0:t0 + TT], writes=[("yst", s)])
            if k2 % 2 == 0:
                p.op("dve", lambda e: e.tensor_copy(out=big[:, k2 * 2:k2 * 2 + 2, :], in_=yst[s][:]),
                     reads=[("yst", s)], writes=[("big", k2 * 2), ("big", k2 * 2 + 1)])
            else:
                p.op("act", lambda e: e.activation(out=big[:, k2 * 2:k2 * 2 + 2, :], in_=yst[s][:], func=AF.Copy),
                     reads=[("yst", s)], writes=[("big", k2 * 2), ("big", k2 * 2 + 1)])

        def evac_res(gate_sb, gk):
            def f(n, ps, pk):
                p.op("dve", lambda e: e.scalar_tensor_tensor(out=acc[:, n, :], in0=ps[:], scalar=gate_sb[:, 32 + n:33 + n],
                                                             in1=acc[:, n, :], op0=ALU.mult, op1=ALU.add),
                     reads=[pk, gk, ("acc", n)], writes=[("acc", n)])
            return f

        emit_gemm(cx, wst, Kmix_c, 16, lambda kc: (big[:, kc, :], ("big", kc)), evac_res(modm, mmk))
        emit_norm_mod(cx, acc, "acc", hs, "h", lambda kc: ops[:, kc:kc + 1], lambda kc: modf[:, kc:kc + 1],
                      [mfk, "ops"], tmp, stat)
        sbuf_s = tmp

        def evac13(n, ps, pk):
            q, r = n // 4, n % 4
            if r < 2:
                j = q * 2 + r
                p.op("act", lambda e: e.activation(out=sbuf_s[r][:], in_=ps[:], func=AF.Silu), reads=[pk], writes=[("tmpn", r)])
            else:
                j = q * 2 + (r - 2)
                p.op("dve", lambda e: e.tensor_tensor(out=big[:, j, :], in0=ps[:], in1=sbuf_s[r - 2][:], op=ALU.mult),
                     reads=[pk, ("tmpn", r - 2)], writes=[("big", j)])

        emit_gemm(cx, wst, 16, 88, lambda kc: (hs[:, kc, :], ("h", kc)), evac13)
        emit_gemm(cx, wst, 44, 16, lambda kc: (big[:, kc, :], ("big", kc)), evac_res(modf, mfk))
        if final:
            emit_norm_mod(cx, acc, "acc", hs, "h", None, None, [], tmp, stat)
            for kc in range(16):
                t = tmp[kc % 3]
                tk = ("tmpn", kc % 3)
                p.op("dve", lambda e: e.scalar_tensor_tensor(out=acc[:, kc, :], in0=acc[:, kc, :], scalar=fw[:, kc:kc + 1],
                                                             in1=stat[1][:], op0=ALU.mult, op1=ALU.mult),
                     reads=[("acc", kc), "fw", "rstd"], writes=[("acc", kc)])
        for kc4 in range(4):
            p.dma("pool", oTv[:, kc4 * 4:(kc4 + 1) * 4, t0:t0 + TT], acc[:, kc4 * 4:(kc4 + 1) * 4, :],
                  reads=[("acc", kc) for kc in range(kc4 * 4, kc4 * 4 + 4)])
    p.finish()
    return p


_PROGS = {}


def _prog(key, fn):
    if key not in _PROGS:
        _PROGS[key] = fn()
    return _PROGS[key]


def _run(p, in_maps):
    res = run_bass_kernel_spmd(p.nc, in_maps, core_ids=list(range(8)))
    return res.results


def cast_weights(wd):
    names = list(wd)
    parts = [np.ascontiguousarray(wd[n]).reshape(8, 128, -1) for n in names]
    sizes = [q.shape[2] for q in parts]
    F = sum(sizes)
    p = _prog(("cast", F), lambda: build_cast(F))
    in_maps = [{"x": np.ascontiguousarray(np.concatenate([q[i] for q in parts], axis=1))} for i in range(8)]
    r = _run(p, in_maps)
    ys = [np.asarray(r[i]["y"]) for i in range(8)]
    out = {}
    o = 0
    for n, sz in zip(names, sizes):
        out[n] = np.stack([ys[i][:, o:o + sz] for i in range(8)]).reshape(wd[n].shape)
        o += sz
    return out


def fm(a):
    return np.ascontiguousarray(a.reshape(-1, 128).T)


def pad_cols(W, n):
    if W.shape[1] == n:
        return W
    o = np.zeros((W.shape[0], n), dtype=W.dtype)
    o[:, :W.shape[1]] = W
    return o


def run_inproj(xT_cores, c, adaw_bf, adab, W_bf):
    ncc = W_bf.shape[1] // 128
    wpk = pack_w(W_bf)
    apk = pack_w(adaw_bf[:, :4096])
    ws = np.ascontiguousarray(np.concatenate([apk, wpk], axis=0))
    p = _prog(("inproj", ncc), lambda: build_inproj(ncc, wpk.shape[0]))
    in_maps = [{"xT": xT_cores[i], "cs": fm(c[i // 2]), "adab": fm(adab), "ws": ws} for i in range(8)]
    r = _run(p, in_maps)
    return [np.asarray(r[i]["pT"]) for i in range(8)]


def run_outffn(xT_cores, yT_cores, c, adaw_m, adab_m, adaw_f, adab_f, Wout, W1, W3, W2, fnw=None):
    Kc = Wout.shape[0] // 128
    W13 = np.concatenate([np.concatenate([W1[:, q * 256:(q + 1) * 256], W3[:, q * 256:(q + 1) * 256]], axis=1)
                          for q in range(22)], axis=1)
    pk = [pack_w(adaw_m[:, 4096:]), pack_w(adaw_f), pack_w(Wout), pack_w(W13), pack_w(W2)]
    ws = np.ascontiguousarray(np.concatenate(pk, axis=0))
    final = fnw is not None
    p = _prog(("outffn", Kc, final), lambda: build_outffn(Kc, pk[2].shape[0], pk[3].shape[0], pk[4].shape[0], final))
    in_maps = []
    for i in range(8):
        m = {"xT": xT_cores[i], "yT": yT_cores[i], "cs": fm(c[i // 2]), "adab_m": fm(adab_m), "adab_f": fm(adab_f), "ws": ws}
        if final:
            m["fnw"] = fm(fnw)
        in_maps.append(m)
    r = _run(p, in_maps)
    return [np.asarray(r[i]["oT"]) for i in range(8)]


SEQ = 4096


def ev_copy(p, eng, out, in_, reads, writes):
    if eng == "act":
        return p.op("act", lambda e: e.activation(out=out, in_=in_, func=AF.Copy), reads=reads, writes=writes)
    return p.op(eng, lambda e: e.tensor_copy(out=out, in_=in_), reads=reads, writes=writes)


def build_ssd(p=None, io=None):
    standalone = p is None
    if standalone:
        p = Prog()
    pre = "" if standalone else "ssd_"
    NH, NG_, CH = 32, 4, 128
    TS = 256
    zT = io["zT"] if io else p.dram("zT", [2048, SEQ], F32, "ExternalInput")
    xT = io["xT"] if io else p.dram("xT", [2048, SEQ + 3], F32, "ExternalInput")
    bcT = io["bcT"] if io else p.dram("bcT", [1024, SEQ + 3], F32, "ExternalInput")
    dtT = io["dtT"] if io else p.dram("dtT", [32, SEQ], F32, "ExternalInput")
    cwx_d = p.dram(pre + "cwx", [128, 16, 4], F32, "ExternalInput")
    cbx_d = p.dram(pre + "cbx", [128, 16], F32, "ExternalInput")
    cwbc_d = p.dram(pre + "cwbc", [128, 8, 4], F32, "ExternalInput")
    cbbc_d = p.dram(pre + "cbbc", [128, 8], F32, "ExternalInput")
    hp_d = p.dram(pre + "hp", [32, 2], F32, "ExternalInput")
    dsk_d = p.dram(pre + "dsk", [128, 16], F32, "ExternalInput")
    nw_d = p.dram(pre + "nw", [128, 16], F32, "ExternalInput")
    ident_d = p.dram(pre + "ident", [128, 128], F32, "ExternalInput")
    selh_d = p.dram(pre + "selh", [32, 32, 128], F32, "ExternalInput")
    maskT_d = p.dram(pre + "maskT", [128, 128], F32, "ExternalInput")
    yT = io["yT"] if io else p.dram("yT", [2048, SEQ], F32, "ExternalOutput")
    cx = Ctx(p)

    def const(name, shape, src):
        t = p.sbuf(name, shape, F32)
        p.dma("pool", t[:], src, writes=[name])
        return t
    cwx = const("cwx", [128, 16, 4], cwx_d)
    cbx = const("cbx", [128, 16], cbx_d)
    cwbc = const("cwbc", [128, 8, 4], cwbc_d)
    cbbc = const("cbbc", [128, 8], cbbc_d)
    hp = const("hp", [32, 2], hp_d)
    dsk = const("dsk", [128, 16], dsk_d)
    nw = const("nw", [128, 16], nw_d)
    ident = const("ident", [128, 128], ident_d)
    selh = const("selh", [32, 32, 128], selh_d)
    maskT = const("maskT", [128, 128], maskT_d)
    epsb = p.sbuf("epsb", [128, 1], F32)
    p.op("dve", lambda e: e.memset(epsb[:], EPS), writes=["epsb"])
    ones32 = p.sbuf("ones32", [32, 128], F32)
    p.op("dve", lambda e: e.memset(ones32[:], 1.0), writes=["ones32"])
    Aneg = p.sbuf("Aneg", [32, 1], F32)
    p.op("act", lambda e: e.activation(out=Aneg[:], in_=hp[:, 1:2], func=AF.Exp), reads=["hp"], writes=["Aneg"])
    p.op("dve", lambda e: e.tensor_scalar(out=Aneg[:], in0=Aneg[:], scalar1=-1.0, scalar2=None, op0=ALU.mult),
         reads=["Aneg"], writes=["Aneg"])

    xin2 = [p.sbuf(f"xin{i}", [128, 16, TS + 3], F32) for i in range(2)]
    zb2 = [p.sbuf(f"zb{i}", [128, 16, TS], F32) for i in range(2)]
    xc = p.sbuf("xc", [128, 16, TS], F32)
    bcin2 = [p.sbuf(f"bcin{i}", [128, 8, TS + 3], F32) for i in range(2)]
    bcc = p.sbuf("bcc", [128, 8, TS], BF16)
    bf = p.sbuf("bf", [128, 4, TS], F32)
    yb = p.sbuf("yb", [128, 16, TS], F32)
    ctmp = [p.sbuf(f"ctmp{i}", [128, TS], F32) for i in range(4)]
    dtin2 = [p.sbuf(f"dtin{i}", [32, TS], F32) for i in range(2)]
    dtp = p.sbuf("dtp", [32, TS], F32)
    av = p.sbuf("av", [32, TS], F32)
    acum = p.sbuf("acum", [32, TS], F32)
    m2 = p.sbuf("m2", [32, TS], F32)
    tmr = p.sbuf("tmr", [128, 3, 32], F32)
    dg = p.sbuf("dg", [32, 32], F32)
    ntm = p.sbuf("ntm", [128, 32], F32)
    negm = p.sbuf("negm", [128, 128], F32)
    p.op("dve", lambda e: e.tensor_scalar(out=negm[:], in0=maskT[:], scalar1=30000.0, scalar2=-30000.0, op0=ALU.mult, op1=ALU.add),
         reads=["maskT"], writes=["negm"])
    ea = p.sbuf("ea", [128, 32], F32)
    xtm = p.sbuf("xtm", [128, 32, 64], BF16)
    xw = p.sbuf("xw", [128, 32, 64], BF16)
    btm = p.sbuf("btm", [128, 4, 128], BF16)
    cbm = p.sbuf("cbm", [128, 4, 128], F32)
    state = p.sbuf("state", [128, 32, 64], F32)
    stbf = p.sbuf("stbf", [128, 32, 64], BF16)
    p.op("dve", lambda e: e.memset(state[:], 0.0), writes=[("st", g) for g in range(4)])
    p.op("pool", lambda e: e.memset(stbf[:], 0.0), writes=[("stb", g) for g in range(4)])
    NR = 16
    t1r = [p.sbuf(f"t1r{i}", [128, 128], F32) for i in range(NR)]
    E2r = [p.sbuf(f"E2r{i}", [128, 128], F32) for i in range(NR)]
    MTh = [p.sbuf(f"MTh{i}", [128, 128], BF16) for i in range(16)]
    Chh = [p.sbuf(f"Chh{i}", [128, 128], BF16) for i in range(16)]
    stat = (p.sbuf("rs", [128, TS], F32), p.sbuf("rstd", [128, TS], F32))
    sq = p.sbuf("sq", [128, 4, TS], BF16)
    ost = [p.sbuf(f"ost{i}", [128, TS], F32) for i in range(2)]

    xTv = xT.rearrange("(kc q) t -> q kc t", q=128)
    bcTv = bcT.rearrange("(kc q) t -> q kc t", q=128)
    zTv = zT.rearrange("(kc q) t -> q kc t", q=128)
    yTv = yT.rearrange("(kc q) t -> q kc t", q=128)
    hcount = 0
    def issue_loads(tl_):
        par_ = tl_ % 2
        t0_ = tl_ * TS
        for k4 in range(4):
            p.dma("sp", xin2[par_][:, k4 * 4:k4 * 4 + 4, :], xTv[:, k4 * 4:k4 * 4 + 4, t0_:t0_ + TS + 3],
                  writes=[(f"xin{par_}", kc) for kc in range(k4 * 4, k4 * 4 + 4)])
        for k4 in range(2):
            p.dma("sp", bcin2[par_][:, k4 * 4:k4 * 4 + 4, :], bcTv[:, k4 * 4:k4 * 4 + 4, t0_:t0_ + TS + 3],
                  writes=[(f"bcin{par_}", kc) for kc in range(k4 * 4, k4 * 4 + 4)])
        p.dma("sp", dtin2[par_][:], dtT[:, t0_:t0_ + TS], writes=[f"dtin{par_}"])
        for k4 in range(4):
            p.dma("sp", zb2[par_][:, k4 * 4:k4 * 4 + 4, :], zTv[:, k4 * 4:k4 * 4 + 4, t0_:t0_ + TS],
                  writes=[(f"zb{par_}", kc) for kc in range(k4 * 4, k4 * 4 + 4)])

    NTS = SEQ // TS
    issue_loads(0)
    for tl in range(NTS):
        t0 = tl * TS
        par = tl % 2
        xin, bcin, dtin, zb = xin2[par], bcin2[par], dtin2[par], zb2[par]
        if tl + 1 < NTS:
            issue_loads(tl + 1)

        def conv_group(items):
            for gi, (src, skey, kc, w, b, wk, outs) in enumerate(items):
                t, tk = ctmp[gi], ("ctmp", gi)
                p.op("act", lambda e: e.activation(out=t[:], in_=src[:, kc, 0:TS], func=AF.Identity,
                                                   scale=w[:, kc, 0:1], bias=b[:, kc:kc + 1]),
                     reads=[(skey, kc)] + wk, writes=[tk])
            for j in (1, 2, 3):
                for gi, (src, skey, kc, w, b, wk, outs) in enumerate(items):
                    t, tk = ctmp[gi], ("ctmp", gi)
                    p.op("dve", lambda e: e.scalar_tensor_tensor(out=t[:], in0=src[:, kc, j:j + TS], scalar=w[:, kc, j:j + 1],
                                                                 in1=t[:], op0=ALU.mult, op1=ALU.add),
                         reads=[(skey, kc), tk] + wk, writes=[tk])
            for gi, (src, skey, kc, w, b, wk, outs) in enumerate(items):
                t, tk = ctmp[gi], ("ctmp", gi)
                for (o_ap, okey) in outs:
                    p.op("act", lambda e: e.activation(out=o_ap, in_=t[:], func=AF.Silu), reads=[tk], writes=[okey])

        items = [(xin, f"xin{par}", kc, cwx, cbx, ["cwx", "cbx"], [(xc[:, kc, :], ("xc", kc))]) for kc in range(16)]
        for kc in range(8):
            outs = [(bcc[:, kc, :], ("bcc", kc))]
            if kc < 4:
                outs.append((bf[:, kc, :], ("bf", kc)))
            items.append((bcin, f"bcin{par}", kc, cwbc, cbbc, ["cwbc", "cbbc"], outs))
        for g0 in range(0, len(items), 4):
            conv_group(items[g0:g0 + 4])

        p.op("act", lambda e: e.activation(out=dtp[:], in_=dtin[:], func=AF.Exp, bias=hp[:, 0:1]),
             reads=[f"dtin{par}", "hp"], writes=["dtp"])
        p.op("act", lambda e: e.activation(out=dtp[:], in_=dtp[:], func=AF.Ln, bias=ones32[:, 0:1]),
             reads=["dtp", "ones32"], writes=["dtp"])
        p.op("dve", lambda e: e.tensor_scalar(out=av[:], in0=dtp[:], scalar1=Aneg[:, 0:1], scalar2=None, op0=ALU.mult),
             reads=["dtp", "Aneg"], writes=["av"])
        for c in range(TS // CH):
            cs_ = slice(c * CH, (c + 1) * CH)
            p.op("dve", lambda e: e.tensor_tensor_scan(out=acum[:, cs_], data0=ones32[:, 0:CH], data1=av[:, cs_],
                                                       initial=0.0, op0=ALU.mult, op1=ALU.add),
                 reads=["av", "ones32"], writes=[("acum", c)])
            p.op("act", lambda e: e.activation(out=m2[:, cs_], in_=acum[:, cs_], func=AF.Exp, scale=-1.0,
                                               bias=acum[:, (c + 1) * CH - 1:(c + 1) * CH]),
                 reads=[("acum", c)], writes=[("m2", c)])
            p.op("dve", lambda e: e.tensor_tensor(out=m2[:, cs_], in0=m2[:, cs_], in1=dtp[:, cs_], op=ALU.mult),
                 reads=[("m2", c), "dtp"], writes=[("m2", c)])

        for c in range(TS // CH):
            cs_ = slice(c * CH, (c + 1) * CH)
            ps, pk = cx.next_ps()
            for i, (src, sk) in enumerate(((dtp, "dtp"), (acum, ("acum", c)), (m2, ("m2", c)))):
                p.op("pe", lambda e: e.matmul(ps[:, i * 32:(i + 1) * 32], lhsT=src[:, cs_], rhs=ident[0:32, 0:32],
                                              start=True, stop=True),
                     reads=[sk, "ident"], writes=[pk], nosync_same=True)
            p.op("dve", lambda e: e.tensor_copy(out=tmr[:].rearrange("q a h -> q (a h)"), in_=ps[:, 0:96]),
                 reads=[pk], writes=["tmr"])
            p.op("dve", lambda e: e.tensor_scalar(out=dg[:], in0=ident[0:32, 0:32],
                                                  scalar1=acum[:, (c + 1) * CH - 1:(c + 1) * CH], scalar2=None, op0=ALU.mult),
                 reads=["ident", ("acum", c)], writes=["dg"])
            ps, pk = cx.next_ps()
            p.op("pe", lambda e: e.matmul(ps[:, 0:32], lhsT=ones32[:, :], rhs=dg[:], start=True, stop=True),
                 reads=["ones32", "dg"], writes=[pk])
            p.op("act", lambda e: e.activation(out=ea[:], in_=ps[:, 0:32], func=AF.Exp), reads=[pk], writes=["ea"])
            for k4 in range(4):
                ps, pk = cx.next_ps()
                for kk in range(4):
                    kc = k4 * 4 + kk
                    p.op("pe", lambda e: e.transpose(out=ps[:, kk * 128:(kk + 1) * 128], in_=xc[:, kc, cs_], identity=ident[:]),
                         reads=[("xc", kc), "ident"], writes=[pk], nosync_same=True)
                ev_copy(p, cx.ev_eng(), xtm[:, k4 * 8:(k4 + 1) * 8, :].rearrange("q h d -> q (h d)"), ps[:],
                        [pk], [("xtm", k4)])
            ps, pk = cx.next_ps()
            for g in range(4):
                p.op("pe", lambda e: e.transpose(out=ps[:, g * 128:(g + 1) * 128], in_=bf[:, g, cs_], identity=ident[:]),
                     reads=[("bf", g), "ident"], writes=[pk], nosync_same=True)
            ev_copy(p, cx.ev_eng(), btm[:].rearrange("q g n -> q (g n)"), ps[:], [pk], ["btm"])
            ps, pk = cx.next_ps()
            for g in range(4):
                p.op("pe", lambda e: e.matmul(ps[:, g * 128:(g + 1) * 128], lhsT=bcc[:, g, cs_], rhs=bcc[:, 4 + g, cs_],
                                              start=True, stop=True),
                     reads=[("bcc", g), ("bcc", 4 + g)], writes=[pk], nosync_same=True)
            for g in range(4):
                p.op("dve", lambda e: e.tensor_tensor(out=cbm[:, g, :], in0=ps[:, g * 128:(g + 1) * 128], in1=maskT[:], op=ALU.mult),
                     reads=[pk, "maskT"], writes=[("cbm", g)])
            p.op("dve", lambda e: e.tensor_tensor(out=xw[:], in0=xtm[:], in1=tmr[:, 2, :].unsqueeze(2).to_broadcast([128, 32, 64]), op=ALU.mult),
                 reads=[("xtm", k4) for k4 in range(4)] + ["tmr"], writes=[("xw", g) for g in range(4)])
            p.op("pool", lambda e: e.tensor_scalar(out=ntm[:], in0=tmr[:, 1, :], scalar1=-1.0, scalar2=0.0, op0=ALU.mult, op1=ALU.add),
                 reads=["tmr"], writes=["ntm"])
            def stageA(hb):
              hs_ = list(range(hb * 8, hb * 8 + 8))
              pbs = {}
              for h in hs_:
                psb, pkb = cx.next_ps()
                pbs[h] = (psb, pkb)
                p.op("pe", lambda e: e.matmul(psb[:, 0:128], lhsT=selh[:, h, :], rhs=acum[:, cs_], start=True, stop=False),
                     reads=["selh", ("acum", c)], writes=[pkb], nosync_same=True)
                p.op("pe", lambda e: e.matmul(psb[:, 0:128], lhsT=ident[:], rhs=negm[:], start=False, stop=True),
                     reads=["ident", "negm"], writes=[pkb], nosync_same=True)
                p.op("pe", lambda e: e.matmul(psb[:, 128:256], lhsT=selh[:, h, :], rhs=acum[:, cs_], start=True, stop=True),
                     reads=["selh", ("acum", c)], writes=[pkb], nosync_same=True)
              for h in hs_:
                psb, pkb = pbs[h]
                r = h % 16
                p.op("act", lambda e: e.activation(out=t1r[r][:], in_=psb[:, 0:128], func=AF.Exp, bias=ntm[:, h:h + 1]),
                     reads=[pkb, "ntm"], writes=[("t1", r)])
                p.op("act", lambda e: e.activation(out=E2r[r][:], in_=psb[:, 128:256], func=AF.Exp), reads=[pkb], writes=[("E2", r)])
              for h in hs_:
                r = h % 16
                g = h // 8
                p.op("dve", lambda e: e.scalar_tensor_tensor(out=MTh[r][:], in0=t1r[r][:], scalar=tmr[:, 0, h:h + 1],
                                                             in1=cbm[:, g, :], op0=ALU.mult, op1=ALU.mult),
                     reads=[("t1", r), "tmr", ("cbm", g)], writes=[("MT", r)])
                p.op("pool", lambda e: e.tensor_tensor(out=Chh[r][:], in0=bcc[:, 4 + g, cs_], in1=E2r[r][:], op=ALU.mult),
                     reads=[("bcc", 4 + g), ("E2", r)], writes=[("Ch", r)])

            def stageB(hb):
              for kc in range(hb * 4, hb * 4 + 4):
                psY, pkY = cx.next_ps()
                for hh in range(2):
                    h = kc * 2 + hh
                    g = h // 8
                    p.op("pe", lambda e: e.matmul(psY[hh * 64:(hh + 1) * 64, 0:128], lhsT=xtm[:, h, :], rhs=MTh[h % 16][:],
                                                  start=True, stop=False),
                         reads=[("xtm", h // 8), ("MT", h % 16)], writes=[pkY], nosync_same=True)
                    p.op("pe", lambda e: e.matmul(psY[hh * 64:(hh + 1) * 64, 0:128], lhsT=stbf[:, h, :], rhs=Chh[h % 16][:],
                                                  start=False, stop=True),
                         reads=[("stb", g), ("Ch", h % 16)], writes=[pkY], nosync_same=True)
                ev_copy(p, cx.ev_eng(), yb[:, kc, cs_], psY[:, 0:128], [pkY], [("yb", kc)])

            stageA(0)
            for hb in range(4):
                if hb < 3:
                    stageA(hb + 1)
                stageB(hb)
            for g in range(4):
                psS, pkS = cx.next_ps()
                p.op("pe", lambda e: e.matmul(psS[:], lhsT=btm[:, g, :], rhs=xw[:, g * 8:(g + 1) * 8, :].rearrange("q h d -> q (h d)"),
                                              start=True, stop=True),
                     reads=["btm", ("xw", g)], writes=[pkS])
                sg_ = state[:, g * 8:(g + 1) * 8, :]
                p.op("pool", lambda e: e.tensor_tensor(out=sg_, in0=sg_, in1=ea[:, g * 8:(g + 1) * 8].unsqueeze(2).to_broadcast([128, 8, 64]), op=ALU.mult),
                     reads=[("st", g), "ea"], writes=[("st", g)])
                p.op("dve", lambda e: e.tensor_tensor(out=sg_, in0=sg_, in1=psS[:].rearrange("q (h d) -> q h d", d=64), op=ALU.add),
                     reads=[("st", g), pkS], writes=[("st", g)])
                p.op("act", lambda e: e.activation(out=stbf[:, g * 8:(g + 1) * 8, :], in_=state[:, g * 8:(g + 1) * 8, :], func=AF.Copy),
                     reads=[("st", g)], writes=[("stb", g)])

        for g0 in range(0, 16, 4):
            for kc in range(g0, g0 + 4):
                p.op("dve", lambda e: e.scalar_tensor_tensor(out=yb[:, kc, :], in0=xc[:, kc, :], scalar=dsk[:, kc:kc + 1],
                                                             in1=yb[:, kc, :], op0=ALU.mult, op1=ALU.add),
                     reads=[("xc", kc), "dsk", ("yb", kc)], writes=[("yb", kc)])
            for kc in range(g0, g0 + 4):
                t, tk = ctmp[kc % 4], ("ctmp", kc % 4)
                p.op("act", lambda e: e.activation(out=t[:], in_=zb[:, kc, :], func=AF.Silu), reads=[(f"zb{par}", kc)], writes=[tk])
            for kc in range(g0, g0 + 4):
                t, tk = ctmp[kc % 4], ("ctmp", kc % 4)
                p.op("dve", lambda e: e.tensor_tensor(out=yb[:, kc, :], in0=yb[:, kc, :], in1=t[:], op=ALU.mult),
                     reads=[("yb", kc), tk], writes=[("yb", kc)])
        for g in range(4):
            for kk in range(4):
                kc = g * 4 + kk
                p.op("act", lambda e: e.activation(out=sq[:, kk, :], in_=yb[:, kc, :], func=AF.Square),
                     reads=[("yb", kc)], writes=[("sq", kk)])
            ps, pk = cx.next_ps()
            for kk in range(4):
                p.op("pe", lambda e: e.matmul(ps[:, 0:TS], lhsT=cx.ones[:], rhs=sq[:, kk, :], start=(kk == 0), stop=(kk == 3)),
                     reads=["ones", ("sq", kk)], writes=[pk], nosync_same=True)
            rs, rstd = stat
            p.op("act", lambda e: e.activation(out=rs[:], in_=ps[:, 0:TS], func=AF.Sqrt, scale=1.0 / 512, bias=epsb[:, 0:1]),
                 reads=[pk, "epsb"], writes=["rs"])
            p.op("dve", lambda e: e.reciprocal(out=rstd[:], in_=rs[:]), reads=["rs"], writes=["rstd"])
            for kk in range(4):
                kc = g * 4 + kk
                o = ost[kc % 2]
                ok = ("ost", kc % 2)
                p.op("dve", lambda e: e.scalar_tensor_tensor(out=o[:], in0=yb[:, kc, :], scalar=nw[:, kc:kc + 1], in1=rstd[:],
                                                             op0=ALU.mult, op1=ALU.mult),
                     reads=[("yb", kc), "nw", "rstd"], writes=[ok])
                p.dma("pool", yTv[:, kc, t0:t0 + TS], o[:], reads=[ok])
    if standalone:
        p.finish()
    return p


def run_ssd(p1T_b, prm):
    p = _prog("ssd", build_ssd)
    ident = np.eye(128, dtype=np.float32)
    selh = np.zeros((32, 32, 128), np.float32)
    for h in range(32):
        selh[h, h, :] = 1.0
    maskT = np.triu(np.ones((128, 128), np.float32))
    in_maps = []
    for core in range(8):
        b, hh = core // 2, core % 2
        P = p1T_b[b]
        chx = slice(4096 + hh * 2048, 4096 + (hh + 1) * 2048)
        z = P[hh * 2048:(hh + 1) * 2048]
        x = P[chx]
        Bm = P[8192 + hh * 512: 8192 + (hh + 1) * 512]
        Cm = P[9216 + hh * 512: 9216 + (hh + 1) * 512]
        dt = P[10240 + hh * 32: 10240 + (hh + 1) * 32]
        pad = lambda a: np.ascontiguousarray(np.concatenate([np.zeros((a.shape[0], 3), np.float32), a], axis=1))
        cw = prm["ssm_conv_w"]
        cb = prm["ssm_conv_b"]
        cix = np.arange(hh * 2048, (hh + 1) * 2048)
        cibc = np.concatenate([4096 + hh * 512 + np.arange(512), 5120 + hh * 512 + np.arange(512)])
        hs = slice(hh * 32, (hh + 1) * 32)
        m = {
            "zT": np.ascontiguousarray(z), "xT": pad(x), "bcT": pad(np.concatenate([Bm, Cm], axis=0)),
            "dtT": np.ascontiguousarray(dt),
            "cwx": np.ascontiguousarray(cw[:, cix].T.reshape(16, 128, 4).transpose(1, 0, 2)),
            "cbx": fm(cb[cix]),
            "cwbc": np.ascontiguousarray(cw[:, cibc].T.reshape(8, 128, 4).transpose(1, 0, 2)),
            "cbbc": fm(cb[cibc]),
            "hp": np.ascontiguousarray(np.stack([prm["ssm_dt_bias"][hs], prm["ssm_a_log"][hs]], axis=1)),
            "dsk": fm(np.repeat(prm["ssm_d"][hs], 64)),
            "nw": fm(prm["ssm_norm_w"][hh * 2048:(hh + 1) * 2048]),
            "ident": ident, "selh": selh, "maskT": maskT,
        }
        in_maps.append(m)
    r = _run(p, in_maps)
    return [np.asarray(r[i]["yT"]) for i in range(8)]


class PsQ:
    def __init__(self, p, nbanks, name="pq"):
        self.banks = [p.psum(f"{name}{i}", [128, 512], F32) for i in range(nbanks)]
        self.n = nbanks
        self.i = 0
        self.name = name

    def nextbank(self):
        i = self.i % self.n
        self.i += 1
        return self.banks[i], (self.name, i)


INV_DT = BF16


class InvBufs:
    def __init__(self, p, name, dt=F32):
        mk = lambda s_, sh: p.sbuf(f"{name}_{s_}", sh, dt)
        self.name = name
        self.NN, self.PP, self.QQ, self.XX = (mk(x, [128, 2, 128]) for x in ("NN", "PP", "QQ", "XX"))
        self.O, self.WT = mk("O", [128, 128]), mk("WT", [128, 128])

    def k(self, s_):
        return (self.name, s_)


def _kl(k):
    return list(k) if isinstance(k, list) else [k]


def emit_inverse(p, pq, chains, masks, ident2):
    m16, o32, o64 = masks["m16x2"], masks["o32"], masks["o64"]

    def mm(out_ps, pk, lhsT, lk, rhs, rk):
        p.op("pe", lambda e: e.matmul(out_ps, lhsT=lhsT, rhs=rhs, start=True, stop=True), reads=[lk, rk], writes=[pk],
             nosync_same=True)

    def rounds(fn_mm, fn_ev):
        for s0 in range(0, len(chains), pq.n):
            pend = []
            for ch in chains[s0:s0 + pq.n]:
                bk, k = pq.nextbank()
                fn_mm(ch, bk, k)
                pend.append((ch, bk, k))
            for ci, (ch, bk, k) in enumerate(pend):
                fn_ev(ci, ch, bk, k)

    f2 = lambda t: t[:].rearrange("q a c -> q (a c)")
    for ci, (b, AA, AAk) in enumerate(chains):
        p.op("dve", lambda e: e.scalar_tensor_tensor(out=f2(b.NN), in0=f2(AA), scalar=-1.0, in1=f2(m16), op0=ALU.mult, op1=ALU.mult),
             reads=_kl(AAk) + ["m16x2"], writes=[b.k("NN")])
        p.op("pool", lambda e: e.tensor_tensor(out=f2(b.XX), in0=f2(b.NN), in1=f2(ident2), op=ALU.add),
             reads=[b.k("NN"), "ident2"], writes=[b.k("XX")])
    names = ["NN", "PP", "QQ", "PP"]
    for lvl in range(3):
        src, dst = names[lvl], names[lvl + 1]

        def sq_mm(ch, bk, k):
            b = ch[0]
            S = getattr(b, src)
            mm(bk[:, 0:128], k, S[:, 1, :], b.k(src), S[:, 0, :], b.k(src))
            mm(bk[:, 128:256], k, S[:, 0, :], b.k(src), S[:, 1, :], b.k(src))

        def sq_ev(ci, ch, bk, k):
            b = ch[0]
            ev_copy(p, "act" if ci % 2 == 0 else "dve", f2(getattr(b, dst)), bk[:, 0:256], [k], [b.k(dst)])
        rounds(sq_mm, sq_ev)

        def pr_mm(ch, bk, k):
            b = ch[0]
            Pm = getattr(b, dst)
            mm(bk[:, 0:128], k, Pm[:, 1, :], b.k(dst), b.XX[:, 0, :], b.k("XX"))
            mm(bk[:, 128:256], k, Pm[:, 0, :], b.k(dst), b.XX[:, 1, :], b.k("XX"))

        def pr_ev(ci, ch, bk, k):
            b = ch[0]
            p.op("dve", lambda e: e.tensor_tensor(out=f2(b.XX), in0=f2(b.XX), in1=bk[:, 0:256], op=ALU.add),
                 reads=[b.k("XX"), k], writes=[b.k("XX")])
        rounds(pr_mm, pr_ev)
    for li, om in enumerate((o32, o64)):
        def w_mm(ch, bk, k):
            b, AA, AAk = ch
            p.op("pool", lambda e: e.tensor_tensor(out=b.O[:], in0=AA[:, 0, :], in1=om[:], op=ALU.mult),
                 reads=_kl(AAk) + ["o32" if li == 0 else "o64"], writes=[b.k("O")])
            mm(bk[:, 0:128], k, b.O[:], b.k("O"), b.XX[:, 1, :], b.k("XX"))

        def w_ev(ci, ch, bk, k):
            b = ch[0]
            ev_copy(p, "act", b.WT[:], bk[:, 0:128], [k], [b.k("WT")])
        rounds(w_mm, w_ev)

        def z_mm(ch, bk, k):
            b = ch[0]
            mm(bk[:, 0:128], k, b.WT[:], b.k("WT"), b.XX[:, 0, :], b.k("XX"))
            mm(bk[:, 128:256], k, b.XX[:, 0, :], b.k("XX"), b.WT[:], b.k("WT"))

        def z_ev(ci, ch, bk, k):
            b = ch[0]
            p.op("dve", lambda e: e.tensor_tensor(out=f2(b.XX), in0=f2(b.XX), in1=bk[:, 0:256], op=ALU.subtract),
                 reads=[b.k("XX"), k], writes=[b.k("XX")])
        rounds(z_mm, z_ev)


def inv_masks_np():
    i = np.arange(128)
    same = lambda n: ((i[:, None] // n) == (i[None, :] // n)).astype(np.float32)
    m16 = same(16)
    o32 = same(32) - same(16)
    o64 = same(64) - same(32)
    return np.ascontiguousarray(np.stack([m16, m16], axis=1)), o32, o64


def build_gdn(p=None, io=None):
    standalone = p is None
    if standalone:
        p = Prog()
    pre = "" if standalone else "gdn_"
    C = 64
    NCI = TT // C
    qkvT = io["qkvT"] if io else p.dram("qkvT", [1536, SEQ + 3], F32, "ExternalInput")
    ztm = None if io else p.dram("ztm", [SEQ, 512], F32, "ExternalInput")
    zfm = io["zfm"] if io else None
    bT = io["bT"] if io else p.dram("bT", [4, SEQ], F32, "ExternalInput")
    aT = io["aT"] if io else p.dram("aT", [4, SEQ], F32, "ExternalInput")
    cw_d = p.dram(pre + "cw", [128, 12, 4], F32, "ExternalInput")
    hp_d = p.dram(pre + "hp", [4, 2], F32, "ExternalInput")
    nwrow_d = p.dram(pre + "nwrow", [128, 128], F32, "ExternalInput")
    m2_d = p.dram(pre + "m2", [2, 128, 2, 128], F32, "ExternalInput")
    mk_d = p.dram(pre + "mk", [5, 128, 128], F32, "ExternalInput")
    selrow_d = p.dram(pre + "selrow", [4, 4, 128], F32, "ExternalInput")
    i4p_d = p.dram(pre + "i4p", [4, 4, 2], F32, "ExternalInput")
    yb = io["yb"] if io else p.dram("yb", [SEQ, 512], F32, "ExternalOutput")

    def const(name, shape, src, dt=F32):
        t = p.sbuf(name, shape, dt)
        p.dma("pool", t[:], src, writes=[name])
        return t
    cw = const("cw", [128, 12, 4], cw_d)
    hp = const("hp", [4, 2], hp_d)
    nwrow = const("nwrow", [128, 128], nwrow_d)
    m16x2 = const("m16x2", [128, 2, 128], m2_d[0])
    ident2 = const("ident2", [128, 2, 128], m2_d[1])
    o32 = const("o32", [128, 128], mk_d[0])
    o64 = const("o64", [128, 128], mk_d[1])
    maskSL = const("maskSL", [128, 128], mk_d[2])
    maskUT = const("maskUT", [128, 128], mk_d[3])
    ident = const("ident", [128, 128], mk_d[4])
    selrow = const("selrow", [4, 4, 128], selrow_d)
    i4p = const("i4p", [4, 4, 2], i4p_d)
    identb = p.sbuf("identb", [128, 128], BF16)
    p.op("dve", lambda e: e.tensor_copy(out=identb[:], in_=ident[:]), reads=["ident"], writes=["identb"])
    onesb = p.sbuf("onesb", [128, 128], BF16)
    p.op("dve", lambda e: e.memset(onesb[:], 1.0), writes=["onesb"])
    ones4 = p.sbuf("ones4", [4, C], F32)
    p.op("dve", lambda e: e.memset(ones4[:], 1.0), writes=["ones4"])
    eps6 = p.sbuf("eps6", [128, 1], F32)
    p.op("dve", lambda e: e.memset(eps6[:], 1e-6), writes=["eps6"])
    eps5 = p.sbuf("eps5", [128, 1], F32)
    p.op("dve", lambda e: e.memset(eps5[:], EPS), writes=["eps5"])
    Aneg = p.sbuf("Aneg", [4, 1], F32)
    p.op("act", lambda e: e.activation(out=Aneg[:], in_=hp[:, 0:1], func=AF.Exp), reads=["hp"], writes=["Aneg"])
    p.op("dve", lambda e: e.tensor_scalar(out=Aneg[:], in0=Aneg[:], scalar1=-1.0, scalar2=None, op0=ALU.mult),
         reads=["Aneg"], writes=["Aneg"])

    pq = PsQ(p, 4, "pq")
    pg = PsQ(p, 3, "pg")
    psb = p.psum("psb16", [128, 1024], BF16)

    qin = p.sbuf("qin", [128, 12, TT + 3], F32)
    ctmp = [p.sbuf(f"ctmp{i}", [128, TT], F32) for i in range(4)]
    sqb = [p.sbuf(f"sqb{i}", [128, TT], BF16) for i in range(2)]
    rnt = [p.sbuf(f"rnt{i}", [128, TT], F32) for i in range(2)]
    xnp = [[p.sbuf(f"xnp{w}{pr}", [128, NCI, 2, C], BF16) for pr in range(2)] for w in range(3)]
    braw = p.sbuf("braw", [4, TT], F32)
    araw = p.sbuf("araw", [4, TT], F32)
    beta = p.sbuf("beta", [4, TT], F32)
    gv = p.sbuf("gv", [4, TT], F32)
    gc = p.sbuf("gc", [4, TT], F32)
    egc = p.sbuf("egc", [4, TT], F32)
    ekd = p.sbuf("ekd", [4, TT], F32)
    NPC = 2 * NCI
    tm = [p.sbuf(f"tm{i}", [128, 8], F32) for i in range(NPC)]
    uS = [p.sbuf(f"uS{i}", [128, 128], F32) for i in range(NPC)]
    wT = [p.sbuf(f"wT{i}", [128, 128], BF16) for i in range(NPC)]
    qkT = [p.sbuf(f"qkT{i}", [128, 128], BF16) for i in range(NPC)]
    qgT = [p.sbuf(f"qgT{i}", [128, 128], BF16) for i in range(NPC)]
    Kd = [p.sbuf(f"Kd{i}", [128, 128], BF16) for i in range(NPC)]
    egl = [p.sbuf(f"egl{i}", [128, 2], F32) for i in range(NPC)]
    NTR = 8
    Dm = [p.sbuf(f"Dm{i}", [128, 128], F32) for i in range(NTR)]
    DTm = [p.sbuf(f"DTm{i}", [128, 128], F32) for i in range(NTR)]
    tA = [p.sbuf(f"tA{i}", [128, 128], F32) for i in range(NTR)]
    AA = [p.sbuf(f"AA{i}", [128, 2, 128], F32) for i in range(NTR)]
    Kb = [p.sbuf(f"Kb{i}", [128, 128], BF16) for i in range(NTR)]
    Vb = [p.sbuf(f"Vb{i}", [128, 128], BF16) for i in range(NTR)]
    XTb = [p.sbuf(f"XTb{i}", [128, 128], BF16) for i in range(NTR)]
    NIV = 8
    ivb = [InvBufs(p, f"iv{i}", INV_DT) for i in range(NIV)]
    zin = [p.sbuf(f"zin{pr}", [128, NCI, 128], F32) for pr in range(2)]
    if zfm is not None:
        zf_in = p.sbuf("zf_in", [128, 4, TT], F32)
        zsb = p.sbuf("zsb", [128, 4, TT], BF16)
    ost = [p.sbuf(f"ostg{pr}", [128, NCI, 128], F32) for pr in range(2)]
    Sf = [p.sbuf(f"Sf{pr}", [128, 2, 128], F32) for pr in range(2)]
    Sb = [p.sbuf(f"Sb{pr}", [128, 2, 128], BF16) for pr in range(2)]
    for pr in range(2):
        p.op("dve", lambda e: e.memset(Sf[pr][:], 0.0), writes=[("Sf", pr)])
        p.op("pool", lambda e: e.memset(Sb[pr][:], 0.0), writes=[("Sb", pr)])
    vnew = [p.sbuf(f"vnew{i}", [128, 128], BF16) for i in range(2)]
    ss = [p.sbuf(f"ss{i}", [128, 1], F32) for i in range(2)]
    junk = [p.sbuf(f"junk{i}", [128, 128], F32) for i in range(2)]
    yt = [p.sbuf(f"yt{i}", [128, 128], F32) for i in range(2)]
    masks = {"m16x2": m16x2, "o32": o32, "o64": o64}

    qkvTv = qkvT.rearrange("(kc q) t -> q kc t", q=128)
    tctr = 0
    for tl in range(SEQ // TT):
        t0 = tl * TT
        for k4 in range(3):
            p.dma("sp", qin[:, k4 * 4:k4 * 4 + 4, :], qkvTv[:, k4 * 4:k4 * 4 + 4, t0:t0 + TT + 3],
                  writes=[("qin", kc) for kc in range(k4 * 4, k4 * 4 + 4)])
        p.dma("sp", braw[:], bT[:, t0:t0 + TT], writes=["braw"])
        p.dma("sp", araw[:], aT[:, t0:t0 + TT], writes=["araw"])
        if zfm is None:
            for pr in range(2):
                for half in range(2):
                    col0 = (2 * pr + half) * 128
                    p.dma("sp", zin[pr][half * 64:(half + 1) * 64, :, :],
                          ztm[t0:t0 + TT, col0:col0 + 128].rearrange("(ci t) v -> t ci v", t=C),
                          writes=[("zin", pr)])
        else:
            p.dma("sp", zf_in[:], zfm.rearrange("(h q) t -> q h t", q=128)[:, :, t0:t0 + TT], writes=["zf_in"])
            p.op("act", lambda e: e.activation(out=zsb[:], in_=zf_in[:], func=AF.Silu), reads=["zf_in"], writes=["zsb"])
        for kc in range(12):
            if kc % 4 == 0:
                for k2 in range(kc, kc + 4):
                    t_, tk_ = ctmp[k2 % 4], ("ctmp", k2 % 4)
                    p.op("act", lambda e: e.activation(out=t_[:], in_=qin[:, k2, 0:TT], func=AF.Copy, scale=cw[:, k2, 0:1]),
                         reads=[("qin", k2), "cw"], writes=[tk_])
                for j in (1, 2, 3):
                    for k2 in range(kc, kc + 4):
                        t_, tk_ = ctmp[k2 % 4], ("ctmp", k2 % 4)
                        p.op("dve", lambda e: e.scalar_tensor_tensor(out=t_[:], in0=qin[:, k2, j:j + TT], scalar=cw[:, k2, j:j + 1],
                                                                     in1=t_[:], op0=ALU.mult, op1=ALU.add),
                             reads=[("qin", k2), tk_, "cw"], writes=[tk_])
            w_, hl = kc // 4, kc % 4
            pr, half = hl // 2, hl % 2
            t = ctmp[kc % 4]
            tk = ("ctmp", kc % 4)
            dst = xnp[w_][pr][:, :, half, :]
            dk = ("xnp", w_, pr)
            if w_ == 2:
                p.op("act", lambda e: e.activation(out=dst, in_=t[:].rearrange("q (a c) -> q a c", c=C), func=AF.Silu),
                     reads=[tk], writes=[dk])
                continue
            p.op("act", lambda e: e.activation(out=t[:], in_=t[:], func=AF.Silu), reads=[tk], writes=[tk])
            sq, sk = sqb[kc % 2], ("sqb", kc % 2)
            p.op("act", lambda e: e.activation(out=sq[:], in_=t[:], func=AF.Square), reads=[tk], writes=[sk])
            bk, k = pg.nextbank()
            p.op("pe", lambda e: e.matmul(bk[:], lhsT=onesb[:], rhs=sq[:], start=True, stop=True), reads=["onesb", sk], writes=[k])
            rn, rk = rnt[kc % 2], ("rnt", kc % 2)
            p.op("act", lambda e: e.activation(out=rn[:], in_=bk[:], func=AF.Sqrt, bias=eps6[:, 0:1]), reads=[k, "eps6"], writes=[rk])
            p.op("dve", lambda e: e.reciprocal(out=rn[:], in_=rn[:]), reads=[rk], writes=[rk])
            sc = 128 ** -0.5 if w_ == 0 else 1.0
            p.op("dve", lambda e: e.scalar_tensor_tensor(out=dst, in0=t[:].rearrange("q (a c) -> q a c", c=C), scalar=sc,
                                                         in1=rn[:].rearrange("q (a c) -> q a c", c=C), op0=ALU.mult, op1=ALU.mult),
                 reads=[tk, rk], writes=[dk])
        p.op("act", lambda e: e.activation(out=beta[:], in_=braw[:], func=AF.Sigmoid), reads=["braw"], writes=["beta"])
        p.op("act", lambda e: e.activation(out=gv[:], in_=araw[:], func=AF.Exp, bias=hp[:, 1:2]), reads=["araw", "hp"], writes=["gv"])
        p.op("act", lambda e: e.activation(out=gv[:], in_=gv[:], func=AF.Ln, bias=ones4[:, 0:1]), reads=["gv", "ones4"], writes=["gv"])
        p.op("dve", lambda e: e.tensor_scalar(out=gv[:], in0=gv[:], scalar1=Aneg[:, 0:1], scalar2=None, op0=ALU.mult),
             reads=["gv", "Aneg"], writes=["gv"])
        for ci in range(NCI):
            cs_ = slice(ci * C, (ci + 1) * C)
            p.op("dve", lambda e: e.tensor_tensor_scan(out=gc[:, cs_], data0=ones4[:, 0:C], data1=gv[:, cs_], initial=0.0,
                                                       op0=ALU.mult, op1=ALU.add),
                 reads=["gv", "ones4"], writes=["gc"])
            p.op("act", lambda e: e.activation(out=ekd[:, cs_], in_=gc[:, cs_], func=AF.Exp, scale=-1.0,
                                               bias=gc[:, (ci + 1) * C - 1:(ci + 1) * C]),
                 reads=["gc"], writes=["ekd"])
        p.op("act", lambda e: e.activation(out=egc[:], in_=gc[:], func=AF.Exp), reads=["gc"], writes=["egc"])
        for pr in range(2):
            if zfm is None:
                p.op("act", lambda e: e.activation(out=zin[pr][:], in_=zin[pr][:], func=AF.Silu), reads=[("zin", pr)], writes=[("zin", pr)])

        pcs = [(ci, pr) for ci in range(NCI) for pr in range(2)]
        for g0 in range(0, len(pcs), NIV):
            grp = pcs[g0:g0 + NIV]
            chains = []
            for gi, (ci, pr) in enumerate(grp):
                pc = ci * 2 + pr
                r = tctr % NTR
                tctr += 1
                hA, hB = 2 * pr, 2 * pr + 1
                cs_ = slice(ci * C, (ci + 1) * C)
                bk, k = pg.nextbank()
                for half in range(2):
                    for j, (qt, qk_) in enumerate(((beta, "beta"), (gc, "gc"), (egc, "egc"), (ekd, "ekd"))):
                        p.op("pe", lambda e: e.matmul(bk[half * 64:(half + 1) * 64, 2 * j:2 * j + 2], lhsT=qt[:, cs_],
                                                      rhs=i4p[:, 2 * pr + half, :], start=True, stop=True),
                             reads=[qk_, "i4p"], writes=[k], nosync_same=True)
                p.op("dve", lambda e: e.tensor_copy(out=tm[pc][:], in_=bk[:, 0:8]), reads=[k], writes=[("tm", pc)])
                p.op("dve", lambda e: e.tensor_tensor(out=tm[pc][:, 1:2], in0=tm[pc][:, 0:1], in1=tm[pc][:, 4:5], op=ALU.mult),
                     reads=[("tm", pc)], writes=[("tm", pc)])
                bk, k = pg.nextbank()
                for half in range(2):
                    p.op("pe", lambda e: e.matmul(bk[:, half * 64:(half + 1) * 64], lhsT=selrow[:, 2 * pr + half, :], rhs=gc[:, cs_],
                                                  start=True, stop=True),
                         reads=["selrow", "gc"], writes=[k], nosync_same=True)
                p.op("dve", lambda e: e.tensor_scalar(out=Dm[r][:], in0=bk[:, 0:128], scalar1=tm[pc][:, 2:3], scalar2=0.0,
                                                      op0=ALU.subtract, op1=ALU.max),
                     reads=[k, ("tm", pc)], writes=[("Dm", r)])
                p.op("dve", lambda e: e.tensor_scalar(out=DTm[r][:], in0=bk[:, 0:128], scalar1=tm[pc][:, 2:3], scalar2=0.0,
                                                      op0=ALU.subtract, op1=ALU.min),
                     reads=[k, ("tm", pc)], writes=[("DTm", r)])
                p.op("act", lambda e: e.activation(out=Dm[r][:], in_=Dm[r][:], func=AF.Exp, scale=-1.0), reads=[("Dm", r)], writes=[("Dm", r)])
                p.op("act", lambda e: e.activation(out=DTm[r][:], in_=DTm[r][:], func=AF.Exp), reads=[("DTm", r)], writes=[("DTm", r)])
                kn, qn, vn = xnp[1][pr], xnp[0][pr], xnp[2][pr]
                bk, k = pg.nextbank()
                for half in range(2):
                    p.op("pe", lambda e: e.matmul(bk[half * 64:(half + 1) * 64, 0:128], lhsT=kn[:, ci, half, :],
                                                  rhs=kn[:, ci, :, :].rearrange("q h t -> q (h t)"), start=True, stop=True),
                         reads=[("xnp", 1, pr)], writes=[k], nosync_same=True)
                    p.op("pe", lambda e: e.matmul(bk[half * 64:(half + 1) * 64, 128:256], lhsT=kn[:, ci, half, :],
                                                  rhs=qn[:, ci, :, :].rearrange("q h t -> q (h t)"), start=True, stop=True),
                         reads=[("xnp", 1, pr), ("xnp", 0, pr)], writes=[k], nosync_same=True)
                p.op("dve", lambda e: e.tensor_tensor(out=tA[r][:], in0=bk[:, 0:128], in1=Dm[r][:], op=ALU.mult),
                     reads=[k, ("Dm", r)], writes=[("tA", r)])
                p.op("dve", lambda e: e.scalar_tensor_tensor(out=AA[r][:, 0, :], in0=tA[r][:], scalar=tm[pc][:, 0:1], in1=maskSL[:],
                                                             op0=ALU.mult, op1=ALU.mult),
                     reads=[("tA", r), ("tm", pc), "maskSL"], writes=[("AA0", r)])
                p.op("dve", lambda e: e.tensor_tensor(out=tA[r][:], in0=bk[:, 128:256], in1=DTm[r][:], op=ALU.mult),
                     reads=[k, ("DTm", r), ("tA", r)], writes=[("tA", r)])
                p.op("pool", lambda e: e.tensor_tensor(out=qkT[pc][:], in0=tA[r][:], in1=maskUT[:], op=ALU.mult),
                     reads=[("tA", r), "maskUT"], writes=[("qkT", pc)])
                bk, k = pg.nextbank()
                p.op("pe", lambda e: e.transpose(out=bk[:, 0:128], in_=AA[r][:, 0, :], identity=ident[:]),
                     reads=[("AA0", r), "ident"], writes=[k])
                p.op("act", lambda e: e.activation(out=AA[r][:, 1, :], in_=bk[:, 0:128], func=AF.Copy), reads=[k], writes=[("AA1", r)])
                for half in range(2):
                    p.op("pe", lambda e: e.transpose(out=psb[half * 64:(half + 1) * 64, 0:128], in_=kn[:, ci, half, :], identity=identb[:]),
                         reads=[("xnp", 1, pr), "identb"], writes=["psb"], nosync_same=True)
                    p.op("pe", lambda e: e.transpose(out=psb[half * 64:(half + 1) * 64, 128:256], in_=vn[:, ci, half, :], identity=identb[:]),
                         reads=[("xnp", 2, pr), "identb"], writes=["psb"], nosync_same=True)
                p.op("dve", lambda e: e.tensor_scalar(out=Kb[r][:], in0=psb[:, 0:128], scalar1=tm[pc][:, 1:2], scalar2=None, op0=ALU.mult),
                     reads=["psb", ("tm", pc)], writes=[("Kb", r)])
                p.op("dve", lambda e: e.tensor_scalar(out=Kd[pc][:], in0=psb[:, 0:128], scalar1=tm[pc][:, 6:7], scalar2=None, op0=ALU.mult),
                     reads=["psb", ("tm", pc)], writes=[("Kd", pc)])
                p.op("dve", lambda e: e.tensor_scalar(out=Vb[r][:], in0=psb[:, 128:256], scalar1=tm[pc][:, 0:1], scalar2=None, op0=ALU.mult),
                     reads=["psb", ("tm", pc)], writes=[("Vb", r)])
                if zfm is not None:
                    for half in range(2):
                        p.op("pe", lambda e: e.transpose(out=psb[half * 64:(half + 1) * 64, 256:384], in_=zsb[:, 2 * pr + half, cs_],
                                                         identity=identb[:]),
                             reads=["zsb", "identb"], writes=["psb"], nosync_same=True)
                    p.op("dve", lambda e: e.tensor_copy(out=zin[pr][:, ci, :], in_=psb[:, 256:384]), reads=["psb"], writes=[("zin", pr)])
                bk, k = pg.nextbank()
                for half in range(2):
                    p.op("pe", lambda e: e.matmul(bk[:, half * 64:(half + 1) * 64], lhsT=selrow[:, 2 * pr + half, :], rhs=egc[:, cs_],
                                                  start=True, stop=True),
                         reads=["selrow", "egc"], writes=[k], nosync_same=True)
                p.op("dve", lambda e: e.tensor_tensor(out=qgT[pc][:], in0=qn[:, ci, :, :].rearrange("q h t -> q (h t)"), in1=bk[:, 0:128],
                                                      op=ALU.mult),
                     reads=[k, ("xnp", 0, pr)], writes=[("qgT", pc)])
                p.op("dve", lambda e: e.tensor_copy(out=egl[pc][:], in_=bk[:, 63:128:64]), reads=[k], writes=[("egl", pc)])
                chains.append((ivb[gi], AA[r], [("AA0", r), ("AA1", r)], pc, r))
            emit_inverse(p, pq, [(b_, aa_, ak_) for (b_, aa_, ak_, _pc, _r) in chains], masks, ident2)
            for (b_, aa_, ak_, pc, r) in chains:
                p.op("act", lambda e: e.activation(out=XTb[r][:], in_=b_.XX[:, 1, :], func=AF.Copy), reads=[b_.k("XX")], writes=[("XTb", r)])
                bk, k = pg.nextbank()
                p.op("pe", lambda e: e.matmul(bk[:, 0:128], lhsT=XTb[r][:], rhs=Vb[r][:], start=True, stop=True),
                     reads=[("XTb", r), ("Vb", r)], writes=[k], nosync_same=True)
                p.op("pe", lambda e: e.matmul(bk[:, 128:256], lhsT=Kb[r][:], rhs=XTb[r][:], start=True, stop=True),
                     reads=[("XTb", r), ("Kb", r)], writes=[k], nosync_same=True)
                p.op("act", lambda e: e.activation(out=uS[pc][:], in_=bk[:, 0:128], func=AF.Copy), reads=[k], writes=[("uS", pc)])
                p.op("act", lambda e: e.activation(out=wT[pc][:], in_=bk[:, 128:256], func=AF.Copy), reads=[k], writes=[("wT", pc)])

        for ci in range(NCI):
            for pr in range(2):
                pc = ci * 2 + pr
                vi = pc % 2
                bk, k = pg.nextbank()
                for half in range(2):
                    p.op("pe", lambda e: e.matmul(bk[half * 64:(half + 1) * 64, 0:128], lhsT=wT[pc][:, half * 64:(half + 1) * 64],
                                                  rhs=Sb[pr][:, half, :], start=True, stop=True),
                         reads=[("wT", pc), ("Sb", pr)], writes=[k], nosync_same=True)
                p.op("dve", lambda e: e.tensor_tensor(out=vnew[vi][:], in0=uS[pc][:], in1=bk[:, 0:128], op=ALU.subtract),
                     reads=[("uS", pc), k], writes=[("vnew", vi)])
                bko, ko = pg.nextbank()
                p.op("pe", lambda e: e.matmul(bko[:, 0:128], lhsT=qkT[pc][:], rhs=vnew[vi][:], start=True, stop=False),
                     reads=[("qkT", pc), ("vnew", vi)], writes=[ko], nosync_same=True)
                for half in range(2):
                    p.op("pe", lambda e: e.matmul(bko[half * 64:(half + 1) * 64, 0:128], lhsT=qgT[pc][:, half * 64:(half + 1) * 64],
                                                  rhs=Sb[pr][:, half, :], start=False, stop=True),
                         reads=[("qgT", pc), ("Sb", pr)], writes=[ko], nosync_same=True)
                bks2 = [pg.nextbank(), pg.nextbank()]
                for half in range(2):
                    bks, ks = bks2[half]
                    p.op("pe", lambda e: e.matmul(bks[:, 0:128], lhsT=Kd[pc][half * 64:(half + 1) * 64, :],
                                                  rhs=vnew[vi][half * 64:(half + 1) * 64, :], start=True, stop=True),
                         reads=[("Kd", pc), ("vnew", vi)], writes=[ks], nosync_same=True)
                for half in range(2):
                    bks, ks = bks2[half]
                    p.op("dve", lambda e: e.scalar_tensor_tensor(out=Sf[pr][:, half, :], in0=Sf[pr][:, half, :], scalar=egl[pc][:, half:half + 1],
                                                                 in1=bks[:, 0:128], op0=ALU.mult, op1=ALU.add),
                         reads=[("Sf", pr), ("egl", pc), ks], writes=[("Sf", pr)])
                p.op("act", lambda e: e.activation(out=Sb[pr][:], in_=Sf[pr][:], func=AF.Copy), reads=[("Sf", pr)], writes=[("Sb", pr)])
                p.op("act", lambda e: e.activation(out=junk[vi][:], in_=bko[:, 0:128], func=AF.Square, accum_out=ss[vi][:]),
                     reads=[ko], writes=[("ss", vi), ("junk", vi)])
                p.op("act", lambda e: e.activation(out=ss[vi][:], in_=ss[vi][:], func=AF.Sqrt, scale=1.0 / 128, bias=eps5[:, 0:1]),
                     reads=[("ss", vi), "eps5"], writes=[("ss", vi)])
                p.op("dve", lambda e: e.reciprocal(out=ss[vi][:], in_=ss[vi][:]), reads=[("ss", vi)], writes=[("ss", vi)])
                p.op("dve", lambda e: e.scalar_tensor_tensor(out=yt[vi][:], in0=bko[:, 0:128], scalar=ss[vi][:, 0:1], in1=nwrow[:],
                                                             op0=ALU.mult, op1=ALU.mult),
                     reads=[ko, ("ss", vi), "nwrow"], writes=[("yt", vi)])
                p.op("pool", lambda e: e.tensor_tensor(out=ost[pr][:, ci, :], in0=yt[vi][:], in1=zin[pr][:, ci, :], op=ALU.mult),
                     reads=[("yt", vi), ("zin", pr)], writes=[("ost", pr)])
        for pr in range(2):
            for half in range(2):
                col0 = (2 * pr + half) * 128
                p.dma("pool", yb[t0:t0 + TT, col0:col0 + 128].rearrange("(ci t) v -> t ci v", t=C),
                      ost[pr][half * 64:(half + 1) * 64, :, :], reads=[("ost", pr)])
    if standalone:
        p.finish()
    return p


def gdn_consts():
    m16x2, o32, o64 = inv_masks_np()
    I = np.eye(128, dtype=np.float32)
    i = np.arange(128)
    same64 = (i[:, None] // 64) == (i[None, :] // 64)
    maskSL = (same64 & (i[:, None] > i[None, :])).astype(np.float32)
    maskUT = (same64 & (i[None, :] >= i[:, None])).astype(np.float32)
    selrow = np.zeros((4, 4, 128), np.float32)
    i4p = np.zeros((4, 4, 2), np.float32)
    for h in range(4):
        selrow[h, h, :] = 1.0
        i4p[h, h, 0] = 1.0
    return {"m2": np.ascontiguousarray(np.stack([m16x2, np.stack([I, I], axis=1)])),
            "mk": np.ascontiguousarray(np.stack([o32, o64, maskSL, maskUT, I])), "selrow": selrow, "i4p": i4p}


def run_gdn(p0T_b, prm):
    p = _prog("gdn", build_gdn)
    cst = gdn_consts()
    in_maps = []
    G0 = 3328
    for core in range(8):
        b, hh = core // 2, core % 2
        P = p0T_b[b]
        sl = lambda w: P[G0 + w * 1024 + hh * 512: G0 + w * 1024 + (hh + 1) * 512]
        qkv = np.concatenate([sl(0), sl(1), sl(2)], axis=0)
        pad = np.ascontiguousarray(np.concatenate([np.zeros((1536, 3), np.float32), qkv], axis=1))
        cidx = np.concatenate([w * 1024 + hh * 512 + np.arange(512) for w in range(3)])
        cw = prm["gdn_conv_w"][:, cidx]
        m = {"qkvT": pad, "ztm": np.ascontiguousarray(sl(3).T),
             "bT": np.ascontiguousarray(P[G0 + 4096 + hh * 4: G0 + 4096 + hh * 4 + 4]),
             "aT": np.ascontiguousarray(P[G0 + 4104 + hh * 4: G0 + 4104 + hh * 4 + 4]),
             "cw": np.ascontiguousarray(cw.T.reshape(12, 128, 4).transpose(1, 0, 2)),
             "hp": np.ascontiguousarray(np.stack([prm["gdn_a_log"][hh * 4:hh * 4 + 4], prm["gdn_dt_bias"][hh * 4:hh * 4 + 4]], axis=1)),
             "nwrow": np.ascontiguousarray(np.broadcast_to(prm["gdn_norm_w"][None, :], (128, 128))),
             }
        m.update(cst)
        in_maps.append(m)
    r = _run(p, in_maps)
    return [np.asarray(r[i]["yb"]) for i in range(8)]


TTR = 256
PRM_COLS = {"mu_r": 0, "mu_k": 4, "mu_v": 8, "om_r": 12, "om_k": 16, "om_v": 20, "w0": 24, "a0": 28,
            "k_k": 32, "k_a": 36, "omka": 40, "r_k": 44}


def build_rwkv(p=None, io=None):
    standalone = p is None
    if standalone:
        p = Prog()
    pre = "" if standalone else "rwkv_"
    C = 64
    NCI = TTR // C
    NP_ = 4
    rkvT = io["rkvT"] if io else p.dram("rkvT", [1536, SEQ + 1], F32, "ExternalInput")
    lwT = io["lwT"] if io else p.dram("lwT", [64, SEQ + 1], F32, "ExternalInput")
    laT = io["laT"] if io else p.dram("laT", [64, SEQ + 1], F32, "ExternalInput")
    lgT = io["lgT"] if io else p.dram("lgT", [128, SEQ + 1], F32, "ExternalInput")
    prm_d = p.dram(pre + "prm", [128, 64], F32, "ExternalInput")
    lmu_d = p.dram(pre + "lmu", [128, 6], F32, "ExternalInput")
    w2_d = p.dram(pre + "w2", [64, 512], F32, "ExternalInput")
    a2_d = p.dram(pre + "a2", [64, 512], F32, "ExternalInput")
    g2_d = p.dram(pre + "g2", [128, 512], F32, "ExternalInput")
    lnw_d = p.dram(pre + "lnw", [128, 4, 64], F32, "ExternalInput")
    lnb_d = p.dram(pre + "lnb", [128, 4, 64], F32, "ExternalInput")
    m2_d = p.dram(pre + "m2", [4, 128, 2, 128], F32, "ExternalInput")
    mk_d = p.dram(pre + "mk", [4, 128, 128], F32, "ExternalInput")
    ya = io["ya"] if io else p.dram("ya", [SEQ, 512], F32, "ExternalOutput")

    def const(name, shape, src, dt=F32):
        t = p.sbuf(name, shape, dt)
        p.dma("pool", t[:], src, writes=[name])
        return t
    prm = const("prm", [128, 64], prm_d)
    lmu = const("lmu", [128, 6], lmu_d)
    w2f = const("w2f", [64, 512], w2_d)
    a2f = const("a2f", [64, 512], a2_d)
    g2f = const("g2f", [128, 512], g2_d)
    lnw = const("lnw", [128, 4, 64], lnw_d)
    lnb = const("lnb", [128, 4, 64], lnb_d)
    m16x2 = const("m16x2", [128, 2, 128], m2_d[0])
    ident2 = const("ident2", [128, 2, 128], m2_d[1])
    mAA = const("mAA", [128, 2, 128], m2_d[2])
    mUT2 = const("mUT2", [128, 2, 128], m2_d[3])
    o32 = const("o32", [128, 128], mk_d[0])
    o64 = const("o64", [128, 128], mk_d[1])
    maskSUT = const("maskSUT", [128, 128], mk_d[2])
    ident = const("ident", [128, 128], mk_d[3])
    masks = {"m16x2": m16x2, "o32": o32, "o64": o64}
    w2b = p.sbuf("w2b", [64, 512], BF16)
    a2b = p.sbuf("a2b", [64, 512], BF16)
    g2b = p.sbuf("g2b", [128, 512], BF16)
    identb = p.sbuf("identb", [128, 128], BF16)
    p.op("dve", lambda e: e.tensor_copy(out=w2b[:], in_=w2f[:]), reads=["w2f"], writes=["w2b"])
    p.op("dve", lambda e: e.tensor_copy(out=a2b[:], in_=a2f[:]), reads=["a2f"], writes=["a2b"])
    p.op("dve", lambda e: e.tensor_copy(out=g2b[:], in_=g2f[:]), reads=["g2f"], writes=["g2b"])
    p.op("dve", lambda e: e.tensor_copy(out=identb[:], in_=ident[:]), reads=["ident"], writes=["identb"])
    bd1 = p.sbuf("bd1", [128, 128], BF16)
    p.op("dve", lambda e: e.memset(bd1[:], 0.0), writes=["bd1"])
    p.op("dve", lambda e: e.memset(bd1[0:64, 0:64], 1.0), writes=["bd1"])
    p.op("dve", lambda e: e.memset(bd1[64:128, 64:128], 1.0), writes=["bd1"])
    ones2 = p.sbuf("ones2", [128, 2], BF16)
    p.op("dve", lambda e: e.memset(ones2[:], 1.0), writes=["ones2"])
    onesf = p.sbuf("onesf", [128, C], F32)
    p.op("dve", lambda e: e.memset(onesf[:], 1.0), writes=["onesf"])
    eps24 = p.sbuf("eps24", [128, 1], F32)
    p.op("dve", lambda e: e.memset(eps24[:], 1e-24), writes=["eps24"])
    epsln = p.sbuf("epsln", [128, 1], F32)
    p.op("dve", lambda e: e.memset(epsln[:], 64e-5), writes=["epsln"])
    PCOL = lambda name, pr: prm[:, PRM_COLS[name] + pr:PRM_COLS[name] + pr + 1]

    pq = PsQ(p, 4, "pq")
    pg = PsQ(p, 3, "pg")
    psb = p.psum("psb16", [128, 1024], BF16)

    rin = p.sbuf("rin", [128, 12, TTR + 1], F32)
    lwin = p.sbuf("lwin", [64, TTR + 1], F32)
    lain = p.sbuf("lain", [64, TTR + 1], F32)
    lgin = p.sbuf("lgin", [128, TTR + 1], F32)
    twb = p.sbuf("twb", [64, TTR], BF16)
    lab = p.sbuf("lab", [64, TTR], BF16)
    sgb = p.sbuf("sgb", [128, TTR], BF16)
    sqb = p.sbuf("sqb", [128, TTR], BF16)
    FT = {n: p.sbuf("ft_" + n, [128, TTR], F32) for n in
          ("rs", "ks", "vs", "logd", "av", "G", "kk", "rn", "kf", "bv", "eG", "eGm", "enG", "lw", "la", "lg")}
    ML = {}
    for nm in ("Rm", "KKm", "Km", "Bm", "Vm", "Pm"):
        ML[nm] = [p.sbuf(f"{nm}{pr}", [128, NCI, 2, C], BF16) for pr in range(NP_)]
        for pr in range(NP_):
            p.op("pool", lambda e: e.memset(ML[nm][pr][:], 0.0), writes=[(nm, pr)])
    eGC = [p.sbuf(f"eGC{pr}", [128, NCI], F32) for pr in range(NP_)]
    NPC = NP_ * NCI
    AakT = [p.sbuf(f"AakT{i}", [128, 128], BF16) for i in range(NPC)]
    Arx = [p.sbuf(f"Arx{i}", [128, 2, 128], BF16) for i in range(NPC)]
    XTb = [p.sbuf(f"XTb{i}", [128, 128], BF16) for i in range(NPC)]
    KBt = [p.sbuf(f"KBt{i}", [128, 2, 128], BF16) for i in range(NPC)]
    Vc = [p.sbuf(f"Vc{i}", [128, C], BF16) for i in range(NPC)]
    bon = [p.sbuf(f"bon{i}", [128, 2], F32) for i in range(NPC)]
    gtm = [p.sbuf(f"gtm{i}", [128, C], F32) for i in range(NPC)]
    NTR = 8
    AA = [p.sbuf(f"AA{i}", [128, 2, 128], F32) for i in range(NTR)]
    ivb = [InvBufs(p, f"iv{i}", INV_DT) for i in range(8)]
    Zf = [p.sbuf(f"Zf{pr}", [128, C], F32) for pr in range(NP_)]
    Zb = [p.sbuf(f"Zb{pr}", [128, C], BF16) for pr in range(NP_)]
    for pr in range(NP_):
        p.op("dve", lambda e: e.memset(Zf[pr][:], 0.0), writes=[("Zf", pr)])
        p.op("pool", lambda e: e.memset(Zb[pr][:], 0.0), writes=[("Zb", pr)])
    r1 = [p.sbuf(f"r1_{pr}", [128, C], BF16) for pr in range(NP_)]
    Us = [p.sbuf(f"Us{pr}", [128, C], BF16) for pr in range(NP_)]
    Ys = [p.sbuf(f"Ys{pr}", [128, C], F32) for pr in range(NP_)]
    yn = [p.sbuf(f"yn{pr}", [128, C], F32) for pr in range(NP_)]
    bst = [p.sbuf(f"bst{pr}", [128, 6], F32) for pr in range(NP_)]
    mv = [p.sbuf(f"mv{pr}", [128, 2], F32) for pr in range(NP_)]
    ostg = [p.sbuf(f"ostg{pr}", [128, NCI, C], F32) for pr in range(NP_)]

    f2 = lambda t: t[:].rearrange("q a c -> q (a c)")
    rkvTv = rkvT.rearrange("(kc q) t -> q kc t", q=128)
    tctr = 0
    for tl in range(SEQ // TTR):
        t0 = tl * TTR
        for k4 in range(3):
            p.dma("sp", rin[:, k4 * 4:k4 * 4 + 4, :], rkvTv[:, k4 * 4:k4 * 4 + 4, t0:t0 + TTR + 1],
                  writes=[("rin", kc) for kc in range(k4 * 4, k4 * 4 + 4)])
        p.dma("sp", lwin[:], lwT[:, t0:t0 + TTR + 1], writes=["lwin"])
        p.dma("sp", lain[:], laT[:, t0:t0 + TTR + 1], writes=["lain"])
        p.dma("sp", lgin[:], lgT[:, t0:t0 + TTR + 1], writes=["lgin"])

        def lerp(out_ap, okey, src0, src1, skey, mu_ap, om_ap):
            p.op("pool", lambda e: e.tensor_scalar(out=out_ap, in0=src0, scalar1=mu_ap, scalar2=None, op0=ALU.mult),
                 reads=[skey, "prm", "lmu"], writes=[okey])
            p.op("dve", lambda e: e.scalar_tensor_tensor(out=out_ap, in0=src1, scalar=om_ap, in1=out_ap, op0=ALU.mult, op1=ALU.add),
                 reads=[skey, okey, "prm", "lmu"], writes=[okey])

        lerp(FT["lw"][0:64, :], "lw", lwin[:, 0:TTR], lwin[:, 1:TTR + 1], "lwin", lmu[0:64, 0:1], lmu[0:64, 1:2])
        lerp(FT["la"][0:64, :], "la", lain[:, 0:TTR], lain[:, 1:TTR + 1], "lain", lmu[0:64, 2:3], lmu[0:64, 3:4])
        lerp(FT["lg"][:, :], "lg", lgin[:, 0:TTR], lgin[:, 1:TTR + 1], "lgin", lmu[:, 4:5], lmu[:, 5:6])
        p.op("act", lambda e: e.activation(out=twb[:], in_=FT["lw"][0:64, :], func=AF.Tanh), reads=["lw"], writes=["twb"])
        p.op("act", lambda e: e.activation(out=lab[:], in_=FT["la"][0:64, :], func=AF.Copy), reads=["la"], writes=["lab"])
        p.op("act", lambda e: e.activation(out=sgb[:], in_=FT["lg"][:, :], func=AF.Sigmoid), reads=["lg"], writes=["sgb"])

        for pr in range(NP_):
            rs, ks, vs, logd, av, G, kk, rn, kf, bv, eG, eGm, enG = (FT[n] for n in
                ("rs", "ks", "vs", "logd", "av", "G", "kk", "rn", "kf", "bv", "eG", "eGm", "enG"))
            lerp(rs[:], "rs", rin[:, pr, 0:TTR], rin[:, pr, 1:TTR + 1], ("rin", pr), PCOL("mu_r", pr), PCOL("om_r", pr))
            lerp(ks[:], "ks", rin[:, 4 + pr, 0:TTR], rin[:, 4 + pr, 1:TTR + 1], ("rin", 4 + pr), PCOL("mu_k", pr), PCOL("om_k", pr))
            lerp(vs[:], "vs", rin[:, 8 + pr, 0:TTR], rin[:, 8 + pr, 1:TTR + 1], ("rin", 8 + pr), PCOL("mu_v", pr), PCOL("om_v", pr))
            bk, k = pg.nextbank()
            p.op("pe", lambda e: e.matmul(bk[:, 0:TTR], lhsT=w2b[:, pr * 128:(pr + 1) * 128], rhs=twb[:], start=True, stop=True),
                 reads=["w2b", "twb"], writes=[k])
            p.op("act", lambda e: e.activation(out=logd[:], in_=bk[:, 0:TTR], func=AF.Sigmoid, bias=PCOL("w0", pr)),
                 reads=[k, "prm"], writes=["logd"])
            p.op("pool", lambda e: e.tensor_scalar(out=logd[:], in0=logd[:], scalar1=-float(np.exp(-0.5)), scalar2=None, op0=ALU.mult),
                 reads=["logd"], writes=["logd"])
            bk, k = pg.nextbank()
            p.op("pe", lambda e: e.matmul(bk[:, 0:TTR], lhsT=a2b[:, pr * 128:(pr + 1) * 128], rhs=lab[:], start=True, stop=True),
                 reads=["a2b", "lab"], writes=[k])
            p.op("act", lambda e: e.activation(out=av[:], in_=bk[:, 0:TTR], func=AF.Sigmoid, bias=PCOL("a0", pr)),
                 reads=[k, "prm"], writes=["av"])
            for ci in range(NCI):
                cs_ = slice(ci * C, (ci + 1) * C)
                p.op("dve", lambda e: e.tensor_tensor_scan(out=G[:, cs_], data0=onesf[:, 0:C], data1=logd[:, cs_], initial=0.0,
                                                           op0=ALU.mult, op1=ALU.add),
                     reads=["logd", "onesf"], writes=["G"])
            p.op("pool", lambda e: e.tensor_scalar(out=kk[:], in0=ks[:], scalar1=PCOL("k_k", pr), scalar2=None, op0=ALU.mult),
                 reads=["ks", "prm"], writes=["kk"])
            p.op("act", lambda e: e.activation(out=sqb[:], in_=kk[:], func=AF.Square), reads=["kk"], writes=["sqb"])
            bk, k = pg.nextbank()
            p.op("pe", lambda e: e.matmul(bk[:, 0:TTR], lhsT=bd1[:], rhs=sqb[:], start=True, stop=True), reads=["bd1", "sqb"], writes=[k])
            p.op("act", lambda e: e.activation(out=rn[:], in_=bk[:, 0:TTR], func=AF.Sqrt, bias=eps24[:, 0:1]), reads=[k, "eps24"], writes=["rn"])
            p.op("dve", lambda e: e.reciprocal(out=rn[:], in_=rn[:]), reads=["rn"], writes=["rn"])
            p.op("dve", lambda e: e.tensor_tensor(out=kk[:], in0=kk[:], in1=rn[:], op=ALU.mult), reads=["kk", "rn"], writes=["kk"])
            p.op("dve", lambda e: e.tensor_scalar(out=kf[:], in0=av[:], scalar1=PCOL("k_a", pr), scalar2=PCOL("omka", pr),
                                                  op0=ALU.mult, op1=ALU.add),
                 reads=["av", "prm"], writes=["kf"])
            p.op("pool", lambda e: e.tensor_tensor(out=ks[:], in0=ks[:], in1=kf[:], op=ALU.mult), reads=["ks", "kf"], writes=["ks"])
            p.op("pool", lambda e: e.tensor_tensor(out=bv[:], in0=kk[:], in1=av[:], op=ALU.mult), reads=["kk", "av"], writes=["bv"])
            p.op("act", lambda e: e.activation(out=eG[:], in_=G[:], func=AF.Exp), reads=["G"], writes=["eG"])
            p.op("act", lambda e: e.activation(out=enG[:], in_=G[:], func=AF.Exp, scale=-1.0), reads=["G"], writes=["enG"])
            p.op("pool", lambda e: e.tensor_tensor(out=eGm[:], in0=G[:], in1=logd[:], op=ALU.subtract), reads=["G", "logd"], writes=["eGm"])
            p.op("act", lambda e: e.activation(out=eGm[:], in_=eGm[:], func=AF.Exp), reads=["eGm"], writes=["eGm"])
            p.op("dve", lambda e: e.tensor_copy(out=eGC[pr][:], in_=eG[:, C - 1:TTR:C]), reads=["eG"], writes=[("eGC", pr)])

            def masked(nm, fn_half):
                for half in range(2):
                    ps_ = slice(half * 64, (half + 1) * 64)
                    out_ap = ML[nm][pr][ps_, :, half, :]
                    fn_half("dve" if half == 0 else "pool", out_ap, ps_)
            v3 = lambda t, ps_: t[ps_, :].rearrange("q (a c) -> q a c", c=C)
            masked("Rm", lambda eng, o, ps_: p.op(eng, lambda e: e.tensor_tensor(out=o, in0=v3(rs, ps_), in1=v3(eG, ps_), op=ALU.mult),
                                                  reads=["rs", "eG"], writes=[("Rm", pr)]))
            masked("KKm", lambda eng, o, ps_: p.op(eng, lambda e: e.tensor_tensor(out=o, in0=v3(kk, ps_), in1=v3(eGm, ps_), op=ALU.mult),
                                                   reads=["kk", "eGm"], writes=[("KKm", pr)]))
            masked("Km", lambda eng, o, ps_: p.op(eng, lambda e: e.tensor_tensor(out=o, in0=v3(ks, ps_), in1=v3(enG, ps_), op=ALU.mult),
                                                  reads=["ks", "enG"], writes=[("Km", pr)]))
            masked("Bm", lambda eng, o, ps_: p.op(eng, lambda e: e.tensor_tensor(out=o, in0=v3(bv, ps_), in1=v3(enG, ps_), op=ALU.mult),
                                                  reads=["bv", "enG"], writes=[("Bm", pr)]))
            masked("Vm", lambda eng, o, ps_: p.op(eng, lambda e: e.tensor_copy(out=o, in_=v3(vs, ps_)),
                                                  reads=["vs"], writes=[("Vm", pr)]))
            p.op("dve", lambda e: e.scalar_tensor_tensor(out=kf[:], in0=rs[:], scalar=PCOL("r_k", pr), in1=ks[:], op0=ALU.mult, op1=ALU.mult),
                 reads=["rs", "ks", "prm", "kf"], writes=["kf"])
            masked("Pm", lambda eng, o, ps_: p.op(eng, lambda e: e.tensor_copy(out=o, in_=v3(kf, ps_)),
                                                  reads=["kf"], writes=[("Pm", pr)]))

        pcs = [(ci, pr) for ci in range(NCI) for pr in range(NP_)]
        for g0 in range(0, len(pcs), 8):
            grp = pcs[g0:g0 + 8]
            chains = []
            for gi, (ci, pr) in enumerate(grp):
                pc = ci * NP_ + pr
                r = tctr % NTR
                tctr += 1
                cs_ = slice(ci * C, (ci + 1) * C)
                m = lambda nm: ML[nm][pr][:, ci, :, :].rearrange("q h t -> q (h t)")
                bk, k = pg.nextbank()
                p.op("pe", lambda e: e.matmul(bk[:, 0:128], lhsT=m("KKm"), rhs=m("Bm"), start=True, stop=True),
                     reads=[("KKm", pr), ("Bm", pr)], writes=[k], nosync_same=True)
                p.op("pe", lambda e: e.matmul(bk[:, 128:256], lhsT=m("Bm"), rhs=m("KKm"), start=True, stop=True),
                     reads=[("KKm", pr), ("Bm", pr)], writes=[k], nosync_same=True)
                p.op("dve", lambda e: e.tensor_tensor(out=f2(AA[r]), in0=bk[:, 0:256], in1=f2(mAA), op=ALU.mult),
                     reads=[k, "mAA"], writes=[("AA", r)])
                bk, k = pg.nextbank()
                p.op("pe", lambda e: e.matmul(bk[:, 0:128], lhsT=m("Km"), rhs=m("KKm"), start=True, stop=True),
                     reads=[("KKm", pr), ("Km", pr)], writes=[k], nosync_same=True)
                p.op("pe", lambda e: e.matmul(bk[:, 128:256], lhsT=m("Km"), rhs=m("Rm"), start=True, stop=True),
                     reads=[("Rm", pr), ("Km", pr)], writes=[k], nosync_same=True)
                p.op("pe", lambda e: e.matmul(bk[:, 256:384], lhsT=m("Bm"), rhs=m("Rm"), start=True, stop=True),
                     reads=[("Rm", pr), ("Bm", pr)], writes=[k], nosync_same=True)
                p.op("dve", lambda e: e.tensor_tensor(out=AakT[pc][:], in0=bk[:, 0:128], in1=maskSUT[:], op=ALU.mult),
                     reads=[k, "maskSUT"], writes=[("AakT", pc)])
                p.op("dve", lambda e: e.tensor_tensor(out=f2(Arx[pc]), in0=bk[:, 128:384], in1=f2(mUT2), op=ALU.mult),
                     reads=[k, "mUT2"], writes=[("Arx", pc)])
                for j, nm in enumerate(("Vm", "Km", "Bm")):
                    p.op("pe", lambda e: e.transpose(out=psb[:, j * 128:(j + 1) * 128], in_=m(nm), identity=identb[:]),
                         reads=[(nm, pr), "identb"], writes=["psb"], nosync_same=True)
                for half in range(2):
                    ps_ = slice(half * 64, (half + 1) * 64)
                    p.op("act", lambda e: e.activation(out=Vc[pc][ps_, :], in_=psb[ps_, half * 64:(half + 1) * 64], func=AF.Copy),
                         reads=["psb"], writes=[("Vc", pc)])
                p.op("act", lambda e: e.activation(out=f2(KBt[pc]), in_=psb[:, 128:384], func=AF.Copy), reads=["psb"], writes=[("KBt", pc)])
                bk, k = pg.nextbank()
                for half in range(2):
                    hcol = (2 * pr + half) * 64
                    p.op("pe", lambda e: e.matmul(bk[half * 64:(half + 1) * 64, 0:64], lhsT=sgb[:, cs_], rhs=g2b[:, hcol:hcol + 64],
                                                  start=True, stop=True),
                         reads=["sgb", "g2b"], writes=[k], nosync_same=True)
                p.op("pe", lambda e: e.matmul(bk[:, 64:66], lhsT=m("Pm"), rhs=ones2[:], start=True, stop=True),
                     reads=[("Pm", pr), "ones2"], writes=[k], nosync_same=True)
                p.op("act", lambda e: e.activation(out=gtm[pc][:], in_=bk[:, 0:64], func=AF.Copy), reads=[k], writes=[("gtm", pc)])
                p.op("act", lambda e: e.activation(out=bon[pc][:], in_=bk[:, 64:66], func=AF.Copy), reads=[k], writes=[("bon", pc)])
                chains.append((ivb[gi], AA[r], ("AA", r), pc))
            emit_inverse(p, pq, [(b_, aa_, ak_) for (b_, aa_, ak_, _pc) in chains], masks, ident2)
            for (b_, aa_, ak_, pc) in chains:
                p.op("pool", lambda e: e.tensor_copy(out=XTb[pc][:], in_=b_.XX[:, 1, :]), reads=[b_.k("XX")], writes=[("XTb", pc)])

        for ci in range(NCI):
            PCI = lambda pr: ci * NP_ + pr
            m = lambda nm, pr: ML[nm][pr][:, ci, :, :].rearrange("q h t -> q (h t)")
            for s0 in range(0, NP_, pg.n):
                st = {}
                for pr in range(s0, min(NP_, s0 + pg.n)):
                    pc = PCI(pr)
                    bk, k = pg.nextbank()
                    p.op("pe", lambda e: e.matmul(bk[:, 0:C], lhsT=AakT[pc][:], rhs=Vc[pc][:], start=True, stop=False),
                         reads=[("AakT", pc), ("Vc", pc)], writes=[k], nosync_same=True)
                    p.op("pe", lambda e: e.matmul(bk[:, 0:C], lhsT=m("KKm", pr), rhs=Zb[pr][:], start=False, stop=True),
                         reads=[("KKm", pr), ("Zb", pr)], writes=[k], nosync_same=True)
                    st[pr] = (bk, k)
                for pr in st:
                    bk, k = st[pr]
                    p.op("act", lambda e: e.activation(out=r1[pr][:], in_=bk[:, 0:C], func=AF.Copy), reads=[k], writes=[("r1", pr)])
            for s0 in range(0, NP_, pg.n):
                st = {}
                for pr in range(s0, min(NP_, s0 + pg.n)):
                    pc = PCI(pr)
                    bk, k = pg.nextbank()
                    p.op("pe", lambda e: e.matmul(bk[:, 0:C], lhsT=XTb[pc][:], rhs=r1[pr][:], start=True, stop=True),
                         reads=[("XTb", pc), ("r1", pr)], writes=[k], nosync_same=True)
                    st[pr] = (bk, k)
                for pr in st:
                    bk, k = st[pr]
                    p.op("act", lambda e: e.activation(out=Us[pr][:], in_=bk[:, 0:C], func=AF.Copy, scale=-1.0), reads=[k], writes=[("Us", pr)])
            for pr in range(NP_):
                pc = PCI(pr)
                bky, ky = pg.nextbank()
                p.op("pe", lambda e: e.matmul(bky[:, 0:C], lhsT=m("Rm", pr), rhs=Zb[pr][:], start=True, stop=False),
                     reads=[("Rm", pr), ("Zb", pr)], writes=[ky], nosync_same=True)
                p.op("pe", lambda e: e.matmul(bky[:, 0:C], lhsT=Arx[pc][:, 0, :], rhs=Vc[pc][:], start=False, stop=False),
                     reads=[("Arx", pc), ("Vc", pc)], writes=[ky], nosync_same=True)
                p.op("pe", lambda e: e.matmul(bky[:, 0:C], lhsT=Arx[pc][:, 1, :], rhs=Us[pr][:], start=False, stop=True),
                     reads=[("Arx", pc), ("Us", pr)], writes=[ky], nosync_same=True)
                bkz, kz = pg.nextbank()
                p.op("pe", lambda e: e.matmul(bkz[:, 0:C], lhsT=KBt[pc][:, 0, :], rhs=Vc[pc][:], start=True, stop=False),
                     reads=[("KBt", pc), ("Vc", pc)], writes=[kz], nosync_same=True)
                p.op("pe", lambda e: e.matmul(bkz[:, 0:C], lhsT=KBt[pc][:, 1, :], rhs=Us[pr][:], start=False, stop=True),
                     reads=[("KBt", pc), ("Us", pr)], writes=[kz], nosync_same=True)
                p.op("act", lambda e: e.activation(out=Ys[pr][:], in_=bky[:, 0:C], func=AF.Copy), reads=[ky], writes=[("Ys", pr)])
                p.op("pool", lambda e: e.tensor_scalar(out=Zf[pr][:], in0=Zf[pr][:], scalar1=eGC[pr][:, ci:ci + 1], scalar2=None, op0=ALU.mult),
                     reads=[("Zf", pr), ("eGC", pr)], writes=[("Zf", pr)])
                p.op("dve", lambda e: e.scalar_tensor_tensor(out=Zf[pr][:], in0=bkz[:, 0:C], scalar=eGC[pr][:, ci:ci + 1], in1=Zf[pr][:],
                                                             op0=ALU.mult, op1=ALU.add),
                     reads=[kz, ("eGC", pr), ("Zf", pr)], writes=[("Zf", pr)])
                p.op("act", lambda e: e.activation(out=Zb[pr][:], in_=Zf[pr][:], func=AF.Copy), reads=[("Zf", pr)], writes=[("Zb", pr)])
            for pr in range(NP_):
                pc = PCI(pr)
                p.op("dve", lambda e: e.bn_stats(out=bst[pr][:], in_=Ys[pr][:]), reads=[("Ys", pr)], writes=[("bst", pr)])
                p.op("dve", lambda e: e.bn_aggr(out=mv[pr][:], in_=bst[pr][:]), reads=[("bst", pr)], writes=[("mv", pr)])
                p.op("act", lambda e: e.activation(out=mv[pr][:, 1:2], in_=mv[pr][:, 1:2], func=AF.Sqrt, bias=epsln[:, 0:1]),
                     reads=[("mv", pr), "epsln"], writes=[("mv", pr)])
                p.op("dve", lambda e: e.reciprocal(out=mv[pr][:, 1:2], in_=mv[pr][:, 1:2]), reads=[("mv", pr)], writes=[("mv", pr)])
                p.op("dve", lambda e: e.tensor_scalar(out=yn[pr][:], in0=Ys[pr][:], scalar1=mv[pr][:, 0:1], scalar2=mv[pr][:, 1:2],
                                                      op0=ALU.subtract, op1=ALU.mult),
                     reads=[("Ys", pr), ("mv", pr)], writes=[("yn", pr)])
                p.op("pool", lambda e: e.tensor_tensor(out=yn[pr][:], in0=yn[pr][:], in1=lnw[:, pr, :], op=ALU.mult),
                     reads=[("yn", pr), "lnw"], writes=[("yn", pr)])
                p.op("pool", lambda e: e.tensor_tensor(out=yn[pr][:], in0=yn[pr][:], in1=lnb[:, pr, :], op=ALU.add),
                     reads=[("yn", pr), "lnb"], writes=[("yn", pr)])
                p.op("dve", lambda e: e.scalar_tensor_tensor(out=yn[pr][:], in0=Vc[pc][:], scalar=bon[pc][:, 0:1], in1=yn[pr][:],
                                                             op0=ALU.mult, op1=ALU.add),
                     reads=[("Vc", pc), ("bon", pc), ("yn", pr)], writes=[("yn", pr)])
                p.op("pool", lambda e: e.tensor_tensor(out=ostg[pr][:, ci, :], in0=yn[pr][:], in1=gtm[pc][:], op=ALU.mult),
                     reads=[("yn", pr), ("gtm", pc)], writes=[("ostg", pr)])
        for pr in range(NP_):
            for half in range(2):
                col0 = (2 * pr + half) * 64
                p.dma("pool", ya[t0:t0 + TTR, col0:col0 + 64].rearrange("(ci t) v -> t ci v", t=C),
                      ostg[pr][half * 64:(half + 1) * 64, :, :], reads=[("ostg", pr)])
    if standalone:
        p.finish()
    return p


def rwkv_consts():
    m16x2, o32, o64 = inv_masks_np()
    I = np.eye(128, dtype=np.float32)
    i = np.arange(128)
    same64 = (i[:, None] // 64) == (i[None, :] // 64)
    maskSL = (same64 & (i[:, None] > i[None, :])).astype(np.float32)
    maskSUT = (same64 & (i[None, :] > i[:, None])).astype(np.float32)
    maskUT = (same64 & (i[None, :] >= i[:, None])).astype(np.float32)
    st = lambda a, b: np.stack([a, b], axis=1)
    return {"m2": np.ascontiguousarray(np.stack([m16x2, st(I, I), st(maskSL, maskSUT), st(maskUT, maskUT)])),
            "mk": np.ascontiguousarray(np.stack([o32, o64, maskSUT, I]))}


def run_rwkv(p0T_b, prm):
    p = _prog("rwkv", build_rwkv)
    cst = rwkv_consts()
    mu = prm["rwkv_mu"]
    in_maps = []
    for core in range(8):
        b, hh = core // 2, core % 2
        P = p0T_b[b]
        my = slice(hh * 512, (hh + 1) * 512)
        z1 = lambda a: np.ascontiguousarray(np.concatenate([np.zeros((a.shape[0], 1), np.float32), a], axis=1))
        rkv = np.concatenate([P[0:1024][my], P[1024:2048][my], P[2048:3072][my]], axis=0)
        cols = np.zeros((128, 64), np.float32)

        def put(name, vec512):
            cols[:, PRM_COLS[name]:PRM_COLS[name] + 4] = vec512.reshape(4, 128).T
        put("mu_r", mu[0:1024][my]); put("mu_k", mu[1024:2048][my]); put("mu_v", mu[2048:3072][my])
        put("om_r", 1 - mu[0:1024][my]); put("om_k", 1 - mu[1024:2048][my]); put("om_v", 1 - mu[2048:3072][my])
        put("w0", prm["rwkv_w0"][my]); put("a0", prm["rwkv_a0"][my]); put("k_k", prm["rwkv_k_k"][my])
        put("k_a", prm["rwkv_k_a"][my]); put("omka", 1 - prm["rwkv_k_a"][my]); put("r_k", prm["rwkv_r_k"].reshape(-1)[my])
        lmu = np.zeros((128, 6), np.float32)
        lmu[0:64, 0] = mu[3072:3136]; lmu[0:64, 1] = 1 - mu[3072:3136]
        lmu[0:64, 2] = mu[3136:3200]; lmu[0:64, 3] = 1 - mu[3136:3200]
        lmu[:, 4] = mu[3200:3328]; lmu[:, 5] = 1 - mu[3200:3328]

        def rows_tm(v512):
            a = v512.reshape(4, 2, 64)
            return np.ascontiguousarray(np.repeat(a.transpose(1, 0, 2)[:, None, :, :], 64, axis=1).reshape(128, 4, 64))
        m = {"rkvT": z1(rkv), "lwT": z1(P[3072:3136]), "laT": z1(P[3136:3200]), "lgT": z1(P[3200:3328]),
             "prm": cols, "lmu": lmu,
             "w2": np.ascontiguousarray(prm["rwkv_w2"][:, my]), "a2": np.ascontiguousarray(prm["rwkv_a2"][:, my]),
             "g2": np.ascontiguousarray(prm["rwkv_g2"][:, my]),
             "lnw": rows_tm(prm["rwkv_ln_w"][my]), "lnb": rows_tm(prm["rwkv_ln_b"][my])}
        m.update(cst)
        in_maps.append(m)
    r = _run(p, in_maps)
    return [np.asarray(r[i]["ya"]) for i in range(8)]


def kernel_unfused(x, c, ada_mix_w, ada_mix_b, ada_ffn_w, ada_ffn_b, hg_w_in, hg_w_out,
           rwkv_mu, rwkv_w0, rwkv_w2, rwkv_a0, rwkv_a2, rwkv_g2, rwkv_k_k, rwkv_k_a,
           rwkv_r_k, rwkv_ln_w, rwkv_ln_b, gdn_conv_w, gdn_a_log, gdn_dt_bias, gdn_norm_w,
           ssm_w_in, ssm_conv_w, ssm_conv_b, ssm_dt_bias, ssm_a_log, ssm_d, ssm_norm_w,
           ssm_w_out, ffn_w1, ffn_w3, ffn_w2, final_norm_w):
    f32 = lambda a: np.ascontiguousarray(np.asarray(a, dtype=np.float32))
    x, c = f32(x), f32(c)
    cw = cast_weights({"ada_mix_w": f32(ada_mix_w), "ada_ffn_w": f32(ada_ffn_w), "hg_w_in": f32(hg_w_in),
                       "hg_w_out": f32(hg_w_out), "ssm_w_in": f32(ssm_w_in), "ssm_w_out": f32(ssm_w_out),
                       "ffn_w1": f32(ffn_w1), "ffn_w3": f32(ffn_w3), "ffn_w2": f32(ffn_w2)})
    ada_mix_b, ada_ffn_b = f32(ada_mix_b), f32(ada_ffn_b)
    xT = [np.ascontiguousarray(x[i // 2, (i % 2) * TOK:(i % 2 + 1) * TOK].T) for i in range(8)]
    pT = run_inproj(xT, c, cw["ada_mix_w"][0], ada_mix_b[0], pad_cols(cw["hg_w_in"][0], 7552))
    p0T = [np.concatenate([pT[2 * b], pT[2 * b + 1]], axis=1) for b in range(4)]
    del pT
    prm_r = {"rwkv_mu": f32(rwkv_mu)[0], "rwkv_w0": f32(rwkv_w0)[0], "rwkv_w2": f32(rwkv_w2)[0], "rwkv_a0": f32(rwkv_a0)[0],
             "rwkv_a2": f32(rwkv_a2)[0], "rwkv_g2": f32(rwkv_g2)[0], "rwkv_k_k": f32(rwkv_k_k)[0], "rwkv_k_a": f32(rwkv_k_a)[0],
             "rwkv_r_k": f32(rwkv_r_k)[0], "rwkv_ln_w": f32(rwkv_ln_w)[0], "rwkv_ln_b": f32(rwkv_ln_b)[0]}
    ya = run_rwkv(p0T, prm_r)
    prm_g = {"gdn_conv_w": f32(gdn_conv_w)[0], "gdn_a_log": f32(gdn_a_log)[0], "gdn_dt_bias": f32(gdn_dt_bias)[0],
             "gdn_norm_w": f32(gdn_norm_w)[0]}
    yb = run_gdn(p0T, prm_g)
    del p0T
    yT = []
    for i in range(8):
        b, s = i // 2, i % 2
        ts = slice(s * TOK, (s + 1) * TOK)
        ym = np.concatenate([ya[2 * b][ts], ya[2 * b + 1][ts], yb[2 * b][ts], yb[2 * b + 1][ts]], axis=1)
        yT.append(np.ascontiguousarray(ym.T))
    del ya, yb
    x1T = run_outffn(xT, yT, c, cw["ada_mix_w"][0], ada_mix_b[0], cw["ada_ffn_w"][0], ada_ffn_b[0],
                     cw["hg_w_out"][0], cw["ffn_w1"][0], cw["ffn_w3"][0], cw["ffn_w2"][0])
    del xT, yT
    pT = run_inproj(x1T, c, cw["ada_mix_w"][1], ada_mix_b[1], pad_cols(cw["ssm_w_in"][0], 10368))
    p1T = [np.concatenate([pT[2 * b], pT[2 * b + 1]], axis=1) for b in range(4)]
    del pT
    prm_s = {"ssm_conv_w": f32(ssm_conv_w)[0], "ssm_conv_b": f32(ssm_conv_b)[0], "ssm_dt_bias": f32(ssm_dt_bias)[0],
             "ssm_a_log": f32(ssm_a_log)[0], "ssm_d": f32(ssm_d)[0], "ssm_norm_w": f32(ssm_norm_w)[0]}
    ys = run_ssd(p1T, prm_s)
    del p1T
    yT = []
    for i in range(8):
        b, s = i // 2, i % 2
        ts = slice(s * TOK, (s + 1) * TOK)
        yT.append(np.ascontiguousarray(np.concatenate([ys[2 * b][:, ts], ys[2 * b + 1][:, ts]], axis=0)))
    del ys
    oT = run_outffn(x1T, yT, c, cw["ada_mix_w"][1], ada_mix_b[1], cw["ada_ffn_w"][1], ada_ffn_b[1],
                    cw["ssm_w_out"][0], cw["ffn_w1"][1], cw["ffn_w3"][1], cw["ffn_w2"][1], fnw=f32(final_norm_w))
    out = np.empty((4, SEQ, D), np.float32)
    for i in range(8):
        out[i // 2, (i % 2) * TOK:(i % 2 + 1) * TOK] = oT[i].T
    return out


GSF = 4096
RG = [[0, 1], [2, 3], [4, 5], [6, 7]]


def pack_wg(W):
    K, N = W.shape
    Kc, ncc = K // 128, N // 128
    cpg = GSF // (Kc * 128)
    ng = (ncc + cpg - 1) // cpg
    Wp = np.zeros((K, ng * cpg * 128), dtype=np.float32)
    Wp[:, :N] = W
    A = Wp.reshape(Kc, 128, ng, cpg, 128).transpose(2, 1, 3, 0, 4).reshape(ng, 128, cpg * Kc * 128)
    out = np.zeros((ng, 128, GSF), dtype=np.float32)
    out[:, :, :cpg * Kc * 128] = A
    return out


class WStreamC:
    def __init__(self, p, ws_ap, seq, nst=2, nslot=3):
        self.p, self.ws, self.seq, self.nst, self.nslot = p, ws_ap, list(seq), nst, nslot
        self.st = [p.sbuf(f"wst{i}", [128, GSF], F32) for i in range(nst)]
        self.slots = [p.sbuf(f"wsl{i}", [128, GSF], BF16) for i in range(nslot)]
        self.issued = 0
        self.used = 0

    def _issue(self):
        p = self.p
        while self.issued < len(self.seq) and self.issued < self.used + self.nslot:
            g, u = self.seq[self.issued]
            a, s = self.issued % self.nst, self.issued % self.nslot
            p.dma("sp", self.st[a][:, :u], self.ws[g][:, :u], writes=[("wf", a)])
            if self.issued % 2 == 0:
                p.op("act", lambda e: e.activation(out=self.slots[s][:, :u], in_=self.st[a][:, :u], func=AF.Copy),
                     reads=[("wf", a)], writes=[("w", s)])
            else:
                p.op("pool", lambda e: e.tensor_copy(out=self.slots[s][:, :u], in_=self.st[a][:, :u]),
                     reads=[("wf", a)], writes=[("w", s)])
            self.issued += 1

    def next(self):
        self._issue()
        s = self.used % self.nslot
        self.used += 1
        return self.slots[s], ("w", s)


def emit_mod2(cx, wst, scb, bias, mod, jlist, name):
    p = cx.p
    assert len(jlist) % 2 == 0
    for g in range(len(jlist) // 2):
        slot, wk = wst.next()
        ps, pk = cx.next_ps()
        for jj in range(2):
            for kc in range(16):
                off = (jj * 16 + kc) * 128
                p.op("pe", lambda e: e.matmul(ps[:, jj:jj + 1], lhsT=slot[:, off:off + 128], rhs=scb[:, kc:kc + 1],
                                              start=(kc == 0), stop=(kc == 15)),
                     reads=[wk, "scb"], writes=[pk], nosync_same=True)
        j0 = jlist[g * 2]
        p.op("dve", lambda e: e.tensor_tensor(out=mod[:, j0:j0 + 2], in0=ps[:, 0:2], in1=bias[:, j0:j0 + 2], op=ALU.add),
             reads=[pk, name + "b"], writes=[name])


def build_fused():
    p = Prog()
    nc = p.nc
    NTL = SEQ // TT
    xT = p.dram("xT", [D, SEQ], F32, "ExternalInput")
    cs_d = p.dram("cs", [128, 16], F32, "ExternalInput")
    adab_d = p.dram("adab", [4, 128, 48], F32, "ExternalInput")
    fnw_d = p.dram("fnw", [128, 16], F32, "ExternalInput")
    NG = {"ada": 24, "in0": 16, "in1": 21, "out0": 4, "out1": 8, "w13": 22, "w2": 16}
    order = ["ada_m0", "ada_f0", "ada_m1", "ada_f1", "in0", "out0", "w13_0", "w2_0", "in1", "out1", "w13_1", "w2_1"]
    gbase, o = {}, 0
    for nm in order:
        gbase[nm] = o
        o += NG[nm.split("_")[0] if not nm.startswith("ada") else "ada"]
    ws = p.dram("ws", [o, 128, GSF], F32, "ExternalInput")
    oT = p.dram("oT", [D, SEQ], F32, "ExternalOutput")
    scr = lambda name, shape: nc.dram_tensor(name, list(shape), F32)
    p0s_t, p1s_t = scr("p0s", [31 * 128, SEQ + 4]), scr("p1s", [41 * 128, SEQ + 4])
    ya_t, yb_t, y1_t = scr("ya_s", [SEQ, 512]), scr("yb_s", [SEQ, 512]), scr("y1_s", [2048, SEQ])
    part_t = [[scr(f"part{i}_{tl}", [D, TT]) for tl in range(NTL)] for i in range(4)]
    summ_t = [[scr(f"summ{i}_{tl}", [D, TT]) for tl in range(NTL)] for i in range(4)]
    xs_t = [scr(f"xres{i}", [D, SEQ]) for i in range(3)]
    p0s, p1s = p0s_t.ap(), p1s_t.ap()
    fm = lambda ap: ap.rearrange("(kc q) t -> q kc t", q=128)

    cs = p.sbuf("cs", [128, 16], F32)
    scb = p.sbuf("scb", [128, 16], BF16)
    bias = [p.sbuf(f"adab{i}", [128, 48], F32) for i in range(4)]
    mod = [p.sbuf(f"mod{i}", [128, 48], F32) for i in range(4)]
    ops = [p.sbuf(f"ops{i}", [128, 16], F32) for i in range(4)]
    fw = p.sbuf("fw", [128, 16], F32)
    epsb = p.sbuf("epsb", [128, 1], F32)
    zt = p.sbuf("zt", [128, 4], F32)
    p.dma("pool", cs[:], cs_d, writes=["cs"])
    p.dma("pool", fw[:], fnw_d, writes=["fw"])
    for i in range(4):
        p.dma("pool", bias[i][:], adab_d[i], writes=[f"mod{i}b"])
    p.op("act", lambda e: e.activation(out=scb[:], in_=cs[:], func=AF.Silu), reads=["cs"], writes=["scb"])
    p.op("dve", lambda e: e.memset(epsb[:], EPS), writes=["epsb"])
    p.op("dve", lambda e: e.memset(zt[:], 0.0), writes=["zt"])
    for (t_, nch) in ((p0s, 31), (p1s, 41)):
        for kc in range(nch):
            p.dma("pool", t_[kc * 128:(kc + 1) * 128, 0:4], zt[:], reads=["zt"])

    def wseq(nm, used):
        ng = NG[nm.split("_")[0] if not nm.startswith("ada") else "ada"]
        return [(gbase[nm] + g, used) for g in range(ng)]

    def mk_ctx():
        cx = Ctx(p)
        cx.epsb = epsb
        return cx

    def mods(cx, wst, which, jlist):
        emit_mod2(cx, wst, scb, bias[which], mod[which], jlist, f"mod{which}")
        if jlist[0] == 0:
            p.op("dve", lambda e: e.tensor_scalar(out=ops[which][:], in0=mod[which][:, 16:32], scalar1=1.0, scalar2=None, op0=ALU.add),
                 reads=[f"mod{which}"], writes=[f"ops{which}"])

    def ada_seq(which, jlist):
        nm = ["ada_m0", "ada_f0", "ada_m1", "ada_f1"][which]
        return [(gbase[nm] + j // 2, 4096) for j in jlist[::2]]

    cc_tok = {}
    summ_idx = {}

    def load_x(xs, tl, base_ap, sum_ap, gate_which, store_ap, stg):
        t0 = tl * TT
        for k4 in range(4):
            p.dma("pool", xs[:, k4 * 4:k4 * 4 + 4, :], fm(base_ap)[:, k4 * 4:k4 * 4 + 4, t0:t0 + TT],
                  writes=[("x", kc) for kc in range(k4 * 4, k4 * 4 + 4)])
        if sum_ap is None:
            return
        p._wait("pool", cc_tok[(summ_idx[id(sum_ap)], tl)])
        for k2 in range(8):
            st_ = stg[k2 % 2]
            p.dma("pool", st_[:], fm(sum_ap[tl].ap())[:, k2 * 2:k2 * 2 + 2, :], writes=[("stg", k2 % 2)])
            for kk in range(2):
                kc = k2 * 2 + kk
                p.op("dve", lambda e: e.scalar_tensor_tensor(out=xs[:, kc, :], in0=st_[:, kk, :], scalar=mod[gate_which][:, 32 + kc:33 + kc],
                                                             in1=xs[:, kc, :], op0=ALU.mult, op1=ALU.add),
                     reads=[("stg", k2 % 2), ("x", kc), f"mod{gate_which}"], writes=[("x", kc)])
            if store_ap is not None and k2 % 2 == 1:
                k4 = k2 // 2
                p.dma("pool", fm(store_ap)[:, k4 * 4:k4 * 4 + 4, t0:t0 + TT], xs[:, k4 * 4:k4 * 4 + 4, :],
                      reads=[("x", kc) for kc in range(k4 * 4, k4 * 4 + 4)])

    def store_chunks(dst_fn):
        ost = [p.sbuf(f"ost{i}", [128, TT], F32) for i in range(4)]

        ctr = [0]

        def f(cx, tl):
            def evac(n, ps, pk):
                i_ = ctr[0] % 4
                ctr[0] += 1
                o_, ok = ost[i_], ("ost", i_)
                ev_copy(p, cx.ev_eng(), o_[:], ps[:], [pk], [ok])
                p.dma("pool", dst_fn(tl, n), o_[:], reads=[ok], writes=[("part", tl, n)])
            return evac
        return f

    def part_dst(i):
        return lambda tl, n: part_t[i][tl].ap()[n * 128:(n + 1) * 128, :]

    def allreduce_tile(i, tl):
        cc_tok[(i, tl)] = p.collective("AllReduce", ALU.add, RG, part_t[i][tl], summ_t[i][tl], [("part", tl, n) for n in range(16)], [])

    def phase_inproj(layer, ncc, dst_ap, base_ap, sum_ap, gate_which, store_ap, mod_prev_j):
        mk = p.scope_begin()
        cx = mk_ctx()
        wnm = f"in{layer}"
        mw = 2 * layer
        seq = []
        if mod_prev_j is not None:
            seq += ada_seq(gate_which, list(range(32, 48)))
        seq += ada_seq(mw, list(range(32)))
        ST = min(4, NTL)
        for _ in range(NTL // ST):
            seq += wseq(wnm, 4096)
        wst = WStreamC(p, ws, seq)
        if mod_prev_j is not None:
            mods(cx, wst, gate_which, list(range(32, 48)))
        mods(cx, wst, mw, list(range(32)))
        xs = p.sbuf("xs", [128, 16, TT], F32)
        hs = [p.sbuf(f"hs{j}", [128, 16, TT], BF16) for j in range(ST)]
        tmp = [p.sbuf(f"tmp{i}", [128, TT], F32) for i in range(3)]
        stat = (p.sbuf("rs", [128, TT], F32), p.sbuf("rstd", [128, TT], F32))
        stg = [p.sbuf(f"stg{i}", [128, 2, TT], F32) for i in range(2)]
        mkev = store_chunks(lambda tl, n: dst_ap[n * 128:(n + 1) * 128, 4 + tl * TT:4 + (tl + 1) * TT])
        for sup in range(NTL // ST):
            for j in range(ST):
                load_x(xs, sup * ST + j, base_ap, sum_ap, gate_which, store_ap, stg)
                emit_norm_mod(cx, xs, "x", hs[j], ("h", j), lambda kc: ops[mw][:, kc:kc + 1], lambda kc: mod[mw][:, kc:kc + 1],
                              [f"mod{mw}", f"ops{mw}"], tmp, stat)
            emit_gemm(cx, wst, 16, ncc, lambda j, kc: (hs[j][:, kc, :], (("h", j), kc)),
                      lambda j, n, ps, pk: mkev(cx, sup * ST + j)(n, ps, pk), cpg=2, ST=ST)
        p.scope_end(mk)

    def phase_outproj(layer, pi):
        mk = p.scope_begin()
        cx = mk_ctx()
        Kc = 8 if layer == 0 else 16
        ST = min(2, NTL)
        seq = []
        for _ in range(NTL // ST):
            seq += wseq(f"out{layer}", 4096)
        wst = WStreamC(p, ws, seq)
        bigs = [p.sbuf(f"big{j}", [128, Kc, TT], BF16) for j in range(ST)]
        mkev = store_chunks(part_dst(pi))
        if layer == 0:
            ident = p.sbuf("identf", [128, 128], F32)
            p.dma("pool", ident[:], ident_d, writes=["identf"])
            ytl = [p.sbuf(f"ytl{i}", [128, 4, 512], F32) for i in range(2)]
        else:
            yst = [p.sbuf(f"yst{i}", [128, 2, TT], F32) for i in range(2)]
        for sup in range(NTL // ST):
          for j in range(ST):
            tl = sup * ST + j
            t0 = tl * TT
            big = bigs[j]
            if layer == 0:
                for si, src in enumerate((ya_t.ap(), yb_t.ap())):
                    yt_, yk = ytl[si], ("ytl", si)
                    p.dma("pool", yt_[:], src[t0:t0 + TT, :].rearrange("(a q) c -> q a c", q=128), writes=[yk])
                    for c in range(4):
                        ps, pk = cx.next_ps()
                        for a in range(4):
                            p.op("pe", lambda e: e.transpose(out=ps[:, a * 128:(a + 1) * 128], in_=yt_[:, a, c * 128:(c + 1) * 128],
                                                             identity=ident[:]),
                                 reads=[yk, "identf"], writes=[pk], nosync_same=True)
                        ev_copy(p, cx.ev_eng(), big[:, si * 4 + c, :], ps[:], [pk], [(("big", j), si * 4 + c)])
            else:
                for k2 in range(8):
                    s_ = k2 % 2
                    p.dma("pool", yst[s_][:], fm(y1_t.ap())[:, k2 * 2:k2 * 2 + 2, t0:t0 + TT], writes=[("yst", s_)])
                    ev_copy(p, cx.ev_eng(), big[:, k2 * 2:k2 * 2 + 2, :], yst[s_][:], [("yst", s_)],
                            [(("big", j), k2 * 2), (("big", j), k2 * 2 + 1)])
          emit_gemm(cx, wst, Kc, 16, lambda j, kc: (bigs[j][:, kc, :], (("big", j), kc)),
                    lambda j, n, ps, pk: mkev(cx, sup * ST + j)(n, ps, pk), cpg=GSF // (Kc * 128), ST=ST)
          for j in range(ST):
            allreduce_tile(pi, sup * ST + j)
        p.scope_end(mk)

    def phase_ffn(layer, base_ap, sum_ap, store_ap, pi):
        mk = p.scope_begin()
        cx = mk_ctx()
        gm, mf = 2 * layer, 2 * layer + 1
        seq = ada_seq(gm, list(range(32, 48))) + ada_seq(mf, list(range(32)))
        ST = min(2, NTL)
        for _ in range(NTL // ST):
            seq += wseq(f"w13_{layer}", 4096) + wseq(f"w2_{layer}", 2816)
        wst = WStreamC(p, ws, seq)
        mods(cx, wst, gm, list(range(32, 48)))
        mods(cx, wst, mf, list(range(32)))
        xs = p.sbuf("xs", [128, 16, TT], F32)
        hs = [p.sbuf(f"hs{j}", [128, 16, TT], BF16) for j in range(ST)]
        gb = [p.sbuf(f"gb{j}", [128, 22, TT], BF16) for j in range(ST)]
        sil = [[p.sbuf(f"sil{r}_{j}", [128, TT], F32) for j in range(ST)] for r in range(2)]
        tmp = [p.sbuf(f"tmp{i}", [128, TT], F32) for i in range(3)]
        stat = (p.sbuf("rs", [128, TT], F32), p.sbuf("rstd", [128, TT], F32))
        stg = [p.sbuf(f"stg{i}", [128, 2, TT], F32) for i in range(2)]
        mkev = store_chunks(part_dst(pi))
        for sup in range(NTL // ST):
            for j in range(ST):
                load_x(xs, sup * ST + j, base_ap, sum_ap, gm, store_ap, stg)
                emit_norm_mod(cx, xs, "x", hs[j], ("h", j), lambda kc: ops[mf][:, kc:kc + 1], lambda kc: mod[mf][:, kc:kc + 1],
                              [f"mod{mf}", f"ops{mf}"], tmp, stat)

            def evac13(j, n, ps, pk):
                q, r = n // 4, n % 4
                if r < 2:
                    p.op("act", lambda e: e.activation(out=sil[r][j][:], in_=ps[:], func=AF.Silu), reads=[pk], writes=[("sil", r, j)])
                else:
                    jj = q * 2 + (r - 2)
                    p.op("dve", lambda e: e.tensor_tensor(out=gb[j][:, jj, :], in0=ps[:], in1=sil[r - 2][j][:], op=ALU.mult),
                         reads=[pk, ("sil", r - 2, j)], writes=[(("gb", j), jj)])
            emit_gemm(cx, wst, 16, 44, lambda j, kc: (hs[j][:, kc, :], (("h", j), kc)), evac13, cpg=2, ST=ST)
            emit_gemm(cx, wst, 22, 16, lambda j, kc: (gb[j][:, kc, :], (("gb", j), kc)),
                      lambda j, n, ps, pk: mkev(cx, sup * ST + j)(n, ps, pk), cpg=1, ST=ST)
            for j in range(ST):
                allreduce_tile(pi, sup * ST + j)
        p.scope_end(mk)

    def phase_final(base_ap, sum_ap):
        mk = p.scope_begin()
        cx = mk_ctx()
        wst = WStreamC(p, ws, ada_seq(3, list(range(32, 48))))
        mods(cx, wst, 3, list(range(32, 48)))
        xs = p.sbuf("xs", [128, 16, TT], F32)
        hs = p.sbuf("hs", [128, 16, TT], BF16)
        stat = (p.sbuf("rs", [128, TT], F32), p.sbuf("rstd", [128, TT], F32))
        stg = [p.sbuf(f"stg{i}", [128, 2, TT], F32) for i in range(2)]
        for tl in range(NTL):
            t0 = tl * TT
            load_x(xs, tl, base_ap, sum_ap, 3, None, stg)
            emit_norm_mod(cx, xs, "x", hs, "h", None, None, [], None, stat)
            for kc in range(16):
                p.op("dve", lambda e: e.scalar_tensor_tensor(out=xs[:, kc, :], in0=xs[:, kc, :], scalar=fw[:, kc:kc + 1], in1=stat[1][:],
                                                             op0=ALU.mult, op1=ALU.mult),
                     reads=[("x", kc), "fw", "rstd"], writes=[("x", kc)])
            for k4 in range(4):
                p.dma("pool", fm(oT)[:, k4 * 4:k4 * 4 + 4, t0:t0 + TT], xs[:, k4 * 4:k4 * 4 + 4, :],
                      reads=[("x", kc) for kc in range(k4 * 4, k4 * 4 + 4)])
        p.scope_end(mk)


    ident_d = p.dram("identf_d", [128, 128], F32, "ExternalInput")
    for i_ in range(4):
        summ_idx[id(summ_t[i_])] = i_
    p.barrier()
    phase_inproj(0, 31, p0s, xT, None, 0, None, None)
    mk = p.scope_begin()
    build_rwkv(p, {"rkvT": p0s[0:1536, 3:SEQ + 4], "lwT": p0s[29 * 128:29 * 128 + 64, 3:SEQ + 4],
                   "laT": p0s[29 * 128 + 64:30 * 128, 3:SEQ + 4], "lgT": p0s[30 * 128:31 * 128, 3:SEQ + 4], "ya": ya_t.ap()})
    p.scope_end(mk)
    mk = p.scope_begin()
    build_gdn(p, {"qkvT": p0s[12 * 128:24 * 128, 1:SEQ + 4], "zfm": p0s[24 * 128:28 * 128, 4:SEQ + 4],
                  "bT": p0s[28 * 128:28 * 128 + 4, 4:SEQ + 4], "aT": p0s[28 * 128 + 4:28 * 128 + 8, 4:SEQ + 4], "yb": yb_t.ap()})
    p.scope_end(mk)
    phase_outproj(0, 0)
    phase_ffn(0, xT, summ_t[0], xs_t[0].ap(), 1)
    phase_inproj(1, 41, p1s, xs_t[0].ap(), summ_t[1], 1, xs_t[1].ap(), True)
    mk = p.scope_begin()
    build_ssd(p, {"zT": p1s[0:2048, 4:SEQ + 4], "xT": p1s[2048:4096, 1:SEQ + 4], "bcT": p1s[4096:5120, 1:SEQ + 4],
                  "dtT": p1s[5120:5152, 4:SEQ + 4], "yT": y1_t.ap()})
    p.scope_end(mk)
    phase_outproj(1, 2)
    phase_ffn(1, xs_t[1].ap(), summ_t[2], xs_t[2].ap(), 3)
    phase_final(xs_t[2].ap(), summ_t[3])
    p.finish()
    return p


def _ssd_params(prm, hh):
    cw, cb = prm["ssm_conv_w"], prm["ssm_conv_b"]
    cix = np.arange(hh * 2048, (hh + 1) * 2048)
    cibc = np.concatenate([4096 + hh * 512 + np.arange(512), 5120 + hh * 512 + np.arange(512)])
    hs = slice(hh * 32, (hh + 1) * 32)
    selh = np.zeros((32, 32, 128), np.float32)
    for h in range(32):
        selh[h, h, :] = 1.0
    return {"cwx": np.ascontiguousarray(cw[:, cix].T.reshape(16, 128, 4).transpose(1, 0, 2)), "cbx": fm(cb[cix]),
            "cwbc": np.ascontiguousarray(cw[:, cibc].T.reshape(8, 128, 4).transpose(1, 0, 2)), "cbbc": fm(cb[cibc]),
            "hp": np.ascontiguousarray(np.stack([prm["ssm_dt_bias"][hs], prm["ssm_a_log"][hs]], axis=1)),
            "dsk": fm(np.repeat(prm["ssm_d"][hs], 64)), "nw": fm(prm["ssm_norm_w"][hh * 2048:(hh + 1) * 2048]),
            "ident": np.eye(128, dtype=np.float32), "selh": selh, "maskT": np.triu(np.ones((128, 128), np.float32))}


def _gdn_params(prm, hh):
    cidx = np.concatenate([w * 1024 + hh * 512 + np.arange(512) for w in range(3)])
    cw = prm["gdn_conv_w"][:, cidx]
    m = {"cw": np.ascontiguousarray(cw.T.reshape(12, 128, 4).transpose(1, 0, 2)),
         "hp": np.ascontiguousarray(np.stack([prm["gdn_a_log"][hh * 4:hh * 4 + 4], prm["gdn_dt_bias"][hh * 4:hh * 4 + 4]], axis=1)),
         "nwrow": np.ascontiguousarray(np.broadcast_to(prm["gdn_norm_w"][None, :], (128, 128)))}
    m.update(gdn_consts())
    return m


def _rwkv_params(prm, hh):
    mu = prm["rwkv_mu"]
    my = slice(hh * 512, (hh + 1) * 512)
    cols = np.zeros((128, 64), np.float32)

    def put(name, vec512):
        cols[:, PRM_COLS[name]:PRM_COLS[name] + 4] = vec512.reshape(4, 128).T
    put("mu_r", mu[0:1024][my]); put("mu_k", mu[1024:2048][my]); put("mu_v", mu[2048:3072][my])
    put("om_r", 1 - mu[0:1024][my]); put("om_k", 1 - mu[1024:2048][my]); put("om_v", 1 - mu[2048:3072][my])
    put("w0", prm["rwkv_w0"][my]); put("a0", prm["rwkv_a0"][my]); put("k_k", prm["rwkv_k_k"][my])
    put("k_a", prm["rwkv_k_a"][my]); put("omka", 1 - prm["rwkv_k_a"][my]); put("r_k", prm["rwkv_r_k"].reshape(-1)[my])
    lmu = np.zeros((128, 6), np.float32)
    lmu[0:64, 0] = mu[3072:3136]; lmu[0:64, 1] = 1 - mu[3072:3136]
    lmu[0:64, 2] = mu[3136:3200]; lmu[0:64, 3] = 1 - mu[3136:3200]
    lmu[:, 4] = mu[3200:3328]; lmu[:, 5] = 1 - mu[3200:3328]

    def rows_tm(v512):
        a = v512.reshape(4, 2, 64)
        return np.ascontiguousarray(np.repeat(a.transpose(1, 0, 2)[:, None, :, :], 64, axis=1).reshape(128, 4, 64))
    m = {"prm": cols, "lmu": lmu, "w2": np.ascontiguousarray(prm["rwkv_w2"][:, my]), "a2": np.ascontiguousarray(prm["rwkv_a2"][:, my]),
         "g2": np.ascontiguousarray(prm["rwkv_g2"][:, my]), "lnw": rows_tm(prm["rwkv_ln_w"][my]), "lnb": rows_tm(prm["rwkv_ln_b"][my])}
    m.update(rwkv_consts())
    return m


def fused_inmaps(I, cores=range(8)):
    f32 = lambda a: np.ascontiguousarray(np.asarray(a, dtype=np.float32))
    I = {k: f32(v) for k, v in I.items()}
    G0 = 3328
    ada = [pack_wg(I["ada_mix_w"][0]), pack_wg(I["ada_ffn_w"][0]), pack_wg(I["ada_mix_w"][1]), pack_wg(I["ada_ffn_w"][1])]
    prm_r = {k: I[k][0] for k in ("rwkv_mu", "rwkv_w0", "rwkv_w2", "rwkv_a0", "rwkv_a2", "rwkv_g2", "rwkv_k_k", "rwkv_k_a",
                                  "rwkv_r_k", "rwkv_ln_w", "rwkv_ln_b")}
    prm_g = {k: I[k][0] for k in ("gdn_conv_w", "gdn_a_log", "gdn_dt_bias", "gdn_norm_w")}
    prm_s = {k: I[k][0] for k in ("ssm_conv_w", "ssm_conv_b", "ssm_dt_bias", "ssm_a_log", "ssm_d", "ssm_norm_w")}
    per_hh = {}
    for hh in sorted({c_ % 2 for c_ in cores}):
        W = I["hg_w_in"][0]
        my = lambda base: W[:, base + hh * 512: base + (hh + 1) * 512]
        ba = np.zeros((2048, 128), np.float32)
        ba[:, 0:4] = W[:, G0 + 4096 + hh * 4: G0 + 4096 + hh * 4 + 4]
        ba[:, 4:8] = W[:, G0 + 4104 + hh * 4: G0 + 4104 + hh * 4 + 4]
        Win0 = np.concatenate([my(0), my(1024), my(2048), my(G0), my(G0 + 1024), my(G0 + 2048), my(G0 + 3072), ba,
                               W[:, 3072:3200], W[:, 3200:3328]], axis=1)
        W1s = I["ssm_w_in"][0]
        dtp = np.zeros((2048, 128), np.float32)
        dtp[:, 0:32] = W1s[:, 10240 + hh * 32: 10240 + (hh + 1) * 32]
        Win1 = np.concatenate([W1s[:, hh * 2048:(hh + 1) * 2048], W1s[:, 4096 + hh * 2048: 4096 + (hh + 1) * 2048],
                               W1s[:, 8192 + hh * 512: 8192 + (hh + 1) * 512], W1s[:, 9216 + hh * 512: 9216 + (hh + 1) * 512], dtp], axis=1)
        Wo0 = np.concatenate([I["hg_w_out"][0][hh * 512:(hh + 1) * 512], I["hg_w_out"][0][1024 + hh * 512: 1024 + (hh + 1) * 512]], axis=0)
        Wo1 = I["ssm_w_out"][0][hh * 2048:(hh + 1) * 2048]
        lay = []
        for l in range(2):
            W1 = I["ffn_w1"][l][:, hh * 2816:(hh + 1) * 2816]
            W3 = I["ffn_w3"][l][:, hh * 2816:(hh + 1) * 2816]
            W13 = np.concatenate([np.concatenate([W1[:, q * 256:(q + 1) * 256], W3[:, q * 256:(q + 1) * 256]], axis=1) for q in range(11)], axis=1)
            lay.append((pack_wg(W13), pack_wg(I["ffn_w2"][l][hh * 2816:(hh + 1) * 2816])))
        ws = np.concatenate(ada + [pack_wg(Win0), pack_wg(Wo0), lay[0][0], lay[0][1], pack_wg(Win1), pack_wg(Wo1), lay[1][0], lay[1][1]], axis=0)
        m = {"ws": np.ascontiguousarray(ws)}
        for pre, d in (("rwkv_", _rwkv_params(prm_r, hh)), ("gdn_", _gdn_params(prm_g, hh)), ("ssd_", _ssd_params(prm_s, hh))):
            for k, v in d.items():
                m[pre + k] = v
        per_hh[hh] = m
    adab = np.ascontiguousarray(np.stack([fm(I["ada_mix_b"][0]), fm(I["ada_ffn_b"][0]), fm(I["ada_mix_b"][1]), fm(I["ada_ffn_b"][1])]))
    in_maps = []
    for core in cores:
        b, hh = core // 2, core % 2
        m = dict(per_hh[hh])
        m.update({"xT": np.ascontiguousarray(I["x"][b, :SEQ].T), "cs": fm(I["c"][b]), "adab": adab, "fnw": fm(I["final_norm_w"]),
                  "identf_d": np.eye(128, dtype=np.float32)})
        in_maps.append(m)
    return in_maps


def kernel(**inputs):
    p = _prog("fused", build_fused)
    in_maps = fused_inmaps(inputs)
    r = _run(p, in_maps)
    out = np.empty((4, SEQ, D), np.float32)
    for b in range(4):
        out[b] = np.asarray(r[2 * b]["oT"]).T
    return out
```
